# Optimizing a Trainium2 kernel written in Bass

```python
import jax, jax.numpy as jnp
from jax import lax
import numpy as np

D_MODEL = 2048
BATCH = 2
SEQ = 4096
DEPTH = 4
DEC_BATCH = 8
DEC_SEQ = 16
PAST_LEN = 2048

CHUNK = 64
N_A = DEPTH // 2
N_B = DEPTH - N_A
EPS = 1e-6
A_HEADS = 4
A_DK = D_MODEL // (2 * A_HEADS)
A_DV = D_MODEL // A_HEADS
A_PROJ = 2 * A_HEADS * A_DK + 2 * A_HEADS * A_DV + 2 * A_HEADS
B_HEADS = D_MODEL // 128
Q_LORA = 768
KV_LORA = 512
NOPE = 128
ROPE = 64
V_DIM = 128
ROPE_BASE = 10000.0
Q_BLOCK = 128
ATTN_SCALE = (NOPE + ROPE) ** -0.5
D_FF = 5632
CONV_W = 3

kernel_name = "mlstm_mla_yoco_streaming_encoder_step"


def rmsnorm(x, g):
    xf = x.astype(jnp.float32)
    y = xf * lax.rsqrt(jnp.mean(xf * xf, axis=-1, keepdims=True) + EPS)
    return (y * g.astype(jnp.float32)).astype(x.dtype)


def rope(x, pos):
    half = ROPE // 2
    inv = ROPE_BASE ** (-jnp.arange(half, dtype=jnp.float32) / half)
    ang = pos.astype(jnp.float32)[:, None] * inv[None, :]
    shape = (1, pos.shape[0]) + (1,) * (x.ndim - 3) + (half,)
    cos = jnp.cos(ang).reshape(shape).astype(x.dtype)
    sin = jnp.sin(ang).reshape(shape).astype(x.dtype)
    x1, x2 = x[..., :half], x[..., half:]
    return jnp.concatenate([x1 * cos - x2 * sin, x1 * sin + x2 * cos], axis=-1)


def mlstm_scan(q, k, v, logi, logf, C0, n0, m0, L):
    B, T, H, DK = q.shape
    DV = v.shape[-1]
    nc = T // L

    def to_chunks(a):
        a = a.reshape((B, nc, L, H) + a.shape[3:])
        return a.transpose((1, 0, 3, 2) + tuple(range(4, a.ndim)))

    tri = jnp.tril(jnp.ones((L, L), dtype=bool))

    def step(carry, inp):
        C, n, m = carry
        qc, kc, vc, ic, fc = inp
        b = jnp.cumsum(fc, axis=-1)
        d_log = jnp.where(tri, b[..., :, None] - b[..., None, :] + ic[..., None, :], -jnp.inf)
        inter_log = b + m[..., None]
        m_t = jnp.maximum(inter_log, jnp.max(d_log, axis=-1))
        dmat = jnp.exp(d_log - m_t[..., None])
        inter_w = jnp.exp(inter_log - m_t)
        s = jnp.einsum('bhtk,bhsk->bhts', qc, kc) * dmat
        num = jnp.einsum('bhts,bhsv->bhtv', s, vc) + inter_w[..., None] * jnp.einsum('bhtk,bhvk->bhtv', qc, C)
        qn = jnp.sum(s, axis=-1) + inter_w * jnp.einsum('bhtk,bhk->bht', qc, n)
        h = num / jnp.maximum(jnp.abs(qn), jnp.exp(-m_t))[..., None]
        m_new = m_t[..., -1]
        decay = jnp.exp(b[..., -1] + m - m_new)
        w = jnp.exp(b[..., -1:] - b + ic - m_new[..., None])
        C_new = decay[..., None, None] * C + jnp.einsum('bhs,bhsv,bhsk->bhvk', w, vc, kc)
        n_new = decay[..., None] * n + jnp.einsum('bhs,bhsk->bhk', w, kc)
        return (C_new, n_new, m_new), h

    (C, n, m), hs = lax.scan(step, (C0, n0, m0),
                             (to_chunks(q), to_chunks(k), to_chunks(v), to_chunks(logi), to_chunks(logf)))
    h = hs.transpose(1, 0, 3, 2, 4).reshape(B, T, H, DV)
    return h, C, n, m


def mlstm_mixer(h, w_in, b_gate, g_head, w_out, C0, n0, m0, L):
    B, T, _ = h.shape
    sq, sv = A_HEADS * A_DK, A_HEADS * A_DV
    z = h @ w_in
    q = z[..., :sq].reshape(B, T, A_HEADS, A_DK).astype(jnp.float32)
    k = z[..., sq:2 * sq].reshape(B, T, A_HEADS, A_DK).astype(jnp.float32) * (A_DK ** -0.5)
    v = z[..., 2 * sq:2 * sq + sv].reshape(B, T, A_HEADS, A_DV).astype(jnp.float32)
    o = z[..., 2 * sq + sv:2 * sq + 2 * sv].reshape(B, T, A_HEADS, A_DV).astype(jnp.float32)
    g = (z[..., 2 * sq + 2 * sv:] + b_gate).astype(jnp.float32)
    logi = g[..., :A_HEADS]
    logf = jax.nn.log_sigmoid(g[..., A_HEADS:])
    hh, C, n, m = mlstm_scan(q, k, v, logi, logf, C0.astype(jnp.float32), n0.astype(jnp.float32),
                             m0.astype(jnp.float32), L)
    hh = rmsnorm(hh, g_head)
    out = (jax.nn.sigmoid(o) * hh).reshape(B, T, sv).astype(h.dtype) @ w_out
    return out, C, n, m


def _attend_block(qn, qr, qpos, kn, kr, v, kpos):
    s = (jnp.einsum('bqhd,bkhd->bhqk', qn, kn) + jnp.einsum('bqhr,bkr->bhqk', qr, kr)).astype(jnp.float32) * ATTN_SCALE
    visible = (kpos[None, :] // CHUNK) <= (qpos[:, None] // CHUNK)
    s = jnp.where(visible[None, None], s, -jnp.inf)
    p = jax.nn.softmax(s, axis=-1).astype(v.dtype)
    return jnp.einsum('bhqk,bkhd->bqhd', p, v)


def block_attention(qn, qr, qpos, kn, kr, v, kpos):
    B, T, H, _ = qn.shape
    blk = min(Q_BLOCK, T)
    nb = T // blk

    def split(a):
        return a.reshape((B, nb, blk) + a.shape[2:]).swapaxes(0, 1)

    out = lax.map(lambda a: _attend_block(a[0], a[1], a[2], kn, kr, v, kpos),
                  (split(qn), split(qr), qpos.reshape(nb, blk)))
    return out.swapaxes(0, 1).reshape(B, T, H, V_DIM)


def shared_kv(x, qpos, ckv_past, kpe_past, kv_norm, kv_w_down, kv_g_c, kv_g_r, kv_w_up, kv_g_kn):
    B, T, _ = x.shape
    z = rmsnorm(x, kv_norm) @ kv_w_down
    c = rmsnorm(z[..., :KV_LORA], kv_g_c)
    kp = rope(rmsnorm(z[..., KV_LORA:], kv_g_r), qpos)
    if ckv_past is None:
        c_all, kp_all, kpos = c, kp, qpos
    else:
        c_all = jnp.concatenate([ckv_past.astype(c.dtype), c], axis=1)
        kp_all = jnp.concatenate([kpe_past.astype(kp.dtype), kp], axis=1)
        kpos = jnp.arange(ckv_past.shape[1] + T, dtype=jnp.int32)
    S = c_all.shape[1]
    kv = (c_all @ kv_w_up).reshape(B, S, B_HEADS, NOPE + V_DIM)
    kn = rmsnorm(kv[..., :NOPE], kv_g_kn)
    v = kv[..., NOPE:]
    return c, kp, kn, kp_all, v, kpos


def mla_mixer(h, qpos, kn, kr, v, kpos, w_dq, g_cq, w_uq, g_qn, g_qr, w_o):
    B, T, _ = h.shape
    cq = rmsnorm(h @ w_dq, g_cq)
    qf = (cq @ w_uq).reshape(B, T, B_HEADS, NOPE + ROPE)
    qn = rmsnorm(qf[..., :NOPE], g_qn)
    qr = rope(rmsnorm(qf[..., NOPE:], g_qr), qpos)
    o = block_attention(qn, qr, qpos, kn, kr, v, kpos)
    return o.reshape(B, T, B_HEADS * V_DIM) @ w_o


def conv_ffn(h, w_up, conv_w, conv_b, w_down, conv_prev):
    T = h.shape[1]
    u = h @ w_up
    u_pad = jnp.concatenate([conv_prev.astype(u.dtype), u], axis=1)
    c = conv_b
    for j in range(CONV_W):
        c = c + conv_w[j] * u_pad[:, j:j + T]
    gate, val = c[..., :D_FF], c[..., D_FF:]
    return (jax.nn.silu(gate) * val) @ w_down, u_pad[:, -(CONV_W - 1):]


def trunk(x, pos0, ckv_past, kpe_past, C0, n0, m0, conv0,
          norm_mix, norm_ffn, a_w_in, a_b_gate, a_g_head, a_w_out,
          kv_norm, kv_w_down, kv_g_c, kv_g_r, kv_w_up, kv_g_kn,
          b_w_dq, b_g_cq, b_w_uq, b_g_qn, b_g_qr, b_w_o,
          f_w_up, f_conv_w, f_conv_b, f_w_down):
    B, T, _ = x.shape
    qpos = pos0 + jnp.arange(T, dtype=jnp.int32)
    L = CHUNK if T % CHUNK == 0 else T
    Cs, ns, ms, convs = [], [], [], []
    c_new = kp_new = kn = kr = v = kpos = None
    for layer in range(DEPTH):
        h = rmsnorm(x, norm_mix[layer])
        if layer < N_A:
            y, C, n, m = mlstm_mixer(h, a_w_in[layer], a_b_gate[layer], a_g_head[layer], a_w_out[layer],
                                     C0[layer], n0[layer], m0[layer], L)
            Cs.append(C.astype(x.dtype)); ns.append(n.astype(x.dtype)); ms.append(m.astype(x.dtype))
        else:
            j = layer - N_A
            y = mla_mixer(h, qpos, kn, kr, v, kpos, b_w_dq[j], b_g_cq[j], b_w_uq[j], b_g_qn[j], b_g_qr[j], b_w_o[j])
        x = x + y
        f, cst = conv_ffn(rmsnorm(x, norm_ffn[layer]), f_w_up[layer], f_conv_w[layer], f_conv_b[layer],
                          f_w_down[layer], conv0[layer])
        x = x + f
        convs.append(cst)
        if layer == N_A - 1:
            c_new, kp_new, kn, kr, v, kpos = shared_kv(x, qpos, ckv_past, kpe_past, kv_norm, kv_w_down,
                                                       kv_g_c, kv_g_r, kv_w_up, kv_g_kn)
    return x, c_new, kp_new, jnp.stack(Cs), jnp.stack(ns), jnp.stack(ms), jnp.stack(convs)


def setup_inputs(seed: int = 0) -> dict:
    key = jax.random.key(seed)
    ks = jax.random.split(key, 40)
    f32 = jnp.float32

    def nrm(k, shape, fan_in):
        return jax.random.normal(k, shape, f32) * (fan_in ** -0.5)

    def gain(k, shape):
        return 1.0 + 0.05 * jax.random.normal(k, shape, f32)

    gate_bias = jnp.concatenate([0.1 * jax.random.normal(ks[8], (N_A, A_HEADS), f32),
                                 3.0 + 0.5 * jax.random.normal(ks[9], (N_A, A_HEADS), f32)], axis=-1)
    return {
        "x_prompt": jax.random.normal(ks[0], (BATCH, SEQ, D_MODEL), f32),
        "x_sample": jax.random.normal(ks[1], (DEC_BATCH, DEC_SEQ, D_MODEL), f32),
        "cache_ckv": jax.random.normal(ks[2], (DEC_BATCH, PAST_LEN, KV_LORA), f32),
        "cache_kpe": jax.random.normal(ks[3], (DEC_BATCH, PAST_LEN, ROPE), f32),
        "state_C": 0.1 * jax.random.normal(ks[4], (N_A, DEC_BATCH, A_HEADS, A_DV, A_DK), f32),
        "state_n": 0.1 * jax.random.normal(ks[5], (N_A, DEC_BATCH, A_HEADS, A_DK), f32),
        "state_m": jax.random.normal(ks[6], (N_A, DEC_BATCH, A_HEADS), f32),
        "state_conv": jax.random.normal(ks[7], (DEPTH, DEC_BATCH, CONV_W - 1, 2 * D_FF), f32),
        "norm_mix": gain(ks[10], (DEPTH, D_MODEL)),
        "norm_ffn": gain(ks[11], (DEPTH, D_MODEL)),
        "a_w_in": nrm(ks[12], (N_A, D_MODEL, A_PROJ), D_MODEL),
        "a_b_gate": gate_bias,
        "a_g_head": gain(ks[13], (N_A, A_HEADS, A_DV)),
        "a_w_out": nrm(ks[14], (N_A, A_HEADS * A_DV, D_MODEL), A_HEADS * A_DV),
        "kv_norm": gain(ks[15], (D_MODEL,)),
        "kv_w_down": nrm(ks[16], (D_MODEL, KV_LORA + ROPE), D_MODEL),
        "kv_g_c": gain(ks[17], (KV_LORA,)),
        "kv_g_r": gain(ks[18], (ROPE,)),
        "kv_w_up": nrm(ks[19], (KV_LORA, B_HEADS * (NOPE + V_DIM)), KV_LORA),
        "kv_g_kn": gain(ks[20], (NOPE,)),
        "b_w_dq": nrm(ks[21], (N_B, D_MODEL, Q_LORA), D_MODEL),
        "b_g_cq": gain(ks[22], (N_B, Q_LORA)),
        "b_w_uq": nrm(ks[23], (N_B, Q_LORA, B_HEADS * (NOPE + ROPE)), Q_LORA),
        "b_g_qn": gain(ks[24], (N_B, NOPE)),
        "b_g_qr": gain(ks[25], (N_B, ROPE)),
        "b_w_o": nrm(ks[26], (N_B, B_HEADS * V_DIM, D_MODEL), B_HEADS * V_DIM),
        "f_w_up": nrm(ks[27], (DEPTH, D_MODEL, 2 * D_FF), D_MODEL),
        "f_conv_w": nrm(ks[28], (DEPTH, CONV_W, 2 * D_FF), CONV_W),
        "f_conv_b": 0.02 * jax.random.normal(ks[29], (DEPTH, 2 * D_FF), f32),
        "f_w_down": nrm(ks[30], (DEPTH, D_FF, D_MODEL), D_FF),
    }


def reference(x_prompt, x_sample, cache_ckv, cache_kpe, state_C, state_n, state_m, state_conv,
              norm_mix, norm_ffn, a_w_in, a_b_gate, a_g_head, a_w_out,
              kv_norm, kv_w_down, kv_g_c, kv_g_r, kv_w_up, kv_g_kn,
              b_w_dq, b_g_cq, b_w_uq, b_g_qn, b_g_qr, b_w_o,
              f_w_up, f_conv_w, f_conv_b, f_w_down):
    params = (norm_mix, norm_ffn, a_w_in, a_b_gate, a_g_head, a_w_out,
              kv_norm, kv_w_down, kv_g_c, kv_g_r, kv_w_up, kv_g_kn,
              b_w_dq, b_g_cq, b_w_uq, b_g_qn, b_g_qr, b_w_o,
              f_w_up, f_conv_w, f_conv_b, f_w_down)
    B = x_prompt.shape[0]
    C0 = jnp.zeros((N_A, B, A_HEADS, A_DV, A_DK), jnp.float32)
    n0 = jnp.zeros((N_A, B, A_HEADS, A_DK), jnp.float32)
    m0 = jnp.zeros((N_A, B, A_HEADS), jnp.float32)
    conv0 = jnp.zeros((DEPTH, B, CONV_W - 1, 2 * D_FF), x_prompt.dtype)
    y_prompt, p_ckv, p_kpe, p_C, p_n, p_m, p_conv = trunk(x_prompt, 0, None, None, C0, n0, m0, conv0, *params)
    y_sample, s_ckv, s_kpe, s_C, s_n, s_m, s_conv = trunk(x_sample, PAST_LEN, cache_ckv, cache_kpe,
                                                          state_C, state_n, state_m, state_conv, *params)
    return (y_prompt, y_sample, p_ckv, p_kpe, p_C, p_n, p_m, p_conv,
            s_ckv, s_kpe, s_C, s_n, s_m, s_conv)
```

```python
import numpy as np
from contextlib import ExitStack
import concourse.bass as bass
import concourse.mybir as mybir
from concourse.bass_utils import run_bass_kernel_spmd

F32 = mybir.dt.float32
BF16 = mybir.dt.bfloat16
AF = mybir.ActivationFunctionType
ALU = mybir.AluOpType
AX = mybir.AxisListType

D = 2048; KC = 16; SEQ = 4096; DEPTH = 4; NA = 2
DS = 16; PAST = 2048
H = 4; DK = 256; DV = 512; APROJ = 6152
BH = 16; QL = 768; KVL = 512; NOPE = 128; ROPE = 64; VD = 128
DFF = 5632; FC = 44
EPS = 1e-6
SEG = 512
NSEG = SEQ // SEG
NR = 2
XW = 2 * H * DV + 16
KVROWS = BH * 128 + ROPE + 4 * SEG
GROUPS = [[0, 1, 2, 3], [4, 5, 6, 7]]
ATT_SCALE = (NOPE + ROPE) ** -0.5
NEG = -1.0e30


class Buf:
    def __init__(self, name=""):
        self.name = name
        self.w = None
        self.r = []


class K:
    def __init__(self, nc, stack):
        self.nc = nc
        self.stack = stack
        self.eng = {'pe': nc.tensor, 'dve': nc.vector, 'act': nc.scalar, 'pool': nc.gpsimd, 'sp': nc.sync}
        self.sem = {}
        self.cnt = {}
        self.waited = {e: {} for e in self.eng}
        for e in self.eng:
            self.sem[e] = stack.enter_context(nc.semaphore('s_' + e))
            self.cnt[e] = 0
        self.nins = 0

    def new_dma_sem(self, key):
        self.sem[key] = self.stack.enter_context(self.nc.semaphore(key))
        self.cnt[key] = 0
        return key

    def _wait(self, e, deps):
        need = {}
        for d in deps:
            if d is None:
                continue
            k, v = d
            if e == 'pe' and k == 'pe':
                continue
            if v > need.get(k, 0):
                need[k] = v
        for k, v in need.items():
            if self.waited[e].get(k, 0) >= v:
                continue
            self.eng[e].wait_ge(self.sem[k], v)
            self.waited[e][k] = v

    @staticmethod
    def _deps(reads, writes):
        deps = []
        for b in reads:
            deps.append(b.w)
        for b in writes:
            deps.append(b.w)
            deps.extend(b.r)
        return deps

    def op(self, e, fn, reads=(), writes=()):
        self._wait(e, self._deps(reads, writes))
        ins = fn()
        self.cnt[e] += 1
        self.nins += 1
        ins.then_inc(self.sem[e], 1)
        tag = (e, self.cnt[e])
        for b in reads:
            b.r.append(tag)
            if len(b.r) > 24:
                b.r = b.r[-24:] if False else self._compact(b.r)
        for b in writes:
            b.w = tag
            b.r = []
        return ins

    @staticmethod
    def _compact(r):
        m = {}
        for k, v in r:
            if v > m.get(k, 0):
                m[k] = v
        return list(m.items())

    def dma(self, q, out, in_, reads=(), writes=(), dsem='io', **kw):
        if dsem in ('io', 'out', 'kv', 'ld'):
            dsem = self.pool[self.pi % len(self.pool)]
            self.pi += 1
        prev = self.cnt[dsem]
        self._wait(q, self._deps(reads, writes) + ([(dsem, prev)] if prev else []))
        ins = self.eng[q].dma_start(out=out, in_=in_, **kw)
        self.cnt[dsem] += 16
        self.nins += 1
        ins.then_inc(self.sem[dsem], 16)
        tag = (dsem, self.cnt[dsem])
        for b in reads:
            b.r.append(tag)
        for b in writes:
            b.w = tag
            b.r = []
        return ins

    def collective(self, src, dst, groups, reads=(), writes=()):
        self._wait('pool', self._deps(reads, writes))
        ins = self.nc.gpsimd.collective_compute("AllGather", mybir.AluOpType.bypass, replica_groups=groups,
                                                ins=[src.opt()], outs=[dst.opt()])
        self.cnt['cc'] += 1
        self.nins += 1
        ins.then_inc(self.sem['cc'], 1)
        tag = ('cc', self.cnt['cc'])
        for b in reads:
            b.r.append(tag)
        for b in writes:
            b.w = tag
            b.r = []
        return ins

    def barrier(self):
        allv = [(k, v) for k, v in self.cnt.items() if v > 0]
        for e in self.eng:
            self._wait(e, allv)


def bc(ap, shape):
    return ap.broadcast_to(list(shape))


STOP = None


class _Stop(Exception):
    pass


class Prog:
    def dbg(self, tag):
        if STOP is not None and tag == STOP:
            raise _Stop()

    def __init__(self):
        self.nc = nc = bass.Bass("TRN2", target_bir_lowering=False)
        self.st = ExitStack()
        self.k = K(nc, self.st)
        for s in ['w0', 'w1', 'w2', 'w3', 'cc']:
            self.k.new_dma_sem(s)
        self.k.pool = [self.k.new_dma_sem('p%d' % i) for i in range(40)]
        self.k.pi = 0
        self.build()

    def din(self, name, shape):
        return self.nc.dram_tensor(name, list(shape), F32, kind="ExternalInput").ap()

    def dout(self, name, shape):
        return self.nc.dram_tensor(name, list(shape), F32, kind="ExternalOutput").ap()

    def dint(self, name, shape, dt=F32):
        return self.nc.dram_tensor(name, list(shape), dt, kind="Internal").ap()

    def sb(self, name, shape, dt=F32):
        return self.st.enter_context(self.nc.sbuf_tensor(name, list(shape), dt))

    def fv(self, a, n, rows=128):
        return self.arena[0:rows, a:a + n]

    def bv(self, a, n, rows=128):
        assert n % 2 == 0
        return self.arena[0:rows, a:a + n // 2].bitcast(BF16)

    def V(self, e='dve'):
        return self.nc.vector if e == 'dve' else self.nc.gpsimd

    def tt(self, out, a, b, op, R, W, e='dve'):
        self.k.op(e, lambda: self.V(e).tensor_tensor(out=out, in0=a, in1=b, op=op), R, W)

    def ts(self, out, a, s1, s2, op0, op1, R, W, e='dve'):
        if op1 is None:
            self.k.op(e, lambda: self.V(e).tensor_scalar(out=out, in0=a, scalar1=s1, scalar2=None, op0=op0), R, W)
        else:
            self.k.op(e, lambda: self.V(e).tensor_scalar(out=out, in0=a, scalar1=s1, scalar2=s2, op0=op0, op1=op1), R, W)

    def stt(self, out, a, s, b, op0, op1, R, W):
        self.k.op('dve', lambda: self.nc.vector.scalar_tensor_tensor(out=out, in0=a, scalar=s, in1=b, op0=op0, op1=op1), R, W)

    def cp(self, out, a, R, W, e='dve'):
        if e == 'act':
            self.k.op('act', lambda: self.nc.scalar.copy(out=out, in_=a), R, W)
        else:
            self.k.op(e, lambda: self.V(e).tensor_copy(out=out, in_=a), R, W)

    def act(self, out, a, func, R, W, bias=None, scale=None, accum=None):
        kw = {}
        if bias is not None:
            kw['bias'] = bias
        if scale is not None:
            kw['scale'] = scale
        if accum is not None:
            kw['accum_out'] = accum
        self.k.op('act', lambda: self.nc.scalar.activation(out=out, in_=a, func=func, **kw), R, W)

    def mm(self, out, lhsT, rhs, start, stop, R, W):
        self.k.op('pe', lambda: self.nc.tensor.matmul(out, lhsT=lhsT, rhs=rhs, start=start, stop=stop), R, W)

    def tr(self, out, a, n, R, W, bf=False):
        idn = self.ident_b if bf else self.ident
        self.k.op('pe', lambda: self.nc.tensor.transpose(out, a, idn[0:n, 0:n]), list(R) + [self.b_const], W)

    def ms(self, ap, c, W, e='dve'):
        self.k.op(e, lambda: self.V(e).memset(ap, c), (), W)

    def recip(self, out, a, R, W):
        self.k.op('dve', lambda: self.nc.vector.reciprocal(out=out, in_=a), R, W)

    def load_fm(self, dst, src1d, n, rows=128):
        tmp = self.fv(self.O_TMP, 128)
        if not hasattr(self, 'b_tmpfm'):
            self.b_tmpfm = Buf()
        tb = self.b_tmpfm
        self.k.dma('sp', tmp[0:n, 0:rows], src1d.rearrange("(c p) -> c p", p=rows), (), [tb])
        ps, pb = self.ps[7], self.bps[7]
        self.tr(ps[0:rows, 0:n], tmp[0:n, 0:rows], n, [tb], [pb])
        self.cp(dst, ps[0:rows, 0:n], [pb], [self.b_gn])

    def build(self):
        nc, k = self.nc, self.k
        d = {}
        d['xp'] = self.din('xp', [NR * SEG, D]); d['xs'] = self.din('xs', [DS, D])
        d['ckv'] = self.din('ckv', [PAST, KVL]); d['kpe'] = self.din('kpe', [PAST, ROPE])
        d['sC'] = self.din('sC', [NA, H, DV, DK]); d['sn'] = self.din('sn', [NA, H * DK]); d['sm'] = self.din('sm', [NA, H])
        d['sconv'] = self.din('sconv', [DEPTH, 2, 2 * DFF])
        d['norm_mix'] = self.din('norm_mix', [DEPTH, D]); d['norm_ffn'] = self.din('norm_ffn', [DEPTH, D])
        d['a_w_in'] = self.din('a_w_in', [NA, D, APROJ]); d['a_b_gate'] = self.din('a_b_gate', [NA, 8])
        d['a_g_head'] = self.din('a_g_head', [NA, H * DV]); d['a_w_out'] = self.din('a_w_out', [NA, D, D])
        d['kv_norm'] = self.din('kv_norm', [1, D]); d['kv_w_down'] = self.din('kv_w_down', [D, KVL + ROPE])
        d['kv_g_c'] = self.din('kv_g_c', [1, KVL]); d['kv_g_r'] = self.din('kv_g_r', [1, ROPE])
        d['kv_w_up'] = self.din('kv_w_up', [KVL, BH * 256]); d['kv_g_kn'] = self.din('kv_g_kn', [1, NOPE])
        d['b_w_dq'] = self.din('b_w_dq', [2, D, QL]); d['b_g_cq'] = self.din('b_g_cq', [2, QL])
        d['b_w_uq'] = self.din('b_w_uq', [2, QL, BH * 192]); d['b_g_qn'] = self.din('b_g_qn', [2, NOPE])
        d['b_g_qr'] = self.din('b_g_qr', [2, ROPE]); d['b_w_o'] = self.din('b_w_o', [2, D, D])
        d['f_w_up'] = self.din('f_w_up', [DEPTH, D, 2 * DFF]); d['f_conv_w'] = self.din('f_conv_w', [DEPTH, 3, 2 * DFF])
        d['f_conv_b'] = self.din('f_conv_b', [DEPTH, 2 * DFF]); d['f_w_down'] = self.din('f_w_down', [DEPTH, DFF, D])
        d['cos2'] = self.din('cos2', [ROPE, NR * SEG + DS]); d['sin2'] = self.din('sin2', [ROPE, NR * SEG + DS])
        d['cmask'] = self.din('cmask', [128, 40])
        o = {}
        o['yp'] = self.dout('yp', [NR * SEG, D]); o['ys'] = self.dout('ys', [DS, D])
        o['p_ckv'] = self.dout('p_ckv', [NR * SEG, KVL]); o['p_kpe'] = self.dout('p_kpe', [NR * SEG, ROPE])
        o['pC'] = self.dout('pC', [NA, H, DV, DK]); o['pn'] = self.dout('pn', [NA, H * DK]); o['pm'] = self.dout('pm', [NA, H])
        o['pconv'] = self.dout('pconv', [DEPTH, 2, 2 * DFF])
        o['s_ckv'] = self.dout('s_ckv', [DS, KVL]); o['s_kpe'] = self.dout('s_kpe', [DS, ROPE])
        o['sCo'] = self.dout('sCo', [NA, H, DV, DK]); o['sno'] = self.dout('sno', [NA, H * DK]); o['smo'] = self.dout('smo', [NA, H])
        o['sconvo'] = self.dout('sconvo', [DEPTH, 2, 2 * DFF])
        self.d, self.o = d, o
        self.kn_dram = [None, self.dint('kns', [BH, 128, PAST + DS], BF16)]
        self.kr_dram = [None, self.dint('krs', [ROPE, PAST + DS], BF16)]
        self.v_dram = [None, self.dint('vs', [PAST + DS, BH * VD], BF16)]
        self.xsC = [[[self.dint('xsC%d%d%d' % (l, t, c), [128, 2048]) for c in range(2)] for t in range(2)] for l in range(NA)]
        self.xgC = [[[self.dint('xgC%d%d%d' % (l, t, c), [512, 2048]) for c in range(2)] for t in range(2)] for l in range(NA)]
        self.xsS = [self.dint('xsS%d' % l, [256, 16]) for l in range(NA)]
        self.xgS = [self.dint('xgS%d' % l, [1024, 16]) for l in range(NA)]
        self.b_xs2 = [Buf(), Buf()]; self.b_xg = [Buf(), Buf()]
        self.ts2 = [self.dint('ts2_%d' % l, [256, 176]) for l in range(DEPTH)]
        self.tg = [self.dint('tg_%d' % l, [4 * 256, 176]) for l in range(DEPTH)]
        self.b_ts2 = [Buf() for _ in range(DEPTH)]; self.b_tg = [Buf() for _ in range(DEPTH)]
        self.KP = [self.dint('KP%d' % j, [512, SEG], BF16) for j in range(4)]
        self.KRp = self.dint('KRp', [ROPE, SEG], BF16)
        self.VP = [self.dint('VP%d' % q, [SEG, 512], BF16) for q in range(4)]
        self.b_kvloc = Buf()
        self.b_kvd = [Buf(), Buf()]
        self.KG = [[self.dint('KG%d%d' % (r, j), [4 * 512, SEG], BF16) for j in range(4)] for r in range(NR)]
        self.KRG = [self.dint('KRG%d' % r, [4 * ROPE, SEG], BF16) for r in range(NR)]
        self.VG = [[self.dint('VG%d%d' % (r, q), [4 * SEG, 512], BF16) for q in range(4)] for r in range(NR)]
        self.b_kvall = [Buf() for _ in range(NR)]
        self.b_out = Buf('out')

        self.ident = self.sb('ident', [128, 128])
        self.ident_b = self.sb('ident_b', [128, 128], BF16)
        self.ones_b = self.sb('ones_b', [128, 128], BF16)
        self.ones_f = self.sb('ones_f', [128, 128])
        self.uneg = {L: self.sb('uneg%d' % L, [L, L]) for L in (64, 16)}
        self.mask = {L: self.sb('mask%d' % L, [L, L]) for L in (64, 16)}
        self.maskT = {L: self.sb('maskT%d' % L, [L, L]) for L in (64, 16)}
        self.sel = {L: self.sb('sel%d' % L, [L, 128]) for L in (64, 16)}
        self.perm = self.sb('perm', [64, 64], BF16)
        self.epsc = self.sb('epsc', [128, 1])
        self.b_const = Buf('const')
        self.xT = self.sb('xT', [128, KC, SEG]); self.b_x = Buf('x')
        self.xb = self.sb('xb', [128, KC, SEG], BF16); self.b_xb = Buf('xb')
        self.tail = self.sb('tail', [128, DEPTH, 2, 2, 88]); self.b_tail = Buf('tail')
        self.nst = self.sb('nst', [128, NA, 2, 2, H, 2]); self.b_nst = Buf('nst')
        self.nsb = self.sb('nsb', [128, NA, 2, 2, H, 2], BF16)
        self.mst = self.sb('mst', [128, NA, 2, H]); self.b_mst = Buf('mst')
        self.gn = self.sb('gn', [128, 9, KC]); self.b_gn = Buf('gn')
        self.cw = self.sb('cw', [128, DEPTH, 3, 88]); self.cb = self.sb('cb', [128, DEPTH, 88])
        self.gmisc = self.sb('gmisc', [128, 64])
        self.cm = self.sb('cm', [128, 40])
        self.fsum = self.sb('fsum', [128, H])
        self.b_cw = self.b_gn; self.b_gm = self.b_gn
        self.arena = self.sb('arena', [128, 33000])
        self.ps = [self.st.enter_context(nc.psum_tensor('ps%d' % i, [128, 512], F32)) for i in range(8)]
        self.bps = [Buf('ps%d' % i) for i in range(8)]
        self.O_STG = 0
        self.O_WR = 8192
        self.O_SQ = 12288
        self.O_RSTD = 12800
        self.O_PH = 13312
        self.O_TMP = 13312

        try:
            self.init_consts()
            self.dbg('init')
            self.segment(grp=1, s=0, T=DS, L=DS)
            for s in range(NR if NSEG > 0 else 0):
                self.segment(grp=0, s=s, T=SEG, L=64)
        except _Stop:
            pass
        k.barrier()
        k._wait('sp', [(p, k.cnt[p]) for p in k.pool if k.cnt[p] > 0])

    def init_consts(self):
        nc, k, d = self.nc, self.k, self.d
        W = [self.b_const]
        g = nc.gpsimd

        def sel(t, pattern, cmp, fill, base, cm):
            k.op('pool', lambda: g.affine_select(out=t, in_=t, pattern=pattern, compare_op=cmp, fill=fill,
                                                base=base, channel_multiplier=cm), W, W)
        self.ms(self.ident[:], 0.0, W, 'pool')
        sel(self.ident[:], [[-1, 128]], ALU.not_equal, 1.0, 0, 1)
        self.cp(self.ident_b[:], self.ident[:], W, W, 'dve')
        self.ms(self.ones_f[:], 1.0, W, 'pool')
        self.cp(self.ones_b[:], self.ones_f[:], W, W, 'dve')
        self.ms(self.epsc[:], EPS, W, 'pool')
        for L in (64, 16):
            self.ms(self.uneg[L][:], -1.0, W, 'pool')
            sel(self.uneg[L][:], [[1, L]], ALU.is_ge, 0.0, 0, -1)
            self.ms(self.mask[L][:], 0.0, W, 'pool')
            sel(self.mask[L][:], [[-1, L]], ALU.is_ge, NEG, 0, 1)
            self.ms(self.maskT[L][:], 0.0, W, 'pool')
            sel(self.maskT[L][:], [[1, L]], ALU.is_ge, NEG, 0, -1)
            self.ms(self.sel[L][:], 1.0, W, 'pool')
            sel(self.sel[L][:], [[0, 128]], ALU.is_equal, 0.0, -(L - 1), 1)
        pf = self.fv(20000, 64, 64)
        self.ms(pf, 0.0, W, 'pool')
        sel(pf, [[-1, 64]], ALU.not_equal, 1.0, -32, 1)
        sel(pf, [[-1, 64]], ALU.not_equal, 1.0, 32, 1)
        self.cp(self.perm[:], pf, W, W, 'dve')
        k.barrier()
        for i in range(4):
            self.load_fm(self.gn[:, i, :], d['norm_mix'][i], KC)
            self.load_fm(self.gn[:, 4 + i, :], d['norm_ffn'][i], KC)
        self.load_fm(self.gn[:, 8, :], d['kv_norm'][0], KC)
        for l in range(DEPTH):
            for j in range(3):
                self.load_fm(self.cw[:, l, j, :], d['f_conv_w'][l, j], 88)
            self.load_fm(self.cb[:, l, :], d['f_conv_b'][l], 88)
        gm = self.gmisc
        self.ms(gm[:], 0.0, [self.b_gn], 'pool')
        for j in range(2):
            self.load_fm(gm[:, 6 * j:6 * j + 6], d['b_g_cq'][j], 6)
            self.load_fm(gm[:, 12 + j:13 + j], d['b_g_qn'][j], 1)
            self.load_fm(gm[0:64, 14 + j:15 + j], d['b_g_qr'][j], 1, rows=64)
            self.load_fm(gm[:, 22 + 16 * j:38 + 16 * j], d['a_g_head'][j], 16)
            self.load_fm(gm[0:8, 54 + j:55 + j], d['a_b_gate'][j], 1, rows=8)
        self.load_fm(gm[:, 16:20], d['kv_g_c'][0], 4)
        self.load_fm(gm[0:64, 20:21], d['kv_g_r'][0], 1, rows=64)
        self.load_fm(gm[:, 21:22], d['kv_g_kn'][0], 1)
        self.ms(self.tail[:], 0.0, [self.b_tail], 'pool')
        self.ms(self.mst[:], 0.0, [self.b_mst], 'pool')
        self.ms(self.fsum[:], 0.0, [self.b_mst], 'pool')
        self.ms(self.nst[:], 0.0, [self.b_nst], 'pool')
        k.barrier()
        for l in range(DEPTH):
            for r in range(2):
                self.load_fm(self.tail[:, l, 1, r, :], d['sconv'][l, r], 88)
        for l in range(NA):
            k.dma('sp', self.mst[:, l, 1, :], d['sm'][l:l + 1, :].broadcast_to([128, H]), (), [self.b_mst])
            nt = self.fv(20100, 8)
            self.load_fm(nt, d['sn'][l], 8)
            for e in range(2):
                self.cp(self.nst[:, l, 1, :, :, e], nt.rearrange("p (h c) -> p c h", h=H), [self.b_gn], [self.b_nst], 'dve')
        self.cp(self.nsb[:], self.nst[:], [self.b_nst], [self.b_nst], 'dve')
        k.dma('sp', self.cm[:], d['cmask'][:, :], (), [self.b_gn])
        zc = self.fv(0, XW)
        self.ms(zc, 0.0, W, 'pool')
        for l in range(NA):
            for c in range(2):
                k.dma('sp', self.xsC[l][1][c][:, :], zc[:, 0:2048], W, [self.b_xs2[l]])
            k.dma('sp', self.xsS[l][128:256, :], zc[:, 0:16], W, [self.b_xs2[l]])
        for l in range(DEPTH):
            k.dma('sp', self.ts2[l][128:256, :], zc[:, 0:176], W, [self.b_ts2[l]])
        k.barrier()

    def linear(self, w, blocks, kgroups, rhs_fn, T, evac, gcol=None, ps_ids=(0, 1), extra_R=(), sets=None):
        k = self.k
        wv = w.rearrange("(c p) n -> p c n", p=128)
        NS = 4
        if not hasattr(self, 'wslot'):
            self.wslot = 0
            self.b_wst = [Buf() for _ in range(NS)]
            self.b_wr = [Buf() for _ in range(NS)]
        if sets is None:
            sets = [[0, 1, 4, 5], [6, 7, 2, 3]]
        kcs = [kc for (k0, nk) in kgroups for kc in range(k0, k0 + nk)]
        groups = []
        cur = []
        for bi, (c0, m) in enumerate(blocks):
            if cur and (cur[-1][1] + cur[-1][2] == c0) and (sum(x[2] for x in cur) + m <= 512) and (len(cur) < min(len(x) for x in sets)):
                cur.append((bi, c0, m))
            else:
                if cur:
                    groups.append(cur)
                cur = [(bi, c0, m)]
        if cur:
            groups.append(cur)
        si = 0
        for grp_ in groups:
            banks = sets[si % len(sets)]
            si += 1
            cols = sum(x[2] for x in grp_)
            cbase = grp_[0][1]
            nk_t = max(1, min(2048 // cols, len(kcs)))
            ntile = (len(kcs) + nk_t - 1) // nk_t
            for ti in range(ntile):
                kk = kcs[ti * nk_t:(ti + 1) * nk_t]
                assert kk == list(range(kk[0], kk[0] + len(kk)))
                nk = len(kk)
                s = self.wslot
                self.wslot = (self.wslot + 1) % NS
                stg = self.fv(self.O_STG + s * 2048, nk * cols).rearrange("p (c n) -> p c n", n=cols)
                wr = self.bv(self.O_WR + s * 1024, nk * cols).rearrange("p (c n) -> p c n", n=cols)
                k.dma('sp', stg, wv[:, kk[0]:kk[0] + nk, cbase:cbase + cols], (), [self.b_wst[s]], dsem='w%d' % s)
                self.cp(wr, stg, [self.b_wst[s]], [self.b_wr[s]], ('dve', 'act', 'dve', 'act')[s])
                for gi, (bi, c0, m) in enumerate(grp_):
                    ps, pb = self.ps[banks[gi]], self.bps[banks[gi]]
                    for j in range(nk):
                        rhs, rb = rhs_fn(kk[j])
                        first = (ti == 0 and j == 0)
                        last = (ti == ntile - 1 and j == nk - 1)
                        self.mm(ps[0:m, 0:T], wr[:, j, c0 - cbase:c0 - cbase + m], rhs, first, last,
                                [self.b_wr[s], rb] + list(extra_R), [pb])
            for gi, (bi, c0, m) in enumerate(grp_):
                evac(bi, self.ps[banks[gi]][0:m, 0:T], self.bps[banks[gi]])

    def sumsq_bc(self, src_fn, nch, T, out, outb, n, rows=128, ps_id=2):
        if not hasattr(self, 'b_sq'):
            self.b_sq = [Buf(), Buf()]
        ps, pb = self.ps[ps_id], self.bps[ps_id]
        for c in range(nch):
            src, sbuf = src_fn(c)
            s = c % 2
            sq = self.bv(self.O_SQ + s * 256, 512)
            self.act(sq[0:rows, 0:T], src, AF.Square, [sbuf], [self.b_sq[s]])
            self.mm(ps[:, 0:T], self.ones_b[0:rows, :], sq[0:rows, 0:T], c == 0, c == nch - 1, [self.b_sq[s], self.b_const], [pb])
        orow = out.shape[0]
        self.act(out, ps[0:orow, 0:T], AF.Sqrt, [pb, self.b_const], [outb], bias=self.epsc[0:orow, 0:1], scale=1.0 / n)
        self.recip(out, out, [outb], [outb])

    def x_rstd(self):
        T = self.T
        rstd = self.fv(self.O_RSTD, SEG)
        b = Buf()
        self.sumsq_bc(lambda c: (self.xT[:, c, 0:T], self.b_x), KC, T, rstd[:, 0:T], b, D)
        return rstd, b

    def xb_refresh(self, gidx):
        T = self.T
        rstd, b_rstd = self.x_rstd()
        tmpf = self.fv(self.O_STG, 2 * SEG).rearrange("p (s t) -> p s t", s=2)
        bt = [Buf(), Buf()]
        j = 0
        for c in range(KC):
            if c % 2 == 0:
                self.stt(self.xb[:, c, 0:T], self.xT[:, c, 0:T], self.gn[:, gidx, c:c + 1], rstd[:, 0:T], ALU.mult, ALU.mult,
                         [self.b_x, self.b_gn, b_rstd], [self.b_xb])
            else:
                sl = j % 2; j += 1
                self.act(tmpf[:, sl, 0:T], self.xT[:, c, 0:T], AF.Copy, [self.b_x, self.b_gn], [bt[sl]], scale=self.gn[:, gidx, c:c + 1])
                self.tt(self.xb[:, c, 0:T], tmpf[:, sl, 0:T], rstd[:, 0:T], ALU.mult, [bt[sl], b_rstd], [self.b_xb], 'pool')

    def resid_evac(self):
        T = self.T
        def ev(bi, ps, pb):
            self.tt(self.xT[:, bi, 0:T], ps, self.xT[:, bi, 0:T], ALU.add, [pb, self.b_x], [self.b_x])
        return ev

    def segment(self, grp, s, T, L):
        nc, k, d, o = self.nc, self.k, self.d, self.o
        self.grp, self.sidx, self.T, self.L = grp, s, T, L
        pos0 = s * SEG if grp == 0 else NR * SEG
        xsrc = d['xp'][s * SEG:(s + 1) * SEG, :] if grp == 0 else d['xs']
        k.barrier()
        tmp = self.fv(self.O_PH, D)
        tb = Buf()
        for t0 in range(0, T, 128):
            n = min(128, T - t0)
            k.dma('sp', tmp[0:n, :], xsrc[t0:t0 + n, :], (), [tb])
            for c4 in range(0, KC, 4):
                ps, pb = self.ps[(c4 // 4) % 2], self.bps[(c4 // 4) % 2]
                for j in range(4):
                    self.tr(ps[:, j * 128:j * 128 + n], tmp[0:n, (c4 + j) * 128:(c4 + j + 1) * 128], n, [tb], [pb])
                self.cp(self.xT[:, c4:c4 + 4, t0:t0 + n], ps[:, :].rearrange("p (j n) -> p j n", j=4)[:, :, 0:n], [pb], [self.b_x])
        self.dbg('xload')
        for layer in range(DEPTH):
            k.barrier()
            self.xb_refresh(layer)
            k.barrier()
            if layer < NA:
                self.mlstm(layer)
            else:
                self.mla(layer - NA)
            k.barrier()
            self.dbg('mixer%d' % layer)
            self.xb_refresh(4 + layer)
            k.barrier()
            self.ffn(layer)
            self.dbg('ffn%d' % layer)
            if layer == NA - 1:
                k.barrier()
                self.xb_refresh(8)
                k.barrier()
                self.shared_kv(pos0)
        k.barrier()
        ydst = o['yp'][s * SEG:(s + 1) * SEG, :] if grp == 0 else o['ys']
        for t0 in range(0, T, 128):
            n = min(128, T - t0)
            for c4 in range(0, KC, 4):
                ps, pb = self.ps[(c4 // 4) % 2], self.bps[(c4 // 4) % 2]
                for j in range(4):
                    self.tr(ps[0:n, j * 128:(j + 1) * 128], self.xT[:, c4 + j, t0:t0 + n], 128, [self.b_x], [pb])
                self.cp(tmp[0:n, c4 * 128:(c4 + 4) * 128], ps[0:n, :], [pb], [tb])
            k.dma('sp', ydst[t0:t0 + n, :], tmp[0:n, :], [tb], [self.b_out], dsem='out')

    def ffn(self, layer):
        nc, k, d, o = self.nc, self.k, self.d, self.o
        T, grp = self.T, self.grp
        P0 = self.O_PH
        ug = self.fv(P0, SEG + 2); b_ug = Buf()
        uv = self.fv(P0 + 514, SEG + 2); b_uv = Buf()
        cg = self.fv(P0 + 1028, SEG); b_cg = Buf()
        cv = self.fv(P0 + 1540, SEG); b_cv = Buf()
        sg4 = self.fv(P0 + 14300, 4 * SEG).rearrange("p (s t) -> p s t", s=4); b_sg4 = [Buf() for _ in range(4)]
        t1 = self.fv(P0 + 2564, 128); t2 = self.fv(P0 + 2692, 128)
        actT = self.bv(P0 + 3000, FC * SEG).rearrange("p (c t) -> p c t", c=FC); b_act = Buf()
        gcol = self.gn[:, 4 + layer, :]
        tl = self.tail[:, layer, grp]
        blocks = []; bmap = []
        for j0 in range(0, FC, 4):
            for j in range(j0, j0 + 4):
                blocks.append((j * 128, 128)); bmap.append((j, 0))
            for j in range(j0, j0 + 4):
                blocks.append((DFF + j * 128, 128)); bmap.append((j, 1))
        sgs = self.fv(self.O_SQ, 4 * SEG).rearrange("p (s t) -> p s t", s=4) if False else None

        def conv(u, ub, fc, out, outb):
            self.ts(out[:, 0:T], u[:, 0:T], self.cw[:, layer, 0, fc:fc + 1], self.cb[:, layer, fc:fc + 1], ALU.mult, ALU.add,
                    [ub, self.b_cw], [outb])
            for jj in (1, 2):
                self.stt(out[:, 0:T], u[:, jj:jj + T], self.cw[:, layer, jj, fc:fc + 1], out[:, 0:T], ALU.mult, ALU.add,
                         [ub, self.b_cw, outb], [outb])

        def evac(bi, ps, pb):
            j, isv = bmap[bi]
            fc = j + (FC if isv else 0)
            u, ub = (uv, b_uv) if isv else (ug, b_ug)
            cc, cb_ = (cv, b_cv) if isv else (cg, b_cg)
            sg, b_sg = sg4[:, j % 4, :], b_sg4[j % 4]
            if grp == 1:
                self.cp(u[:, 0:2], tl[:, :, fc], [self.b_tail], [ub], 'pool')
                self.cp(u[:, 2:2 + T], ps, [pb], [ub], 'act')
                self.cp(tl[:, :, fc], u[:, T:T + 2], [ub], [self.b_tail], 'pool')
                conv(u, ub, fc, cc, cb_)
                lo = 0
            else:
                self.cp(tl[:, :, fc], ps[:, T - 2:T], [pb], [self.b_tail], 'dve')
                self.cp(ufirst[:, fc, :], ps[:, 0:2], [pb], [b_uf], 'dve')
                self.ts(cc[:, 2:T], ps[:, 0:T - 2], self.cw[:, layer, 0, fc:fc + 1], self.cb[:, layer, fc:fc + 1], ALU.mult, ALU.add,
                        [pb, self.b_cw], [cb_])
                for jj in (1, 2):
                    self.stt(cc[:, 2:T], ps[:, jj:T - 2 + jj], self.cw[:, layer, jj, fc:fc + 1], cc[:, 2:T], ALU.mult, ALU.add,
                             [pb, self.b_cw, cb_], [cb_])
                lo = 2
            if not isv:
                self.act(sg[:, lo:T], cg[:, lo:T], AF.Silu, [b_cg], [b_sg])
            else:
                self.tt(actT[:, j, lo:T], sg[:, lo:T], cv[:, lo:T], ALU.mult, [b_sg, b_cv], [b_act], 'pool')

        ufirst = self.fv(P0 + 2820, 176).rearrange("p (c r) -> p c r", r=2); b_uf = Buf()
        self.linear(d['f_w_up'][layer], blocks, [(0, KC)], lambda c: (self.xb[:, c, 0:T], self.b_xb), T, evac)
        if grp == 0:
            k.barrier()
            tcur = self.tail[:, layer, 0].rearrange("p r c -> p (r c)")
            k.dma('sp', self.ts2[layer][0:128, :], tcur, [self.b_tail, self.b_tg[layer]], [self.b_ts2[layer]])
            k.collective(self.ts2[layer][:, :], self.tg[layer][:, :], GROUPS, [self.b_ts2[layer]], [self.b_tg[layer]])
            tgs = self.fv(P0, 1408).rearrange("p (i t c) -> p i t c", i=4, t=2); bfx = Buf()
            k.dma('sp', tgs, self.tg[layer].rearrange("(i t p) c -> p i t c", t=2, p=128), [self.b_tg[layer]], [bfx])
            halo = self.fv(P0 + 1408, 176)
            Rf = [bfx, self.b_gn, b_uf]
            self.ts(halo, tgs[:, 3, 1, :], self.cm[:, 28:29], None, ALU.mult, None, Rf, [bfx])
            for i in range(4):
                self.stt(halo, tgs[:, i, 0, :], self.cm[:, 24 + i:25 + i], halo, ALU.mult, ALU.add, Rf, [bfx])
            k.dma('sp', self.ts2[layer][128:256, :], tcur, [self.b_tail, self.b_tg[layer]], [self.b_ts2[layer]])
            h0 = halo[:, 0:88]; h1 = halo[:, 88:176]
            u0 = ufirst[:, :, 0]; u1 = ufirst[:, :, 1]
            w0, w1, w2 = (self.cw[:, layer, jj, :] for jj in range(3)); bb = self.cb[:, layer, :]
            c0 = self.fv(P0 + 1584, 88); c1 = self.fv(P0 + 1672, 88); ta = self.fv(P0 + 1760, 88); sgl = self.fv(P0 + 1848, 88)
            for (cc, x0, x1, x2) in ((c0, h0, h1, u0), (c1, h1, u0, u1)):
                self.tt(cc, w0, x0, ALU.mult, Rf, [bfx]); self.tt(cc, cc, bb, ALU.add, Rf, [bfx])
                self.tt(ta, w1, x1, ALU.mult, Rf, [bfx]); self.tt(cc, cc, ta, ALU.add, Rf, [bfx])
                self.tt(ta, w2, x2, ALU.mult, Rf, [bfx]); self.tt(cc, cc, ta, ALU.add, Rf, [bfx])
            for t, cc in ((0, c0), (1, c1)):
                self.act(sgl[:, 0:44], cc[:, 0:44], AF.Silu, Rf, [bfx])
                self.tt(actT[:, :, t], sgl[:, 0:44], cc[:, 44:88], ALU.mult, Rf, [b_act])
            k.barrier()
        self.linear(d['f_w_down'][layer], [(c * 128, 128) for c in range(KC)], [(0, FC)],
                    lambda c: (actT[:, c, 0:T], b_act), T, self.resid_evac(), extra_R=[self.b_x])
        if grp == 1 or self.sidx == NR - 1:
            dst = o['sconvo'] if grp == 1 else o['pconv']
            tb = Buf()
            tf = self.tail[:, layer, grp].rearrange("p r c -> p (r c)")
            ps, pb = self.ps[2], self.bps[2]
            self.tr(ps[0:128, 0:128], tf[:, 0:128], 128, [self.b_tail], [pb])
            self.tr(ps[0:48, 128:256], tf[:, 128:176], 128, [self.b_tail], [pb])
            self.cp(t1, ps[0:128, 0:128], [pb], [tb]); self.cp(t2[0:48, :], ps[0:48, 128:256], [pb], [tb])
            dv = [dst[layer, r].rearrange("(c p) -> c p", p=128) for r in range(2)]
            k.dma('sp', dv[0][0:88, :], t1[0:88, :], [tb], [self.b_out], dsem='out')
            k.dma('sp', dv[1][0:40, :], t1[88:128, :], [tb], [self.b_out], dsem='out')
            k.dma('sp', dv[1][40:88, :], t2[0:48, :], [tb], [self.b_out], dsem='out')

    def mlstm(self, layer):
        nc, k, d, o = self.nc, self.k, self.d, self.o
        T, L, grp = self.T, self.L, self.grp
        P0 = self.O_PH
        qT = self.bv(P0, 8 * SEG).rearrange("p (c t) -> p c t", c=8); b_q = Buf()
        kT = self.bv(P0 + 2048, 8 * SEG).rearrange("p (c t) -> p c t", c=8); b_k = Buf()
        vT = self.fv(P0 + 4096, 16 * SEG).rearrange("p (c t) -> p c t", c=16); b_v = Buf()
        gT = self.fv(P0 + 12288, SEG, 8); b_g = Buf()
        sgt2 = self.fv(self.O_SQ, 2 * SEG).rearrange("p (s t) -> p s t", s=2); b_sg2 = [Buf(), Buf()]
        O_SSTG = 6144
        O_GH = 10240
        S0 = P0 + 13312
        gcol = self.gn[:, layer, :]
        w_in = d['a_w_in'][layer]
        sq_, sv_ = H * DK, H * DV
        blocks = [(c * 128, 128) for c in range(8)] + [(sq_ + c * 128, 128) for c in range(8)] + \
                 [(2 * sq_ + c * 128, 128) for c in range(16)] + [(2 * sq_ + 2 * sv_, 8)]

        def evac(bi, ps, pb):
            e = 'act' if bi % 2 else 'dve'
            if bi < 8:
                self.cp(qT[:, bi, 0:T], ps, [pb], [b_q], e)
            elif bi < 16:
                self.cp(kT[:, bi - 8, 0:T], ps, [pb], [b_k], e)
            elif bi < 32:
                self.cp(vT[:, bi - 16, 0:T], ps, [pb], [b_v], e)
            else:
                self.ts(gT[:, 0:T], ps, self.gmisc[0:8, 54 + layer:55 + layer], None, ALU.add, None, [pb, self.b_gm], [b_g])

        self.linear(w_in, blocks, [(0, KC)], lambda c: (self.xb[:, c, 0:T], self.b_xb), T, evac)
        k.barrier()
        self.dbg('mproj')
        CT = self.fv(0, 4096).rearrange("p (c h v) -> p c h v", c=2, h=H); b_ct = Buf()
        CTb = self.bv(4096, 4096).rearrange("p (c h v) -> p c h v", c=2, h=H)
        nT = self.nst[:, layer, grp]
        nTb = self.nsb[:, layer, grp]
        mbc = self.mst[:, layer, grp, :]
        ghb = self.fv(O_GH, 2048, 64).rearrange("p (h v) -> p h v", h=H); b_gh = Buf()
        k.dma('sp', ghb[0:L], d['a_g_head'][layer:layer + 1, :].broadcast_to([L, H * DV]).rearrange("p (h v) -> p h v", h=H), (), [b_gh])
        env = dict(layer=layer, qT=qT, kT=kT, vT=vT, gT=gT, CT=CT, CTb=CTb, nT=nT, nTb=nTb, mbc=mbc, ghb=ghb, S0=S0,
                   b_q=b_q, b_k=b_k, b_v=b_v, b_g=b_g, b_ct=b_ct, b_gh=b_gh)
        if grp == 1:
            stg = self.fv(O_SSTG, 4096).rearrange("p (h c k) -> p h c k", h=H, c=4); sb_ = Buf()
            k.dma('sp', stg, d['sC'][layer].rearrange("h (c p) k -> p h c k", p=128), (), [sb_])
            for h in range(H):
                for kc in range(2):
                    ps, pb = self.ps[(h * 2 + kc) % 2], self.bps[(h * 2 + kc) % 2]
                    for vc in range(4):
                        self.tr(ps[:, vc * 128:(vc + 1) * 128], stg[:, h, vc, kc * 128:(kc + 1) * 128], 128, [sb_], [pb])
                    self.cp(CT[:, kc, h, :], ps[:, :], [pb], [b_ct])
        else:
            self.ms(self.fv(0, 4096), 0.0, [b_ct], 'pool')
            self.ms(nT, 0.0, [self.b_nst], 'dve')
            self.ms(mbc, NEG, [self.b_mst], 'dve')
            self.ms(self.fsum[:], 0.0, [self.b_mst], 'dve')
            k.barrier()
            self.cp(self.bv(4096, 4096), self.fv(0, 4096), [b_ct], [b_ct])
            self.cp(nTb, nT, [self.b_nst], [self.b_nst])
            self.scan_chunks(env, True)
            k.barrier()
            self.exchange_state(layer, env)
            k.barrier()
        self.cp(self.bv(4096, 4096), self.fv(0, 4096), [b_ct], [b_ct])
        self.cp(nTb, nT, [self.b_nst], [self.b_nst])
        self.scan_chunks(env, False)
        k.barrier()
        self.dbg('scan')
        last = (grp == 1) or (self.sidx == NR - 1)
        if grp == 0:
            stt_ = self.fv(S0 + 200, 16); sbb = Buf()
            self.cp(stt_[:, 0:8].rearrange("p (c h) -> p c h", c=2), nT[:, :, :, 0], [self.b_nst], [sbb])
            self.cp(stt_[:, 8:12], mbc, [self.b_mst], [sbb])
            self.ms(stt_[:, 12:16], 0.0, [sbb], 'dve')
            for c in range(2):
                k.dma('sp', self.xsC[layer][1][c][:, :], self.fv(c * 2048, 2048), [b_ct, self.b_xg[layer]], [self.b_xs2[layer]])
            k.dma('sp', self.xsS[layer][128:256, :], stt_, [sbb, self.b_xg[layer]], [self.b_xs2[layer]])
        if last:
            Cd = (o['sCo'] if grp == 1 else o['pC'])[layer]
            nd = (o['sno'] if grp == 1 else o['pn'])[layer]
            md = (o['smo'] if grp == 1 else o['pm'])[layer]
            stg = self.fv(O_SSTG, 4096).rearrange("p (h c k) -> p h c k", h=H, c=4); sb_ = Buf()
            for h in range(H):
                for vc in range(4):
                    ps, pb = self.ps[vc % 2], self.bps[vc % 2]
                    for kc in range(2):
                        self.tr(ps[:, kc * 128:(kc + 1) * 128], CT[:, kc, h, vc * 128:(vc + 1) * 128], 128, [b_ct], [pb])
                    self.cp(stg[:, h, vc, :], ps[:, 0:256], [pb], [sb_])
            k.dma('sp', Cd.rearrange("h (c p) k -> p h c k", p=128), stg, [sb_], [self.b_out], dsem='out')
            nf = self.fv(S0, 8); nb = Buf()
            self.cp(nf.rearrange("p (h c) -> p c h", h=H), nT[:, :, :, 0], [self.b_nst], [nb])
            ps, pb = self.ps[2], self.bps[2]
            self.tr(ps[0:8, 0:128], nf, 128, [nb], [pb])
            nf2 = self.fv(S0 + 16, 128, 8)
            self.cp(nf2, ps[0:8, 0:128], [pb], [nb])
            k.dma('sp', nd.rearrange("(c p) -> c p", p=128), nf2, [nb], [self.b_out], dsem='out')
            k.dma('sp', md.rearrange("(o h) -> o h", o=1), mbc[0:1, :], [self.b_mst], [self.b_out], dsem='out')
        k.barrier()
        def evac_o(bi, ps, pb):
            sg2 = sgt2[:, bi % 2, 0:T]
            self.act(sg2, ps, AF.Sigmoid, [pb], [b_sg2[bi % 2]])
            self.tt(vT[:, bi, 0:T], vT[:, bi, 0:T], sg2, ALU.mult, [b_sg2[bi % 2], b_v], [b_v], 'pool' if bi % 2 else 'dve')
        self.linear(w_in, [(2 * sq_ + sv_ + c * 128, 128) for c in range(16)], [(0, KC)], lambda c: (self.xb[:, c, 0:T], self.b_xb), T, evac_o)
        hr = self.bv(S0, 16 * SEG).rearrange("p (c t) -> p c t", c=16); b_hr = Buf()
        k.barrier()
        for c in range(16):
            self.cp(hr[:, c, 0:T], vT[:, c, 0:T], [b_v], [b_hr], 'dve' if c % 2 else 'act')
        self.linear(d['a_w_out'][layer], [(c * 128, 128) for c in range(KC)], [(0, KC)], lambda c: (hr[:, c, 0:T], b_hr), T,
                    self.resid_evac(), extra_R=[self.b_x])

    def scan_chunks(self, env, state_only):
        nc, k = self.nc, self.k
        T, L = self.T, self.L
        layer = env['layer']
        qT, kT, vT, gT, CT, CTb, nT, nTb, mbc, ghb, S0 = (env[x] for x in ('qT', 'kT', 'vT', 'gT', 'CT', 'CTb', 'nT', 'nTb', 'mbc', 'ghb', 'S0'))
        b_q, b_k, b_v, b_g, b_ct, b_gh = (env[x] for x in ('b_q', 'b_k', 'b_v', 'b_g', 'b_ct', 'b_gh'))
        k_c = self.bv(S0, 1024, 64); v_c = self.bv(S0 + 512, 2048, 64); wv = self.bv(S0 + 1536, 2048, 64)
        junk = self.fv(S0 + 1536, 512, 64)
        hh = self.fv(S0 + 2560, 2048, 64)
        sm_ = self.fv(S0 + 4608, 64, 64)
        dg = self.fv(S0 + 4672, 256, 64); dl = self.fv(S0 + 4928, 256, 64); dT = self.fv(S0 + 5184, 256, 64)
        sdT = self.bv(S0 + 5440, 256, 64)
        qs = self.bv(S0 + 5568, 512).rearrange("p (c t) -> p c t", c=8)
        bcs = self.fv(S0 + 5824, 32)
        w2r = self.bv(S0 + 5856, 2, 64)
        bS = Buf('scan')
        P = self.ps; B = self.bps
        for c in range(T // L):
            c0 = c * L
            R = [bS, b_q, b_k, b_v, b_g, b_ct, b_gh, self.b_nst, self.b_mst, self.b_const]
            Wb = [bS]
            for g4 in range(2):
                pbf = P[0][0:L, :].bitcast(BF16)
                for j in range(4):
                    self.tr(pbf[:, j * 128:(j + 1) * 128], kT[:, g4 * 4 + j, c0:c0 + L], 128, R, [B[0]], bf=True)
                self.cp(k_c[0:L, g4 * 512:(g4 + 1) * 512], pbf[:, 0:512], [B[0]], Wb)
            for g4 in range(4):
                pp = 1 + g4 % 2
                for j in range(4):
                    self.tr(P[pp][0:L, j * 128:(j + 1) * 128], vT[:, g4 * 4 + j, c0:c0 + L], 128, R, [B[pp]])
                self.cp(v_c[0:L, g4 * 512:(g4 + 1) * 512], P[pp][0:L, :], [B[pp]], Wb, 'act')
            self.tr(P[3][0:L, 0:8], gT[:, c0:c0 + L], 8, R, [B[3]])
            gi = sm_[0:L, 0:4]; sp_ = sm_[0:L, 4:8]; b_ = sm_[0:L, 8:12]; a_ = sm_[0:L, 12:16]
            il = sm_[0:L, 16:20]; mt = sm_[0:L, 20:24]; mb = sm_[0:L, 20:28]; u_ = sm_[0:L, 28:32]
            iw = sm_[0:L, 32:36]; en = sm_[0:L, 36:40]; w_ = sm_[0:L, 40:44]; den = sm_[0:L, 44:48]
            ss = sm_[0:L, 48:52]; rr = sm_[0:L, 52:56]; mloc = sm_[0:L, 56:60]; tmp4 = sm_[0:L, 60:64]
            self.cp(gi, P[3][0:L, 0:4], [B[3]], Wb)
            self.act(sp_, P[3][0:L, 4:8], AF.Exp, [B[3]], Wb, scale=-1.0)
            self.act(sp_, sp_, AF.Ln, [bS], Wb, bias=1.0)
            self.mm(P[3][0:L, 8:12], self.uneg[L][:, :], sp_, True, True, R, [B[3]])
            self.cp(b_, P[3][0:L, 8:12], [B[3]], Wb)
            self.cp(sm_[0:L, 24:28], b_, R, Wb)
            self.tt(a_, gi, b_, ALU.subtract, R, Wb)
            idl = self.ident[0:L, 0:L]
            dg3 = dg[0:L, 0:4 * L].rearrange("p (h s) -> p h s", h=H)
            dl3 = dl[0:L, 0:4 * L].rearrange("p (h s) -> p h s", h=H)
            dT3 = dT[0:L, 0:4 * L].rearrange("p (h s) -> p h s", h=H)
            sd3 = sdT[0:L, 0:4 * L].rearrange("p (h s) -> p h s", h=H)

            def rowbc(col4, ps_ap, psb, lhs):
                self.tt(dg3, bc(idl.unsqueeze(1), [L, H, L]), bc(col4.unsqueeze(2), [L, H, L]), ALU.mult, R, Wb)
                self.mm(ps_ap, lhs, dg[0:L, 0:4 * L], True, True, R, [psb])
            rowbc(a_, P[4][0:L, 0:4 * L], B[4], self.ones_f[0:L, 0:L])
            p4 = P[4][0:L, 0:4 * L].rearrange("p (h s) -> p h s", h=H)
            self.tt(dl3, p4, bc(b_.unsqueeze(2), [L, H, L]), ALU.add, [B[4]] + R, Wb)
            self.tt(dl3, dl3, bc(self.mask[L][:, :].unsqueeze(1), [L, H, L]), ALU.add, R, Wb)
            self.k.op('dve', lambda: nc.vector.tensor_reduce(out=mloc, in_=dl3, axis=AX.X, op=ALU.max), R, Wb)
            self.tt(il, b_, mbc[0:L, :], ALU.add, R, Wb)
            self.tt(mt, il, mloc, ALU.max, R, Wb)
            if not state_only:
                self.tt(u_, b_, mt, ALU.subtract, R, Wb)
                self.tt(tmp4, il, mt, ALU.subtract, R, Wb)
                self.act(iw, tmp4, AF.Exp, R, Wb)
                self.act(en, mt, AF.Exp, R, Wb, scale=-1.0)
                rowbc(u_, P[4][0:L, 0:4 * L], B[4], self.ones_f[0:L, 0:L])
                self.tt(dT3, p4, bc(a_.unsqueeze(2), [L, H, L]), ALU.add, [B[4]] + R, Wb)
                self.tt(dT3, dT3, bc(self.maskT[L][:, :].unsqueeze(1), [L, H, L]), ALU.add, R, Wb)
                self.act(dT[0:L, 0:4 * L], dT[0:L, 0:4 * L], AF.Exp, R, Wb)
                for h in range(H):
                    for kc in range(2):
                        self.mm(P[5][0:L, h * L:(h + 1) * L], kT[:, h * 2 + kc, c0:c0 + L], qT[:, h * 2 + kc, c0:c0 + L], kc == 0, kc == 1, R, [B[5]])
                self.stt(sdT[0:L, 0:4 * L], P[5][0:L, 0:4 * L], DK ** -0.5, dT[0:L, 0:4 * L], ALU.mult, ALU.mult, [B[5]] + R, Wb)
                rowbc(iw, P[4][:, 0:4 * L], B[4], self.ones_f[0:L, :])
                pw = P[4][:, 0:4 * L].rearrange("p (h t) -> p h t", h=H)
                for kc in range(2):
                    qv = qT[:, :, c0:c0 + L].rearrange("p (h c) t -> p h c t", c=2)[:, :, kc, :]
                    qsv = qs[:, :, 0:L].rearrange("p (h c) t -> p h c t", c=2)[:, :, kc, :]
                    self.tt(qsv, qv, pw, ALU.mult, [B[4]] + R, Wb)
                for h in range(H):
                    pn_, bn_ = P[6 + h % 2], B[6 + h % 2]
                    self.mm(pn_[0:L, :], sd3[:, h, :], v_c[0:L, h * DV:(h + 1) * DV], True, False, R, [bn_])
                    for kc in range(2):
                        self.mm(pn_[0:L, :], qs[:, h * 2 + kc, 0:L], CTb[:, kc, h, :], False, kc == 1, R, [bn_])
                    self.mm(P[3][0:L, 16 + 2 * h:18 + 2 * h], sd3[:, h, :], self.ones_b[0:L, 0:2], True, False, R, [B[3]])
                    for kc in range(2):
                        self.mm(P[3][0:L, 16 + 2 * h:18 + 2 * h], qs[:, h * 2 + kc, 0:L], nTb[:, kc, h, :], False, kc == 1, R, [B[3]])
                    qn = P[3][0:L, 16 + 2 * h:17 + 2 * h]
                    self.act(den[:, h:h + 1], qn, AF.Abs, [B[3]] + R, Wb)
                    self.tt(den[:, h:h + 1], den[:, h:h + 1], en[:, h:h + 1], ALU.max, R, Wb)
                    self.recip(den[:, h:h + 1], den[:, h:h + 1], R, Wb)
                    self.act(junk[0:L, :], pn_[0:L, :], AF.Square, [bn_] + R, Wb, scale=den[:, h:h + 1], accum=ss[:, h:h + 1])
                    self.act(rr[:, h:h + 1], ss[:, h:h + 1], AF.Sqrt, R, Wb, bias=self.epsc[0:L, 0:1], scale=1.0 / DV)
                    self.recip(rr[:, h:h + 1], rr[:, h:h + 1], R, Wb)
                    self.tt(rr[:, h:h + 1], rr[:, h:h + 1], den[:, h:h + 1], ALU.mult, R, Wb)
                    self.stt(hh[0:L, h * DV:(h + 1) * DV], pn_[0:L, :], rr[:, h:h + 1], ghb[0:L, h, :], ALU.mult, ALU.mult, [bn_] + R, Wb)
            self.mm(P[3][:, 32:40], self.sel[L][:, :], mb, True, True, R, [B[3]])
            self.cp(bcs[:, 0:8], P[3][:, 32:40], [B[3]], Wb)
            mnew = bcs[:, 0:4]; blast = bcs[:, 4:8]; dec = bcs[:, 8:12]; t12 = bcs[:, 12:16]
            self.tt(self.fsum[:], self.fsum[:], blast, ALU.add, R, [self.b_mst, bS])
            self.tt(t12, blast, mnew, ALU.subtract, R, Wb)
            self.tt(dec, t12, mbc, ALU.add, R, Wb)
            self.act(dec, dec, AF.Exp, R, Wb)
            self.tt(w_, a_, t12[0:L, :], ALU.add, R, Wb)
            self.act(w_, w_, AF.Exp, R, Wb)
            self.ts(w_, w_, DK ** -0.5, None, ALU.mult, None, R, Wb)
            self.tt(wv[0:L, :].rearrange("p (h v) -> p h v", h=H), v_c[0:L, :].rearrange("p (h v) -> p h v", h=H),
                    bc(w_.unsqueeze(2), [L, H, DV]), ALU.mult, R, Wb)
            for h in range(H):
                self.cp(w2r[0:L, :], bc(w_[:, h:h + 1], [L, 2]), R, Wb)
                for kc in range(2):
                    pc, bcb = P[6 + kc], B[6 + kc]
                    self.mm(pc[:, :], k_c[0:L, h * DK + kc * 128:h * DK + (kc + 1) * 128], wv[0:L, h * DV:(h + 1) * DV], True, True, R, [bcb])
                    self.stt(CT[:, kc, h, :], CT[:, kc, h, :], dec[:, h:h + 1], pc[:, :], ALU.mult, ALU.add, [bcb, b_ct] + R, [b_ct])
                    if not state_only:
                        self.cp(CTb[:, kc, h, :], CT[:, kc, h, :], [b_ct], [b_ct], 'act')
                    self.mm(P[3][:, 48:50], k_c[0:L, h * DK + kc * 128:h * DK + (kc + 1) * 128], w2r[0:L, :], True, True, R, [B[3]])
                    self.stt(nT[:, kc, h, :], nT[:, kc, h, :], dec[:, h:h + 1], P[3][:, 48:50], ALU.mult, ALU.add,
                             [B[3], self.b_nst] + R, [self.b_nst])
                    if not state_only:
                        self.cp(nTb[:, kc, h, :], nT[:, kc, h, :], [self.b_nst], [self.b_nst])
            self.cp(mbc, mnew, R, [self.b_mst])
            if not state_only:
                for g4 in range(4):
                    pp = 4 + g4 % 2
                    for j in range(4):
                        self.tr(P[pp][:, j * L:(j + 1) * L], hh[0:L, (g4 * 4 + j) * 128:(g4 * 4 + j + 1) * 128], L, R, [B[pp]])
                    self.cp(vT[:, g4 * 4:g4 * 4 + 4, c0:c0 + L], P[pp][:, 0:4 * L].rearrange("p (j t) -> p j t", j=4), [B[pp]], [b_v, bS])

    def exchange_state(self, layer, env):
        nc, k = self.nc, self.k
        CT, nT, mbc, S0, b_ct = env['CT'], env['nT'], env['mbc'], env['S0'], env['b_ct']
        grp = self.grp
        sc = self.fv(S0 + 300, 600)
        bsc = Buf()
        stt_ = sc[:, 0:16]
        self.cp(stt_[:, 0:8].rearrange("p (c h) -> p c h", c=2), nT[:, :, :, 0], [self.b_nst], [bsc])
        self.cp(stt_[:, 8:12], mbc, [self.b_mst], [bsc])
        self.cp(stt_[:, 12:16], self.fsum[:], [self.b_mst], [bsc])
        for c in range(2):
            k.dma('sp', self.xsC[layer][0][c][:, :], self.fv(c * 2048, 2048), [b_ct, self.b_xg[layer]], [self.b_xs2[layer]])
        k.dma('sp', self.xsS[layer][0:128, :], stt_, [bsc, self.b_xg[layer]], [self.b_xs2[layer]])
        for t in range(2):
            for c in range(2):
                k.collective(self.xsC[layer][t][c][:, :], self.xgC[layer][t][c][:, :], GROUPS, [self.b_xs2[layer]], [self.b_xg[layer]])
        k.collective(self.xsS[layer][:, :], self.xgS[layer][:, :], GROUPS, [self.b_xs2[layer]], [self.b_xg[layer]])
        sg = sc[:, 16:16 + 128].rearrange("p (i t c) -> p i t c", i=4, t=2)
        k.dma('sp', sg, self.xgS[layer].rearrange("(i t p) c -> p i t c", t=2, p=128), [self.b_xg[layer]], [bsc])
        cm = self.cm
        F3 = sg[:, :, 0, 12:16]
        m3 = sg[:, :, 0, 8:12]
        mcar = sg[:, 3, 1, 8:12]
        Rr = [bsc, self.b_gn]
        T1 = sc[:, 144:208].rearrange("p (i h l) -> p i h l", i=4, h=4)
        Mv = cm[:, 8:24].rearrange("p (i l) -> p i l", i=4)
        self.tt(T1, bc(Mv.unsqueeze(2), [128, 4, 4, 4]), bc(F3.rearrange("p l h -> p h l").unsqueeze(1), [128, 4, 4, 4]), ALU.mult, Rr, [bsc])
        G = sc[:, 208:224].rearrange("p (i h) -> p i h", i=4)
        self.k.op('dve', lambda: nc.vector.tensor_reduce(out=G, in_=T1, axis=AX.X, op=ALU.add), Rr, [bsc])
        E = sc[:, 224:240].rearrange("p (i h) -> p i h", i=4)
        self.tt(E, m3, G, ALU.add, Rr, [bsc])
        self.tt(E, E, bc(cm[:, 0:4].unsqueeze(2), [128, 4, 4]), ALU.add, Rr, [bsc])
        T2 = sc[:, 240:256].rearrange("p (h l) -> p h l", h=4)
        self.tt(T2, bc(cm[:, 4:8].unsqueeze(1), [128, 4, 4]), F3.rearrange("p l h -> p h l"), ALU.mult, Rr, [bsc])
        Ec = sc[:, 256:260]
        self.k.op('dve', lambda: nc.vector.tensor_reduce(out=Ec, in_=T2, axis=AX.X, op=ALU.add), Rr, [bsc])
        self.tt(Ec, Ec, mcar, ALU.add, Rr, [bsc])
        mx = sc[:, 260:264]
        self.k.op('dve', lambda: nc.vector.tensor_reduce(out=mx, in_=E.rearrange("p i h -> p h i"), axis=AX.X, op=ALU.max), Rr, [bsc])
        min_ = sc[:, 264:268]
        self.tt(min_, mx, Ec, ALU.max, Rr, [bsc])
        Wt = sc[:, 268:284].rearrange("p (i h) -> p i h", i=4)
        self.tt(Wt, E, bc(min_.unsqueeze(1), [128, 4, 4]), ALU.subtract, Rr, [bsc])
        self.act(sc[:, 268:284], sc[:, 268:284], AF.Exp, Rr, [bsc])
        Wc = sc[:, 284:288]
        self.tt(Wc, Ec, min_, ALU.subtract, Rr, [bsc])
        self.act(Wc, Wc, AF.Exp, Rr, [bsc])
        nacc = sc[:, 288:296].rearrange("p (c h) -> p c h", c=2)
        ntmp = sc[:, 296:304].rearrange("p (c h) -> p c h", c=2)
        self.tt(nacc, sg[:, 3, 1, 0:8].rearrange("p (c h) -> p c h", c=2), bc(Wc.unsqueeze(1), [128, 2, 4]), ALU.mult, Rr, [bsc])
        for i in range(4):
            self.tt(ntmp, sg[:, i, 0, 0:8].rearrange("p (c h) -> p c h", c=2), bc(Wt[:, i, :].unsqueeze(1), [128, 2, 4]), ALU.mult, Rr, [bsc])
            self.tt(nacc, nacc, ntmp, ALU.add, Rr, [bsc])
        for e in range(2):
            self.cp(nT[:, :, :, e], nacc, Rr, [self.b_nst])
        self.cp(mbc, min_, Rr, [self.b_mst])
        stg = self.fv(6144, 4096).rearrange("p (c h v) -> p c h v", c=2, h=H); bst = Buf()
        srcs = [(1, 3, None)] + [(0, i, i) for i in range(4)]
        for si, (tt_, rk, i) in enumerate(srcs):
            for c in range(2):
                k.dma('sp', self.fv(6144 + c * 2048, 2048),
                      self.xgC[layer][tt_][c][rk * 128:(rk + 1) * 128, :], [self.b_xg[layer]], [bst])
            for kc in range(2):
                for h in range(H):
                    if i is None:
                        self.ts(CT[:, kc, h, :], stg[:, kc, h, :], Wc[:, h:h + 1], None, ALU.mult, None, [bst] + Rr, [b_ct])
                    else:
                        self.stt(CT[:, kc, h, :], stg[:, kc, h, :], Wt[:, i, h:h + 1], CT[:, kc, h, :], ALU.mult, ALU.add, [bst, b_ct] + Rr, [b_ct])

    def rope(self, dst, dstb, src, srcb, pos0, T, o_cs):
        cs = self.fv(o_cs, SEG, 64); sn = self.fv(o_cs + 512, SEG, 64)
        if not hasattr(self, 'b_rope'):
            self.b_rope = Buf()
        tb = self.b_rope
        self.k.dma('sp', cs[:, 0:T], self.d['cos2'][:, pos0:pos0 + T], (), [tb])
        self.k.dma('sp', sn[:, 0:T], self.d['sin2'][:, pos0:pos0 + T], (), [tb])
        ps, pb = self.ps[3], self.bps[3]
        self.mm(ps[0:64, 0:T], self.perm[:, :], src, True, True, [srcb, self.b_const], [pb])
        self.tt(sn[:, 0:T], ps[0:64, 0:T], sn[:, 0:T], ALU.mult, [pb, tb], [tb])
        self.tt(cs[:, 0:T], src, cs[:, 0:T], ALU.mult, [srcb, tb], [tb])
        self.tt(dst, cs[:, 0:T], sn[:, 0:T], ALU.add, [tb], [dstb])

    def shared_kv(self, pos0):
        nc, k, d, o = self.nc, self.k, self.d, self.o
        T, grp, s = self.T, self.grp, self.sidx
        P0 = self.O_PH
        zT = self.fv(P0, 5 * SEG).rearrange("p (c t) -> p c t", c=5); b_z = Buf()
        cTf = self.fv(P0 + 2560, 4 * SEG).rearrange("p (c t) -> p c t", c=4); b_c = Buf()
        cTb = self.bv(P0 + 4608, 4 * SEG).rearrange("p (c t) -> p c t", c=4)
        kpT = self.fv(P0 + 5632, SEG, 64); b_kp = Buf()
        kpn = self.bv(P0 + 6144, SEG, 64)
        r2 = self.fv(P0 + 6400, SEG); b_r2 = Buf()
        tm = self.fv(P0 + 6912, 576); ld = self.fv(P0 + 7488, 576)
        self.o_kv = P0 + 8300
        blocks = [(c * 128, 128) for c in range(4)] + [(KVL, 64)]

        def evac(bi, ps, pb):
            m = 128 if bi < 4 else 64
            self.cp(zT[0:m, bi, 0:T], ps, [pb], [b_z], 'act' if bi % 2 else 'dve')
        self.linear(d['kv_w_down'], blocks, [(0, KC)], lambda c: (self.xb[:, c, 0:T], self.b_xb), T, evac)
        self.sumsq_bc(lambda c: (zT[:, c, 0:T], b_z), 4, T, r2[:, 0:T], b_r2, KVL)
        for c in range(4):
            self.stt(cTf[:, c, 0:T], zT[:, c, 0:T], self.gmisc[:, 16 + c:17 + c], r2[:, 0:T], ALU.mult, ALU.mult, [b_z, b_r2, self.b_gm], [b_c])
            self.cp(cTb[:, c, 0:T], cTf[:, c, 0:T], [b_c], [b_c], 'act')
        self.sumsq_bc(lambda c: (zT[0:64, 4, 0:T], b_z), 1, T, r2[0:64, 0:T], b_r2, ROPE, rows=64)
        self.stt(kpn[:, 0:T], zT[0:64, 4, 0:T], self.gmisc[0:64, 20:21], r2[0:64, 0:T], ALU.mult, ALU.mult, [b_z, b_r2, self.b_gm], [b_kp])
        self.rope(kpT[:, 0:T], b_kp, kpn[:, 0:T], b_kp, pos0, T, P0 + 18000)
        k0 = 0 if grp == 0 else PAST
        kb = self.b_kvloc if grp == 0 else self.b_kvd[grp]
        kpb = self.bv(P0 + 8000, SEG, 64); b_kpb = Buf()
        self.cp(kpb[:, 0:T], kpT[:, 0:T], [b_kp], [b_kpb])
        if grp == 0:
            k.dma('sp', self.KRp[:, 0:T], kpb[:, 0:T], [b_kpb] + [self.b_kvall[r] for r in range(NR)], [kb], dsem='kv')
        else:
            k.dma('sp', self.kr_dram[grp][:, k0:k0 + T], kpb[:, 0:T], [b_kpb], [kb], dsem='kv')
        cdst = (o['p_ckv'][s * SEG:(s + 1) * SEG] if grp == 0 else o['s_ckv'])
        kdst = (o['p_kpe'][s * SEG:(s + 1) * SEG] if grp == 0 else o['s_kpe'])
        tb = Buf()
        for t0 in range(0, T, 128):
            n = min(128, T - t0)
            ps, pb = self.ps[2], self.bps[2]
            for c in range(4):
                self.tr(ps[0:n, c * 128:(c + 1) * 128], cTf[:, c, t0:t0 + n], 128, [b_c], [pb])
            self.cp(tm[0:n, 0:512], ps[0:n, :], [pb], [tb])
            ps, pb = self.ps[3], self.bps[3]
            self.tr(ps[0:n, 0:64], kpT[:, t0:t0 + n], 64, [b_kp], [pb])
            self.cp(tm[0:n, 512:576], ps[0:n, 0:64], [pb], [tb])
            k.dma('sp', cdst[t0:t0 + n, :], tm[0:n, 0:512], [tb], [self.b_out], dsem='out')
            k.dma('sp', kdst[t0:t0 + n, :], tm[0:n, 512:576], [tb], [self.b_out], dsem='out')
        self.kv_up(cTb, b_c, T, k0)
        if grp == 0:
            k.barrier()
            for j in range(4):
                k.collective(self.KP[j][:, :], self.KG[s][j][:, :], GROUPS, [self.b_kvloc], [self.b_kvall[s]])
                k.collective(self.VP[j][:, :], self.VG[s][j][:, :], GROUPS, [self.b_kvloc], [self.b_kvall[s]])
            k.collective(self.KRp[:, :], self.KRG[s][:, :], GROUPS, [self.b_kvloc], [self.b_kvall[s]])
        if grp == 1:
            k.barrier()
            lb = Buf()
            for blk in range(PAST // SEG):
                for t0 in range(0, SEG, 128):
                    r0 = blk * SEG + t0
                    k.dma('sp', ld[:, 0:512], d['ckv'][r0:r0 + 128, :], (), [lb])
                    k.dma('sp', ld[:, 512:576], d['kpe'][r0:r0 + 128, :], (), [lb])
                    ps, pb = self.ps[2], self.bps[2]
                    for c in range(4):
                        self.tr(ps[:, c * 128:(c + 1) * 128], ld[:, c * 128:(c + 1) * 128], 128, [lb], [pb])
                    self.cp(cTb[:, :, t0:t0 + 128], ps[:, :].rearrange("p (c t) -> p c t", c=4), [pb], [b_c])
                    ps, pb = self.ps[3], self.bps[3]
                    self.tr(ps[0:64, 0:128], ld[:, 512:576], 128, [lb], [pb])
                    self.cp(kpT[:, t0:t0 + 128], ps[0:64, 0:128], [pb], [b_kp])
                self.cp(kpb[:, 0:SEG], kpT[:, 0:SEG], [b_kp], [b_kpb])
                k.dma('sp', self.kr_dram[1][:, blk * SEG:(blk + 1) * SEG], kpb[:, 0:SEG], [b_kpb], [kb], dsem='kv')
                self.kv_up(cTb, b_c, SEG, blk * SEG)

    def kv_up(self, cTb, b_c, T, k0):
        nc, k, d = self.nc, self.k, self.d
        grp = self.grp
        kb = self.b_kvloc if grp == 0 else self.b_kvd[grp]
        O = self.o_kv
        k.barrier()
        kn = self.fv(O, 2 * SEG).rearrange("p (s t) -> p s t", s=2); bkn = [Buf(), Buf()]
        r3 = self.fv(O + 1024, SEG); b_r3 = Buf()
        wvs = self.bv(O + 1536, 4 * 2048).rearrange("p (c n) -> p c n", c=4); b_wv = Buf()
        stg = self.fv(O + 5632, 2048); b_st = Buf()
        vt = self.bv(O + 7680, 2048); b_vt = Buf()
        knh = self.bv(O + 8704, 2 * SEG).rearrange("p (s t) -> p s t", s=2)
        blocks = [(h * 256, 128) for h in range(BH)]

        def evac(bi, ps, pb):
            sl_ = bi % 2
            self.cp(kn[:, sl_, 0:T], ps, [pb], [bkn[sl_]], 'act')
            self.sumsq_bc(lambda c: (kn[:, sl_, 0:T], bkn[sl_]), 1, T, r3[:, 0:T], b_r3, NOPE, ps_id=3)
            self.stt(kn[:, sl_, 0:T], kn[:, sl_, 0:T], self.gmisc[:, 21:22], r3[:, 0:T], ALU.mult, ALU.mult, [bkn[sl_], b_r3, self.b_gm], [bkn[sl_]])
            self.cp(knh[:, sl_, 0:T], kn[:, sl_, 0:T], [bkn[sl_]], [bkn[sl_]], 'pool')
            kdst = self.KP[bi // 4][(bi % 4) * 128:(bi % 4 + 1) * 128, 0:T] if grp == 0 else self.kn_dram[grp][bi, :, k0:k0 + T]
            k.dma('sp', kdst, knh[:, sl_, 0:T], [bkn[sl_]], [kb], dsem='kv')
        self.linear(d['kv_w_up'], blocks, [(0, 4)], lambda c: (cTb[:, c, 0:T], b_c), T, evac, sets=[[0], [1], [4], [5]])
        wv_ = d['kv_w_up'].rearrange("(c p) (h two x) -> p c h two x", p=128, two=2, x=128)
        for c in range(4):
            k.dma('sp', stg.rearrange("p (h x) -> p h x", h=BH), wv_[:, c, :, 1, :], (), [b_st])
            self.cp(wvs[:, c, :], stg, [b_st], [b_wv])
        for t0 in range(0, T, 128):
            n = min(128, T - t0)
            for q4 in range(4):
                ps, pb = self.ps[q4 % 2], self.bps[q4 % 2]
                for c in range(4):
                    self.mm(ps[0:n, :], cTb[:, c, t0:t0 + n], wvs[:, c, q4 * 512:(q4 + 1) * 512], c == 0, c == 3, [b_c, b_wv], [pb])
                self.cp(vt[0:n, q4 * 512:(q4 + 1) * 512], ps[0:n, :], [pb], [b_vt], 'act' if q4 % 2 else 'dve')
            if grp == 0:
                for q in range(4):
                    k.dma('sp', self.VP[q][t0:t0 + n, :], vt[0:n, q * 512:(q + 1) * 512], [b_vt], [kb], dsem='kv')
            else:
                k.dma('sp', self.v_dram[grp][k0 + t0:k0 + t0 + n, :], vt[0:n, :], [b_vt], [kb], dsem='kv')

    def mla(self, j):
        nc, k, d, o = self.nc, self.k, self.d, self.o
        T, grp, s = self.T, self.grp, self.sidx
        pos0 = s * SEG if grp == 0 else NR * SEG
        P0 = self.O_PH
        cq = self.bv(P0, 6 * SEG).rearrange("p (c t) -> p c t", c=6); b_cq = Buf()
        r2 = self.fv(P0 + 1536, SEG); b_r2 = Buf()
        QN = self.bv(P0 + 2048, 16 * SEG).rearrange("p (c t) -> p c t", c=16); b_qn = Buf()
        QR = self.bv(P0 + 6144, 16 * SEG, 64).rearrange("p (c t) -> p c t", c=16); b_qr = Buf()
        r3 = self.fv(P0 + 10240, SEG); b_r3 = Buf()
        qtmp = self.fv(P0 + 10752, SEG); b_qt = Buf()
        qrn = self.bv(P0 + 11264, SEG, 64); b_qrn = Buf()
        O_CS = P0 + 11520
        rd = self.fv(P0 + 12544, SEG); b_rd = Buf()
        cqg = self.bv(P0 + 13100, 6 * SEG).rearrange("p (c t) -> p c t", c=6)

        def evac(bi, ps, pb):
            self.cp(cq[:, bi, 0:T], ps, [pb], [b_cq], 'act')
            self.ts(cqg[:, bi, 0:T], ps, self.gmisc[:, 6 * j + bi:6 * j + bi + 1], None, ALU.mult, None, [pb, self.b_gm], [b_cq])
        self.linear(d['b_w_dq'][j], [(c * 128, 128) for c in range(6)], [(0, KC)], lambda c: (self.xb[:, c, 0:T], self.b_xb), T, evac)
        self.sumsq_bc(lambda c: (cq[:, c, 0:T], b_cq), 6, T, r2[:, 0:T], b_r2, QL)
        blocks = []
        for h in range(BH):
            blocks.append((h * 192, 128)); blocks.append((h * 192 + 128, 64))

        def evac2(bi, ps, pb):
            h, isr = bi // 2, bi % 2
            m = 64 if isr else 128
            self.tt(qtmp[0:m, 0:T], ps, r2[0:m, 0:T], ALU.mult, [pb, b_r2], [b_qt])
            self.sumsq_bc(lambda c: (qtmp[0:m, 0:T], b_qt), 1, T, r3[0:m, 0:T], b_r3, m, rows=m, ps_id=3)
            if not isr:
                self.stt(QN[:, h, 0:T], qtmp[:, 0:T], self.gmisc[:, 12 + j:13 + j], r3[:, 0:T], ALU.mult, ALU.mult, [b_qt, b_r3, self.b_gm], [b_qn])
            else:
                self.stt(qrn[:, 0:T], qtmp[0:64, 0:T], self.gmisc[0:64, 14 + j:15 + j], r3[0:64, 0:T], ALU.mult, ALU.mult, [b_qt, b_r3, self.b_gm], [b_qrn])
                self.rope(QR[:, h, 0:T], b_qr, qrn[:, 0:T], b_qrn, pos0, T, O_CS)
        self.linear(d['b_w_uq'][j], blocks, [(0, 6)], lambda c: (cqg[:, c, 0:T], b_cq), T, evac2, sets=[[0, 1], [4, 5], [6, 7]])
        k.barrier()
        KB = 1024 if grp == 1 else SEG
        knb = self.bv(0, 2 * 1024).rearrange("p (s t) -> p s t", s=2); b_knb = [Buf(), Buf()]
        vb = self.bv(1024, 2 * 1024).rearrange("p (s t x) -> p s t x", s=2, x=128); b_vb = [Buf(), Buf()]
        krall = self.bv(3072, 9 * SEG, 64); b_kr = Buf()
        dacc = self.fv(5376, SEG); daccb = self.bv(5888, SEG); b_da = Buf()
        blocks = []
        if grp == 1:
            nkeys = PAST + DS
            for kk0 in range(0, nkeys, KB):
                nk = min(KB, nkeys - kk0)
                blocks.append((lambda h, kk0=kk0, nk=nk: self.kn_dram[1][h, :, kk0:kk0 + nk],
                               self.kr_dram[1][:, kk0:kk0 + nk],
                               lambda h, t, n, kk0=kk0: self.v_dram[1][kk0 + t * 128:kk0 + t * 128 + n, h * 128:(h + 1) * 128],
                               nk, False, None, self.b_kvd[1]))
        else:
            def mk(KS, KRS, VS, i, diag, bias, dep):
                return (lambda h: KS[h // 4][i * 512 + (h % 4) * 128:i * 512 + (h % 4 + 1) * 128, :],
                        KRS[i * ROPE:(i + 1) * ROPE, :],
                        lambda h, t, n: VS[h // 4][i * SEG + t * 128:i * SEG + t * 128 + n, (h % 4) * 128:(h % 4 + 1) * 128],
                        SEG, diag, bias, dep)
            for rdi in range(s):
                for i in range(4):
                    blocks.append(mk(self.KG[rdi], self.KRG[rdi], self.VG[rdi], i, False, None, self.b_kvall[rdi]))
            for i in range(4):
                blocks.append(mk(self.KG[s], self.KRG[s], self.VG[s], i, False, 32 + i, self.b_kvall[s]))
            blocks.append(mk(self.KP, self.KRp, self.VP, 0, True, None, self.b_kvloc))
        ntot = sum((blk[3] + 127) // 128 for blk in blocks)
        kro = 0
        for (knf, krap, vf, nk, diag, biascol, dep) in blocks:
            k.dma('sp', krall[:, kro:kro + nk], krap, [dep], [b_kr], dsem='ld')
            kro += nk
        DEPTH_ = 3
        sbanks = [0, 1, 6, 7]
        pTs = [self.bv(2048 + 256 * i, SEG) for i in range(4)]
        b_pTs = [Buf() for _ in range(4)]
        slot = 0
        pi = 0
        for h in range(BH):
            po, bo = self.ps[4 + h % 2], self.bps[4 + h % 2]
            pd, bd = self.ps[2 + h % 2], self.bps[2 + h % 2]
            tiles = []
            kro = 0
            for (knf, krap, vf, nk, diag, biascol, dep) in blocks:
                tiles.append(('load', knf, vf, nk, dep))
                for t in range((nk + 127) // 128):
                    n = min(128, nk - t * 128)
                    tiles.append(('tile', t, n, t * 128 if diag else 0, diag, biascol, kro))
                kro += nk
            pend = []
            state = dict(first=True, cnt=0, sl=0)

            def finish(item):
                (ps_, bs_, pp, bp, sl_, t, n, q0, diag, biascol) = item
                if biascol is None:
                    self.act(pp[0:n, q0:T], ps_[0:n, q0:T], AF.Exp, [bs_], [bp], scale=ATT_SCALE)
                else:
                    self.act(pp[0:n, q0:T], ps_[0:n, q0:T], AF.Exp, [bs_, self.b_gn], [bp], scale=ATT_SCALE,
                             bias=self.cm[0:n, biascol:biascol + 1])
                if diag:
                    self.ms(pp[64:128, q0:q0 + 64], 0.0, [bp], 'dve')
                state['cnt'] += 1
                self.mm(po[:, q0:T], vb[0:n, sl_, t, :], pp[0:n, q0:T], state['first'], state['cnt'] == ntot, [b_vb[sl_], bp], [bo])
                if state['first']:
                    assert n == 128 and q0 == 0
                    self.cp(dacc[:, 0:T], pp[:, 0:T], [bp], [b_da])
                else:
                    self.tt(dacc[0:n, q0:T], dacc[0:n, q0:T], pp[0:n, q0:T], ALU.add, [bp, b_da], [b_da])
                state['first'] = False

            for it in tiles:
                if it[0] == 'load':
                    _, knf, vf, nk, dep = it
                    sl_ = slot; slot ^= 1
                    state['sl'] = sl_
                    k.dma('sp', knb[:, sl_, 0:nk], knf(h), [dep], [b_knb[sl_]], dsem='ld')
                    nfull = nk // 128
                    if nfull:
                        k.dma('sp', vb[:, sl_, 0:nfull, :], vf(h, 0, nfull * 128).rearrange("(t p) d -> p t d", p=128), [dep], [b_vb[sl_]], dsem='ld')
                    if nk % 128:
                        n_ = nk % 128
                        k.dma('sp', vb[0:n_, sl_, nfull, :], vf(h, nfull, n_), [dep], [b_vb[sl_]], dsem='ld')
                    continue
                _, t, n, q0, diag, biascol, kro_ = it
                sl_ = state['sl']
                ps_, bs_ = self.ps[sbanks[pi % 4]], self.bps[sbanks[pi % 4]]
                pp, bp = pTs[pi % 4], b_pTs[pi % 4]
                pi += 1
                self.mm(ps_[0:n, q0:T], knb[:, sl_, t * 128:t * 128 + n], QN[:, h, q0:T], True, False, [b_knb[sl_], b_qn], [bs_])
                self.mm(ps_[0:n, q0:T], krall[:, kro_ + t * 128:kro_ + t * 128 + n], QR[:, h, q0:T], False, True, [b_kr, b_qr], [bs_])
                pend.append((ps_, bs_, pp, bp, sl_, t, n, q0, diag, biascol))
                if len(pend) > DEPTH_:
                    finish(pend.pop(0))
            while pend:
                finish(pend.pop(0))
            self.cp(daccb[:, 0:T], dacc[:, 0:T], [b_da], [b_da], 'act')
            self.mm(pd[:, 0:T], self.ones_b[:, :], daccb[:, 0:T], True, True, [b_da, self.b_const], [bd])
            self.recip(rd[:, 0:T], pd[:, 0:T], [bd], [b_rd])
            self.tt(QN[:, h, 0:T], po[:, 0:T], rd[:, 0:T], ALU.mult, [bo, b_rd], [b_qn])
        k.barrier()
        self.linear(d['b_w_o'][j], [(c * 128, 128) for c in range(KC)], [(0, KC)], lambda c: (QN[:, c, 0:T], b_qn), T,
                    self.resid_evac(), extra_R=[self.b_x])


_PROG = None


def _rope_tables():
    half = ROPE // 2
    inv = (10000.0 ** (-np.arange(half, dtype=np.float32) / half)).astype(np.float32)
    pos = np.concatenate([np.arange(SEQ), PAST + np.arange(DS)]).astype(np.float32)
    ang = pos[None, :] * inv[:, None]
    c, s = np.cos(ang).astype(np.float32), np.sin(ang).astype(np.float32)
    return np.concatenate([c, c], 0), np.concatenate([-s, s], 0)


def _core_mask(r):
    m = np.zeros((40,), np.float32)
    for i in range(4):
        m[i] = 0.0 if i < r else NEG
        m[4 + i] = 1.0 if i < r else 0.0
        for l in range(4):
            m[8 + 4 * i + l] = 1.0 if (i < l < r) else 0.0
        m[24 + i] = 1.0 if i == r - 1 else 0.0
        m[32 + i] = 0.0 if i < r else -30000.0
    m[28] = 1.0 if r == 0 else 0.0
    return np.ascontiguousarray(np.broadcast_to(m[None, :], (128, 40)))


def kernel(**inp):
    global _PROG
    if _PROG is None:
        _PROG = Prog()
    prog = _PROG
    f = lambda a: np.ascontiguousarray(np.asarray(a, dtype=np.float32))
    cos2, sin2 = _rope_tables()
    shared = {}
    for n in ['norm_mix', 'norm_ffn', 'a_w_in', 'a_b_gate', 'a_w_out', 'kv_w_down', 'kv_w_up', 'b_w_dq', 'b_g_cq', 'b_w_uq',
              'b_g_qn', 'b_g_qr', 'b_w_o', 'f_w_up', 'f_conv_w', 'f_conv_b', 'f_w_down']:
        shared[n] = f(inp[n])
    shared['a_g_head'] = f(inp['a_g_head']).reshape(NA, H * DV)
    for n in ['kv_norm', 'kv_g_c', 'kv_g_r', 'kv_g_kn']:
        shared[n] = f(inp[n]).reshape(1, -1)
    in_maps = []
    for c in range(8):
        g, r = c // 4, c % 4
        m = dict(shared)
        segs = [rd * 4 + r for rd in range(NR)]
        m['xp'] = f(np.concatenate([inp['x_prompt'][g][sg * SEG:(sg + 1) * SEG] for sg in segs], 0))
        cols = np.concatenate([np.arange(sg * SEG, (sg + 1) * SEG) for sg in segs] + [SEQ + np.arange(DS)])
        m['cos2'] = f(cos2[:, cols]); m['sin2'] = f(sin2[:, cols])
        m['cmask'] = _core_mask(r)
        m['xs'] = f(inp['x_sample'][c])
        m['ckv'] = f(inp['cache_ckv'][c]); m['kpe'] = f(inp['cache_kpe'][c])
        m['sC'] = f(inp['state_C'][:, c]); m['sn'] = f(inp['state_n'][:, c]).reshape(NA, H * DK); m['sm'] = f(inp['state_m'][:, c])
        m['sconv'] = f(inp['state_conv'][:, c])
        in_maps.append(m)
    res = run_bass_kernel_spmd(prog.nc, in_maps, core_ids=list(range(8))).results

    def seqcat(n):
        out = []
        for g in range(2):
            parts = [None] * (4 * NR)
            for r in range(4):
                for rd in range(NR):
                    parts[rd * 4 + r] = res[g * 4 + r][n][rd * SEG:(rd + 1) * SEG]
            out.append(np.concatenate(parts, 0))
        return np.stack(out, 0)
    pc = [3, 7]
    st = lambda n, cores, ax: np.stack([res[c][n] for c in cores], axis=ax)
    allc = list(range(8))
    rs = lambda a: a.reshape(a.shape[0], a.shape[1], H, DK)
    return (seqcat('yp'), st('ys', allc, 0), seqcat('p_ckv'), seqcat('p_kpe'),
            st('pC', pc, 1), rs(st('pn', pc, 1)), st('pm', pc, 1), st('pconv', pc, 1),
            st('s_ckv', allc, 0), st('s_kpe', allc, 0), st('sCo', allc, 1), rs(st('sno', allc, 1)), st('smo', allc, 1),
            st('sconvo', allc, 1))
```

```python
import numpy as np
from contextlib import ExitStack
import concourse.bass as bass
import concourse.mybir as mybir
from concourse.bass_utils import run_bass_kernel_spmd

F32 = mybir.dt.float32
BF16 = mybir.dt.bfloat16
AF = mybir.ActivationFunctionType
ALU = mybir.AluOpType
AX = mybir.AxisListType

D = 2048; KC = 16; SEQ = 4096; DEPTH = 4; NA = 2
DS = 16; PAST = 2048
H = 4; DK = 256; DV = 512; APROJ = 6152
BH = 16; QL = 768; KVL = 512; NOPE = 128; ROPE = 64; VD = 128
DFF = 5632; FC = 44
EPS = 1e-6
SEG = 512
NSEG = SEQ // SEG
NR = 2
XW = 2 * H * DV + 16
KVROWS = BH * 128 + ROPE + 4 * SEG
GROUPS = [[0, 1, 2, 3], [4, 5, 6, 7]]
ATT_SCALE = (NOPE + ROPE) ** -0.5
NEG = -1.0e30


class Buf:
    def __init__(self, name=""):
        self.name = name
        self.w = None
        self.r = []


class K:
    def __init__(self, nc, stack):
        self.nc = nc
        self.stack = stack
        self.eng = {'pe': nc.tensor, 'dve': nc.vector, 'act': nc.scalar, 'pool': nc.gpsimd, 'sp': nc.sync}
        self.sem = {}
        self.cnt = {}
        self.waited = {e: {} for e in self.eng}
        for e in self.eng:
            self.sem[e] = stack.enter_context(nc.semaphore('s_' + e))
            self.cnt[e] = 0
        self.nins = 0

    def new_dma_sem(self, key):
        self.sem[key] = self.stack.enter_context(self.nc.semaphore(key))
        self.cnt[key] = 0
        return key

    def _wait(self, e, deps):
        need = {}
        for d in deps:
            if d is None:
                continue
            k, v = d
            if e == 'pe' and k == 'pe':
                continue
            if v > need.get(k, 0):
                need[k] = v
        for k, v in need.items():
            if self.waited[e].get(k, 0) >= v:
                continue
            self.eng[e].wait_ge(self.sem[k], v)
            self.waited[e][k] = v

    @staticmethod
    def _deps(reads, writes):
        deps = []
        for b in reads:
            deps.append(b.w)
        for b in writes:
            deps.append(b.w)
            deps.extend(b.r)
        return deps

    def op(self, e, fn, reads=(), writes=()):
        self._wait(e, self._deps(reads, writes))
        ins = fn()
        self.cnt[e] += 1
        self.nins += 1
        ins.then_inc(self.sem[e], 1)
        tag = (e, self.cnt[e])
        for b in reads:
            b.r.append(tag)
            if len(b.r) > 24:
                b.r = b.r[-24:] if False else self._compact(b.r)
        for b in writes:
            b.w = tag
            b.r = []
        return ins

    @staticmethod
    def _compact(r):
        m = {}
        for k, v in r:
            if v > m.get(k, 0):
                m[k] = v
        return list(m.items())

    def dma(self, q, out, in_, reads=(), writes=(), dsem='io', **kw):
        if dsem in ('io', 'out', 'kv', 'ld'):
            dsem = self.pool[self.pi % len(self.pool)]
            self.pi += 1
        prev = self.cnt[dsem]
        self._wait(q, self._deps(reads, writes) + ([(dsem, prev)] if prev else []))
        ins = self.eng[q].dma_start(out=out, in_=in_, **kw)
        self.cnt[dsem] += 16
        self.nins += 1
        ins.then_inc(self.sem[dsem], 16)
        tag = (dsem, self.cnt[dsem])
        for b in reads:
            b.r.append(tag)
        for b in writes:
            b.w = tag
            b.r = []
        return ins

    def collective(self, src, dst, groups, reads=(), writes=()):
        self._wait('pool', self._deps(reads, writes))
        ins = self.nc.gpsimd.collective_compute("AllGather", mybir.AluOpType.bypass, replica_groups=groups,
                                                ins=[src.opt()], outs=[dst.opt()])
        self.cnt['cc'] += 1
        self.nins += 1
        ins.then_inc(self.sem['cc'], 1)
        tag = ('cc', self.cnt['cc'])
        for b in reads:
            b.r.append(tag)
        for b in writes:
            b.w = tag
            b.r = []
        return ins

    def barrier(self):
        allv = [(k, v) for k, v in self.cnt.items() if v > 0]
        for e in self.eng:
            self._wait(e, allv)


def bc(ap, shape):
    return ap.broadcast_to(list(shape))


STOP = None


class _Stop(Exception):
    pass


class Prog:
    def dbg(self, tag):
        if STOP is not None and tag == STOP:
            raise _Stop()

    def __init__(self):
        self.nc = nc = bass.Bass("TRN2", target_bir_lowering=False)
        self.st = ExitStack()
        self.k = K(nc, self.st)
        for s in ['w0', 'w1', 'w2', 'w3', 'cc']:
            self.k.new_dma_sem(s)
        self.k.pool = [self.k.new_dma_sem('p%d' % i) for i in range(40)]
        self.k.pi = 0
        self.build()

    def din(self, name, shape):
        return self.nc.dram_tensor(name, list(shape), F32, kind="ExternalInput").ap()

    def dout(self, name, shape):
        return self.nc.dram_tensor(name, list(shape), F32, kind="ExternalOutput").ap()

    def dint(self, name, shape, dt=F32):
        return self.nc.dram_tensor(name, list(shape), dt, kind="Internal").ap()

    def sb(self, name, shape, dt=F32):
        return self.st.enter_context(self.nc.sbuf_tensor(name, list(shape), dt))

    def fv(self, a, n, rows=128):
        return self.arena[0:rows, a:a + n]

    def bv(self, a, n, rows=128):
        assert n % 2 == 0
        return self.arena[0:rows, a:a + n // 2].bitcast(BF16)

    def V(self, e='dve'):
        return self.nc.vector if e == 'dve' else self.nc.gpsimd

    def tt(self, out, a, b, op, R, W, e='dve'):
        self.k.op(e, lambda: self.V(e).tensor_tensor(out=out, in0=a, in1=b, op=op), R, W)

    def ts(self, out, a, s1, s2, op0, op1, R, W, e='dve'):
        if op1 is None:
            self.k.op(e, lambda: self.V(e).tensor_scalar(out=out, in0=a, scalar1=s1, scalar2=None, op0=op0), R, W)
        else:
            self.k.op(e, lambda: self.V(e).tensor_scalar(out=out, in0=a, scalar1=s1, scalar2=s2, op0=op0, op1=op1), R, W)

    def stt(self, out, a, s, b, op0, op1, R, W):
        self.k.op('dve', lambda: self.nc.vector.scalar_tensor_tensor(out=out, in0=a, scalar=s, in1=b, op0=op0, op1=op1), R, W)

    def cp(self, out, a, R, W, e='dve'):
        if e == 'act':
            self.k.op('act', lambda: self.nc.scalar.copy(out=out, in_=a), R, W)
        else:
            self.k.op(e, lambda: self.V(e).tensor_copy(out=out, in_=a), R, W)

    def act(self, out, a, func, R, W, bias=None, scale=None, accum=None):
        kw = {}
        if bias is not None:
            kw['bias'] = bias
        if scale is not None:
            kw['scale'] = scale
        if accum is not None:
            kw['accum_out'] = accum
        self.k.op('act', lambda: self.nc.scalar.activation(out=out, in_=a, func=func, **kw), R, W)

    def mm(self, out, lhsT, rhs, start, stop, R, W):
        self.k.op('pe', lambda: self.nc.tensor.matmul(out, lhsT=lhsT, rhs=rhs, start=start, stop=stop), R, W)

    def tr(self, out, a, n, R, W, bf=False):
        idn = self.ident_b if bf else self.ident
        self.k.op('pe', lambda: self.nc.tensor.transpose(out, a, idn[0:n, 0:n]), list(R) + [self.b_const], W)

    def ms(self, ap, c, W, e='dve'):
        self.k.op(e, lambda: self.V(e).memset(ap, c), (), W)

    def recip(self, out, a, R, W):
        self.k.op('dve', lambda: self.nc.vector.reciprocal(out=out, in_=a), R, W)

    def load_fm(self, dst, src1d, n, rows=128):
        tmp = self.fv(self.O_TMP, 128)
        if not hasattr(self, 'b_tmpfm'):
            self.b_tmpfm = Buf()
        tb = self.b_tmpfm
        self.k.dma('sp', tmp[0:n, 0:rows], src1d.rearrange("(c p) -> c p", p=rows), (), [tb])
        ps, pb = self.ps[7], self.bps[7]
        self.tr(ps[0:rows, 0:n], tmp[0:n, 0:rows], n, [tb], [pb])
        self.cp(dst, ps[0:rows, 0:n], [pb], [self.b_gn])

    def build(self):
        nc, k = self.nc, self.k
        d = {}
        d['xp'] = self.din('xp', [NR * SEG, D]); d['xs'] = self.din('xs', [DS, D])
        d['ckv'] = self.din('ckv', [PAST, KVL]); d['kpe'] = self.din('kpe', [PAST, ROPE])
        d['sC'] = self.din('sC', [NA, H, DV, DK]); d['sn'] = self.din('sn', [NA, H * DK]); d['sm'] = self.din('sm', [NA, H])
        d['sconv'] = self.din('sconv', [DEPTH, 2, 2 * DFF])
        d['norm_mix'] = self.din('norm_mix', [DEPTH, D]); d['norm_ffn'] = self.din('norm_ffn', [DEPTH, D])
        d['a_w_in'] = self.din('a_w_in', [NA, D, APROJ]); d['a_b_gate'] = self.din('a_b_gate', [NA, 8])
        d['a_g_head'] = self.din('a_g_head', [NA, H * DV]); d['a_w_out'] = self.din('a_w_out', [NA, D, D])
        d['kv_norm'] = self.din('kv_norm', [1, D]); d['kv_w_down'] = self.din('kv_w_down', [D, KVL + ROPE])
        d['kv_g_c'] = self.din('kv_g_c', [1, KVL]); d['kv_g_r'] = self.din('kv_g_r', [1, ROPE])
        d['kv_w_up'] = self.din('kv_w_up', [KVL, BH * 256]); d['kv_g_kn'] = self.din('kv_g_kn', [1, NOPE])
        d['b_w_dq'] = self.din('b_w_dq', [2, D, QL]); d['b_g_cq'] = self.din('b_g_cq', [2, QL])
        d['b_w_uq'] = self.din('b_w_uq', [2, QL, BH * 192]); d['b_g_qn'] = self.din('b_g_qn', [2, NOPE])
        d['b_g_qr'] = self.din('b_g_qr', [2, ROPE]); d['b_w_o'] = self.din('b_w_o', [2, D, D])
        d['f_w_up'] = self.din('f_w_up', [DEPTH, D, 2 * DFF]); d['f_conv_w'] = self.din('f_conv_w', [DEPTH, 3, 2 * DFF])
        d['f_conv_b'] = self.din('f_conv_b', [DEPTH, 2 * DFF]); d['f_w_down'] = self.din('f_w_down', [DEPTH, DFF, D])
        d['cos2'] = self.din('cos2', [ROPE, NR * SEG + DS]); d['sin2'] = self.din('sin2', [ROPE, NR * SEG + DS])
        d['cmask'] = self.din('cmask', [128, 40])
        o = {}
        o['yp'] = self.dout('yp', [NR * SEG, D]); o['ys'] = self.dout('ys', [DS, D])
        o['p_ckv'] = self.dout('p_ckv', [NR * SEG, KVL]); o['p_kpe'] = self.dout('p_kpe', [NR * SEG, ROPE])
        o['pC'] = self.dout('pC', [NA, H, DV, DK]); o['pn'] = self.dout('pn', [NA, H * DK]); o['pm'] = self.dout('pm', [NA, H])
        o['pconv'] = self.dout('pconv', [DEPTH, 2, 2 * DFF])
        o['s_ckv'] = self.dout('s_ckv', [DS, KVL]); o['s_kpe'] = self.dout('s_kpe', [DS, ROPE])
        o['sCo'] = self.dout('sCo', [NA, H, DV, DK]); o['sno'] = self.dout('sno', [NA, H * DK]); o['smo'] = self.dout('smo', [NA, H])
        o['sconvo'] = self.dout('sconvo', [DEPTH, 2, 2 * DFF])
        self.d, self.o = d, o
        self.kn_dram = [None, self.dint('kns', [BH, 128, PAST + DS], BF16)]
        self.kr_dram = [None, self.dint('krs', [ROPE, PAST + DS], BF16)]
        self.v_dram = [None, self.dint('vs', [PAST + DS, BH * VD], BF16)]
        self.xsC = [[[self.dint('xsC%d%d%d' % (l, t, c), [128, 2048]) for c in range(2)] for t in range(2)] for l in range(NA)]
        self.xgC = [[[self.dint('xgC%d%d%d' % (l, t, c), [512, 2048]) for c in range(2)] for t in range(2)] for l in range(NA)]
        self.xsS = [self.dint('xsS%d' % l, [256, 16]) for l in range(NA)]
        self.xgS = [self.dint('xgS%d' % l, [1024, 16]) for l in range(NA)]
        self.b_xs2 = [Buf(), Buf()]; self.b_xg = [Buf(), Buf()]
        self.ts2 = [self.dint('ts2_%d' % l, [256, 176]) for l in range(DEPTH)]
        self.tg = [self.dint('tg_%d' % l, [4 * 256, 176]) for l in range(DEPTH)]
        self.b_ts2 = [Buf() for _ in range(DEPTH)]; self.b_tg = [Buf() for _ in range(DEPTH)]
        self.KP = [self.dint('KP%d' % j, [512, SEG], BF16) for j in range(4)]
        self.KRp = self.dint('KRp', [ROPE, SEG], BF16)
        self.VP = [self.dint('VP%d' % q, [SEG, 512], BF16) for q in range(4)]
        self.b_kvloc = Buf()
        self.b_kvd = [Buf(), Buf()]
        self.KG = [[self.dint('KG%d%d' % (r, j), [4 * 512, SEG], BF16) for j in range(4)] for r in range(NR)]
        self.KRG = [self.dint('KRG%d' % r, [4 * ROPE, SEG], BF16) for r in range(NR)]
        self.VG = [[self.dint('VG%d%d' % (r, q), [4 * SEG, 512], BF16) for q in range(4)] for r in range(NR)]
        self.b_kvall = [Buf() for _ in range(NR)]
        self.b_out = Buf('out')

        self.ident = self.sb('ident', [128, 128])
        self.ident_b = self.sb('ident_b', [128, 128], BF16)
        self.ones_b = self.sb('ones_b', [128, 128], BF16)
        self.ones_f = self.sb('ones_f', [128, 128])
        self.uneg = {L: self.sb('uneg%d' % L, [L, L]) for L in (128, 16)}
        self.mask = {L: self.sb('mask%d' % L, [L, L]) for L in (128, 16)}
        self.maskT = {L: self.sb('maskT%d' % L, [L, L]) for L in (128, 16)}
        self.sel = {L: self.sb('sel%d' % L, [L, 128]) for L in (128, 16)}
        self.perm = self.sb('perm', [64, 64], BF16)
        self.epsc = self.sb('epsc', [128, 1])
        self.b_const = Buf('const')
        self.xT = self.sb('xT', [128, KC, SEG]); self.b_x = Buf('x')
        self.xb = self.sb('xb', [128, KC, SEG], BF16); self.b_xb = Buf('xb')
        self.tail = self.sb('tail', [128, DEPTH, 2, 2, 88]); self.b_tail = Buf('tail')
        self.nst = self.sb('nst', [128, NA, 2, 2, H, 2]); self.b_nst = Buf('nst')
        self.nsb = self.sb('nsb', [128, NA, 2, 2, H, 2], BF16)
        self.mst = self.sb('mst', [128, NA, 2, H]); self.b_mst = Buf('mst')
        self.gn = self.sb('gn', [128, 9, KC]); self.b_gn = Buf('gn')
        self.cw = self.sb('cw', [128, DEPTH, 3, 88]); self.cb = self.sb('cb', [128, DEPTH, 88])
        self.gmisc = self.sb('gmisc', [128, 64])
        self.cm = self.sb('cm', [128, 40])
        self.fsum = self.sb('fsum', [128, H])
        self.b_cw = self.b_gn; self.b_gm = self.b_gn
        self.arena = self.sb('arena', [128, 34200])
        self.ps = [self.st.enter_context(nc.psum_tensor('ps%d' % i, [128, 512], F32)) for i in range(8)]
        self.bps = [Buf('ps%d' % i) for i in range(8)]
        self.O_STG = 0
        self.O_WR = 8192
        self.O_SQ = 12288
        self.O_RSTD = 12800
        self.O_PH = 13312
        self.O_TMP = 13312

        try:
            self.init_consts()
            self.dbg('init')
            self.segment(grp=1, s=0, T=DS, L=DS)
            for s in range(NR if NSEG > 0 else 0):
                self.segment(grp=0, s=s, T=SEG, L=128)
        except _Stop:
            pass
        k.barrier()
        k._wait('sp', [(p, k.cnt[p]) for p in k.pool if k.cnt[p] > 0])

    def init_consts(self):
        nc, k, d = self.nc, self.k, self.d
        W = [self.b_const]
        g = nc.gpsimd

        def sel(t, pattern, cmp, fill, base, cm):
            k.op('pool', lambda: g.affine_select(out=t, in_=t, pattern=pattern, compare_op=cmp, fill=fill,
                                                base=base, channel_multiplier=cm), W, W)
        self.ms(self.ident[:], 0.0, W, 'pool')
        sel(self.ident[:], [[-1, 128]], ALU.not_equal, 1.0, 0, 1)
        self.cp(self.ident_b[:], self.ident[:], W, W, 'dve')
        self.ms(self.ones_f[:], 1.0, W, 'pool')
        self.cp(self.ones_b[:], self.ones_f[:], W, W, 'dve')
        self.ms(self.epsc[:], EPS, W, 'pool')
        for L in (128, 16):
            self.ms(self.uneg[L][:], -1.0, W, 'pool')
            sel(self.uneg[L][:], [[1, L]], ALU.is_ge, 0.0, 0, -1)
            self.ms(self.mask[L][:], 0.0, W, 'pool')
            sel(self.mask[L][:], [[-1, L]], ALU.is_ge, NEG, 0, 1)
            self.ms(self.maskT[L][:], 0.0, W, 'pool')
            sel(self.maskT[L][:], [[1, L]], ALU.is_ge, NEG, 0, -1)
            self.ms(self.sel[L][:], 1.0, W, 'pool')
            sel(self.sel[L][:], [[0, 128]], ALU.is_equal, 0.0, -(L - 1), 1)
        pf = self.fv(20000, 64, 64)
        self.ms(pf, 0.0, W, 'pool')
        sel(pf, [[-1, 64]], ALU.not_equal, 1.0, -32, 1)
        sel(pf, [[-1, 64]], ALU.not_equal, 1.0, 32, 1)
        self.cp(self.perm[:], pf, W, W, 'dve')
        k.barrier()
        for i in range(4):
            self.load_fm(self.gn[:, i, :], d['norm_mix'][i], KC)
            self.load_fm(self.gn[:, 4 + i, :], d['norm_ffn'][i], KC)
        self.load_fm(self.gn[:, 8, :], d['kv_norm'][0], KC)
        for l in range(DEPTH):
            for j in range(3):
                self.load_fm(self.cw[:, l, j, :], d['f_conv_w'][l, j], 88)
            self.load_fm(self.cb[:, l, :], d['f_conv_b'][l], 88)
        gm = self.gmisc
        self.ms(gm[:], 0.0, [self.b_gn], 'pool')
        for j in range(2):
            self.load_fm(gm[:, 6 * j:6 * j + 6], d['b_g_cq'][j], 6)
            self.load_fm(gm[:, 12 + j:13 + j], d['b_g_qn'][j], 1)
            self.load_fm(gm[0:64, 14 + j:15 + j], d['b_g_qr'][j], 1, rows=64)
            self.load_fm(gm[:, 22 + 16 * j:38 + 16 * j], d['a_g_head'][j], 16)
            self.load_fm(gm[0:8, 54 + j:55 + j], d['a_b_gate'][j], 1, rows=8)
        self.load_fm(gm[:, 16:20], d['kv_g_c'][0], 4)
        self.load_fm(gm[0:64, 20:21], d['kv_g_r'][0], 1, rows=64)
        self.load_fm(gm[:, 21:22], d['kv_g_kn'][0], 1)
        self.ms(self.tail[:], 0.0, [self.b_tail], 'pool')
        self.ms(self.mst[:], 0.0, [self.b_mst], 'pool')
        self.ms(self.fsum[:], 0.0, [self.b_mst], 'pool')
        self.ms(self.nst[:], 0.0, [self.b_nst], 'pool')
        k.barrier()
        for l in range(DEPTH):
            for r in range(2):
                self.load_fm(self.tail[:, l, 1, r, :], d['sconv'][l, r], 88)
        for l in range(NA):
            k.dma('sp', self.mst[:, l, 1, :], d['sm'][l:l + 1, :].broadcast_to([128, H]), (), [self.b_mst])
            nt = self.fv(20100, 8)
            self.load_fm(nt, d['sn'][l], 8)
            for e in range(2):
                self.cp(self.nst[:, l, 1, :, :, e], nt.rearrange("p (h c) -> p c h", h=H), [self.b_gn], [self.b_nst], 'dve')
        self.cp(self.nsb[:], self.nst[:], [self.b_nst], [self.b_nst], 'dve')
        k.dma('sp', self.cm[:], d['cmask'][:, :], (), [self.b_gn])
        zc = self.fv(0, XW)
        self.ms(zc, 0.0, W, 'pool')
        for l in range(NA):
            for c in range(2):
                k.dma('sp', self.xsC[l][1][c][:, :], zc[:, 0:2048], W, [self.b_xs2[l]])
            k.dma('sp', self.xsS[l][128:256, :], zc[:, 0:16], W, [self.b_xs2[l]])
        for l in range(DEPTH):
            k.dma('sp', self.ts2[l][128:256, :], zc[:, 0:176], W, [self.b_ts2[l]])
        k.barrier()

    def linear(self, w, blocks, kgroups, rhs_fn, T, evac, gcol=None, ps_ids=(0, 1), extra_R=(), sets=None):
        k = self.k
        wv = w.rearrange("(c p) n -> p c n", p=128)
        NS = 4
        if not hasattr(self, 'wslot'):
            self.wslot = 0
            self.b_wst = [Buf() for _ in range(NS)]
            self.b_wr = [Buf() for _ in range(NS)]
        if sets is None:
            sets = [[0, 1, 4, 5], [6, 7, 2, 3]]
        kcs = [kc for (k0, nk) in kgroups for kc in range(k0, k0 + nk)]
        groups = []
        cur = []
        for bi, (c0, m) in enumerate(blocks):
            if cur and (cur[-1][1] + cur[-1][2] == c0) and (sum(x[2] for x in cur) + m <= 512) and (len(cur) < min(len(x) for x in sets)):
                cur.append((bi, c0, m))
            else:
                if cur:
                    groups.append(cur)
                cur = [(bi, c0, m)]
        if cur:
            groups.append(cur)
        si = 0
        for grp_ in groups:
            banks = sets[si % len(sets)]
            si += 1
            cols = sum(x[2] for x in grp_)
            cbase = grp_[0][1]
            nk_t = max(1, min(2048 // cols, len(kcs)))
            ntile = (len(kcs) + nk_t - 1) // nk_t
            for ti in range(ntile):
                kk = kcs[ti * nk_t:(ti + 1) * nk_t]
                assert kk == list(range(kk[0], kk[0] + len(kk)))
                nk = len(kk)
                s = self.wslot
                self.wslot = (self.wslot + 1) % NS
                stg = self.fv(self.O_STG + s * 2048, nk * cols).rearrange("p (c n) -> p c n", n=cols)
                wr = self.bv(self.O_WR + s * 1024, nk * cols).rearrange("p (c n) -> p c n", n=cols)
                k.dma('sp', stg, wv[:, kk[0]:kk[0] + nk, cbase:cbase + cols], (), [self.b_wst[s]], dsem='w%d' % s)
                self.cp(wr, stg, [self.b_wst[s]], [self.b_wr[s]], ('dve', 'act', 'dve', 'act')[s])
                for gi, (bi, c0, m) in enumerate(grp_):
                    ps, pb = self.ps[banks[gi]], self.bps[banks[gi]]
                    for j in range(nk):
                        rhs, rb = rhs_fn(kk[j])
                        first = (ti == 0 and j == 0)
                        last = (ti == ntile - 1 and j == nk - 1)
                        self.mm(ps[0:m, 0:T], wr[:, j, c0 - cbase:c0 - cbase + m], rhs, first, last,
                                [self.b_wr[s], rb] + list(extra_R), [pb])
            for gi, (bi, c0, m) in enumerate(grp_):
                evac(bi, self.ps[banks[gi]][0:m, 0:T], self.bps[banks[gi]])

    def sumsq_bc(self, src_fn, nch, T, out, outb, n, rows=128, ps_id=2):
        if not hasattr(self, 'b_sq'):
            self.b_sq = [Buf(), Buf()]
        ps, pb = self.ps[ps_id], self.bps[ps_id]
        for c in range(nch):
            src, sbuf = src_fn(c)
            s = c % 2
            sq = self.bv(self.O_SQ + s * 256, 512)
            self.act(sq[0:rows, 0:T], src, AF.Square, [sbuf], [self.b_sq[s]])
            self.mm(ps[:, 0:T], self.ones_b[0:rows, :], sq[0:rows, 0:T], c == 0, c == nch - 1, [self.b_sq[s], self.b_const], [pb])
        orow = out.shape[0]
        self.act(out, ps[0:orow, 0:T], AF.Sqrt, [pb, self.b_const], [outb], bias=self.epsc[0:orow, 0:1], scale=1.0 / n)
        self.recip(out, out, [outb], [outb])

    def x_rstd(self):
        T = self.T
        rstd = self.fv(self.O_RSTD, SEG)
        b = Buf()
        self.sumsq_bc(lambda c: (self.xT[:, c, 0:T], self.b_x), KC, T, rstd[:, 0:T], b, D)
        return rstd, b

    def xb_refresh(self, gidx):
        T = self.T
        rstd, b_rstd = self.x_rstd()
        tmpf = self.fv(self.O_STG, 2 * SEG).rearrange("p (s t) -> p s t", s=2)
        bt = [Buf(), Buf()]
        j = 0
        for c in range(KC):
            if c % 2 == 0:
                self.stt(self.xb[:, c, 0:T], self.xT[:, c, 0:T], self.gn[:, gidx, c:c + 1], rstd[:, 0:T], ALU.mult, ALU.mult,
                         [self.b_x, self.b_gn, b_rstd], [self.b_xb])
            else:
                sl = j % 2; j += 1
                self.act(tmpf[:, sl, 0:T], self.xT[:, c, 0:T], AF.Copy, [self.b_x, self.b_gn], [bt[sl]], scale=self.gn[:, gidx, c:c + 1])
                self.tt(self.xb[:, c, 0:T], tmpf[:, sl, 0:T], rstd[:, 0:T], ALU.mult, [bt[sl], b_rstd], [self.b_xb], 'pool')

    def resid_evac(self):
        T = self.T
        def ev(bi, ps, pb):
            self.tt(self.xT[:, bi, 0:T], ps, self.xT[:, bi, 0:T], ALU.add, [pb, self.b_x], [self.b_x])
        return ev

    def segment(self, grp, s, T, L):
        nc, k, d, o = self.nc, self.k, self.d, self.o
        self.grp, self.sidx, self.T, self.L = grp, s, T, L
        pos0 = s * SEG if grp == 0 else NR * SEG
        xsrc = d['xp'][s * SEG:(s + 1) * SEG, :] if grp == 0 else d['xs']
        k.barrier()
        tmp = self.fv(self.O_PH, D)
        tb = Buf()
        for t0 in range(0, T, 128):
            n = min(128, T - t0)
            k.dma('sp', tmp[0:n, :], xsrc[t0:t0 + n, :], (), [tb])
            for c4 in range(0, KC, 4):
                ps, pb = self.ps[(c4 // 4) % 2], self.bps[(c4 // 4) % 2]
                for j in range(4):
                    self.tr(ps[:, j * 128:j * 128 + n], tmp[0:n, (c4 + j) * 128:(c4 + j + 1) * 128], n, [tb], [pb])
                self.cp(self.xT[:, c4:c4 + 4, t0:t0 + n], ps[:, :].rearrange("p (j n) -> p j n", j=4)[:, :, 0:n], [pb], [self.b_x])
        self.dbg('xload')
        for layer in range(DEPTH):
            k.barrier()
            self.xb_refresh(layer)
            k.barrier()
            if layer < NA:
                self.mlstm(layer)
            else:
                self.mla(layer - NA)
            k.barrier()
            self.dbg('mixer%d' % layer)
            self.xb_refresh(4 + layer)
            k.barrier()
            self.ffn(layer)
            self.dbg('ffn%d' % layer)
            if layer == NA - 1:
                k.barrier()
                self.xb_refresh(8)
                k.barrier()
                self.shared_kv(pos0)
        k.barrier()
        ydst = o['yp'][s * SEG:(s + 1) * SEG, :] if grp == 0 else o['ys']
        for t0 in range(0, T, 128):
            n = min(128, T - t0)
            for c4 in range(0, KC, 4):
                ps, pb = self.ps[(c4 // 4) % 2], self.bps[(c4 // 4) % 2]
                for j in range(4):
                    self.tr(ps[0:n, j * 128:(j + 1) * 128], self.xT[:, c4 + j, t0:t0 + n], 128, [self.b_x], [pb])
                self.cp(tmp[0:n, c4 * 128:(c4 + 4) * 128], ps[0:n, :], [pb], [tb])
            k.dma('sp', ydst[t0:t0 + n, :], tmp[0:n, :], [tb], [self.b_out], dsem='out')

    def ffn(self, layer):
        nc, k, d, o = self.nc, self.k, self.d, self.o
        T, grp = self.T, self.grp
        P0 = self.O_PH
        ug = self.fv(P0, SEG + 2); b_ug = Buf()
        uv = self.fv(P0 + 514, SEG + 2); b_uv = Buf()
        cg = self.fv(P0 + 1028, SEG); b_cg = Buf()
        cv = self.fv(P0 + 1540, SEG); b_cv = Buf()
        sg4 = self.fv(P0 + 14300, 4 * SEG).rearrange("p (s t) -> p s t", s=4); b_sg4 = [Buf() for _ in range(4)]
        t1 = self.fv(P0 + 2564, 128); t2 = self.fv(P0 + 2692, 128)
        actT = self.bv(P0 + 3000, FC * SEG).rearrange("p (c t) -> p c t", c=FC); b_act = Buf()
        gcol = self.gn[:, 4 + layer, :]
        tl = self.tail[:, layer, grp]
        blocks = []; bmap = []
        for j0 in range(0, FC, 4):
            for j in range(j0, j0 + 4):
                blocks.append((j * 128, 128)); bmap.append((j, 0))
            for j in range(j0, j0 + 4):
                blocks.append((DFF + j * 128, 128)); bmap.append((j, 1))
        sgs = self.fv(self.O_SQ, 4 * SEG).rearrange("p (s t) -> p s t", s=4) if False else None

        def conv(u, ub, fc, out, outb):
            self.ts(out[:, 0:T], u[:, 0:T], self.cw[:, layer, 0, fc:fc + 1], self.cb[:, layer, fc:fc + 1], ALU.mult, ALU.add,
                    [ub, self.b_cw], [outb])
            for jj in (1, 2):
                self.stt(out[:, 0:T], u[:, jj:jj + T], self.cw[:, layer, jj, fc:fc + 1], out[:, 0:T], ALU.mult, ALU.add,
                         [ub, self.b_cw, outb], [outb])

        def evac(bi, ps, pb):
            j, isv = bmap[bi]
            fc = j + (FC if isv else 0)
            u, ub = (uv, b_uv) if isv else (ug, b_ug)
            cc, cb_ = (cv, b_cv) if isv else (cg, b_cg)
            sg, b_sg = sg4[:, j % 4, :], b_sg4[j % 4]
            if grp == 1:
                self.cp(u[:, 0:2], tl[:, :, fc], [self.b_tail], [ub], 'pool')
                self.cp(u[:, 2:2 + T], ps, [pb], [ub], 'act')
                self.cp(tl[:, :, fc], u[:, T:T + 2], [ub], [self.b_tail], 'pool')
                conv(u, ub, fc, cc, cb_)
                lo = 0
            else:
                self.cp(tl[:, :, fc], ps[:, T - 2:T], [pb], [self.b_tail], 'dve')
                self.cp(ufirst[:, fc, :], ps[:, 0:2], [pb], [b_uf], 'dve')
                self.ts(cc[:, 2:T], ps[:, 0:T - 2], self.cw[:, layer, 0, fc:fc + 1], self.cb[:, layer, fc:fc + 1], ALU.mult, ALU.add,
                        [pb, self.b_cw], [cb_])
                for jj in (1, 2):
                    self.stt(cc[:, 2:T], ps[:, jj:T - 2 + jj], self.cw[:, layer, jj, fc:fc + 1], cc[:, 2:T], ALU.mult, ALU.add,
                             [pb, self.b_cw, cb_], [cb_])
                lo = 2
            if not isv:
                self.act(sg[:, lo:T], cg[:, lo:T], AF.Silu, [b_cg], [b_sg])
            else:
                self.tt(actT[:, j, lo:T], sg[:, lo:T], cv[:, lo:T], ALU.mult, [b_sg, b_cv], [b_act], 'pool')

        ufirst = self.fv(P0 + 2820, 176).rearrange("p (c r) -> p c r", r=2); b_uf = Buf()
        self.linear(d['f_w_up'][layer], blocks, [(0, KC)], lambda c: (self.xb[:, c, 0:T], self.b_xb), T, evac)
        if grp == 0:
            k.barrier()
            tcur = self.tail[:, layer, 0].rearrange("p r c -> p (r c)")
            k.dma('sp', self.ts2[layer][0:128, :], tcur, [self.b_tail, self.b_tg[layer]], [self.b_ts2[layer]])
            k.collective(self.ts2[layer][:, :], self.tg[layer][:, :], GROUPS, [self.b_ts2[layer]], [self.b_tg[layer]])
            tgs = self.fv(P0, 1408).rearrange("p (i t c) -> p i t c", i=4, t=2); bfx = Buf()
            k.dma('sp', tgs, self.tg[layer].rearrange("(i t p) c -> p i t c", t=2, p=128), [self.b_tg[layer]], [bfx])
            halo = self.fv(P0 + 1408, 176)
            Rf = [bfx, self.b_gn, b_uf]
            self.ts(halo, tgs[:, 3, 1, :], self.cm[:, 28:29], None, ALU.mult, None, Rf, [bfx])
            for i in range(4):
                self.stt(halo, tgs[:, i, 0, :], self.cm[:, 24 + i:25 + i], halo, ALU.mult, ALU.add, Rf, [bfx])
            k.dma('sp', self.ts2[layer][128:256, :], tcur, [self.b_tail, self.b_tg[layer]], [self.b_ts2[layer]])
            h0 = halo[:, 0:88]; h1 = halo[:, 88:176]
            u0 = ufirst[:, :, 0]; u1 = ufirst[:, :, 1]
            w0, w1, w2 = (self.cw[:, layer, jj, :] for jj in range(3)); bb = self.cb[:, layer, :]
            c0 = self.fv(P0 + 1584, 88); c1 = self.fv(P0 + 1672, 88); ta = self.fv(P0 + 1760, 88); sgl = self.fv(P0 + 1848, 88)
            for (cc, x0, x1, x2) in ((c0, h0, h1, u0), (c1, h1, u0, u1)):
                self.tt(cc, w0, x0, ALU.mult, Rf, [bfx]); self.tt(cc, cc, bb, ALU.add, Rf, [bfx])
                self.tt(ta, w1, x1, ALU.mult, Rf, [bfx]); self.tt(cc, cc, ta, ALU.add, Rf, [bfx])
                self.tt(ta, w2, x2, ALU.mult, Rf, [bfx]); self.tt(cc, cc, ta, ALU.add, Rf, [bfx])
            for t, cc in ((0, c0), (1, c1)):
                self.act(sgl[:, 0:44], cc[:, 0:44], AF.Silu, Rf, [bfx])
                self.tt(actT[:, :, t], sgl[:, 0:44], cc[:, 44:88], ALU.mult, Rf, [b_act])
            k.barrier()
        self.linear(d['f_w_down'][layer], [(c * 128, 128) for c in range(KC)], [(0, FC)],
                    lambda c: (actT[:, c, 0:T], b_act), T, self.resid_evac(), extra_R=[self.b_x])
        if grp == 1 or self.sidx == NR - 1:
            dst = o['sconvo'] if grp == 1 else o['pconv']
            tb = Buf()
            tf = self.tail[:, layer, grp].rearrange("p r c -> p (r c)")
            ps, pb = self.ps[2], self.bps[2]
            self.tr(ps[0:128, 0:128], tf[:, 0:128], 128, [self.b_tail], [pb])
            self.tr(ps[0:48, 128:256], tf[:, 128:176], 128, [self.b_tail], [pb])
            self.cp(t1, ps[0:128, 0:128], [pb], [tb]); self.cp(t2[0:48, :], ps[0:48, 128:256], [pb], [tb])
            dv = [dst[layer, r].rearrange("(c p) -> c p", p=128) for r in range(2)]
            k.dma('sp', dv[0][0:88, :], t1[0:88, :], [tb], [self.b_out], dsem='out')
            k.dma('sp', dv[1][0:40, :], t1[88:128, :], [tb], [self.b_out], dsem='out')
            k.dma('sp', dv[1][40:88, :], t2[0:48, :], [tb], [self.b_out], dsem='out')

    def mlstm(self, layer):
        nc, k, d, o = self.nc, self.k, self.d, self.o
        T, L, grp = self.T, self.L, self.grp
        P0 = self.O_PH
        qT = self.bv(P0, 8 * SEG).rearrange("p (c t) -> p c t", c=8); b_q = Buf()
        kT = self.bv(P0 + 2048, 8 * SEG).rearrange("p (c t) -> p c t", c=8); b_k = Buf()
        vT = self.fv(P0 + 4096, 16 * SEG).rearrange("p (c t) -> p c t", c=16); b_v = Buf()
        gT = self.fv(P0 + 12288, SEG, 8); b_g = Buf()
        sgt2 = self.fv(self.O_SQ, 2 * SEG).rearrange("p (s t) -> p s t", s=2); b_sg2 = [Buf(), Buf()]
        O_SSTG = 6144
        O_GH = 10240
        S0 = P0 + 13312
        gcol = self.gn[:, layer, :]
        w_in = d['a_w_in'][layer]
        sq_, sv_ = H * DK, H * DV
        blocks = [(c * 128, 128) for c in range(8)] + [(sq_ + c * 128, 128) for c in range(8)] + \
                 [(2 * sq_ + c * 128, 128) for c in range(16)] + [(2 * sq_ + 2 * sv_, 8)]

        def evac(bi, ps, pb):
            e = 'act' if bi % 2 else 'dve'
            if bi < 8:
                self.cp(qT[:, bi, 0:T], ps, [pb], [b_q], e)
            elif bi < 16:
                self.cp(kT[:, bi - 8, 0:T], ps, [pb], [b_k], e)
            elif bi < 32:
                self.cp(vT[:, bi - 16, 0:T], ps, [pb], [b_v], e)
            else:
                self.ts(gT[:, 0:T], ps, self.gmisc[0:8, 54 + layer:55 + layer], None, ALU.add, None, [pb, self.b_gm], [b_g])

        self.linear(w_in, blocks, [(0, KC)], lambda c: (self.xb[:, c, 0:T], self.b_xb), T, evac)
        k.barrier()
        self.dbg('mproj')
        CT = self.fv(0, 4096).rearrange("p (c h v) -> p c h v", c=2, h=H); b_ct = Buf()
        CTb = self.bv(4096, 4096).rearrange("p (c h v) -> p c h v", c=2, h=H)
        nT = self.nst[:, layer, grp]
        nTb = self.nsb[:, layer, grp]
        mbc = self.mst[:, layer, grp, :]
        ghb = self.fv(O_GH, 2048).rearrange("p (h v) -> p h v", h=H); b_gh = Buf()
        k.dma('sp', ghb[0:L], d['a_g_head'][layer:layer + 1, :].broadcast_to([L, H * DV]).rearrange("p (h v) -> p h v", h=H), (), [b_gh])
        env = dict(layer=layer, qT=qT, kT=kT, vT=vT, gT=gT, CT=CT, CTb=CTb, nT=nT, nTb=nTb, mbc=mbc, ghb=ghb, S0=S0,
                   b_q=b_q, b_k=b_k, b_v=b_v, b_g=b_g, b_ct=b_ct, b_gh=b_gh)
        if grp == 1:
            stg = self.fv(O_SSTG, 4096).rearrange("p (h c k) -> p h c k", h=H, c=4); sb_ = Buf()
            k.dma('sp', stg, d['sC'][layer].rearrange("h (c p) k -> p h c k", p=128), (), [sb_])
            for h in range(H):
                for kc in range(2):
                    ps, pb = self.ps[(h * 2 + kc) % 2], self.bps[(h * 2 + kc) % 2]
                    for vc in range(4):
                        self.tr(ps[:, vc * 128:(vc + 1) * 128], stg[:, h, vc, kc * 128:(kc + 1) * 128], 128, [sb_], [pb])
                    self.cp(CT[:, kc, h, :], ps[:, :], [pb], [b_ct])
        else:
            self.ms(self.fv(0, 4096), 0.0, [b_ct], 'pool')
            self.ms(nT, 0.0, [self.b_nst], 'dve')
            self.ms(mbc, NEG, [self.b_mst], 'dve')
            self.ms(self.fsum[:], 0.0, [self.b_mst], 'dve')
            k.barrier()
            self.cp(self.bv(4096, 4096), self.fv(0, 4096), [b_ct], [b_ct])
            self.cp(nTb, nT, [self.b_nst], [self.b_nst])
            self.scan_chunks(env, True)
            k.barrier()
            self.exchange_state(layer, env)
            k.barrier()
        self.cp(self.bv(4096, 4096), self.fv(0, 4096), [b_ct], [b_ct])
        self.cp(nTb, nT, [self.b_nst], [self.b_nst])
        self.scan_chunks(env, False)
        k.barrier()
        self.dbg('scan')
        last = (grp == 1) or (self.sidx == NR - 1)
        if grp == 0:
            stt_ = self.fv(S0 + 200, 16); sbb = Buf()
            self.cp(stt_[:, 0:8].rearrange("p (c h) -> p c h", c=2), nT[:, :, :, 0], [self.b_nst], [sbb])
            self.cp(stt_[:, 8:12], mbc, [self.b_mst], [sbb])
            self.ms(stt_[:, 12:16], 0.0, [sbb], 'dve')
            for c in range(2):
                k.dma('sp', self.xsC[layer][1][c][:, :], self.fv(c * 2048, 2048), [b_ct, self.b_xg[layer]], [self.b_xs2[layer]])
            k.dma('sp', self.xsS[layer][128:256, :], stt_, [sbb, self.b_xg[layer]], [self.b_xs2[layer]])
        if last:
            Cd = (o['sCo'] if grp == 1 else o['pC'])[layer]
            nd = (o['sno'] if grp == 1 else o['pn'])[layer]
            md = (o['smo'] if grp == 1 else o['pm'])[layer]
            stg = self.fv(O_SSTG, 4096).rearrange("p (h c k) -> p h c k", h=H, c=4); sb_ = Buf()
            for h in range(H):
                for vc in range(4):
                    ps, pb = self.ps[vc % 2], self.bps[vc % 2]
                    for kc in range(2):
                        self.tr(ps[:, kc * 128:(kc + 1) * 128], CT[:, kc, h, vc * 128:(vc + 1) * 128], 128, [b_ct], [pb])
                    self.cp(stg[:, h, vc, :], ps[:, 0:256], [pb], [sb_])
            k.dma('sp', Cd.rearrange("h (c p) k -> p h c k", p=128), stg, [sb_], [self.b_out], dsem='out')
            nf = self.fv(S0, 8); nb = Buf()
            self.cp(nf.rearrange("p (h c) -> p c h", h=H), nT[:, :, :, 0], [self.b_nst], [nb])
            ps, pb = self.ps[2], self.bps[2]
            self.tr(ps[0:8, 0:128], nf, 128, [nb], [pb])
            nf2 = self.fv(S0 + 16, 128, 8)
            self.cp(nf2, ps[0:8, 0:128], [pb], [nb])
            k.dma('sp', nd.rearrange("(c p) -> c p", p=128), nf2, [nb], [self.b_out], dsem='out')
            k.dma('sp', md.rearrange("(o h) -> o h", o=1), mbc[0:1, :], [self.b_mst], [self.b_out], dsem='out')
        k.barrier()
        def evac_o(bi, ps, pb):
            sg2 = sgt2[:, bi % 2, 0:T]
            self.act(sg2, ps, AF.Sigmoid, [pb], [b_sg2[bi % 2]])
            self.tt(vT[:, bi, 0:T], vT[:, bi, 0:T], sg2, ALU.mult, [b_sg2[bi % 2], b_v], [b_v], 'pool' if bi % 2 else 'dve')
        self.linear(w_in, [(2 * sq_ + sv_ + c * 128, 128) for c in range(16)], [(0, KC)], lambda c: (self.xb[:, c, 0:T], self.b_xb), T, evac_o)
        hr = self.bv(S0, 16 * SEG).rearrange("p (c t) -> p c t", c=16); b_hr = Buf()
        k.barrier()
        for c in range(16):
            self.cp(hr[:, c, 0:T], vT[:, c, 0:T], [b_v], [b_hr], 'dve' if c % 2 else 'act')
        self.linear(d['a_w_out'][layer], [(c * 128, 128) for c in range(KC)], [(0, KC)], lambda c: (hr[:, c, 0:T], b_hr), T,
                    self.resid_evac(), extra_R=[self.b_x])

    def scan_chunks(self, env, state_only):
        nc, k = self.nc, self.k
        T, L = self.T, self.L
        layer = env['layer']
        qT, kT, vT, gT, CT, CTb, nT, nTb, mbc, ghb, S0 = (env[x] for x in ('qT', 'kT', 'vT', 'gT', 'CT', 'CTb', 'nT', 'nTb', 'mbc', 'ghb', 'S0'))
        b_q, b_k, b_v, b_g, b_ct, b_gh = (env[x] for x in ('b_q', 'b_k', 'b_v', 'b_g', 'b_ct', 'b_gh'))
        k_c = self.bv(S0, 1024); v_c = self.bv(S0 + 512, 2048); wv = self.bv(S0 + 1536, 2048)
        junk = self.fv(S0 + 1536, 512)
        hh = self.fv(S0 + 2560, 2048)
        sm_ = self.fv(S0 + 4608, 64)
        dg = self.fv(S0 + 4672, 512); dl = self.fv(S0 + 5184, 512); dT = self.fv(S0 + 5696, 512)
        sdT = self.bv(S0 + 6208, 512)
        qs = self.bv(S0 + 6464, 1024).rearrange("p (c t) -> p c t", c=8)
        bcs = self.fv(S0 + 6976, 32)
        w2r = self.bv(S0 + 7008, 2)
        bS = Buf('scan')
        P = self.ps; B = self.bps
        for c in range(T // L):
            c0 = c * L
            R = [bS, b_q, b_k, b_v, b_g, b_ct, b_gh, self.b_nst, self.b_mst, self.b_const]
            Wb = [bS]
            for g4 in range(2):
                pbf = P[0][0:L, :].bitcast(BF16)
                for j in range(4):
                    self.tr(pbf[:, j * 128:(j + 1) * 128], kT[:, g4 * 4 + j, c0:c0 + L], 128, R, [B[0]], bf=True)
                self.cp(k_c[0:L, g4 * 512:(g4 + 1) * 512], pbf[:, 0:512], [B[0]], Wb)
            for g4 in range(4):
                pp = 1 + g4 % 2
                for j in range(4):
                    self.tr(P[pp][0:L, j * 128:(j + 1) * 128], vT[:, g4 * 4 + j, c0:c0 + L], 128, R, [B[pp]])
                self.cp(v_c[0:L, g4 * 512:(g4 + 1) * 512], P[pp][0:L, :], [B[pp]], Wb, 'act')
            self.tr(P[3][0:L, 0:8], gT[:, c0:c0 + L], 8, R, [B[3]])
            gi = sm_[0:L, 0:4]; sp_ = sm_[0:L, 4:8]; b_ = sm_[0:L, 8:12]; a_ = sm_[0:L, 12:16]
            il = sm_[0:L, 16:20]; mt = sm_[0:L, 20:24]; mb = sm_[0:L, 20:28]; u_ = sm_[0:L, 28:32]
            iw = sm_[0:L, 32:36]; en = sm_[0:L, 36:40]; w_ = sm_[0:L, 40:44]; den = sm_[0:L, 44:48]
            ss = sm_[0:L, 48:52]; rr = sm_[0:L, 52:56]; mloc = sm_[0:L, 56:60]; tmp4 = sm_[0:L, 60:64]
            self.cp(gi, P[3][0:L, 0:4], [B[3]], Wb)
            self.act(sp_, P[3][0:L, 4:8], AF.Exp, [B[3]], Wb, scale=-1.0)
            self.act(sp_, sp_, AF.Ln, [bS], Wb, bias=1.0)
            self.mm(P[3][0:L, 8:12], self.uneg[L][:, :], sp_, True, True, R, [B[3]])
            self.cp(b_, P[3][0:L, 8:12], [B[3]], Wb)
            self.cp(sm_[0:L, 24:28], b_, R, Wb)
            self.tt(a_, gi, b_, ALU.subtract, R, Wb)
            idl = self.ident[0:L, 0:L]
            dg3 = dg[0:L, 0:4 * L].rearrange("p (h s) -> p h s", h=H)
            dl3 = dl[0:L, 0:4 * L].rearrange("p (h s) -> p h s", h=H)
            dT3 = dT[0:L, 0:4 * L].rearrange("p (h s) -> p h s", h=H)
            sd3 = sdT[0:L, 0:4 * L].rearrange("p (h s) -> p h s", h=H)

            def rowbc(col4, ps_ap, psb, lhs):
                self.tt(dg3, bc(idl.unsqueeze(1), [L, H, L]), bc(col4.unsqueeze(2), [L, H, L]), ALU.mult, R, Wb)
                self.mm(ps_ap, lhs, dg[0:L, 0:4 * L], True, True, R, [psb])
            rowbc(a_, P[4][0:L, 0:4 * L], B[4], self.ones_f[0:L, 0:L])
            p4 = P[4][0:L, 0:4 * L].rearrange("p (h s) -> p h s", h=H)
            self.tt(dl3, p4, bc(b_.unsqueeze(2), [L, H, L]), ALU.add, [B[4]] + R, Wb)
            self.tt(dl3, dl3, bc(self.mask[L][:, :].unsqueeze(1), [L, H, L]), ALU.add, R, Wb)
            self.k.op('dve', lambda: nc.vector.tensor_reduce(out=mloc, in_=dl3, axis=AX.X, op=ALU.max), R, Wb)
            self.tt(il, b_, mbc[0:L, :], ALU.add, R, Wb)
            self.tt(mt, il, mloc, ALU.max, R, Wb)
            if not state_only:
                self.tt(u_, b_, mt, ALU.subtract, R, Wb)
                self.tt(tmp4, il, mt, ALU.subtract, R, Wb)
                self.act(iw, tmp4, AF.Exp, R, Wb)
                self.act(en, mt, AF.Exp, R, Wb, scale=-1.0)
                rowbc(u_, P[4][0:L, 0:4 * L], B[4], self.ones_f[0:L, 0:L])
                self.tt(dT3, p4, bc(a_.unsqueeze(2), [L, H, L]), ALU.add, [B[4]] + R, Wb)
                self.tt(dT3, dT3, bc(self.maskT[L][:, :].unsqueeze(1), [L, H, L]), ALU.add, R, Wb)
                self.act(dT[0:L, 0:4 * L], dT[0:L, 0:4 * L], AF.Exp, R, Wb)
                for h in range(H):
                    for kc in range(2):
                        self.mm(P[5][0:L, h * L:(h + 1) * L], kT[:, h * 2 + kc, c0:c0 + L], qT[:, h * 2 + kc, c0:c0 + L], kc == 0, kc == 1, R, [B[5]])
                self.stt(sdT[0:L, 0:4 * L], P[5][0:L, 0:4 * L], DK ** -0.5, dT[0:L, 0:4 * L], ALU.mult, ALU.mult, [B[5]] + R, Wb)
                rowbc(iw, P[4][:, 0:4 * L], B[4], self.ones_f[0:L, :])
                pw = P[4][:, 0:4 * L].rearrange("p (h t) -> p h t", h=H)
                for kc in range(2):
                    qv = qT[:, :, c0:c0 + L].rearrange("p (h c) t -> p h c t", c=2)[:, :, kc, :]
                    qsv = qs[:, :, 0:L].rearrange("p (h c) t -> p h c t", c=2)[:, :, kc, :]
                    self.tt(qsv, qv, pw, ALU.mult, [B[4]] + R, Wb)
                for h in range(H):
                    pn_, bn_ = P[6 + h % 2], B[6 + h % 2]
                    self.mm(pn_[0:L, :], sd3[:, h, :], v_c[0:L, h * DV:(h + 1) * DV], True, False, R, [bn_])
                    for kc in range(2):
                        self.mm(pn_[0:L, :], qs[:, h * 2 + kc, 0:L], CTb[:, kc, h, :], False, kc == 1, R, [bn_])
                    self.mm(P[3][0:L, 16 + 2 * h:18 + 2 * h], sd3[:, h, :], self.ones_b[0:L, 0:2], True, False, R, [B[3]])
                    for kc in range(2):
                        self.mm(P[3][0:L, 16 + 2 * h:18 + 2 * h], qs[:, h * 2 + kc, 0:L], nTb[:, kc, h, :], False, kc == 1, R, [B[3]])
                    qn = P[3][0:L, 16 + 2 * h:17 + 2 * h]
                    self.act(den[:, h:h + 1], qn, AF.Abs, [B[3]] + R, Wb)
                    self.tt(den[:, h:h + 1], den[:, h:h + 1], en[:, h:h + 1], ALU.max, R, Wb)
                    self.recip(den[:, h:h + 1], den[:, h:h + 1], R, Wb)
                    self.act(junk[0:L, :], pn_[0:L, :], AF.Square, [bn_] + R, Wb, scale=den[:, h:h + 1], accum=ss[:, h:h + 1])
                    self.act(rr[:, h:h + 1], ss[:, h:h + 1], AF.Sqrt, R, Wb, bias=self.epsc[0:L, 0:1], scale=1.0 / DV)
                    self.recip(rr[:, h:h + 1], rr[:, h:h + 1], R, Wb)
                    self.tt(rr[:, h:h + 1], rr[:, h:h + 1], den[:, h:h + 1], ALU.mult, R, Wb)
                    self.stt(hh[0:L, h * DV:(h + 1) * DV], pn_[0:L, :], rr[:, h:h + 1], ghb[0:L, h, :], ALU.mult, ALU.mult, [bn_] + R, Wb)
            self.mm(P[3][:, 32:40], self.sel[L][:, :], mb, True, True, R, [B[3]])
            self.cp(bcs[:, 0:8], P[3][:, 32:40], [B[3]], Wb)
            mnew = bcs[:, 0:4]; blast = bcs[:, 4:8]; dec = bcs[:, 8:12]; t12 = bcs[:, 12:16]
            self.tt(self.fsum[:], self.fsum[:], blast, ALU.add, R, [self.b_mst, bS])
            self.tt(t12, blast, mnew, ALU.subtract, R, Wb)
            self.tt(dec, t12, mbc, ALU.add, R, Wb)
            self.act(dec, dec, AF.Exp, R, Wb)
            self.tt(w_, a_, t12[0:L, :], ALU.add, R, Wb)
            self.act(w_, w_, AF.Exp, R, Wb)
            self.ts(w_, w_, DK ** -0.5, None, ALU.mult, None, R, Wb)
            self.tt(wv[0:L, :].rearrange("p (h v) -> p h v", h=H), v_c[0:L, :].rearrange("p (h v) -> p h v", h=H),
                    bc(w_.unsqueeze(2), [L, H, DV]), ALU.mult, R, Wb)
            for h in range(H):
                self.cp(w2r[0:L, :], bc(w_[:, h:h + 1], [L, 2]), R, Wb)
                for kc in range(2):
                    pc, bcb = P[6 + kc], B[6 + kc]
                    self.mm(pc[:, :], k_c[0:L, h * DK + kc * 128:h * DK + (kc + 1) * 128], wv[0:L, h * DV:(h + 1) * DV], True, True, R, [bcb])
                    self.stt(CT[:, kc, h, :], CT[:, kc, h, :], dec[:, h:h + 1], pc[:, :], ALU.mult, ALU.add, [bcb, b_ct] + R, [b_ct])
                    if not state_only:
                        self.cp(CTb[:, kc, h, :], CT[:, kc, h, :], [b_ct], [b_ct], 'act')
                    self.mm(P[3][:, 48:50], k_c[0:L, h * DK + kc * 128:h * DK + (kc + 1) * 128], w2r[0:L, :], True, True, R, [B[3]])
                    self.stt(nT[:, kc, h, :], nT[:, kc, h, :], dec[:, h:h + 1], P[3][:, 48:50], ALU.mult, ALU.add,
                             [B[3], self.b_nst] + R, [self.b_nst])
                    if not state_only:
                        self.cp(nTb[:, kc, h, :], nT[:, kc, h, :], [self.b_nst], [self.b_nst])
            self.cp(mbc, mnew, R, [self.b_mst])
            if not state_only:
                for g4 in range(4):
                    pp = 4 + g4 % 2
                    for j in range(4):
                        self.tr(P[pp][:, j * L:(j + 1) * L], hh[0:L, (g4 * 4 + j) * 128:(g4 * 4 + j + 1) * 128], L, R, [B[pp]])
                    self.cp(vT[:, g4 * 4:g4 * 4 + 4, c0:c0 + L], P[pp][:, 0:4 * L].rearrange("p (j t) -> p j t", j=4), [B[pp]], [b_v, bS])

    def exchange_state(self, layer, env):
        nc, k = self.nc, self.k
        CT, nT, mbc, S0, b_ct = env['CT'], env['nT'], env['mbc'], env['S0'], env['b_ct']
        grp = self.grp
        sc = self.fv(S0 + 300, 600)
        bsc = Buf()
        stt_ = sc[:, 0:16]
        self.cp(stt_[:, 0:8].rearrange("p (c h) -> p c h", c=2), nT[:, :, :, 0], [self.b_nst], [bsc])
        self.cp(stt_[:, 8:12], mbc, [self.b_mst], [bsc])
        self.cp(stt_[:, 12:16], self.fsum[:], [self.b_mst], [bsc])
        for c in range(2):
            k.dma('sp', self.xsC[layer][0][c][:, :], self.fv(c * 2048, 2048), [b_ct, self.b_xg[layer]], [self.b_xs2[layer]])
        k.dma('sp', self.xsS[layer][0:128, :], stt_, [bsc, self.b_xg[layer]], [self.b_xs2[layer]])
        for t in range(2):
            for c in range(2):
                k.collective(self.xsC[layer][t][c][:, :], self.xgC[layer][t][c][:, :], GROUPS, [self.b_xs2[layer]], [self.b_xg[layer]])
        k.collective(self.xsS[layer][:, :], self.xgS[layer][:, :], GROUPS, [self.b_xs2[layer]], [self.b_xg[layer]])
        sg = sc[:, 16:16 + 128].rearrange("p (i t c) -> p i t c", i=4, t=2)
        k.dma('sp', sg, self.xgS[layer].rearrange("(i t p) c -> p i t c", t=2, p=128), [self.b_xg[layer]], [bsc])
        cm = self.cm
        F3 = sg[:, :, 0, 12:16]
        m3 = sg[:, :, 0, 8:12]
        mcar = sg[:, 3, 1, 8:12]
        Rr = [bsc, self.b_gn]
        T1 = sc[:, 144:208].rearrange("p (i h l) -> p i h l", i=4, h=4)
        Mv = cm[:, 8:24].rearrange("p (i l) -> p i l", i=4)
        self.tt(T1, bc(Mv.unsqueeze(2), [128, 4, 4, 4]), bc(F3.rearrange("p l h -> p h l").unsqueeze(1), [128, 4, 4, 4]), ALU.mult, Rr, [bsc])
        G = sc[:, 208:224].rearrange("p (i h) -> p i h", i=4)
        self.k.op('dve', lambda: nc.vector.tensor_reduce(out=G, in_=T1, axis=AX.X, op=ALU.add), Rr, [bsc])
        E = sc[:, 224:240].rearrange("p (i h) -> p i h", i=4)
        self.tt(E, m3, G, ALU.add, Rr, [bsc])
        self.tt(E, E, bc(cm[:, 0:4].unsqueeze(2), [128, 4, 4]), ALU.add, Rr, [bsc])
        T2 = sc[:, 240:256].rearrange("p (h l) -> p h l", h=4)
        self.tt(T2, bc(cm[:, 4:8].unsqueeze(1), [128, 4, 4]), F3.rearrange("p l h -> p h l"), ALU.mult, Rr, [bsc])
        Ec = sc[:, 256:260]
        self.k.op('dve', lambda: nc.vector.tensor_reduce(out=Ec, in_=T2, axis=AX.X, op=ALU.add), Rr, [bsc])
        self.tt(Ec, Ec, mcar, ALU.add, Rr, [bsc])
        mx = sc[:, 260:264]
        self.k.op('dve', lambda: nc.vector.tensor_reduce(out=mx, in_=E.rearrange("p i h -> p h i"), axis=AX.X, op=ALU.max), Rr, [bsc])
        min_ = sc[:, 264:268]
        self.tt(min_, mx, Ec, ALU.max, Rr, [bsc])
        Wt = sc[:, 268:284].rearrange("p (i h) -> p i h", i=4)
        self.tt(Wt, E, bc(min_.unsqueeze(1), [128, 4, 4]), ALU.subtract, Rr, [bsc])
        self.act(sc[:, 268:284], sc[:, 268:284], AF.Exp, Rr, [bsc])
        Wc = sc[:, 284:288]
        self.tt(Wc, Ec, min_, ALU.subtract, Rr, [bsc])
        self.act(Wc, Wc, AF.Exp, Rr, [bsc])
        nacc = sc[:, 288:296].rearrange("p (c h) -> p c h", c=2)
        ntmp = sc[:, 296:304].rearrange("p (c h) -> p c h", c=2)
        self.tt(nacc, sg[:, 3, 1, 0:8].rearrange("p (c h) -> p c h", c=2), bc(Wc.unsqueeze(1), [128, 2, 4]), ALU.mult, Rr, [bsc])
        for i in range(4):
            self.tt(ntmp, sg[:, i, 0, 0:8].rearrange("p (c h) -> p c h", c=2), bc(Wt[:, i, :].unsqueeze(1), [128, 2, 4]), ALU.mult, Rr, [bsc])
            self.tt(nacc, nacc, ntmp, ALU.add, Rr, [bsc])
        for e in range(2):
            self.cp(nT[:, :, :, e], nacc, Rr, [self.b_nst])
        self.cp(mbc, min_, Rr, [self.b_mst])
        stg = self.fv(6144, 4096).rearrange("p (c h v) -> p c h v", c=2, h=H); bst = Buf()
        srcs = [(1, 3, None)] + [(0, i, i) for i in range(4)]
        for si, (tt_, rk, i) in enumerate(srcs):
            for c in range(2):
                k.dma('sp', self.fv(6144 + c * 2048, 2048),
                      self.xgC[layer][tt_][c][rk * 128:(rk + 1) * 128, :], [self.b_xg[layer]], [bst])
            for kc in range(2):
                for h in range(H):
                    if i is None:
                        self.ts(CT[:, kc, h, :], stg[:, kc, h, :], Wc[:, h:h + 1], None, ALU.mult, None, [bst] + Rr, [b_ct])
                    else:
                        self.stt(CT[:, kc, h, :], stg[:, kc, h, :], Wt[:, i, h:h + 1], CT[:, kc, h, :], ALU.mult, ALU.add, [bst, b_ct] + Rr, [b_ct])

    def rope(self, dst, dstb, src, srcb, pos0, T, o_cs):
        cs = self.fv(o_cs, SEG, 64); sn = self.fv(o_cs + 512, SEG, 64)
        if not hasattr(self, 'b_rope'):
            self.b_rope = Buf()
        tb = self.b_rope
        self.k.dma('sp', cs[:, 0:T], self.d['cos2'][:, pos0:pos0 + T], (), [tb])
        self.k.dma('sp', sn[:, 0:T], self.d['sin2'][:, pos0:pos0 + T], (), [tb])
        ps, pb = self.ps[3], self.bps[3]
        self.mm(ps[0:64, 0:T], self.perm[:, :], src, True, True, [srcb, self.b_const], [pb])
        self.tt(sn[:, 0:T], ps[0:64, 0:T], sn[:, 0:T], ALU.mult, [pb, tb], [tb])
        self.tt(cs[:, 0:T], src, cs[:, 0:T], ALU.mult, [srcb, tb], [tb])
        self.tt(dst, cs[:, 0:T], sn[:, 0:T], ALU.add, [tb], [dstb])

    def shared_kv(self, pos0):
        nc, k, d, o = self.nc, self.k, self.d, self.o
        T, grp, s = self.T, self.grp, self.sidx
        P0 = self.O_PH
        zT = self.fv(P0, 5 * SEG).rearrange("p (c t) -> p c t", c=5); b_z = Buf()
        cTf = self.fv(P0 + 2560, 4 * SEG).rearrange("p (c t) -> p c t", c=4); b_c = Buf()
        cTb = self.bv(P0 + 4608, 4 * SEG).rearrange("p (c t) -> p c t", c=4)
        kpT = self.fv(P0 + 5632, SEG, 64); b_kp = Buf()
        kpn = self.bv(P0 + 6144, SEG, 64)
        r2 = self.fv(P0 + 6400, SEG); b_r2 = Buf()
        tm = self.fv(P0 + 6912, 576); ld = self.fv(P0 + 7488, 576)
        self.o_kv = P0 + 8300
        blocks = [(c * 128, 128) for c in range(4)] + [(KVL, 64)]

        def evac(bi, ps, pb):
            m = 128 if bi < 4 else 64
            self.cp(zT[0:m, bi, 0:T], ps, [pb], [b_z], 'act' if bi % 2 else 'dve')
        self.linear(d['kv_w_down'], blocks, [(0, KC)], lambda c: (self.xb[:, c, 0:T], self.b_xb), T, evac)
        self.sumsq_bc(lambda c: (zT[:, c, 0:T], b_z), 4, T, r2[:, 0:T], b_r2, KVL)
        for c in range(4):
            self.stt(cTf[:, c, 0:T], zT[:, c, 0:T], self.gmisc[:, 16 + c:17 + c], r2[:, 0:T], ALU.mult, ALU.mult, [b_z, b_r2, self.b_gm], [b_c])
            self.cp(cTb[:, c, 0:T], cTf[:, c, 0:T], [b_c], [b_c], 'act')
        self.sumsq_bc(lambda c: (zT[0:64, 4, 0:T], b_z), 1, T, r2[0:64, 0:T], b_r2, ROPE, rows=64)
        self.stt(kpn[:, 0:T], zT[0:64, 4, 0:T], self.gmisc[0:64, 20:21], r2[0:64, 0:T], ALU.mult, ALU.mult, [b_z, b_r2, self.b_gm], [b_kp])
        self.rope(kpT[:, 0:T], b_kp, kpn[:, 0:T], b_kp, pos0, T, P0 + 18000)
        k0 = 0 if grp == 0 else PAST
        kb = self.b_kvloc if grp == 0 else self.b_kvd[grp]
        kpb = self.bv(P0 + 8000, SEG, 64); b_kpb = Buf()
        self.cp(kpb[:, 0:T], kpT[:, 0:T], [b_kp], [b_kpb])
        if grp == 0:
            k.dma('sp', self.KRp[:, 0:T], kpb[:, 0:T], [b_kpb] + [self.b_kvall[r] for r in range(NR)], [kb], dsem='kv')
        else:
            k.dma('sp', self.kr_dram[grp][:, k0:k0 + T], kpb[:, 0:T], [b_kpb], [kb], dsem='kv')
        cdst = (o['p_ckv'][s * SEG:(s + 1) * SEG] if grp == 0 else o['s_ckv'])
        kdst = (o['p_kpe'][s * SEG:(s + 1) * SEG] if grp == 0 else o['s_kpe'])
        tb = Buf()
        for t0 in range(0, T, 128):
            n = min(128, T - t0)
            ps, pb = self.ps[2], self.bps[2]
            for c in range(4):
                self.tr(ps[0:n, c * 128:(c + 1) * 128], cTf[:, c, t0:t0 + n], 128, [b_c], [pb])
            self.cp(tm[0:n, 0:512], ps[0:n, :], [pb], [tb])
            ps, pb = self.ps[3], self.bps[3]
            self.tr(ps[0:n, 0:64], kpT[:, t0:t0 + n], 64, [b_kp], [pb])
            self.cp(tm[0:n, 512:576], ps[0:n, 0:64], [pb], [tb])
            k.dma('sp', cdst[t0:t0 + n, :], tm[0:n, 0:512], [tb], [self.b_out], dsem='out')
            k.dma('sp', kdst[t0:t0 + n, :], tm[0:n, 512:576], [tb], [self.b_out], dsem='out')
        self.kv_up(cTb, b_c, T, k0)
        if grp == 0:
            k.barrier()
            for j in range(4):
                k.collective(self.KP[j][:, :], self.KG[s][j][:, :], GROUPS, [self.b_kvloc], [self.b_kvall[s]])
                k.collective(self.VP[j][:, :], self.VG[s][j][:, :], GROUPS, [self.b_kvloc], [self.b_kvall[s]])
            k.collective(self.KRp[:, :], self.KRG[s][:, :], GROUPS, [self.b_kvloc], [self.b_kvall[s]])
        if grp == 1:
            k.barrier()
            lb = Buf()
            for blk in range(PAST // SEG):
                for t0 in range(0, SEG, 128):
                    r0 = blk * SEG + t0
                    k.dma('sp', ld[:, 0:512], d['ckv'][r0:r0 + 128, :], (), [lb])
                    k.dma('sp', ld[:, 512:576], d['kpe'][r0:r0 + 128, :], (), [lb])
                    ps, pb = self.ps[2], self.bps[2]
                    for c in range(4):
                        self.tr(ps[:, c * 128:(c + 1) * 128], ld[:, c * 128:(c + 1) * 128], 128, [lb], [pb])
                    self.cp(cTb[:, :, t0:t0 + 128], ps[:, :].rearrange("p (c t) -> p c t", c=4), [pb], [b_c])
                    ps, pb = self.ps[3], self.bps[3]
                    self.tr(ps[0:64, 0:128], ld[:, 512:576], 128, [lb], [pb])
                    self.cp(kpT[:, t0:t0 + 128], ps[0:64, 0:128], [pb], [b_kp])
                self.cp(kpb[:, 0:SEG], kpT[:, 0:SEG], [b_kp], [b_kpb])
                k.dma('sp', self.kr_dram[1][:, blk * SEG:(blk + 1) * SEG], kpb[:, 0:SEG], [b_kpb], [kb], dsem='kv')
                self.kv_up(cTb, b_c, SEG, blk * SEG)

    def kv_up(self, cTb, b_c, T, k0):
        nc, k, d = self.nc, self.k, self.d
        grp = self.grp
        kb = self.b_kvloc if grp == 0 else self.b_kvd[grp]
        O = self.o_kv
        k.barrier()
        kn = self.fv(O, 2 * SEG).rearrange("p (s t) -> p s t", s=2); bkn = [Buf(), Buf()]
        r3 = self.fv(O + 1024, SEG); b_r3 = Buf()
        wvs = self.bv(O + 1536, 4 * 2048).rearrange("p (c n) -> p c n", c=4); b_wv = Buf()
        stg = self.fv(O + 5632, 2048); b_st = Buf()
        vt = self.bv(O + 7680, 2048); b_vt = Buf()
        knh = self.bv(O + 8704, 2 * SEG).rearrange("p (s t) -> p s t", s=2)
        blocks = [(h * 256, 128) for h in range(BH)]

        def evac(bi, ps, pb):
            sl_ = bi % 2
            self.cp(kn[:, sl_, 0:T], ps, [pb], [bkn[sl_]], 'act')
            self.sumsq_bc(lambda c: (kn[:, sl_, 0:T], bkn[sl_]), 1, T, r3[:, 0:T], b_r3, NOPE, ps_id=3)
            self.stt(kn[:, sl_, 0:T], kn[:, sl_, 0:T], self.gmisc[:, 21:22], r3[:, 0:T], ALU.mult, ALU.mult, [bkn[sl_], b_r3, self.b_gm], [bkn[sl_]])
            self.cp(knh[:, sl_, 0:T], kn[:, sl_, 0:T], [bkn[sl_]], [bkn[sl_]], 'pool')
            kdst = self.KP[bi // 4][(bi % 4) * 128:(bi % 4 + 1) * 128, 0:T] if grp == 0 else self.kn_dram[grp][bi, :, k0:k0 + T]
            k.dma('sp', kdst, knh[:, sl_, 0:T], [bkn[sl_]], [kb], dsem='kv')
        self.linear(d['kv_w_up'], blocks, [(0, 4)], lambda c: (cTb[:, c, 0:T], b_c), T, evac, sets=[[0], [1], [4], [5]])
        wv_ = d['kv_w_up'].rearrange("(c p) (h two x) -> p c h two x", p=128, two=2, x=128)
        for c in range(4):
            k.dma('sp', stg.rearrange("p (h x) -> p h x", h=BH), wv_[:, c, :, 1, :], (), [b_st])
            self.cp(wvs[:, c, :], stg, [b_st], [b_wv])
        for t0 in range(0, T, 128):
            n = min(128, T - t0)
            for q4 in range(4):
                ps, pb = self.ps[q4 % 2], self.bps[q4 % 2]
                for c in range(4):
                    self.mm(ps[0:n, :], cTb[:, c, t0:t0 + n], wvs[:, c, q4 * 512:(q4 + 1) * 512], c == 0, c == 3, [b_c, b_wv], [pb])
                self.cp(vt[0:n, q4 * 512:(q4 + 1) * 512], ps[0:n, :], [pb], [b_vt], 'act' if q4 % 2 else 'dve')
            if grp == 0:
                for q in range(4):
                    k.dma('sp', self.VP[q][t0:t0 + n, :], vt[0:n, q * 512:(q + 1) * 512], [b_vt], [kb], dsem='kv')
            else:
                k.dma('sp', self.v_dram[grp][k0 + t0:k0 + t0 + n, :], vt[0:n, :], [b_vt], [kb], dsem='kv')

    def mla(self, j):
        nc, k, d, o = self.nc, self.k, self.d, self.o
        T, grp, s = self.T, self.grp, self.sidx
        pos0 = s * SEG if grp == 0 else NR * SEG
        P0 = self.O_PH
        cq = self.bv(P0, 6 * SEG).rearrange("p (c t) -> p c t", c=6); b_cq = Buf()
        r2 = self.fv(P0 + 1536, SEG); b_r2 = Buf()
        QN = self.bv(P0 + 2048, 16 * SEG).rearrange("p (c t) -> p c t", c=16); b_qn = Buf()
        QR = self.bv(P0 + 6144, 16 * SEG, 64).rearrange("p (c t) -> p c t", c=16); b_qr = Buf()
        r3 = self.fv(P0 + 10240, SEG); b_r3 = Buf()
        qtmp = self.fv(P0 + 10752, SEG); b_qt = Buf()
        qrn = self.bv(P0 + 11264, SEG, 64); b_qrn = Buf()
        O_CS = P0 + 11520
        rd = self.fv(P0 + 12544, SEG); b_rd = Buf()
        cqg = self.bv(P0 + 13100, 6 * SEG).rearrange("p (c t) -> p c t", c=6)

        def evac(bi, ps, pb):
            self.cp(cq[:, bi, 0:T], ps, [pb], [b_cq], 'act')
            self.ts(cqg[:, bi, 0:T], ps, self.gmisc[:, 6 * j + bi:6 * j + bi + 1], None, ALU.mult, None, [pb, self.b_gm], [b_cq])
        self.linear(d['b_w_dq'][j], [(c * 128, 128) for c in range(6)], [(0, KC)], lambda c: (self.xb[:, c, 0:T], self.b_xb), T, evac)
        self.sumsq_bc(lambda c: (cq[:, c, 0:T], b_cq), 6, T, r2[:, 0:T], b_r2, QL)
        blocks = []
        for h in range(BH):
            blocks.append((h * 192, 128)); blocks.append((h * 192 + 128, 64))

        def evac2(bi, ps, pb):
            h, isr = bi // 2, bi % 2
            m = 64 if isr else 128
            self.tt(qtmp[0:m, 0:T], ps, r2[0:m, 0:T], ALU.mult, [pb, b_r2], [b_qt])
            self.sumsq_bc(lambda c: (qtmp[0:m, 0:T], b_qt), 1, T, r3[0:m, 0:T], b_r3, m, rows=m, ps_id=3)
            if not isr:
                self.stt(QN[:, h, 0:T], qtmp[:, 0:T], self.gmisc[:, 12 + j:13 + j], r3[:, 0:T], ALU.mult, ALU.mult, [b_qt, b_r3, self.b_gm], [b_qn])
            else:
                self.stt(qrn[:, 0:T], qtmp[0:64, 0:T], self.gmisc[0:64, 14 + j:15 + j], r3[0:64, 0:T], ALU.mult, ALU.mult, [b_qt, b_r3, self.b_gm], [b_qrn])
                self.rope(QR[:, h, 0:T], b_qr, qrn[:, 0:T], b_qrn, pos0, T, O_CS)
        self.linear(d['b_w_uq'][j], blocks, [(0, 6)], lambda c: (cqg[:, c, 0:T], b_cq), T, evac2, sets=[[0, 1], [4, 5], [6, 7]])
        k.barrier()
        KB = 1024 if grp == 1 else SEG
        knb = self.bv(0, 2 * 1024).rearrange("p (s t) -> p s t", s=2); b_knb = [Buf(), Buf()]
        vb = self.bv(1024, 2 * 1024).rearrange("p (s t x) -> p s t x", s=2, x=128); b_vb = [Buf(), Buf()]
        krall = self.bv(3072, 9 * SEG, 64); b_kr = Buf()
        dacc = self.fv(5376, SEG); daccb = self.bv(5888, SEG); b_da = Buf()
        blocks = []
        if grp == 1:
            nkeys = PAST + DS
            for kk0 in range(0, nkeys, KB):
                nk = min(KB, nkeys - kk0)
                blocks.append((lambda h, kk0=kk0, nk=nk: self.kn_dram[1][h, :, kk0:kk0 + nk],
                               self.kr_dram[1][:, kk0:kk0 + nk],
                               lambda h, t, n, kk0=kk0: self.v_dram[1][kk0 + t * 128:kk0 + t * 128 + n, h * 128:(h + 1) * 128],
                               nk, False, None, self.b_kvd[1]))
        else:
            def mk(KS, KRS, VS, i, diag, bias, dep):
                return (lambda h: KS[h // 4][i * 512 + (h % 4) * 128:i * 512 + (h % 4 + 1) * 128, :],
                        KRS[i * ROPE:(i + 1) * ROPE, :],
                        lambda h, t, n: VS[h // 4][i * SEG + t * 128:i * SEG + t * 128 + n, (h % 4) * 128:(h % 4 + 1) * 128],
                        SEG, diag, bias, dep)
            for rdi in range(s):
                for i in range(4):
                    blocks.append(mk(self.KG[rdi], self.KRG[rdi], self.VG[rdi], i, False, None, self.b_kvall[rdi]))
            for i in range(4):
                blocks.append(mk(self.KG[s], self.KRG[s], self.VG[s], i, False, 32 + i, self.b_kvall[s]))
            blocks.append(mk(self.KP, self.KRp, self.VP, 0, True, None, self.b_kvloc))
        ntot = sum((blk[3] + 127) // 128 for blk in blocks)
        kro = 0
        for (knf, krap, vf, nk, diag, biascol, dep) in blocks:
            k.dma('sp', krall[:, kro:kro + nk], krap, [dep], [b_kr], dsem='ld')
            kro += nk
        DEPTH_ = 3
        sbanks = [0, 1, 6, 7]
        pTs = [self.bv(2048 + 256 * i, SEG) for i in range(4)]
        b_pTs = [Buf() for _ in range(4)]
        slot = 0
        pi = 0
        for h in range(BH):
            po, bo = self.ps[4 + h % 2], self.bps[4 + h % 2]
            pd, bd = self.ps[2 + h % 2], self.bps[2 + h % 2]
            tiles = []
            kro = 0
            for (knf, krap, vf, nk, diag, biascol, dep) in blocks:
                tiles.append(('load', knf, vf, nk, dep))
                for t in range((nk + 127) // 128):
                    n = min(128, nk - t * 128)
                    tiles.append(('tile', t, n, t * 128 if diag else 0, diag, biascol, kro))
                kro += nk
            pend = []
            state = dict(first=True, cnt=0, sl=0)

            def finish(item):
                (ps_, bs_, pp, bp, sl_, t, n, q0, diag, biascol) = item
                if biascol is None:
                    self.act(pp[0:n, q0:T], ps_[0:n, q0:T], AF.Exp, [bs_], [bp], scale=ATT_SCALE)
                else:
                    self.act(pp[0:n, q0:T], ps_[0:n, q0:T], AF.Exp, [bs_, self.b_gn], [bp], scale=ATT_SCALE,
                             bias=self.cm[0:n, biascol:biascol + 1])
                if diag:
                    self.ms(pp[64:128, q0:q0 + 64], 0.0, [bp], 'dve')
                state['cnt'] += 1
                self.mm(po[:, q0:T], vb[0:n, sl_, t, :], pp[0:n, q0:T], state['first'], state['cnt'] == ntot, [b_vb[sl_], bp], [bo])
                if state['first']:
                    assert n == 128 and q0 == 0
                    self.cp(dacc[:, 0:T], pp[:, 0:T], [bp], [b_da])
                else:
                    self.tt(dacc[0:n, q0:T], dacc[0:n, q0:T], pp[0:n, q0:T], ALU.add, [bp, b_da], [b_da])
                state['first'] = False

            for it in tiles:
                if it[0] == 'load':
                    _, knf, vf, nk, dep = it
                    sl_ = slot; slot ^= 1
                    state['sl'] = sl_
                    k.dma('sp', knb[:, sl_, 0:nk], knf(h), [dep], [b_knb[sl_]], dsem='ld')
                    nfull = nk // 128
                    if nfull:
                        k.dma('sp', vb[:, sl_, 0:nfull, :], vf(h, 0, nfull * 128).rearrange("(t p) d -> p t d", p=128), [dep], [b_vb[sl_]], dsem='ld')
                    if nk % 128:
                        n_ = nk % 128
                        k.dma('sp', vb[0:n_, sl_, nfull, :], vf(h, nfull, n_), [dep], [b_vb[sl_]], dsem='ld')
                    continue
                _, t, n, q0, diag, biascol, kro_ = it
                sl_ = state['sl']
                ps_, bs_ = self.ps[sbanks[pi % 4]], self.bps[sbanks[pi % 4]]
                pp, bp = pTs[pi % 4], b_pTs[pi % 4]
                pi += 1
                self.mm(ps_[0:n, q0:T], knb[:, sl_, t * 128:t * 128 + n], QN[:, h, q0:T], True, False, [b_knb[sl_], b_qn], [bs_])
                self.mm(ps_[0:n, q0:T], krall[:, kro_ + t * 128:kro_ + t * 128 + n], QR[:, h, q0:T], False, True, [b_kr, b_qr], [bs_])
                pend.append((ps_, bs_, pp, bp, sl_, t, n, q0, diag, biascol))
                if len(pend) > DEPTH_:
                    finish(pend.pop(0))
            while pend:
                finish(pend.pop(0))
            self.cp(daccb[:, 0:T], dacc[:, 0:T], [b_da], [b_da], 'act')
            self.mm(pd[:, 0:T], self.ones_b[:, :], daccb[:, 0:T], True, True, [b_da, self.b_const], [bd])
            self.recip(rd[:, 0:T], pd[:, 0:T], [bd], [b_rd])
            self.tt(QN[:, h, 0:T], po[:, 0:T], rd[:, 0:T], ALU.mult, [bo, b_rd], [b_qn])
        k.barrier()
        self.linear(d['b_w_o'][j], [(c * 128, 128) for c in range(KC)], [(0, KC)], lambda c: (QN[:, c, 0:T], b_qn), T,
                    self.resid_evac(), extra_R=[self.b_x])


_PROG = None


def _rope_tables():
    half = ROPE // 2
    inv = (10000.0 ** (-np.arange(half, dtype=np.float32) / half)).astype(np.float32)
    pos = np.concatenate([np.arange(SEQ), PAST + np.arange(DS)]).astype(np.float32)
    ang = pos[None, :] * inv[:, None]
    c, s = np.cos(ang).astype(np.float32), np.sin(ang).astype(np.float32)
    return np.concatenate([c, c], 0), np.concatenate([-s, s], 0)


def _core_mask(r):
    m = np.zeros((40,), np.float32)
    for i in range(4):
        m[i] = 0.0 if i < r else NEG
        m[4 + i] = 1.0 if i < r else 0.0
        for l in range(4):
            m[8 + 4 * i + l] = 1.0 if (i < l < r) else 0.0
        m[24 + i] = 1.0 if i == r - 1 else 0.0
        m[32 + i] = 0.0 if i < r else -30000.0
    m[28] = 1.0 if r == 0 else 0.0
    return np.ascontiguousarray(np.broadcast_to(m[None, :], (128, 40)))


def kernel(**inp):
    global _PROG
    if _PROG is None:
        _PROG = Prog()
    prog = _PROG
    f = lambda a: np.ascontiguousarray(np.asarray(a, dtype=np.float32))
    cos2, sin2 = _rope_tables()
    shared = {}
    for n in ['norm_mix', 'norm_ffn', 'a_w_in', 'a_b_gate', 'a_w_out', 'kv_w_down', 'kv_w_up', 'b_w_dq', 'b_g_cq', 'b_w_uq',
              'b_g_qn', 'b_g_qr', 'b_w_o', 'f_w_up', 'f_conv_w', 'f_conv_b', 'f_w_down']:
        shared[n] = f(inp[n])
    shared['a_g_head'] = f(inp['a_g_head']).reshape(NA, H * DV)
    for n in ['kv_norm', 'kv_g_c', 'kv_g_r', 'kv_g_kn']:
        shared[n] = f(inp[n]).reshape(1, -1)
    in_maps = []
    for c in range(8):
        g, r = c // 4, c % 4
        m = dict(shared)
        segs = [rd * 4 + r for rd in range(NR)]
        m['xp'] = f(np.concatenate([inp['x_prompt'][g][sg * SEG:(sg + 1) * SEG] for sg in segs], 0))
        cols = np.concatenate([np.arange(sg * SEG, (sg + 1) * SEG) for sg in segs] + [SEQ + np.arange(DS)])
        m['cos2'] = f(cos2[:, cols]); m['sin2'] = f(sin2[:, cols])
        m['cmask'] = _core_mask(r)
        m['xs'] = f(inp['x_sample'][c])
        m['ckv'] = f(inp['cache_ckv'][c]); m['kpe'] = f(inp['cache_kpe'][c])
        m['sC'] = f(inp['state_C'][:, c]); m['sn'] = f(inp['state_n'][:, c]).reshape(NA, H * DK); m['sm'] = f(inp['state_m'][:, c])
        m['sconv'] = f(inp['state_conv'][:, c])
        in_maps.append(m)
    res = run_bass_kernel_spmd(prog.nc, in_maps, core_ids=list(range(8))).results

    def seqcat(n):
        out = []
        for g in range(2):
            parts = [None] * (4 * NR)
            for r in range(4):
                for rd in range(NR):
                    parts[rd * 4 + r] = res[g * 4 + r][n][rd * SEG:(rd + 1) * SEG]
            out.append(np.concatenate(parts, 0))
        return np.stack(out, 0)
    pc = [3, 7]
    st = lambda n, cores, ax: np.stack([res[c][n] for c in cores], axis=ax)
    allc = list(range(8))
    rs = lambda a: a.reshape(a.shape[0], a.shape[1], H, DK)
    return (seqcat('yp'), st('ys', allc, 0), seqcat('p_ckv'), seqcat('p_kpe'),
            st('pC', pc, 1), rs(st('pn', pc, 1)), st('pm', pc, 1), st('pconv', pc, 1),
            st('s_ckv', allc, 0), st('s_kpe', allc, 0), st('sCo', allc, 1), rs(st('sno', allc, 1)), st('smo', allc, 1),
            st('sconvo', allc, 1))
```

```python
import numpy as np
from contextlib import ExitStack
import concourse.bass as bass
import concourse.mybir as mybir
from concourse.bass_utils import run_bass_kernel_spmd

F32 = mybir.dt.float32
BF16 = mybir.dt.bfloat16
AF = mybir.ActivationFunctionType
ALU = mybir.AluOpType
AX = mybir.AxisListType

D = 2048; KC = 16; SEQ = 4096; DEPTH = 4; NA = 2
DS = 16; PAST = 2048
H = 4; DK = 256; DV = 512; APROJ = 6152
BH = 16; QL = 768; KVL = 512; NOPE = 128; ROPE = 64; VD = 128
DFF = 5632; FC = 44
EPS = 1e-6
SEG = 512
NSEG = SEQ // SEG
NR = 2
XW = 2 * H * DV + 16
KVROWS = BH * 128 + ROPE + 4 * SEG
GROUPS = [[0, 1, 2, 3], [4, 5, 6, 7]]
ATT_SCALE = (NOPE + ROPE) ** -0.5
NEG = -1.0e30


class Buf:
    def __init__(self, name=""):
        self.name = name
        self.w = None
        self.r = []


class K:
    def __init__(self, nc, stack):
        self.nc = nc
        self.stack = stack
        self.eng = {'pe': nc.tensor, 'dve': nc.vector, 'act': nc.scalar, 'pool': nc.gpsimd, 'sp': nc.sync}
        self.sem = {}
        self.cnt = {}
        self.waited = {e: {} for e in self.eng}
        for e in self.eng:
            self.sem[e] = stack.enter_context(nc.semaphore('s_' + e))
            self.cnt[e] = 0
        self.nins = 0

    def new_dma_sem(self, key):
        self.sem[key] = self.stack.enter_context(self.nc.semaphore(key))
        self.cnt[key] = 0
        return key

    def _wait(self, e, deps):
        need = {}
        for d in deps:
            if d is None:
                continue
            k, v = d
            if e == 'pe' and k == 'pe':
                continue
            if v > need.get(k, 0):
                need[k] = v
        for k, v in need.items():
            if self.waited[e].get(k, 0) >= v:
                continue
            self.eng[e].wait_ge(self.sem[k], v)
            self.waited[e][k] = v

    @staticmethod
    def _deps(reads, writes):
        deps = []
        for b in reads:
            deps.append(b.w)
        for b in writes:
            deps.append(b.w)
            deps.extend(b.r)
        return deps

    def op(self, e, fn, reads=(), writes=()):
        self._wait(e, self._deps(reads, writes))
        ins = fn()
        self.cnt[e] += 1
        self.nins += 1
        ins.then_inc(self.sem[e], 1)
        tag = (e, self.cnt[e])
        for b in reads:
            b.r.append(tag)
            if len(b.r) > 24:
                b.r = b.r[-24:] if False else self._compact(b.r)
        for b in writes:
            b.w = tag
            b.r = []
        return ins

    @staticmethod
    def _compact(r):
        m = {}
        for k, v in r:
            if v > m.get(k, 0):
                m[k] = v
        return list(m.items())

    def dma(self, q, out, in_, reads=(), writes=(), dsem='io', **kw):
        if dsem in ('io', 'out', 'kv', 'ld'):
            dsem = self.pool[self.pi % len(self.pool)]
            self.pi += 1
        prev = self.cnt[dsem]
        self._wait(q, self._deps(reads, writes) + ([(dsem, prev)] if prev else []))
        ins = self.eng[q].dma_start(out=out, in_=in_, **kw)
        self.cnt[dsem] += 16
        self.nins += 1
        ins.then_inc(self.sem[dsem], 16)
        tag = (dsem, self.cnt[dsem])
        for b in reads:
            b.r.append(tag)
        for b in writes:
            b.w = tag
            b.r = []
        return ins

    def collective(self, src, dst, groups, reads=(), writes=()):
        self._wait('pool', self._deps(reads, writes))
        ins = self.nc.gpsimd.collective_compute("AllGather", mybir.AluOpType.bypass, replica_groups=groups,
                                                ins=[src.opt()], outs=[dst.opt()])
        self.cnt['cc'] += 1
        self.nins += 1
        ins.then_inc(self.sem['cc'], 1)
        tag = ('cc', self.cnt['cc'])
        for b in reads:
            b.r.append(tag)
        for b in writes:
            b.w = tag
            b.r = []
        return ins

    def barrier(self):
        allv = [(k, v) for k, v in self.cnt.items() if v > 0 and k != 'cc']
        for e in self.eng:
            self._wait(e, allv)


def bc(ap, shape):
    return ap.broadcast_to(list(shape))


STOP = None


class _Stop(Exception):
    pass


class Prog:
    def dbg(self, tag):
        if STOP is not None and tag == STOP:
            raise _Stop()

    def __init__(self):
        self.nc = nc = bass.Bass("TRN2", target_bir_lowering=False)
        self.st = ExitStack()
        self.k = K(nc, self.st)
        for s in ['w0', 'w1', 'w2', 'w3', 'cc']:
            self.k.new_dma_sem(s)
        self.k.pool = [self.k.new_dma_sem('p%d' % i) for i in range(40)]
        self.k.pi = 0
        self.build()

    def din(self, name, shape):
        return self.nc.dram_tensor(name, list(shape), F32, kind="ExternalInput").ap()

    def dout(self, name, shape):
        return self.nc.dram_tensor(name, list(shape), F32, kind="ExternalOutput").ap()

    def dint(self, name, shape, dt=F32):
        return self.nc.dram_tensor(name, list(shape), dt, kind="Internal").ap()

    def sb(self, name, shape, dt=F32):
        return self.st.enter_context(self.nc.sbuf_tensor(name, list(shape), dt))

    def fv(self, a, n, rows=128):
        return self.arena[0:rows, a:a + n]

    def bv(self, a, n, rows=128):
        assert n % 2 == 0
        return self.arena[0:rows, a:a + n // 2].bitcast(BF16)

    def V(self, e='dve'):
        return self.nc.vector if e == 'dve' else self.nc.gpsimd

    def tt(self, out, a, b, op, R, W, e='dve'):
        self.k.op(e, lambda: self.V(e).tensor_tensor(out=out, in0=a, in1=b, op=op), R, W)

    def ts(self, out, a, s1, s2, op0, op1, R, W, e='dve'):
        if op1 is None:
            self.k.op(e, lambda: self.V(e).tensor_scalar(out=out, in0=a, scalar1=s1, scalar2=None, op0=op0), R, W)
        else:
            self.k.op(e, lambda: self.V(e).tensor_scalar(out=out, in0=a, scalar1=s1, scalar2=s2, op0=op0, op1=op1), R, W)

    def stt(self, out, a, s, b, op0, op1, R, W):
        self.k.op('dve', lambda: self.nc.vector.scalar_tensor_tensor(out=out, in0=a, scalar=s, in1=b, op0=op0, op1=op1), R, W)

    def cp(self, out, a, R, W, e='dve'):
        if e == 'act':
            self.k.op('act', lambda: self.nc.scalar.copy(out=out, in_=a), R, W)
        else:
            self.k.op(e, lambda: self.V(e).tensor_copy(out=out, in_=a), R, W)

    def act(self, out, a, func, R, W, bias=None, scale=None, accum=None):
        kw = {}
        if bias is not None:
            kw['bias'] = bias
        if scale is not None:
            kw['scale'] = scale
        if accum is not None:
            kw['accum_out'] = accum
        self.k.op('act', lambda: self.nc.scalar.activation(out=out, in_=a, func=func, **kw), R, W)

    def mm(self, out, lhsT, rhs, start, stop, R, W):
        self.k.op('pe', lambda: self.nc.tensor.matmul(out, lhsT=lhsT, rhs=rhs, start=start, stop=stop), R, W)

    def tr(self, out, a, n, R, W, bf=False):
        idn = self.ident_b if bf else self.ident
        self.k.op('pe', lambda: self.nc.tensor.transpose(out, a, idn[0:n, 0:n]), list(R) + [self.b_const], W)

    def ms(self, ap, c, W, e='dve'):
        self.k.op(e, lambda: self.V(e).memset(ap, c), (), W)

    def recip(self, out, a, R, W):
        self.k.op('dve', lambda: self.nc.vector.reciprocal(out=out, in_=a), R, W)

    def load_fm(self, dst, src1d, n, rows=128):
        tmp = self.fv(self.O_TMP, 128)
        if not hasattr(self, 'b_tmpfm'):
            self.b_tmpfm = Buf()
        tb = self.b_tmpfm
        self.k.dma('sp', tmp[0:n, 0:rows], src1d.rearrange("(c p) -> c p", p=rows), (), [tb])
        ps, pb = self.ps[7], self.bps[7]
        self.tr(ps[0:rows, 0:n], tmp[0:n, 0:rows], n, [tb], [pb])
        self.cp(dst, ps[0:rows, 0:n], [pb], [self.b_gn])

    def build(self):
        nc, k = self.nc, self.k
        d = {}
        d['xp'] = self.din('xp', [NR * SEG, D]); d['xs'] = self.din('xs', [DS, D])
        d['ckv'] = self.din('ckv', [PAST, KVL]); d['kpe'] = self.din('kpe', [PAST, ROPE])
        d['sC'] = self.din('sC', [NA, H, DV, DK]); d['sn'] = self.din('sn', [NA, H * DK]); d['sm'] = self.din('sm', [NA, H])
        d['sconv'] = self.din('sconv', [DEPTH, 2, 2 * DFF])
        d['norm_mix'] = self.din('norm_mix', [DEPTH, D]); d['norm_ffn'] = self.din('norm_ffn', [DEPTH, D])
        d['a_w_in'] = self.din('a_w_in', [NA, D, APROJ]); d['a_b_gate'] = self.din('a_b_gate', [NA, 8])
        d['a_g_head'] = self.din('a_g_head', [NA, H * DV]); d['a_w_out'] = self.din('a_w_out', [NA, D, D])
        d['kv_norm'] = self.din('kv_norm', [1, D]); d['kv_w_down'] = self.din('kv_w_down', [D, KVL + ROPE])
        d['kv_g_c'] = self.din('kv_g_c', [1, KVL]); d['kv_g_r'] = self.din('kv_g_r', [1, ROPE])
        d['kv_w_up'] = self.din('kv_w_up', [KVL, BH * 256]); d['kv_g_kn'] = self.din('kv_g_kn', [1, NOPE])
        d['b_w_dq'] = self.din('b_w_dq', [2, D, QL]); d['b_g_cq'] = self.din('b_g_cq', [2, QL])
        d['b_w_uq'] = self.din('b_w_uq', [2, QL, BH * 192]); d['b_g_qn'] = self.din('b_g_qn', [2, NOPE])
        d['b_g_qr'] = self.din('b_g_qr', [2, ROPE]); d['b_w_o'] = self.din('b_w_o', [2, D, D])
        d['f_w_up'] = self.din('f_w_up', [DEPTH, D, 2 * DFF]); d['f_conv_w'] = self.din('f_conv_w', [DEPTH, 3, 2 * DFF])
        d['f_conv_b'] = self.din('f_conv_b', [DEPTH, 2 * DFF]); d['f_w_down'] = self.din('f_w_down', [DEPTH, DFF, D])
        d['cos2'] = self.din('cos2', [ROPE, NR * SEG + DS]); d['sin2'] = self.din('sin2', [ROPE, NR * SEG + DS])
        d['cmask'] = self.din('cmask', [128, 40])
        o = {}
        o['yp'] = self.dout('yp', [NR * SEG, D]); o['ys'] = self.dout('ys', [DS, D])
        o['p_ckv'] = self.dout('p_ckv', [NR * SEG, KVL]); o['p_kpe'] = self.dout('p_kpe', [NR * SEG, ROPE])
        o['pC'] = self.dout('pC', [NA, H, DV, DK]); o['pn'] = self.dout('pn', [NA, H * DK]); o['pm'] = self.dout('pm', [NA, H])
        o['pconv'] = self.dout('pconv', [DEPTH, 2, 2 * DFF])
        o['s_ckv'] = self.dout('s_ckv', [DS, KVL]); o['s_kpe'] = self.dout('s_kpe', [DS, ROPE])
        o['sCo'] = self.dout('sCo', [NA, H, DV, DK]); o['sno'] = self.dout('sno', [NA, H * DK]); o['smo'] = self.dout('smo', [NA, H])
        o['sconvo'] = self.dout('sconvo', [DEPTH, 2, 2 * DFF])
        self.d, self.o = d, o
        self.kn_dram = [None, self.dint('kns', [BH, 128, PAST + DS], BF16)]
        self.kr_dram = [None, self.dint('krs', [ROPE, PAST + DS], BF16)]
        self.v_dram = [None, self.dint('vs', [PAST + DS, BH * VD], BF16)]
        self.xsC = [[[self.dint('xsC%d%d%d' % (l, t, c), [128, 2048]) for c in range(2)] for t in range(2)] for l in range(NA)]
        self.xgC = [[[self.dint('xgC%d%d%d' % (l, t, c), [512, 2048]) for c in range(2)] for t in range(2)] for l in range(NA)]
        self.xsS = [self.dint('xsS%d' % l, [256, 16]) for l in range(NA)]
        self.xgS = [self.dint('xgS%d' % l, [1024, 16]) for l in range(NA)]
        self.b_xs2 = [Buf(), Buf()]; self.b_xg = [Buf(), Buf()]
        self.ts2 = [self.dint('ts2_%d' % l, [256, 176]) for l in range(DEPTH)]
        self.tg = [self.dint('tg_%d' % l, [4 * 256, 176]) for l in range(DEPTH)]
        self.b_ts2 = [Buf() for _ in range(DEPTH)]; self.b_tg = [Buf() for _ in range(DEPTH)]
        self.KP = [self.dint('KP%d' % j, [1024, SEG], BF16) for j in range(2)]
        self.KRp = self.dint('KRp', [ROPE, SEG], BF16)
        self.VP = [self.dint('VP%d' % q, [SEG, 1024], BF16) for q in range(2)]
        self.b_kvloc = Buf()
        self.b_kvd = [Buf(), Buf()]
        self.KG = [[self.dint('KG%d%d' % (r, j), [4 * 1024, SEG], BF16) for j in range(2)] for r in range(NR)]
        self.KRG = [self.dint('KRG%d' % r, [4 * ROPE, SEG], BF16) for r in range(NR)]
        self.VG = [[self.dint('VG%d%d' % (r, q), [4 * SEG, 1024], BF16) for q in range(2)] for r in range(NR)]
        self.b_kvall = [Buf() for _ in range(NR)]
        self.b_out = Buf('out')

        self.ident = self.sb('ident', [128, 128])
        self.ident_b = self.sb('ident_b', [128, 128], BF16)
        self.ones_b = self.sb('ones_b', [128, 128], BF16)
        self.ones_f = self.sb('ones_f', [128, 128])
        self.uneg = {L: self.sb('uneg%d' % L, [L, L]) for L in (128, 16)}
        self.mask = {L: self.sb('mask%d' % L, [L, L]) for L in (128, 16)}
        self.maskT = {L: self.sb('maskT%d' % L, [L, L]) for L in (128, 16)}
        self.sel = {L: self.sb('sel%d' % L, [L, 128]) for L in (128, 16)}
        self.perm = self.sb('perm', [64, 64], BF16)
        self.epsc = self.sb('epsc', [128, 1])
        self.b_const = Buf('const')
        self.xT = self.sb('xT', [128, KC, SEG]); self.b_x = Buf('x')
        self.xb = self.sb('xb', [128, KC, SEG], BF16); self.b_xb = Buf('xb')
        self.tail = self.sb('tail', [128, DEPTH, 2, 2, 88]); self.b_tail = Buf('tail')
        self.nst = self.sb('nst', [128, NA, 2, 2, H, 2]); self.b_nst = Buf('nst')
        self.nsb = self.sb('nsb', [128, NA, 2, 2, H, 2], BF16)
        self.mst = self.sb('mst', [128, NA, 2, H]); self.b_mst = Buf('mst')
        self.gn = self.sb('gn', [128, 9, KC]); self.b_gn = Buf('gn')
        self.cw = self.sb('cw', [128, DEPTH, 3, 88]); self.cb = self.sb('cb', [128, DEPTH, 88])
        self.gmisc = self.sb('gmisc', [128, 64])
        self.cm = self.sb('cm', [128, 40])
        self.fsum = self.sb('fsum', [128, H])
        self.b_cw = self.b_gn; self.b_gm = self.b_gn
        self.arena = self.sb('arena', [128, 34200])
        self.ps = [self.st.enter_context(nc.psum_tensor('ps%d' % i, [128, 512], F32)) for i in range(8)]
        self.bps = [Buf('ps%d' % i) for i in range(8)]
        self.O_STG = 0
        self.O_WR = 8192
        self.O_SQ = 12288
        self.O_RSTD = 12800
        self.O_PH = 13312
        self.O_TMP = 13312

        try:
            self.init_consts()
            self.dbg('init')
            self.segment(grp=1, s=0, T=DS, L=DS)
            for s in range(NR if NSEG > 0 else 0):
                self.segment(grp=0, s=s, T=SEG, L=128)
        except _Stop:
            pass
        k.barrier()
        k._wait('sp', [(p, k.cnt[p]) for p in k.pool if k.cnt[p] > 0])

    def init_consts(self):
        nc, k, d = self.nc, self.k, self.d
        W = [self.b_const]
        g = nc.gpsimd

        def sel(t, pattern, cmp, fill, base, cm):
            k.op('pool', lambda: g.affine_select(out=t, in_=t, pattern=pattern, compare_op=cmp, fill=fill,
                                                base=base, channel_multiplier=cm), W, W)
        self.ms(self.ident[:], 0.0, W, 'pool')
        sel(self.ident[:], [[-1, 128]], ALU.not_equal, 1.0, 0, 1)
        self.cp(self.ident_b[:], self.ident[:], W, W, 'dve')
        self.ms(self.ones_f[:], 1.0, W, 'pool')
        self.cp(self.ones_b[:], self.ones_f[:], W, W, 'dve')
        self.ms(self.epsc[:], EPS, W, 'pool')
        for L in (128, 16):
            self.ms(self.uneg[L][:], -1.0, W, 'pool')
            sel(self.uneg[L][:], [[1, L]], ALU.is_ge, 0.0, 0, -1)
            self.ms(self.mask[L][:], 0.0, W, 'pool')
            sel(self.mask[L][:], [[-1, L]], ALU.is_ge, NEG, 0, 1)
            self.ms(self.maskT[L][:], 0.0, W, 'pool')
            sel(self.maskT[L][:], [[1, L]], ALU.is_ge, NEG, 0, -1)
            self.ms(self.sel[L][:], 1.0, W, 'pool')
            sel(self.sel[L][:], [[0, 128]], ALU.is_equal, 0.0, -(L - 1), 1)
        pf = self.fv(20000, 64, 64)
        self.ms(pf, 0.0, W, 'pool')
        sel(pf, [[-1, 64]], ALU.not_equal, 1.0, -32, 1)
        sel(pf, [[-1, 64]], ALU.not_equal, 1.0, 32, 1)
        self.cp(self.perm[:], pf, W, W, 'dve')
        k.barrier()
        for i in range(4):
            self.load_fm(self.gn[:, i, :], d['norm_mix'][i], KC)
            self.load_fm(self.gn[:, 4 + i, :], d['norm_ffn'][i], KC)
        self.load_fm(self.gn[:, 8, :], d['kv_norm'][0], KC)
        for l in range(DEPTH):
            for j in range(3):
                self.load_fm(self.cw[:, l, j, :], d['f_conv_w'][l, j], 88)
            self.load_fm(self.cb[:, l, :], d['f_conv_b'][l], 88)
        gm = self.gmisc
        self.ms(gm[:], 0.0, [self.b_gn], 'pool')
        for j in range(2):
            self.load_fm(gm[:, 6 * j:6 * j + 6], d['b_g_cq'][j], 6)
            self.load_fm(gm[:, 12 + j:13 + j], d['b_g_qn'][j], 1)
            self.load_fm(gm[0:64, 14 + j:15 + j], d['b_g_qr'][j], 1, rows=64)
            self.load_fm(gm[:, 22 + 16 * j:38 + 16 * j], d['a_g_head'][j], 16)
            self.load_fm(gm[0:8, 54 + j:55 + j], d['a_b_gate'][j], 1, rows=8)
        self.load_fm(gm[:, 16:20], d['kv_g_c'][0], 4)
        self.load_fm(gm[0:64, 20:21], d['kv_g_r'][0], 1, rows=64)
        self.load_fm(gm[:, 21:22], d['kv_g_kn'][0], 1)
        self.ms(self.tail[:], 0.0, [self.b_tail], 'pool')
        self.ms(self.mst[:], 0.0, [self.b_mst], 'pool')
        self.ms(self.fsum[:], 0.0, [self.b_mst], 'pool')
        self.ms(self.nst[:], 0.0, [self.b_nst], 'pool')
        k.barrier()
        for l in range(DEPTH):
            for r in range(2):
                self.load_fm(self.tail[:, l, 1, r, :], d['sconv'][l, r], 88)
        for l in range(NA):
            k.dma('sp', self.mst[:, l, 1, :], d['sm'][l:l + 1, :].broadcast_to([128, H]), (), [self.b_mst])
            nt = self.fv(20100, 8)
            self.load_fm(nt, d['sn'][l], 8)
            for e in range(2):
                self.cp(self.nst[:, l, 1, :, :, e], nt.rearrange("p (h c) -> p c h", h=H), [self.b_gn], [self.b_nst], 'dve')
        self.cp(self.nsb[:], self.nst[:], [self.b_nst], [self.b_nst], 'dve')
        k.dma('sp', self.cm[:], d['cmask'][:, :], (), [self.b_gn])
        zc = self.fv(0, XW)
        self.ms(zc, 0.0, W, 'pool')
        for l in range(NA):
            for c in range(2):
                k.dma('sp', self.xsC[l][1][c][:, :], zc[:, 0:2048], W, [self.b_xs2[l]])
            k.dma('sp', self.xsS[l][128:256, :], zc[:, 0:16], W, [self.b_xs2[l]])
        for l in range(DEPTH):
            k.dma('sp', self.ts2[l][128:256, :], zc[:, 0:176], W, [self.b_ts2[l]])
        k.barrier()

    def linear(self, w, blocks, kgroups, rhs_fn, T, evac, gcol=None, ps_ids=(0, 1), extra_R=(), sets=None):
        k = self.k
        wv = w.rearrange("(c p) n -> p c n", p=128)
        NS = 4
        if not hasattr(self, 'wslot'):
            self.wslot = 0
            self.b_wst = [Buf() for _ in range(NS)]
            self.b_wr = [Buf() for _ in range(NS)]
        if sets is None:
            sets = [[0, 1, 4, 5], [6, 7, 2, 3]]
        kcs = [kc for (k0, nk) in kgroups for kc in range(k0, k0 + nk)]
        groups = []
        cur = []
        for bi, (c0, m) in enumerate(blocks):
            if cur and (cur[-1][1] + cur[-1][2] == c0) and (sum(x[2] for x in cur) + m <= 512) and (len(cur) < min(len(x) for x in sets)):
                cur.append((bi, c0, m))
            else:
                if cur:
                    groups.append(cur)
                cur = [(bi, c0, m)]
        if cur:
            groups.append(cur)
        si = 0
        for grp_ in groups:
            banks = sets[si % len(sets)]
            si += 1
            cols = sum(x[2] for x in grp_)
            cbase = grp_[0][1]
            nk_t = max(1, min(2048 // cols, len(kcs)))
            ntile = (len(kcs) + nk_t - 1) // nk_t
            for ti in range(ntile):
                kk = kcs[ti * nk_t:(ti + 1) * nk_t]
                assert kk == list(range(kk[0], kk[0] + len(kk)))
                nk = len(kk)
                s = self.wslot
                self.wslot = (self.wslot + 1) % NS
                stg = self.fv(self.O_STG + s * 2048, nk * cols).rearrange("p (c n) -> p c n", n=cols)
                wr = self.bv(self.O_WR + s * 1024, nk * cols).rearrange("p (c n) -> p c n", n=cols)
                k.dma('sp', stg, wv[:, kk[0]:kk[0] + nk, cbase:cbase + cols], (), [self.b_wst[s]], dsem='w%d' % s)
                self.cp(wr, stg, [self.b_wst[s]], [self.b_wr[s]], ('dve', 'act', 'dve', 'act')[s])
                for gi, (bi, c0, m) in enumerate(grp_):
                    ps, pb = self.ps[banks[gi]], self.bps[banks[gi]]
                    for j in range(nk):
                        rhs, rb = rhs_fn(kk[j])
                        first = (ti == 0 and j == 0)
                        last = (ti == ntile - 1 and j == nk - 1)
                        self.mm(ps[0:m, 0:T], wr[:, j, c0 - cbase:c0 - cbase + m], rhs, first, last,
                                [self.b_wr[s], rb] + list(extra_R), [pb])
            for gi, (bi, c0, m) in enumerate(grp_):
                evac(bi, self.ps[banks[gi]][0:m, 0:T], self.bps[banks[gi]])

    def sumsq_bc(self, src_fn, nch, T, out, outb, n, rows=128, ps_id=2):
        if not hasattr(self, 'b_sq'):
            self.b_sq = [Buf(), Buf()]
        ps, pb = self.ps[ps_id], self.bps[ps_id]
        for c in range(nch):
            src, sbuf = src_fn(c)
            s = c % 2
            sq = self.bv(self.O_SQ + s * 256, 512)
            self.act(sq[0:rows, 0:T], src, AF.Square, [sbuf], [self.b_sq[s]])
            self.mm(ps[:, 0:T], self.ones_b[0:rows, :], sq[0:rows, 0:T], c == 0, c == nch - 1, [self.b_sq[s], self.b_const], [pb])
        orow = out.shape[0]
        self.act(out, ps[0:orow, 0:T], AF.Sqrt, [pb, self.b_const], [outb], bias=self.epsc[0:orow, 0:1], scale=1.0 / n)
        self.recip(out, out, [outb], [outb])

    def x_rstd(self):
        T = self.T
        rstd = self.fv(self.O_RSTD, SEG)
        b = Buf()
        self.sumsq_bc(lambda c: (self.xT[:, c, 0:T], self.b_x), KC, T, rstd[:, 0:T], b, D)
        return rstd, b

    def xb_refresh(self, gidx):
        T = self.T
        rstd, b_rstd = self.x_rstd()
        tmpf = self.fv(self.O_STG, 2 * SEG).rearrange("p (s t) -> p s t", s=2)
        bt = [Buf(), Buf()]
        j = 0
        for c in range(KC):
            if c % 2 == 0:
                self.stt(self.xb[:, c, 0:T], self.xT[:, c, 0:T], self.gn[:, gidx, c:c + 1], rstd[:, 0:T], ALU.mult, ALU.mult,
                         [self.b_x, self.b_gn, b_rstd], [self.b_xb])
            else:
                sl = j % 2; j += 1
                self.act(tmpf[:, sl, 0:T], self.xT[:, c, 0:T], AF.Copy, [self.b_x, self.b_gn], [bt[sl]], scale=self.gn[:, gidx, c:c + 1])
                self.tt(self.xb[:, c, 0:T], tmpf[:, sl, 0:T], rstd[:, 0:T], ALU.mult, [bt[sl], b_rstd], [self.b_xb], 'pool')

    def resid_evac(self):
        T = self.T
        def ev(bi, ps, pb):
            self.tt(self.xT[:, bi, 0:T], ps, self.xT[:, bi, 0:T], ALU.add, [pb, self.b_x], [self.b_x])
        return ev

    def segment(self, grp, s, T, L):
        nc, k, d, o = self.nc, self.k, self.d, self.o
        self.grp, self.sidx, self.T, self.L = grp, s, T, L
        pos0 = s * SEG if grp == 0 else NR * SEG
        xsrc = d['xp'][s * SEG:(s + 1) * SEG, :] if grp == 0 else d['xs']
        k.barrier()
        tmp = self.fv(self.O_PH, D)
        tb = Buf()
        for t0 in range(0, T, 128):
            n = min(128, T - t0)
            k.dma('sp', tmp[0:n, :], xsrc[t0:t0 + n, :], (), [tb])
            for c4 in range(0, KC, 4):
                ps, pb = self.ps[(c4 // 4) % 2], self.bps[(c4 // 4) % 2]
                for j in range(4):
                    self.tr(ps[:, j * 128:j * 128 + n], tmp[0:n, (c4 + j) * 128:(c4 + j + 1) * 128], n, [tb], [pb])
                self.cp(self.xT[:, c4:c4 + 4, t0:t0 + n], ps[:, :].rearrange("p (j n) -> p j n", j=4)[:, :, 0:n], [pb], [self.b_x])
        self.dbg('xload')
        for layer in range(DEPTH):
            k.barrier()
            self.xb_refresh(layer)
            k.barrier()
            if layer < NA:
                self.mlstm(layer)
            else:
                self.mla(layer - NA)
            k.barrier()
            self.dbg('mixer%d' % layer)
            self.xb_refresh(4 + layer)
            k.barrier()
            self.ffn(layer)
            self.dbg('ffn%d' % layer)
            if layer == NA - 1:
                k.barrier()
                self.xb_refresh(8)
                k.barrier()
                self.shared_kv(pos0)
        k.barrier()
        ydst = o['yp'][s * SEG:(s + 1) * SEG, :] if grp == 0 else o['ys']
        for t0 in range(0, T, 128):
            n = min(128, T - t0)
            for c4 in range(0, KC, 4):
                ps, pb = self.ps[(c4 // 4) % 2], self.bps[(c4 // 4) % 2]
                for j in range(4):
                    self.tr(ps[0:n, j * 128:(j + 1) * 128], self.xT[:, c4 + j, t0:t0 + n], 128, [self.b_x], [pb])
                self.cp(tmp[0:n, c4 * 128:(c4 + 4) * 128], ps[0:n, :], [pb], [tb])
            k.dma('sp', ydst[t0:t0 + n, :], tmp[0:n, :], [tb], [self.b_out], dsem='out')

    def ffn(self, layer):
        nc, k, d, o = self.nc, self.k, self.d, self.o
        T, grp = self.T, self.grp
        P0 = self.O_PH
        ug = self.fv(P0, SEG + 2); b_ug = Buf()
        uv = self.fv(P0 + 514, SEG + 2); b_uv = Buf()
        cg = self.fv(P0 + 1028, SEG); b_cg = Buf()
        cv = self.fv(P0 + 1540, SEG); b_cv = Buf()
        sg4 = self.fv(P0 + 14300, 4 * SEG).rearrange("p (s t) -> p s t", s=4); b_sg4 = [Buf() for _ in range(4)]
        t1 = self.fv(P0 + 2564, 128); t2 = self.fv(P0 + 2692, 128)
        actT = self.bv(P0 + 3000, FC * SEG).rearrange("p (c t) -> p c t", c=FC); b_act = Buf()
        gcol = self.gn[:, 4 + layer, :]
        tl = self.tail[:, layer, grp]
        blocks = []; bmap = []
        for j0 in range(0, FC, 4):
            for j in range(j0, j0 + 4):
                blocks.append((j * 128, 128)); bmap.append((j, 0))
            for j in range(j0, j0 + 4):
                blocks.append((DFF + j * 128, 128)); bmap.append((j, 1))
        sgs = self.fv(self.O_SQ, 4 * SEG).rearrange("p (s t) -> p s t", s=4) if False else None

        def conv(u, ub, fc, out, outb):
            self.ts(out[:, 0:T], u[:, 0:T], self.cw[:, layer, 0, fc:fc + 1], self.cb[:, layer, fc:fc + 1], ALU.mult, ALU.add,
                    [ub, self.b_cw], [outb])
            for jj in (1, 2):
                self.stt(out[:, 0:T], u[:, jj:jj + T], self.cw[:, layer, jj, fc:fc + 1], out[:, 0:T], ALU.mult, ALU.add,
                         [ub, self.b_cw, outb], [outb])

        def evac(bi, ps, pb):
            j, isv = bmap[bi]
            fc = j + (FC if isv else 0)
            u, ub = (uv, b_uv) if isv else (ug, b_ug)
            cc, cb_ = (cv, b_cv) if isv else (cg, b_cg)
            sg, b_sg = sg4[:, j % 4, :], b_sg4[j % 4]
            if grp == 1:
                self.cp(u[:, 0:2], tl[:, :, fc], [self.b_tail], [ub], 'pool')
                self.cp(u[:, 2:2 + T], ps, [pb], [ub], 'act')
                self.cp(tl[:, :, fc], u[:, T:T + 2], [ub], [self.b_tail], 'pool')
                conv(u, ub, fc, cc, cb_)
                lo = 0
            else:
                self.cp(tl[:, :, fc], ps[:, T - 2:T], [pb], [self.b_tail], 'dve')
                self.cp(ufirst[:, fc, :], ps[:, 0:2], [pb], [b_uf], 'dve')
                self.ts(cc[:, 2:T], ps[:, 0:T - 2], self.cw[:, layer, 0, fc:fc + 1], self.cb[:, layer, fc:fc + 1], ALU.mult, ALU.add,
                        [pb, self.b_cw], [cb_])
                for jj in (1, 2):
                    self.stt(cc[:, 2:T], ps[:, jj:T - 2 + jj], self.cw[:, layer, jj, fc:fc + 1], cc[:, 2:T], ALU.mult, ALU.add,
                             [pb, self.b_cw, cb_], [cb_])
                lo = 2
            if not isv:
                self.act(sg[:, lo:T], cg[:, lo:T], AF.Silu, [b_cg], [b_sg])
            else:
                self.tt(actT[:, j, lo:T], sg[:, lo:T], cv[:, lo:T], ALU.mult, [b_sg, b_cv], [b_act], 'pool')

        ufirst = self.fv(P0 + 2820, 176).rearrange("p (c r) -> p c r", r=2); b_uf = Buf()
        self.linear(d['f_w_up'][layer], blocks, [(0, KC)], lambda c: (self.xb[:, c, 0:T], self.b_xb), T, evac)
        if grp == 0:
            k.barrier()
            tcur = self.tail[:, layer, 0].rearrange("p r c -> p (r c)")
            k.dma('sp', self.ts2[layer][0:128, :], tcur, [self.b_tail, self.b_tg[layer]], [self.b_ts2[layer]])
            k.collective(self.ts2[layer][:, :], self.tg[layer][:, :], GROUPS, [self.b_ts2[layer]], [self.b_tg[layer]])
            tgs = self.fv(P0, 1408).rearrange("p (i t c) -> p i t c", i=4, t=2); bfx = Buf()
            k.dma('sp', tgs, self.tg[layer].rearrange("(i t p) c -> p i t c", t=2, p=128), [self.b_tg[layer]], [bfx])
            halo = self.fv(P0 + 1408, 176)
            Rf = [bfx, self.b_gn, b_uf]
            self.ts(halo, tgs[:, 3, 1, :], self.cm[:, 28:29], None, ALU.mult, None, Rf, [bfx])
            for i in range(4):
                self.stt(halo, tgs[:, i, 0, :], self.cm[:, 24 + i:25 + i], halo, ALU.mult, ALU.add, Rf, [bfx])
            k.dma('sp', self.ts2[layer][128:256, :], tcur, [self.b_tail, self.b_tg[layer]], [self.b_ts2[layer]])
            h0 = halo[:, 0:88]; h1 = halo[:, 88:176]
            u0 = ufirst[:, :, 0]; u1 = ufirst[:, :, 1]
            w0, w1, w2 = (self.cw[:, layer, jj, :] for jj in range(3)); bb = self.cb[:, layer, :]
            c0 = self.fv(P0 + 1584, 88); c1 = self.fv(P0 + 1672, 88); ta = self.fv(P0 + 1760, 88); sgl = self.fv(P0 + 1848, 88)
            for (cc, x0, x1, x2) in ((c0, h0, h1, u0), (c1, h1, u0, u1)):
                self.tt(cc, w0, x0, ALU.mult, Rf, [bfx]); self.tt(cc, cc, bb, ALU.add, Rf, [bfx])
                self.tt(ta, w1, x1, ALU.mult, Rf, [bfx]); self.tt(cc, cc, ta, ALU.add, Rf, [bfx])
                self.tt(ta, w2, x2, ALU.mult, Rf, [bfx]); self.tt(cc, cc, ta, ALU.add, Rf, [bfx])
            for t, cc in ((0, c0), (1, c1)):
                self.act(sgl[:, 0:44], cc[:, 0:44], AF.Silu, Rf, [bfx])
                self.tt(actT[:, :, t], sgl[:, 0:44], cc[:, 44:88], ALU.mult, Rf, [b_act])
            k.barrier()
        self.linear(d['f_w_down'][layer], [(c * 128, 128) for c in range(KC)], [(0, FC)],
                    lambda c: (actT[:, c, 0:T], b_act), T, self.resid_evac(), extra_R=[self.b_x])
        if grp == 1 or self.sidx == NR - 1:
            dst = o['sconvo'] if grp == 1 else o['pconv']
            tb = Buf()
            tf = self.tail[:, layer, grp].rearrange("p r c -> p (r c)")
            ps, pb = self.ps[2], self.bps[2]
            self.tr(ps[0:128, 0:128], tf[:, 0:128], 128, [self.b_tail], [pb])
            self.tr(ps[0:48, 128:256], tf[:, 128:176], 128, [self.b_tail], [pb])
            self.cp(t1, ps[0:128, 0:128], [pb], [tb]); self.cp(t2[0:48, :], ps[0:48, 128:256], [pb], [tb])
            dv = [dst[layer, r].rearrange("(c p) -> c p", p=128) for r in range(2)]
            k.dma('sp', dv[0][0:88, :], t1[0:88, :], [tb], [self.b_out], dsem='out')
            k.dma('sp', dv[1][0:40, :], t1[88:128, :], [tb], [self.b_out], dsem='out')
            k.dma('sp', dv[1][40:88, :], t2[0:48, :], [tb], [self.b_out], dsem='out')

    def mlstm(self, layer):
        nc, k, d, o = self.nc, self.k, self.d, self.o
        T, L, grp = self.T, self.L, self.grp
        P0 = self.O_PH
        qT = self.bv(P0, 8 * SEG).rearrange("p (c t) -> p c t", c=8); b_q = Buf()
        kT = self.bv(P0 + 2048, 8 * SEG).rearrange("p (c t) -> p c t", c=8); b_k = Buf()
        vT = self.fv(P0 + 4096, 16 * SEG).rearrange("p (c t) -> p c t", c=16); b_v = Buf()
        gT = self.fv(P0 + 12288, SEG, 8); b_g = Buf()
        sgt2 = self.fv(self.O_SQ, 2 * SEG).rearrange("p (s t) -> p s t", s=2); b_sg2 = [Buf(), Buf()]
        O_SSTG = 6144
        O_GH = 10240
        S0 = P0 + 13312
        gcol = self.gn[:, layer, :]
        w_in = d['a_w_in'][layer]
        sq_, sv_ = H * DK, H * DV
        blocks = [(c * 128, 128) for c in range(8)] + [(sq_ + c * 128, 128) for c in range(8)] + \
                 [(2 * sq_ + c * 128, 128) for c in range(16)] + [(2 * sq_ + 2 * sv_, 8)]

        def evac(bi, ps, pb):
            e = 'act' if bi % 2 else 'dve'
            if bi < 8:
                self.cp(qT[:, bi, 0:T], ps, [pb], [b_q], e)
            elif bi < 16:
                self.cp(kT[:, bi - 8, 0:T], ps, [pb], [b_k], e)
            elif bi < 32:
                self.cp(vT[:, bi - 16, 0:T], ps, [pb], [b_v], e)
            else:
                self.ts(gT[:, 0:T], ps, self.gmisc[0:8, 54 + layer:55 + layer], None, ALU.add, None, [pb, self.b_gm], [b_g])

        self.linear(w_in, blocks, [(0, KC)], lambda c: (self.xb[:, c, 0:T], self.b_xb), T, evac)
        k.barrier()
        self.dbg('mproj')
        CT = self.fv(0, 4096).rearrange("p (c h v) -> p c h v", c=2, h=H); b_ct = Buf()
        CTb = self.bv(4096, 4096).rearrange("p (c h v) -> p c h v", c=2, h=H)
        nT = self.nst[:, layer, grp]
        nTb = self.nsb[:, layer, grp]
        mbc = self.mst[:, layer, grp, :]
        ghb = self.fv(O_GH, 2048).rearrange("p (h v) -> p h v", h=H); b_gh = Buf()
        k.dma('sp', ghb[0:L], d['a_g_head'][layer:layer + 1, :].broadcast_to([L, H * DV]).rearrange("p (h v) -> p h v", h=H), (), [b_gh])
        env = dict(layer=layer, qT=qT, kT=kT, vT=vT, gT=gT, CT=CT, CTb=CTb, nT=nT, nTb=nTb, mbc=mbc, ghb=ghb, S0=S0,
                   b_q=b_q, b_k=b_k, b_v=b_v, b_g=b_g, b_ct=b_ct, b_gh=b_gh)
        if grp == 1:
            stg = self.fv(O_SSTG, 4096).rearrange("p (h c k) -> p h c k", h=H, c=4); sb_ = Buf()
            k.dma('sp', stg, d['sC'][layer].rearrange("h (c p) k -> p h c k", p=128), (), [sb_])
            for h in range(H):
                for kc in range(2):
                    ps, pb = self.ps[(h * 2 + kc) % 2], self.bps[(h * 2 + kc) % 2]
                    for vc in range(4):
                        self.tr(ps[:, vc * 128:(vc + 1) * 128], stg[:, h, vc, kc * 128:(kc + 1) * 128], 128, [sb_], [pb])
                    self.cp(CT[:, kc, h, :], ps[:, :], [pb], [b_ct])
        else:
            self.ms(self.fv(0, 4096), 0.0, [b_ct], 'pool')
            self.ms(nT, 0.0, [self.b_nst], 'dve')
            self.ms(mbc, NEG, [self.b_mst], 'dve')
            self.ms(self.fsum[:], 0.0, [self.b_mst], 'dve')
            k.barrier()
            self.cp(self.bv(4096, 4096), self.fv(0, 4096), [b_ct], [b_ct])
            self.cp(nTb, nT, [self.b_nst], [self.b_nst])
            self.scan_chunks(env, True)
            k.barrier()
            self.exchange_state(layer, env)
            k.barrier()
        self.cp(self.bv(4096, 4096), self.fv(0, 4096), [b_ct], [b_ct])
        self.cp(nTb, nT, [self.b_nst], [self.b_nst])
        self.scan_chunks(env, False)
        k.barrier()
        self.dbg('scan')
        last = (grp == 1) or (self.sidx == NR - 1)
        if grp == 0:
            stt_ = self.fv(S0 + 200, 16); sbb = Buf()
            self.cp(stt_[:, 0:8].rearrange("p (c h) -> p c h", c=2), nT[:, :, :, 0], [self.b_nst], [sbb])
            self.cp(stt_[:, 8:12], mbc, [self.b_mst], [sbb])
            self.ms(stt_[:, 12:16], 0.0, [sbb], 'dve')
            for c in range(2):
                k.dma('sp', self.xsC[layer][1][c][:, :], self.fv(c * 2048, 2048), [b_ct, self.b_xg[layer]], [self.b_xs2[layer]])
            k.dma('sp', self.xsS[layer][128:256, :], stt_, [sbb, self.b_xg[layer]], [self.b_xs2[layer]])
        if last:
            Cd = (o['sCo'] if grp == 1 else o['pC'])[layer]
            nd = (o['sno'] if grp == 1 else o['pn'])[layer]
            md = (o['smo'] if grp == 1 else o['pm'])[layer]
            stg = self.fv(O_SSTG, 4096).rearrange("p (h c k) -> p h c k", h=H, c=4); sb_ = Buf()
            for h in range(H):
                for vc in range(4):
                    ps, pb = self.ps[vc % 2], self.bps[vc % 2]
                    for kc in range(2):
                        self.tr(ps[:, kc * 128:(kc + 1) * 128], CT[:, kc, h, vc * 128:(vc + 1) * 128], 128, [b_ct], [pb])
                    self.cp(stg[:, h, vc, :], ps[:, 0:256], [pb], [sb_])
            k.dma('sp', Cd.rearrange("h (c p) k -> p h c k", p=128), stg, [sb_], [self.b_out], dsem='out')
            nf = self.fv(S0, 8); nb = Buf()
            self.cp(nf.rearrange("p (h c) -> p c h", h=H), nT[:, :, :, 0], [self.b_nst], [nb])
            ps, pb = self.ps[2], self.bps[2]
            self.tr(ps[0:8, 0:128], nf, 128, [nb], [pb])
            nf2 = self.fv(S0 + 16, 128, 8)
            self.cp(nf2, ps[0:8, 0:128], [pb], [nb])
            k.dma('sp', nd.rearrange("(c p) -> c p", p=128), nf2, [nb], [self.b_out], dsem='out')
            k.dma('sp', md.rearrange("(o h) -> o h", o=1), mbc[0:1, :], [self.b_mst], [self.b_out], dsem='out')
        k.barrier()
        def evac_o(bi, ps, pb):
            sg2 = sgt2[:, bi % 2, 0:T]
            self.act(sg2, ps, AF.Sigmoid, [pb], [b_sg2[bi % 2]])
            self.tt(vT[:, bi, 0:T], vT[:, bi, 0:T], sg2, ALU.mult, [b_sg2[bi % 2], b_v], [b_v], 'pool' if bi % 2 else 'dve')
        self.linear(w_in, [(2 * sq_ + sv_ + c * 128, 128) for c in range(16)], [(0, KC)], lambda c: (self.xb[:, c, 0:T], self.b_xb), T, evac_o)
        hr = self.bv(S0, 16 * SEG).rearrange("p (c t) -> p c t", c=16); b_hr = Buf()
        k.barrier()
        for c in range(16):
            self.cp(hr[:, c, 0:T], vT[:, c, 0:T], [b_v], [b_hr], 'dve' if c % 2 else 'act')
        self.linear(d['a_w_out'][layer], [(c * 128, 128) for c in range(KC)], [(0, KC)], lambda c: (hr[:, c, 0:T], b_hr), T,
                    self.resid_evac(), extra_R=[self.b_x])

    def scan_chunks(self, env, state_only):
        nc, k = self.nc, self.k
        T, L = self.T, self.L
        layer = env['layer']
        qT, kT, vT, gT, CT, CTb, nT, nTb, mbc, ghb, S0 = (env[x] for x in ('qT', 'kT', 'vT', 'gT', 'CT', 'CTb', 'nT', 'nTb', 'mbc', 'ghb', 'S0'))
        b_q, b_k, b_v, b_g, b_ct, b_gh = (env[x] for x in ('b_q', 'b_k', 'b_v', 'b_g', 'b_ct', 'b_gh'))
        k_c = self.bv(S0, 1024); v_c = self.bv(S0 + 512, 2048); wv = self.bv(S0 + 1536, 2048)
        junk = self.fv(S0 + 1536, 512)
        hh = self.fv(S0 + 2560, 2048)
        sm_ = self.fv(S0 + 4608, 64)
        dg = self.fv(S0 + 4672, 512); dl = self.fv(S0 + 5184, 512); dT = self.fv(S0 + 5696, 512)
        sdT = self.bv(S0 + 6208, 512)
        qs = self.bv(S0 + 6464, 1024).rearrange("p (c t) -> p c t", c=8)
        bcs = self.fv(S0 + 6976, 32)
        w2r = self.bv(S0 + 7008, 2)
        bS = Buf('scan')
        P = self.ps; B = self.bps
        for c in range(T // L):
            c0 = c * L
            R = [bS, b_q, b_k, b_v, b_g, b_ct, b_gh, self.b_nst, self.b_mst, self.b_const]
            Wb = [bS]
            for g4 in range(2):
                pbf = P[0][0:L, :].bitcast(BF16)
                for j in range(4):
                    self.tr(pbf[:, j * 128:(j + 1) * 128], kT[:, g4 * 4 + j, c0:c0 + L], 128, R, [B[0]], bf=True)
                self.cp(k_c[0:L, g4 * 512:(g4 + 1) * 512], pbf[:, 0:512], [B[0]], Wb)
            for g4 in range(4):
                pp = 1 + g4 % 2
                for j in range(4):
                    self.tr(P[pp][0:L, j * 128:(j + 1) * 128], vT[:, g4 * 4 + j, c0:c0 + L], 128, R, [B[pp]])
                self.cp(v_c[0:L, g4 * 512:(g4 + 1) * 512], P[pp][0:L, :], [B[pp]], Wb, 'act')
            self.tr(P[3][0:L, 0:8], gT[:, c0:c0 + L], 8, R, [B[3]])
            gi = sm_[0:L, 0:4]; sp_ = sm_[0:L, 4:8]; b_ = sm_[0:L, 8:12]; a_ = sm_[0:L, 12:16]
            il = sm_[0:L, 16:20]; mt = sm_[0:L, 20:24]; mb = sm_[0:L, 20:28]; u_ = sm_[0:L, 28:32]
            iw = sm_[0:L, 32:36]; en = sm_[0:L, 36:40]; w_ = sm_[0:L, 40:44]; den = sm_[0:L, 44:48]
            ss = sm_[0:L, 48:52]; rr = sm_[0:L, 52:56]; mloc = sm_[0:L, 56:60]; tmp4 = sm_[0:L, 60:64]
            self.cp(gi, P[3][0:L, 0:4], [B[3]], Wb)
            self.act(sp_, P[3][0:L, 4:8], AF.Exp, [B[3]], Wb, scale=-1.0)
            self.act(sp_, sp_, AF.Ln, [bS], Wb, bias=1.0)
            self.mm(P[3][0:L, 8:12], self.uneg[L][:, :], sp_, True, True, R, [B[3]])
            self.cp(b_, P[3][0:L, 8:12], [B[3]], Wb)
            self.cp(sm_[0:L, 24:28], b_, R, Wb)
            self.tt(a_, gi, b_, ALU.subtract, R, Wb)
            idl = self.ident[0:L, 0:L]
            dg3 = dg[0:L, 0:4 * L].rearrange("p (h s) -> p h s", h=H)
            dl3 = dl[0:L, 0:4 * L].rearrange("p (h s) -> p h s", h=H)
            dT3 = dT[0:L, 0:4 * L].rearrange("p (h s) -> p h s", h=H)
            sd3 = sdT[0:L, 0:4 * L].rearrange("p (h s) -> p h s", h=H)

            def rowbc(col4, ps_ap, psb, lhs):
                self.tt(dg3, bc(idl.unsqueeze(1), [L, H, L]), bc(col4.unsqueeze(2), [L, H, L]), ALU.mult, R, Wb)
                self.mm(ps_ap, lhs, dg[0:L, 0:4 * L], True, True, R, [psb])
            rowbc(a_, P[4][0:L, 0:4 * L], B[4], self.ones_f[0:L, 0:L])
            p4 = P[4][0:L, 0:4 * L].rearrange("p (h s) -> p h s", h=H)
            self.tt(dl3, p4, bc(b_.unsqueeze(2), [L, H, L]), ALU.add, [B[4]] + R, Wb)
            self.tt(dl3, dl3, bc(self.mask[L][:, :].unsqueeze(1), [L, H, L]), ALU.add, R, Wb)
            self.k.op('dve', lambda: nc.vector.tensor_reduce(out=mloc, in_=dl3, axis=AX.X, op=ALU.max), R, Wb)
            self.tt(il, b_, mbc[0:L, :], ALU.add, R, Wb)
            self.tt(mt, il, mloc, ALU.max, R, Wb)
            if not state_only:
                self.tt(u_, b_, mt, ALU.subtract, R, Wb)
                self.tt(tmp4, il, mt, ALU.subtract, R, Wb)
                self.act(iw, tmp4, AF.Exp, R, Wb)
                self.act(en, mt, AF.Exp, R, Wb, scale=-1.0)
                rowbc(u_, P[4][0:L, 0:4 * L], B[4], self.ones_f[0:L, 0:L])
                self.tt(dT3, p4, bc(a_.unsqueeze(2), [L, H, L]), ALU.add, [B[4]] + R, Wb)
                self.tt(dT3, dT3, bc(self.maskT[L][:, :].unsqueeze(1), [L, H, L]), ALU.add, R, Wb)
                self.act(dT[0:L, 0:4 * L], dT[0:L, 0:4 * L], AF.Exp, R, Wb)
                for h in range(H):
                    for kc in range(2):
                        self.mm(P[5][0:L, h * L:(h + 1) * L], kT[:, h * 2 + kc, c0:c0 + L], qT[:, h * 2 + kc, c0:c0 + L], kc == 0, kc == 1, R, [B[5]])
                self.stt(sdT[0:L, 0:4 * L], P[5][0:L, 0:4 * L], DK ** -0.5, dT[0:L, 0:4 * L], ALU.mult, ALU.mult, [B[5]] + R, Wb)
                rowbc(iw, P[4][:, 0:4 * L], B[4], self.ones_f[0:L, :])
                pw = P[4][:, 0:4 * L].rearrange("p (h t) -> p h t", h=H)
                for kc in range(2):
                    qv = qT[:, :, c0:c0 + L].rearrange("p (h c) t -> p h c t", c=2)[:, :, kc, :]
                    qsv = qs[:, :, 0:L].rearrange("p (h c) t -> p h c t", c=2)[:, :, kc, :]
                    self.tt(qsv, qv, pw, ALU.mult, [B[4]] + R, Wb)
                for h in range(H):
                    pn_, bn_ = P[6 + h % 2], B[6 + h % 2]
                    self.mm(pn_[0:L, :], sd3[:, h, :], v_c[0:L, h * DV:(h + 1) * DV], True, False, R, [bn_])
                    for kc in range(2):
                        self.mm(pn_[0:L, :], qs[:, h * 2 + kc, 0:L], CTb[:, kc, h, :], False, kc == 1, R, [bn_])
                    self.mm(P[3][0:L, 16 + 2 * h:18 + 2 * h], sd3[:, h, :], self.ones_b[0:L, 0:2], True, False, R, [B[3]])
                    for kc in range(2):
                        self.mm(P[3][0:L, 16 + 2 * h:18 + 2 * h], qs[:, h * 2 + kc, 0:L], nTb[:, kc, h, :], False, kc == 1, R, [B[3]])
                    qn = P[3][0:L, 16 + 2 * h:17 + 2 * h]
                    self.act(den[:, h:h + 1], qn, AF.Abs, [B[3]] + R, Wb)
                    self.tt(den[:, h:h + 1], den[:, h:h + 1], en[:, h:h + 1], ALU.max, R, Wb)
                    self.recip(den[:, h:h + 1], den[:, h:h + 1], R, Wb)
                    self.act(junk[0:L, :], pn_[0:L, :], AF.Square, [bn_] + R, Wb, scale=den[:, h:h + 1], accum=ss[:, h:h + 1])
                    self.act(rr[:, h:h + 1], ss[:, h:h + 1], AF.Sqrt, R, Wb, bias=self.epsc[0:L, 0:1], scale=1.0 / DV)
                    self.recip(rr[:, h:h + 1], rr[:, h:h + 1], R, Wb)
                    self.tt(rr[:, h:h + 1], rr[:, h:h + 1], den[:, h:h + 1], ALU.mult, R, Wb)
                    self.stt(hh[0:L, h * DV:(h + 1) * DV], pn_[0:L, :], rr[:, h:h + 1], ghb[0:L, h, :], ALU.mult, ALU.mult, [bn_] + R, Wb)
            self.mm(P[3][:, 32:40], self.sel[L][:, :], mb, True, True, R, [B[3]])
            self.cp(bcs[:, 0:8], P[3][:, 32:40], [B[3]], Wb)
            mnew = bcs[:, 0:4]; blast = bcs[:, 4:8]; dec = bcs[:, 8:12]; t12 = bcs[:, 12:16]
            self.tt(self.fsum[:], self.fsum[:], blast, ALU.add, R, [self.b_mst, bS])
            self.tt(t12, blast, mnew, ALU.subtract, R, Wb)
            self.tt(dec, t12, mbc, ALU.add, R, Wb)
            self.act(dec, dec, AF.Exp, R, Wb)
            self.tt(w_, a_, t12[0:L, :], ALU.add, R, Wb)
            self.act(w_, w_, AF.Exp, R, Wb)
            self.ts(w_, w_, DK ** -0.5, None, ALU.mult, None, R, Wb)
            self.tt(wv[0:L, :].rearrange("p (h v) -> p h v", h=H), v_c[0:L, :].rearrange("p (h v) -> p h v", h=H),
                    bc(w_.unsqueeze(2), [L, H, DV]), ALU.mult, R, Wb)
            for h in range(H):
                self.cp(w2r[0:L, :], bc(w_[:, h:h + 1], [L, 2]), R, Wb)
                for kc in range(2):
                    pc, bcb = P[6 + kc], B[6 + kc]
                    self.mm(pc[:, :], k_c[0:L, h * DK + kc * 128:h * DK + (kc + 1) * 128], wv[0:L, h * DV:(h + 1) * DV], True, True, R, [bcb])
                    self.stt(CT[:, kc, h, :], CT[:, kc, h, :], dec[:, h:h + 1], pc[:, :], ALU.mult, ALU.add, [bcb, b_ct] + R, [b_ct])
                    if not state_only:
                        self.cp(CTb[:, kc, h, :], CT[:, kc, h, :], [b_ct], [b_ct], 'act')
                    self.mm(P[3][:, 48:50], k_c[0:L, h * DK + kc * 128:h * DK + (kc + 1) * 128], w2r[0:L, :], True, True, R, [B[3]])
                    self.stt(nT[:, kc, h, :], nT[:, kc, h, :], dec[:, h:h + 1], P[3][:, 48:50], ALU.mult, ALU.add,
                             [B[3], self.b_nst] + R, [self.b_nst])
                    if not state_only:
                        self.cp(nTb[:, kc, h, :], nT[:, kc, h, :], [self.b_nst], [self.b_nst])
            self.cp(mbc, mnew, R, [self.b_mst])
            if not state_only:
                for g4 in range(4):
                    pp = 4 + g4 % 2
                    for j in range(4):
                        self.tr(P[pp][:, j * L:(j + 1) * L], hh[0:L, (g4 * 4 + j) * 128:(g4 * 4 + j + 1) * 128], L, R, [B[pp]])
                    self.cp(vT[:, g4 * 4:g4 * 4 + 4, c0:c0 + L], P[pp][:, 0:4 * L].rearrange("p (j t) -> p j t", j=4), [B[pp]], [b_v, bS])

    def exchange_state(self, layer, env):
        nc, k = self.nc, self.k
        CT, nT, mbc, S0, b_ct = env['CT'], env['nT'], env['mbc'], env['S0'], env['b_ct']
        grp = self.grp
        sc = self.fv(S0 + 300, 600)
        bsc = Buf()
        stt_ = sc[:, 0:16]
        self.cp(stt_[:, 0:8].rearrange("p (c h) -> p c h", c=2), nT[:, :, :, 0], [self.b_nst], [bsc])
        self.cp(stt_[:, 8:12], mbc, [self.b_mst], [bsc])
        self.cp(stt_[:, 12:16], self.fsum[:], [self.b_mst], [bsc])
        for c in range(2):
            k.dma('sp', self.xsC[layer][0][c][:, :], self.fv(c * 2048, 2048), [b_ct, self.b_xg[layer]], [self.b_xs2[layer]])
        k.dma('sp', self.xsS[layer][0:128, :], stt_, [bsc, self.b_xg[layer]], [self.b_xs2[layer]])
        for t in range(2):
            for c in range(2):
                k.collective(self.xsC[layer][t][c][:, :], self.xgC[layer][t][c][:, :], GROUPS, [self.b_xs2[layer]], [self.b_xg[layer]])
        k.collective(self.xsS[layer][:, :], self.xgS[layer][:, :], GROUPS, [self.b_xs2[layer]], [self.b_xg[layer]])
        sg = sc[:, 16:16 + 128].rearrange("p (i t c) -> p i t c", i=4, t=2)
        k.dma('sp', sg, self.xgS[layer].rearrange("(i t p) c -> p i t c", t=2, p=128), [self.b_xg[layer]], [bsc])
        cm = self.cm
        F3 = sg[:, :, 0, 12:16]
        m3 = sg[:, :, 0, 8:12]
        mcar = sg[:, 3, 1, 8:12]
        Rr = [bsc, self.b_gn]
        T1 = sc[:, 144:208].rearrange("p (i h l) -> p i h l", i=4, h=4)
        Mv = cm[:, 8:24].rearrange("p (i l) -> p i l", i=4)
        self.tt(T1, bc(Mv.unsqueeze(2), [128, 4, 4, 4]), bc(F3.rearrange("p l h -> p h l").unsqueeze(1), [128, 4, 4, 4]), ALU.mult, Rr, [bsc])
        G = sc[:, 208:224].rearrange("p (i h) -> p i h", i=4)
        self.k.op('dve', lambda: nc.vector.tensor_reduce(out=G, in_=T1, axis=AX.X, op=ALU.add), Rr, [bsc])
        E = sc[:, 224:240].rearrange("p (i h) -> p i h", i=4)
        self.tt(E, m3, G, ALU.add, Rr, [bsc])
        self.tt(E, E, bc(cm[:, 0:4].unsqueeze(2), [128, 4, 4]), ALU.add, Rr, [bsc])
        T2 = sc[:, 240:256].rearrange("p (h l) -> p h l", h=4)
        self.tt(T2, bc(cm[:, 4:8].unsqueeze(1), [128, 4, 4]), F3.rearrange("p l h -> p h l"), ALU.mult, Rr, [bsc])
        Ec = sc[:, 256:260]
        self.k.op('dve', lambda: nc.vector.tensor_reduce(out=Ec, in_=T2, axis=AX.X, op=ALU.add), Rr, [bsc])
        self.tt(Ec, Ec, mcar, ALU.add, Rr, [bsc])
        mx = sc[:, 260:264]
        self.k.op('dve', lambda: nc.vector.tensor_reduce(out=mx, in_=E.rearrange("p i h -> p h i"), axis=AX.X, op=ALU.max), Rr, [bsc])
        min_ = sc[:, 264:268]
        self.tt(min_, mx, Ec, ALU.max, Rr, [bsc])
        Wt = sc[:, 268:284].rearrange("p (i h) -> p i h", i=4)
        self.tt(Wt, E, bc(min_.unsqueeze(1), [128, 4, 4]), ALU.subtract, Rr, [bsc])
        self.act(sc[:, 268:284], sc[:, 268:284], AF.Exp, Rr, [bsc])
        Wc = sc[:, 284:288]
        self.tt(Wc, Ec, min_, ALU.subtract, Rr, [bsc])
        self.act(Wc, Wc, AF.Exp, Rr, [bsc])
        nacc = sc[:, 288:296].rearrange("p (c h) -> p c h", c=2)
        ntmp = sc[:, 296:304].rearrange("p (c h) -> p c h", c=2)
        self.tt(nacc, sg[:, 3, 1, 0:8].rearrange("p (c h) -> p c h", c=2), bc(Wc.unsqueeze(1), [128, 2, 4]), ALU.mult, Rr, [bsc])
        for i in range(4):
            self.tt(ntmp, sg[:, i, 0, 0:8].rearrange("p (c h) -> p c h", c=2), bc(Wt[:, i, :].unsqueeze(1), [128, 2, 4]), ALU.mult, Rr, [bsc])
            self.tt(nacc, nacc, ntmp, ALU.add, Rr, [bsc])
        for e in range(2):
            self.cp(nT[:, :, :, e], nacc, Rr, [self.b_nst])
        self.cp(mbc, min_, Rr, [self.b_mst])
        stg = self.fv(6144, 4096).rearrange("p (c h v) -> p c h v", c=2, h=H); bst = Buf()
        srcs = [(1, 3, None)] + [(0, i, i) for i in range(4)]
        for si, (tt_, rk, i) in enumerate(srcs):
            for c in range(2):
                k.dma('sp', self.fv(6144 + c * 2048, 2048),
                      self.xgC[layer][tt_][c][rk * 128:(rk + 1) * 128, :], [self.b_xg[layer]], [bst])
            for kc in range(2):
                for h in range(H):
                    if i is None:
                        self.ts(CT[:, kc, h, :], stg[:, kc, h, :], Wc[:, h:h + 1], None, ALU.mult, None, [bst] + Rr, [b_ct])
                    else:
                        self.stt(CT[:, kc, h, :], stg[:, kc, h, :], Wt[:, i, h:h + 1], CT[:, kc, h, :], ALU.mult, ALU.add, [bst, b_ct] + Rr, [b_ct])

    def rope(self, dst, dstb, src, srcb, pos0, T, o_cs):
        cs = self.fv(o_cs, SEG, 64); sn = self.fv(o_cs + 512, SEG, 64)
        if not hasattr(self, 'b_rope'):
            self.b_rope = Buf()
        tb = self.b_rope
        self.k.dma('sp', cs[:, 0:T], self.d['cos2'][:, pos0:pos0 + T], (), [tb])
        self.k.dma('sp', sn[:, 0:T], self.d['sin2'][:, pos0:pos0 + T], (), [tb])
        ps, pb = self.ps[3], self.bps[3]
        self.mm(ps[0:64, 0:T], self.perm[:, :], src, True, True, [srcb, self.b_const], [pb])
        self.tt(sn[:, 0:T], ps[0:64, 0:T], sn[:, 0:T], ALU.mult, [pb, tb], [tb])
        self.tt(cs[:, 0:T], src, cs[:, 0:T], ALU.mult, [srcb, tb], [tb])
        self.tt(dst, cs[:, 0:T], sn[:, 0:T], ALU.add, [tb], [dstb])

    def shared_kv(self, pos0):
        nc, k, d, o = self.nc, self.k, self.d, self.o
        T, grp, s = self.T, self.grp, self.sidx
        P0 = self.O_PH
        zT = self.fv(P0, 5 * SEG).rearrange("p (c t) -> p c t", c=5); b_z = Buf()
        cTf = self.fv(P0 + 2560, 4 * SEG).rearrange("p (c t) -> p c t", c=4); b_c = Buf()
        cTb = self.bv(P0 + 4608, 4 * SEG).rearrange("p (c t) -> p c t", c=4)
        kpT = self.fv(P0 + 5632, SEG, 64); b_kp = Buf()
        kpn = self.bv(P0 + 6144, SEG, 64)
        r2 = self.fv(P0 + 6400, SEG); b_r2 = Buf()
        tm = self.fv(P0 + 6912, 576); ld = self.fv(P0 + 7488, 576)
        self.o_kv = P0 + 8300
        blocks = [(c * 128, 128) for c in range(4)] + [(KVL, 64)]

        def evac(bi, ps, pb):
            m = 128 if bi < 4 else 64
            self.cp(zT[0:m, bi, 0:T], ps, [pb], [b_z], 'act' if bi % 2 else 'dve')
        self.linear(d['kv_w_down'], blocks, [(0, KC)], lambda c: (self.xb[:, c, 0:T], self.b_xb), T, evac)
        self.sumsq_bc(lambda c: (zT[:, c, 0:T], b_z), 4, T, r2[:, 0:T], b_r2, KVL)
        for c in range(4):
            self.stt(cTf[:, c, 0:T], zT[:, c, 0:T], self.gmisc[:, 16 + c:17 + c], r2[:, 0:T], ALU.mult, ALU.mult, [b_z, b_r2, self.b_gm], [b_c])
            self.cp(cTb[:, c, 0:T], cTf[:, c, 0:T], [b_c], [b_c], 'act')
        self.sumsq_bc(lambda c: (zT[0:64, 4, 0:T], b_z), 1, T, r2[0:64, 0:T], b_r2, ROPE, rows=64)
        self.stt(kpn[:, 0:T], zT[0:64, 4, 0:T], self.gmisc[0:64, 20:21], r2[0:64, 0:T], ALU.mult, ALU.mult, [b_z, b_r2, self.b_gm], [b_kp])
        self.rope(kpT[:, 0:T], b_kp, kpn[:, 0:T], b_kp, pos0, T, P0 + 18000)
        k0 = 0 if grp == 0 else PAST
        kb = self.b_kvloc if grp == 0 else self.b_kvd[grp]
        kpb = self.bv(P0 + 8000, SEG, 64); b_kpb = Buf()
        self.cp(kpb[:, 0:T], kpT[:, 0:T], [b_kp], [b_kpb])
        if grp == 0:
            k.dma('sp', self.KRp[:, 0:T], kpb[:, 0:T], [b_kpb] + [self.b_kvall[r] for r in range(NR)], [kb], dsem='kv')
        else:
            k.dma('sp', self.kr_dram[grp][:, k0:k0 + T], kpb[:, 0:T], [b_kpb], [kb], dsem='kv')
        cdst = (o['p_ckv'][s * SEG:(s + 1) * SEG] if grp == 0 else o['s_ckv'])
        kdst = (o['p_kpe'][s * SEG:(s + 1) * SEG] if grp == 0 else o['s_kpe'])
        tb = Buf()
        for t0 in range(0, T, 128):
            n = min(128, T - t0)
            ps, pb = self.ps[2], self.bps[2]
            for c in range(4):
                self.tr(ps[0:n, c * 128:(c + 1) * 128], cTf[:, c, t0:t0 + n], 128, [b_c], [pb])
            self.cp(tm[0:n, 0:512], ps[0:n, :], [pb], [tb])
            ps, pb = self.ps[3], self.bps[3]
            self.tr(ps[0:n, 0:64], kpT[:, t0:t0 + n], 64, [b_kp], [pb])
            self.cp(tm[0:n, 512:576], ps[0:n, 0:64], [pb], [tb])
            k.dma('sp', cdst[t0:t0 + n, :], tm[0:n, 0:512], [tb], [self.b_out], dsem='out')
            k.dma('sp', kdst[t0:t0 + n, :], tm[0:n, 512:576], [tb], [self.b_out], dsem='out')
        self.kv_up(cTb, b_c, T, k0)
        if grp == 0:
            k.barrier()
            for j in range(2):
                k.collective(self.KP[j][:, :], self.KG[s][j][:, :], GROUPS, [self.b_kvloc], [self.b_kvall[s]])
                k.collective(self.VP[j][:, :], self.VG[s][j][:, :], GROUPS, [self.b_kvloc], [self.b_kvall[s]])
            k.collective(self.KRp[:, :], self.KRG[s][:, :], GROUPS, [self.b_kvloc], [self.b_kvall[s]])
        if grp == 1:
            k.barrier()
            lb = Buf()
            for blk in range(PAST // SEG):
                for t0 in range(0, SEG, 128):
                    r0 = blk * SEG + t0
                    k.dma('sp', ld[:, 0:512], d['ckv'][r0:r0 + 128, :], (), [lb])
                    k.dma('sp', ld[:, 512:576], d['kpe'][r0:r0 + 128, :], (), [lb])
                    ps, pb = self.ps[2], self.bps[2]
                    for c in range(4):
                        self.tr(ps[:, c * 128:(c + 1) * 128], ld[:, c * 128:(c + 1) * 128], 128, [lb], [pb])
                    self.cp(cTb[:, :, t0:t0 + 128], ps[:, :].rearrange("p (c t) -> p c t", c=4), [pb], [b_c])
                    ps, pb = self.ps[3], self.bps[3]
                    self.tr(ps[0:64, 0:128], ld[:, 512:576], 128, [lb], [pb])
                    self.cp(kpT[:, t0:t0 + 128], ps[0:64, 0:128], [pb], [b_kp])
                self.cp(kpb[:, 0:SEG], kpT[:, 0:SEG], [b_kp], [b_kpb])
                k.dma('sp', self.kr_dram[1][:, blk * SEG:(blk + 1) * SEG], kpb[:, 0:SEG], [b_kpb], [kb], dsem='kv')
                self.kv_up(cTb, b_c, SEG, blk * SEG)

    def kv_up(self, cTb, b_c, T, k0):
        nc, k, d = self.nc, self.k, self.d
        grp = self.grp
        kb = self.b_kvloc if grp == 0 else self.b_kvd[grp]
        O = self.o_kv
        k.barrier()
        kn = self.fv(O, 2 * SEG).rearrange("p (s t) -> p s t", s=2); bkn = [Buf(), Buf()]
        r3 = self.fv(O + 1024, SEG); b_r3 = Buf()
        wvs = self.bv(O + 1536, 4 * 2048).rearrange("p (c n) -> p c n", c=4); b_wv = Buf()
        stg = self.fv(O + 5632, 2048); b_st = Buf()
        vt = self.bv(O + 7680, 2048); b_vt = Buf()
        knh = self.bv(O + 8704, 2 * SEG).rearrange("p (s t) -> p s t", s=2)
        blocks = [(h * 256, 128) for h in range(BH)]

        def evac(bi, ps, pb):
            sl_ = bi % 2
            self.cp(kn[:, sl_, 0:T], ps, [pb], [bkn[sl_]], 'act')
            self.sumsq_bc(lambda c: (kn[:, sl_, 0:T], bkn[sl_]), 1, T, r3[:, 0:T], b_r3, NOPE, ps_id=3)
            self.stt(kn[:, sl_, 0:T], kn[:, sl_, 0:T], self.gmisc[:, 21:22], r3[:, 0:T], ALU.mult, ALU.mult, [bkn[sl_], b_r3, self.b_gm], [bkn[sl_]])
            self.cp(knh[:, sl_, 0:T], kn[:, sl_, 0:T], [bkn[sl_]], [bkn[sl_]], 'pool')
            kdst = self.KP[bi // 8][(bi % 8) * 128:(bi % 8 + 1) * 128, 0:T] if grp == 0 else self.kn_dram[grp][bi, :, k0:k0 + T]
            k.dma('sp', kdst, knh[:, sl_, 0:T], [bkn[sl_]], [kb], dsem='kv')
        self.linear(d['kv_w_up'], blocks, [(0, 4)], lambda c: (cTb[:, c, 0:T], b_c), T, evac, sets=[[0], [1], [4], [5]])
        wv_ = d['kv_w_up'].rearrange("(c p) (h two x) -> p c h two x", p=128, two=2, x=128)
        for c in range(4):
            k.dma('sp', stg.rearrange("p (h x) -> p h x", h=BH), wv_[:, c, :, 1, :], (), [b_st])
            self.cp(wvs[:, c, :], stg, [b_st], [b_wv])
        for t0 in range(0, T, 128):
            n = min(128, T - t0)
            for q4 in range(4):
                ps, pb = self.ps[q4 % 2], self.bps[q4 % 2]
                for c in range(4):
                    self.mm(ps[0:n, :], cTb[:, c, t0:t0 + n], wvs[:, c, q4 * 512:(q4 + 1) * 512], c == 0, c == 3, [b_c, b_wv], [pb])
                self.cp(vt[0:n, q4 * 512:(q4 + 1) * 512], ps[0:n, :], [pb], [b_vt], 'act' if q4 % 2 else 'dve')
            if grp == 0:
                for q in range(2):
                    k.dma('sp', self.VP[q][t0:t0 + n, :], vt[0:n, q * 1024:(q + 1) * 1024], [b_vt], [kb], dsem='kv')
            else:
                k.dma('sp', self.v_dram[grp][k0 + t0:k0 + t0 + n, :], vt[0:n, :], [b_vt], [kb], dsem='kv')

    def mla(self, j):
        nc, k, d, o = self.nc, self.k, self.d, self.o
        T, grp, s = self.T, self.grp, self.sidx
        pos0 = s * SEG if grp == 0 else NR * SEG
        P0 = self.O_PH
        cq = self.bv(P0, 6 * SEG).rearrange("p (c t) -> p c t", c=6); b_cq = Buf()
        r2 = self.fv(P0 + 1536, SEG); b_r2 = Buf()
        QN = self.bv(P0 + 2048, 16 * SEG).rearrange("p (c t) -> p c t", c=16); b_qn = Buf()
        QR = self.bv(P0 + 6144, 16 * SEG, 64).rearrange("p (c t) -> p c t", c=16); b_qr = Buf()
        r3 = self.fv(P0 + 10240, SEG); b_r3 = Buf()
        qtmp = self.fv(P0 + 10752, SEG); b_qt = Buf()
        qrn = self.bv(P0 + 11264, SEG, 64); b_qrn = Buf()
        O_CS = P0 + 11520
        rd = self.fv(P0 + 12544, SEG); b_rd = Buf()
        cqg = self.bv(P0 + 13100, 6 * SEG).rearrange("p (c t) -> p c t", c=6)

        def evac(bi, ps, pb):
            self.cp(cq[:, bi, 0:T], ps, [pb], [b_cq], 'act')
            self.ts(cqg[:, bi, 0:T], ps, self.gmisc[:, 6 * j + bi:6 * j + bi + 1], None, ALU.mult, None, [pb, self.b_gm], [b_cq])
        self.linear(d['b_w_dq'][j], [(c * 128, 128) for c in range(6)], [(0, KC)], lambda c: (self.xb[:, c, 0:T], self.b_xb), T, evac)
        self.sumsq_bc(lambda c: (cq[:, c, 0:T], b_cq), 6, T, r2[:, 0:T], b_r2, QL)
        blocks = []
        for h in range(BH):
            blocks.append((h * 192, 128)); blocks.append((h * 192 + 128, 64))

        def evac2(bi, ps, pb):
            h, isr = bi // 2, bi % 2
            m = 64 if isr else 128
            self.tt(qtmp[0:m, 0:T], ps, r2[0:m, 0:T], ALU.mult, [pb, b_r2], [b_qt])
            self.sumsq_bc(lambda c: (qtmp[0:m, 0:T], b_qt), 1, T, r3[0:m, 0:T], b_r3, m, rows=m, ps_id=3)
            if not isr:
                self.stt(QN[:, h, 0:T], qtmp[:, 0:T], self.gmisc[:, 12 + j:13 + j], r3[:, 0:T], ALU.mult, ALU.mult, [b_qt, b_r3, self.b_gm], [b_qn])
            else:
                self.stt(qrn[:, 0:T], qtmp[0:64, 0:T], self.gmisc[0:64, 14 + j:15 + j], r3[0:64, 0:T], ALU.mult, ALU.mult, [b_qt, b_r3, self.b_gm], [b_qrn])
                self.rope(QR[:, h, 0:T], b_qr, qrn[:, 0:T], b_qrn, pos0, T, O_CS)
        self.linear(d['b_w_uq'][j], blocks, [(0, 6)], lambda c: (cqg[:, c, 0:T], b_cq), T, evac2, sets=[[0, 1], [4, 5], [6, 7]])
        k.barrier()
        KB = 1024 if grp == 1 else SEG
        knb = self.bv(0, 2 * 1024).rearrange("p (s t) -> p s t", s=2); b_knb = [Buf(), Buf()]
        vb = self.bv(1024, 2 * 1024).rearrange("p (s t x) -> p s t x", s=2, x=128); b_vb = [Buf(), Buf()]
        krall = self.bv(3072, 9 * SEG, 64); b_kr = Buf()
        dacc = self.fv(5376, SEG); daccb = self.bv(5888, SEG); b_da = Buf()
        blocks = []
        if grp == 1:
            nkeys = PAST + DS
            for kk0 in range(0, nkeys, KB):
                nk = min(KB, nkeys - kk0)
                blocks.append((lambda h, kk0=kk0, nk=nk: self.kn_dram[1][h, :, kk0:kk0 + nk],
                               self.kr_dram[1][:, kk0:kk0 + nk],
                               lambda h, t, n, kk0=kk0: self.v_dram[1][kk0 + t * 128:kk0 + t * 128 + n, h * 128:(h + 1) * 128],
                               nk, False, None, self.b_kvd[1]))
        else:
            def mk(KS, KRS, VS, i, diag, bias, dep):
                return (lambda h: KS[h // 8][i * 1024 + (h % 8) * 128:i * 1024 + (h % 8 + 1) * 128, :],
                        KRS[i * ROPE:(i + 1) * ROPE, :],
                        lambda h, t, n: VS[h // 8][i * SEG + t * 128:i * SEG + t * 128 + n, (h % 8) * 128:(h % 8 + 1) * 128],
                        SEG, diag, bias, dep)
            for rdi in range(s):
                for i in range(4):
                    blocks.append(mk(self.KG[rdi], self.KRG[rdi], self.VG[rdi], i, False, None, self.b_kvall[rdi]))
            for i in range(4):
                blocks.append(mk(self.KG[s], self.KRG[s], self.VG[s], i, False, 32 + i, self.b_kvall[s]))
            blocks.append(mk(self.KP, self.KRp, self.VP, 0, True, None, self.b_kvloc))
        ntot = sum((blk[3] + 127) // 128 for blk in blocks)
        kro = 0
        for (knf, krap, vf, nk, diag, biascol, dep) in blocks:
            k.dma('sp', krall[:, kro:kro + nk], krap, [dep], [b_kr], dsem='ld')
            kro += nk
        DEPTH_ = 3
        sbanks = [0, 1, 6, 7]
        pTs = [self.bv(2048 + 256 * i, SEG) for i in range(4)]
        b_pTs = [Buf() for _ in range(4)]
        slot = 0
        pi = 0
        for h in range(BH):
            po, bo = self.ps[4 + h % 2], self.bps[4 + h % 2]
            pd, bd = self.ps[2 + h % 2], self.bps[2 + h % 2]
            tiles = []
            kro = 0
            for (knf, krap, vf, nk, diag, biascol, dep) in blocks:
                tiles.append(('load', knf, vf, nk, dep))
                for t in range((nk + 127) // 128):
                    n = min(128, nk - t * 128)
                    tiles.append(('tile', t, n, t * 128 if diag else 0, diag, biascol, kro))
                kro += nk
            pend = []
            state = dict(first=True, cnt=0, sl=0)

            def finish(item):
                (ps_, bs_, pp, bp, sl_, t, n, q0, diag, biascol) = item
                if biascol is None:
                    self.act(pp[0:n, q0:T], ps_[0:n, q0:T], AF.Exp, [bs_], [bp], scale=ATT_SCALE)
                else:
                    self.act(pp[0:n, q0:T], ps_[0:n, q0:T], AF.Exp, [bs_, self.b_gn], [bp], scale=ATT_SCALE,
                             bias=self.cm[0:n, biascol:biascol + 1])
                if diag:
                    self.ms(pp[64:128, q0:q0 + 64], 0.0, [bp], 'dve')
                state['cnt'] += 1
                self.mm(po[:, q0:T], vb[0:n, sl_, t, :], pp[0:n, q0:T], state['first'], state['cnt'] == ntot, [b_vb[sl_], bp], [bo])
                if state['first']:
                    assert n == 128 and q0 == 0
                    self.cp(dacc[:, 0:T], pp[:, 0:T], [bp], [b_da])
                else:
                    self.tt(dacc[0:n, q0:T], dacc[0:n, q0:T], pp[0:n, q0:T], ALU.add, [bp, b_da], [b_da])
                state['first'] = False

            for it in tiles:
                if it[0] == 'load':
                    _, knf, vf, nk, dep = it
                    sl_ = slot; slot ^= 1
                    state['sl'] = sl_
                    k.dma('sp', knb[:, sl_, 0:nk], knf(h), [dep], [b_knb[sl_]], dsem='ld')
                    nfull = nk // 128
                    if nfull:
                        k.dma('sp', vb[:, sl_, 0:nfull, :], vf(h, 0, nfull * 128).rearrange("(t p) d -> p t d", p=128), [dep], [b_vb[sl_]], dsem='ld')
                    if nk % 128:
                        n_ = nk % 128
                        k.dma('sp', vb[0:n_, sl_, nfull, :], vf(h, nfull, n_), [dep], [b_vb[sl_]], dsem='ld')
                    continue
                _, t, n, q0, diag, biascol, kro_ = it
                sl_ = state['sl']
                ps_, bs_ = self.ps[sbanks[pi % 4]], self.bps[sbanks[pi % 4]]
                pp, bp = pTs[pi % 4], b_pTs[pi % 4]
                pi += 1
                self.mm(ps_[0:n, q0:T], knb[:, sl_, t * 128:t * 128 + n], QN[:, h, q0:T], True, False, [b_knb[sl_], b_qn], [bs_])
                self.mm(ps_[0:n, q0:T], krall[:, kro_ + t * 128:kro_ + t * 128 + n], QR[:, h, q0:T], False, True, [b_kr, b_qr], [bs_])
                pend.append((ps_, bs_, pp, bp, sl_, t, n, q0, diag, biascol))
                if len(pend) > DEPTH_:
                    finish(pend.pop(0))
            while pend:
                finish(pend.pop(0))
            self.cp(daccb[:, 0:T], dacc[:, 0:T], [b_da], [b_da], 'act')
            self.mm(pd[:, 0:T], self.ones_b[:, :], daccb[:, 0:T], True, True, [b_da, self.b_const], [bd])
            self.recip(rd[:, 0:T], pd[:, 0:T], [bd], [b_rd])
            self.tt(QN[:, h, 0:T], po[:, 0:T], rd[:, 0:T], ALU.mult, [bo, b_rd], [b_qn])
        k.barrier()
        self.linear(d['b_w_o'][j], [(c * 128, 128) for c in range(KC)], [(0, KC)], lambda c: (QN[:, c, 0:T], b_qn), T,
                    self.resid_evac(), extra_R=[self.b_x])


_PROG = None


def _rope_tables():
    half = ROPE // 2
    inv = (10000.0 ** (-np.arange(half, dtype=np.float32) / half)).astype(np.float32)
    pos = np.concatenate([np.arange(SEQ), PAST + np.arange(DS)]).astype(np.float32)
    ang = pos[None, :] * inv[:, None]
    c, s = np.cos(ang).astype(np.float32), np.sin(ang).astype(np.float32)
    return np.concatenate([c, c], 0), np.concatenate([-s, s], 0)


def _core_mask(r):
    m = np.zeros((40,), np.float32)
    for i in range(4):
        m[i] = 0.0 if i < r else NEG
        m[4 + i] = 1.0 if i < r else 0.0
        for l in range(4):
            m[8 + 4 * i + l] = 1.0 if (i < l < r) else 0.0
        m[24 + i] = 1.0 if i == r - 1 else 0.0
        m[32 + i] = 0.0 if i < r else -30000.0
    m[28] = 1.0 if r == 0 else 0.0
    return np.ascontiguousarray(np.broadcast_to(m[None, :], (128, 40)))


def kernel(**inp):
    global _PROG
    if _PROG is None:
        _PROG = Prog()
    prog = _PROG
    f = lambda a: np.ascontiguousarray(np.asarray(a, dtype=np.float32))
    cos2, sin2 = _rope_tables()
    shared = {}
    for n in ['norm_mix', 'norm_ffn', 'a_w_in', 'a_b_gate', 'a_w_out', 'kv_w_down', 'kv_w_up', 'b_w_dq', 'b_g_cq', 'b_w_uq',
              'b_g_qn', 'b_g_qr', 'b_w_o', 'f_w_up', 'f_conv_w', 'f_conv_b', 'f_w_down']:
        shared[n] = f(inp[n])
    shared['a_g_head'] = f(inp['a_g_head']).reshape(NA, H * DV)
    for n in ['kv_norm', 'kv_g_c', 'kv_g_r', 'kv_g_kn']:
        shared[n] = f(inp[n]).reshape(1, -1)
    in_maps = []
    for c in range(8):
        g, r = c // 4, c % 4
        m = dict(shared)
        segs = [rd * 4 + r for rd in range(NR)]
        m['xp'] = f(np.concatenate([inp['x_prompt'][g][sg * SEG:(sg + 1) * SEG] for sg in segs], 0))
        cols = np.concatenate([np.arange(sg * SEG, (sg + 1) * SEG) for sg in segs] + [SEQ + np.arange(DS)])
        m['cos2'] = f(cos2[:, cols]); m['sin2'] = f(sin2[:, cols])
        m['cmask'] = _core_mask(r)
        m['xs'] = f(inp['x_sample'][c])
        m['ckv'] = f(inp['cache_ckv'][c]); m['kpe'] = f(inp['cache_kpe'][c])
        m['sC'] = f(inp['state_C'][:, c]); m['sn'] = f(inp['state_n'][:, c]).reshape(NA, H * DK); m['sm'] = f(inp['state_m'][:, c])
        m['sconv'] = f(inp['state_conv'][:, c])
        in_maps.append(m)
    res = run_bass_kernel_spmd(prog.nc, in_maps, core_ids=list(range(8))).results

    def seqcat(n):
        out = []
        for g in range(2):
            parts = [None] * (4 * NR)
            for r in range(4):
                for rd in range(NR):
                    parts[rd * 4 + r] = res[g * 4 + r][n][rd * SEG:(rd + 1) * SEG]
            out.append(np.concatenate(parts, 0))
        return np.stack(out, 0)
    pc = [3, 7]
    st = lambda n, cores, ax: np.stack([res[c][n] for c in cores], axis=ax)
    allc = list(range(8))
    rs = lambda a: a.reshape(a.shape[0], a.shape[1], H, DK)
    return (seqcat('yp'), st('ys', allc, 0), seqcat('p_ckv'), seqcat('p_kpe'),
            st('pC', pc, 1), rs(st('pn', pc, 1)), st('pm', pc, 1), st('pconv', pc, 1),
            st('s_ckv', allc, 0), st('s_kpe', allc, 0), st('sCo', allc, 1), rs(st('sno', allc, 1)), st('smo', allc, 1),
            st('sconvo', allc, 1))
```

```python
import numpy as np
from contextlib import ExitStack
import concourse.bass as bass
import concourse.mybir as mybir
from concourse.bass_utils import run_bass_kernel_spmd

F32 = mybir.dt.float32
BF16 = mybir.dt.bfloat16
AF = mybir.ActivationFunctionType
ALU = mybir.AluOpType
AX = mybir.AxisListType

D = 2048; KC = 16; SEQ = 4096; DEPTH = 4; NA = 2
DS = 16; PAST = 2048
H = 4; DK = 256; DV = 512; APROJ = 6152
BH = 16; QL = 768; KVL = 512; NOPE = 128; ROPE = 64; VD = 128
DFF = 5632; FC = 44
EPS = 1e-6
SEG = 512
NSEG = SEQ // SEG
NR = 2
XW = 2 * H * DV + 16
KVROWS = BH * 128 + ROPE + 4 * SEG
GROUPS = [[0, 1, 2, 3], [4, 5, 6, 7]]
ATT_SCALE = (NOPE + ROPE) ** -0.5
NEG = -1.0e30


class Buf:
    def __init__(self, name=""):
        self.name = name
        self.w = None
        self.r = []


class K:
    def __init__(self, nc, stack):
        self.nc = nc
        self.stack = stack
        self.eng = {'pe': nc.tensor, 'dve': nc.vector, 'act': nc.scalar, 'pool': nc.gpsimd, 'sp': nc.sync}
        self.sem = {}
        self.cnt = {}
        self.waited = {e: {} for e in self.eng}
        for e in self.eng:
            self.sem[e] = stack.enter_context(nc.semaphore('s_' + e))
            self.cnt[e] = 0
        self.nins = 0

    def new_dma_sem(self, key):
        self.sem[key] = self.stack.enter_context(self.nc.semaphore(key))
        self.cnt[key] = 0
        return key

    def _wait(self, e, deps):
        need = {}
        for d in deps:
            if d is None:
                continue
            k, v = d
            if e == 'pe' and k == 'pe':
                continue
            if v > need.get(k, 0):
                need[k] = v
        for k, v in need.items():
            if self.waited[e].get(k, 0) >= v:
                continue
            self.eng[e].wait_ge(self.sem[k], v)
            self.waited[e][k] = v

    @staticmethod
    def _deps(reads, writes):
        deps = []
        for b in reads:
            deps.append(b.w)
        for b in writes:
            deps.append(b.w)
            deps.extend(b.r)
        return deps

    def op(self, e, fn, reads=(), writes=()):
        self._wait(e, self._deps(reads, writes))
        ins = fn()
        self.cnt[e] += 1
        self.nins += 1
        ins.then_inc(self.sem[e], 1)
        tag = (e, self.cnt[e])
        for b in reads:
            b.r.append(tag)
            if len(b.r) > 24:
                b.r = b.r[-24:] if False else self._compact(b.r)
        for b in writes:
            b.w = tag
            b.r = []
        return ins

    @staticmethod
    def _compact(r):
        m = {}
        for k, v in r:
            if v > m.get(k, 0):
                m[k] = v
        return list(m.items())

    def dma(self, q, out, in_, reads=(), writes=(), dsem='io', **kw):
        if dsem in ('io', 'out', 'kv', 'ld'):
            dsem = self.pool[self.pi % len(self.pool)]
            self.pi += 1
        prev = self.cnt[dsem]
        self._wait(q, self._deps(reads, writes) + ([(dsem, prev)] if prev else []))
        ins = self.eng[q].dma_start(out=out, in_=in_, **kw)
        self.cnt[dsem] += 16
        self.nins += 1
        ins.then_inc(self.sem[dsem], 16)
        tag = (dsem, self.cnt[dsem])
        for b in reads:
            b.r.append(tag)
        for b in writes:
            b.w = tag
            b.r = []
        return ins

    def collective(self, src, dst, groups, reads=(), writes=()):
        self._wait('pool', self._deps(reads, writes))
        ins = self.nc.gpsimd.collective_compute("AllGather", mybir.AluOpType.bypass, replica_groups=groups,
                                                ins=[src.opt()], outs=[dst.opt()])
        self.cnt['cc'] += 1
        self.nins += 1
        ins.then_inc(self.sem['cc'], 1)
        tag = ('cc', self.cnt['cc'])
        for b in reads:
            b.r.append(tag)
        for b in writes:
            b.w = tag
            b.r = []
        return ins

    def barrier(self):
        allv = [(k, v) for k, v in self.cnt.items() if v > 0 and k != 'cc']
        for e in self.eng:
            self._wait(e, allv)


def bc(ap, shape):
    return ap.broadcast_to(list(shape))


STOP = None


class _Stop(Exception):
    pass


class Prog:
    def dbg(self, tag):
        if STOP is not None and tag == STOP:
            raise _Stop()

    def __init__(self):
        self.nc = nc = bass.Bass("TRN2", target_bir_lowering=False)
        self.st = ExitStack()
        self.k = K(nc, self.st)
        for s in ['w0', 'w1', 'w2', 'w3', 'cc']:
            self.k.new_dma_sem(s)
        self.k.pool = [self.k.new_dma_sem('p%d' % i) for i in range(40)]
        self.k.pi = 0
        self.build()

    def din(self, name, shape):
        return self.nc.dram_tensor(name, list(shape), F32, kind="ExternalInput").ap()

    def dout(self, name, shape):
        return self.nc.dram_tensor(name, list(shape), F32, kind="ExternalOutput").ap()

    def dint(self, name, shape, dt=F32):
        return self.nc.dram_tensor(name, list(shape), dt, kind="Internal").ap()

    def sb(self, name, shape, dt=F32):
        return self.st.enter_context(self.nc.sbuf_tensor(name, list(shape), dt))

    def fv(self, a, n, rows=128):
        return self.arena[0:rows, a:a + n]

    def bv(self, a, n, rows=128):
        assert n % 2 == 0
        return self.arena[0:rows, a:a + n // 2].bitcast(BF16)

    def V(self, e='dve'):
        return self.nc.vector if e == 'dve' else self.nc.gpsimd

    def tt(self, out, a, b, op, R, W, e='dve'):
        self.k.op(e, lambda: self.V(e).tensor_tensor(out=out, in0=a, in1=b, op=op), R, W)

    def ts(self, out, a, s1, s2, op0, op1, R, W, e='dve'):
        if op1 is None:
            self.k.op(e, lambda: self.V(e).tensor_scalar(out=out, in0=a, scalar1=s1, scalar2=None, op0=op0), R, W)
        else:
            self.k.op(e, lambda: self.V(e).tensor_scalar(out=out, in0=a, scalar1=s1, scalar2=s2, op0=op0, op1=op1), R, W)

    def stt(self, out, a, s, b, op0, op1, R, W):
        self.k.op('dve', lambda: self.nc.vector.scalar_tensor_tensor(out=out, in0=a, scalar=s, in1=b, op0=op0, op1=op1), R, W)

    def cp(self, out, a, R, W, e='dve'):
        if e == 'act':
            self.k.op('act', lambda: self.nc.scalar.copy(out=out, in_=a), R, W)
        else:
            self.k.op(e, lambda: self.V(e).tensor_copy(out=out, in_=a), R, W)

    def act(self, out, a, func, R, W, bias=None, scale=None, accum=None):
        kw = {}
        if bias is not None:
            kw['bias'] = bias
        if scale is not None:
            kw['scale'] = scale
        if accum is not None:
            kw['accum_out'] = accum
        self.k.op('act', lambda: self.nc.scalar.activation(out=out, in_=a, func=func, **kw), R, W)

    def mm(self, out, lhsT, rhs, start, stop, R, W):
        self.k.op('pe', lambda: self.nc.tensor.matmul(out, lhsT=lhsT, rhs=rhs, start=start, stop=stop), R, W)

    def tr(self, out, a, n, R, W, bf=False):
        idn = self.ident_b if bf else self.ident
        self.k.op('pe', lambda: self.nc.tensor.transpose(out, a, idn[0:n, 0:n]), list(R) + [self.b_const], W)

    def ms(self, ap, c, W, e='dve'):
        self.k.op(e, lambda: self.V(e).memset(ap, c), (), W)

    def recip(self, out, a, R, W):
        self.k.op('dve', lambda: self.nc.vector.reciprocal(out=out, in_=a), R, W)

    def load_fm(self, dst, src1d, n, rows=128):
        tmp = self.fv(self.O_TMP, 128)
        if not hasattr(self, 'b_tmpfm'):
            self.b_tmpfm = Buf()
        tb = self.b_tmpfm
        self.k.dma('sp', tmp[0:n, 0:rows], src1d.rearrange("(c p) -> c p", p=rows), (), [tb])
        ps, pb = self.ps[7], self.bps[7]
        self.tr(ps[0:rows, 0:n], tmp[0:n, 0:rows], n, [tb], [pb])
        self.cp(dst, ps[0:rows, 0:n], [pb], [self.b_gn])

    def build(self):
        nc, k = self.nc, self.k
        d = {}
        d['xp'] = self.din('xp', [NR * SEG, D]); d['xs'] = self.din('xs', [DS, D])
        d['ckv'] = self.din('ckv', [PAST, KVL]); d['kpe'] = self.din('kpe', [PAST, ROPE])
        d['sC'] = self.din('sC', [NA, H, DV, DK]); d['sn'] = self.din('sn', [NA, H * DK]); d['sm'] = self.din('sm', [NA, H])
        d['sconv'] = self.din('sconv', [DEPTH, 2, 2 * DFF])
        d['norm_mix'] = self.din('norm_mix', [DEPTH, D]); d['norm_ffn'] = self.din('norm_ffn', [DEPTH, D])
        d['a_w_in'] = self.din('a_w_in', [NA, D, APROJ]); d['a_b_gate'] = self.din('a_b_gate', [NA, 8])
        d['a_g_head'] = self.din('a_g_head', [NA, H * DV]); d['a_w_out'] = self.din('a_w_out', [NA, D, D])
        d['kv_norm'] = self.din('kv_norm', [1, D]); d['kv_w_down'] = self.din('kv_w_down', [D, KVL + ROPE])
        d['kv_g_c'] = self.din('kv_g_c', [1, KVL]); d['kv_g_r'] = self.din('kv_g_r', [1, ROPE])
        d['kv_w_up'] = self.din('kv_w_up', [KVL, BH * 256]); d['kv_g_kn'] = self.din('kv_g_kn', [1, NOPE])
        d['b_w_dq'] = self.din('b_w_dq', [2, D, QL]); d['b_g_cq'] = self.din('b_g_cq', [2, QL])
        d['b_w_uq'] = self.din('b_w_uq', [2, QL, BH * 192]); d['b_g_qn'] = self.din('b_g_qn', [2, NOPE])
        d['b_g_qr'] = self.din('b_g_qr', [2, ROPE]); d['b_w_o'] = self.din('b_w_o', [2, D, D])
        d['f_w_up'] = self.din('f_w_up', [DEPTH, D, 2 * DFF]); d['f_conv_w'] = self.din('f_conv_w', [DEPTH, 3, 2 * DFF])
        d['f_conv_b'] = self.din('f_conv_b', [DEPTH, 2 * DFF]); d['f_w_down'] = self.din('f_w_down', [DEPTH, DFF, D])
        d['cos2'] = self.din('cos2', [ROPE, NR * SEG + DS]); d['sin2'] = self.din('sin2', [ROPE, NR * SEG + DS])
        d['cmask'] = self.din('cmask', [128, 40])
        o = {}
        o['yp'] = self.dout('yp', [NR * SEG, D]); o['ys'] = self.dout('ys', [DS, D])
        o['p_ckv'] = self.dout('p_ckv', [NR * SEG, KVL]); o['p_kpe'] = self.dout('p_kpe', [NR * SEG, ROPE])
        o['pC'] = self.dout('pC', [NA, H, DV, DK]); o['pn'] = self.dout('pn', [NA, H * DK]); o['pm'] = self.dout('pm', [NA, H])
        o['pconv'] = self.dout('pconv', [DEPTH, 2, 2 * DFF])
        o['s_ckv'] = self.dout('s_ckv', [DS, KVL]); o['s_kpe'] = self.dout('s_kpe', [DS, ROPE])
        o['sCo'] = self.dout('sCo', [NA, H, DV, DK]); o['sno'] = self.dout('sno', [NA, H * DK]); o['smo'] = self.dout('smo', [NA, H])
        o['sconvo'] = self.dout('sconvo', [DEPTH, 2, 2 * DFF])
        self.d, self.o = d, o
        self.kn_dram = [None, self.dint('kns', [BH, 128, PAST + DS], BF16)]
        self.kr_dram = [None, self.dint('krs', [ROPE, PAST + DS], BF16)]
        self.v_dram = [None, self.dint('vs', [PAST + DS, BH * VD], BF16)]
        self.xsC = [[[self.dint('xsC%d%d%d' % (l, t, c), [128, 2048]) for c in range(2)] for t in range(2)] for l in range(NA)]
        self.xgC = [[[self.dint('xgC%d%d%d' % (l, t, c), [512, 2048]) for c in range(2)] for t in range(2)] for l in range(NA)]
        self.xsS = [self.dint('xsS%d' % l, [256, 16]) for l in range(NA)]
        self.xgS = [self.dint('xgS%d' % l, [1024, 16]) for l in range(NA)]
        self.b_xs2 = [Buf(), Buf()]; self.b_xg = [Buf(), Buf()]
        self.ts2 = [self.dint('ts2_%d' % l, [256, 176]) for l in range(DEPTH)]
        self.tg = [self.dint('tg_%d' % l, [4 * 256, 176]) for l in range(DEPTH)]
        self.b_ts2 = [Buf() for _ in range(DEPTH)]; self.b_tg = [Buf() for _ in range(DEPTH)]
        self.KP = [self.dint('KP%d' % j, [1024, SEG], BF16) for j in range(2)]
        self.KRp = self.dint('KRp', [ROPE, SEG], BF16)
        self.VP = [self.dint('VP%d' % q, [SEG, 1024], BF16) for q in range(2)]
        self.b_kvloc = Buf()
        self.b_kvd = [Buf(), Buf()]
        self.KG = [[self.dint('KG%d%d' % (r, j), [4 * 1024, SEG], BF16) for j in range(2)] for r in range(NR)]
        self.KRG = [self.dint('KRG%d' % r, [4 * ROPE, SEG], BF16) for r in range(NR)]
        self.VG = [[self.dint('VG%d%d' % (r, q), [4 * SEG, 1024], BF16) for q in range(2)] for r in range(NR)]
        self.b_kvall = [Buf() for _ in range(NR)]
        self.b_out = Buf('out')

        self.ident = self.sb('ident', [128, 128])
        self.ident_b = self.sb('ident_b', [128, 128], BF16)
        self.ones_b = self.sb('ones_b', [128, 128], BF16)
        self.ones_f = self.sb('ones_f', [128, 128])
        self.uneg = {L: self.sb('uneg%d' % L, [L, L]) for L in (128, 16)}
        self.mask = {L: self.sb('mask%d' % L, [L, L]) for L in (128, 16)}
        self.maskT = {L: self.sb('maskT%d' % L, [L, L]) for L in (128, 16)}
        self.sel = {L: self.sb('sel%d' % L, [L, 128]) for L in (128, 16)}
        self.perm = self.sb('perm', [64, 64], BF16)
        self.epsc = self.sb('epsc', [128, 1])
        self.b_const = Buf('const')
        self.xT = self.sb('xT', [128, KC, SEG]); self.b_x = Buf('x')
        self.xb = self.sb('xb', [128, KC, SEG], BF16); self.b_xb = Buf('xb')
        self.tail = self.sb('tail', [128, DEPTH, 2, 2, 88]); self.b_tail = Buf('tail')
        self.nst = self.sb('nst', [128, NA, 2, 2, H, 2]); self.b_nst = Buf('nst')
        self.nsb = self.sb('nsb', [128, NA, 2, 2, H, 2], BF16)
        self.mst = self.sb('mst', [128, NA, 2, H]); self.b_mst = Buf('mst')
        self.gn = self.sb('gn', [128, 9, KC]); self.b_gn = Buf('gn')
        self.cw = self.sb('cw', [128, DEPTH, 3, 88]); self.cb = self.sb('cb', [128, DEPTH, 88])
        self.gmisc = self.sb('gmisc', [128, 64])
        self.cm = self.sb('cm', [128, 40])
        self.fsum = self.sb('fsum', [128, H])
        self.b_cw = self.b_gn; self.b_gm = self.b_gn
        self.arena = self.sb('arena', [128, 34200])
        self.ps = [self.st.enter_context(nc.psum_tensor('ps%d' % i, [128, 512], F32)) for i in range(8)]
        self.bps = [Buf('ps%d' % i) for i in range(8)]
        self.O_STG = 0
        self.O_WR = 8192
        self.O_SQ = 12288
        self.O_RSTD = 12800
        self.O_PH = 13312
        self.O_TMP = 13312

        try:
            self.init_consts()
            self.dbg('init')
            self.segment(grp=1, s=0, T=DS, L=DS)
            for s in range(NR if NSEG > 0 else 0):
                self.segment(grp=0, s=s, T=SEG, L=128)
        except _Stop:
            pass
        k.barrier()
        k._wait('sp', [(p, k.cnt[p]) for p in k.pool if k.cnt[p] > 0])

    def init_consts(self):
        nc, k, d = self.nc, self.k, self.d
        W = [self.b_const]
        g = nc.gpsimd

        def sel(t, pattern, cmp, fill, base, cm):
            k.op('pool', lambda: g.affine_select(out=t, in_=t, pattern=pattern, compare_op=cmp, fill=fill,
                                                base=base, channel_multiplier=cm), W, W)
        self.ms(self.ident[:], 0.0, W, 'pool')
        sel(self.ident[:], [[-1, 128]], ALU.not_equal, 1.0, 0, 1)
        self.cp(self.ident_b[:], self.ident[:], W, W, 'dve')
        self.ms(self.ones_f[:], 1.0, W, 'pool')
        self.cp(self.ones_b[:], self.ones_f[:], W, W, 'dve')
        self.ms(self.epsc[:], EPS, W, 'pool')
        for L in (128, 16):
            self.ms(self.uneg[L][:], -1.0, W, 'pool')
            sel(self.uneg[L][:], [[1, L]], ALU.is_ge, 0.0, 0, -1)
            self.ms(self.mask[L][:], 0.0, W, 'pool')
            sel(self.mask[L][:], [[-1, L]], ALU.is_ge, NEG, 0, 1)
            self.ms(self.maskT[L][:], 0.0, W, 'pool')
            sel(self.maskT[L][:], [[1, L]], ALU.is_ge, NEG, 0, -1)
            self.ms(self.sel[L][:], 1.0, W, 'pool')
            sel(self.sel[L][:], [[0, 128]], ALU.is_equal, 0.0, -(L - 1), 1)
        pf = self.fv(20000, 64, 64)
        self.ms(pf, 0.0, W, 'pool')
        sel(pf, [[-1, 64]], ALU.not_equal, 1.0, -32, 1)
        sel(pf, [[-1, 64]], ALU.not_equal, 1.0, 32, 1)
        self.cp(self.perm[:], pf, W, W, 'dve')
        k.barrier()
        for i in range(4):
            self.load_fm(self.gn[:, i, :], d['norm_mix'][i], KC)
            self.load_fm(self.gn[:, 4 + i, :], d['norm_ffn'][i], KC)
        self.load_fm(self.gn[:, 8, :], d['kv_norm'][0], KC)
        for l in range(DEPTH):
            for j in range(3):
                self.load_fm(self.cw[:, l, j, :], d['f_conv_w'][l, j], 88)
            self.load_fm(self.cb[:, l, :], d['f_conv_b'][l], 88)
        gm = self.gmisc
        self.ms(gm[:], 0.0, [self.b_gn], 'pool')
        for j in range(2):
            self.load_fm(gm[:, 6 * j:6 * j + 6], d['b_g_cq'][j], 6)
            self.load_fm(gm[:, 12 + j:13 + j], d['b_g_qn'][j], 1)
            self.load_fm(gm[0:64, 14 + j:15 + j], d['b_g_qr'][j], 1, rows=64)
            self.load_fm(gm[:, 22 + 16 * j:38 + 16 * j], d['a_g_head'][j], 16)
            self.load_fm(gm[0:8, 54 + j:55 + j], d['a_b_gate'][j], 1, rows=8)
        self.load_fm(gm[:, 16:20], d['kv_g_c'][0], 4)
        self.load_fm(gm[0:64, 20:21], d['kv_g_r'][0], 1, rows=64)
        self.load_fm(gm[:, 21:22], d['kv_g_kn'][0], 1)
        self.ms(self.tail[:], 0.0, [self.b_tail], 'pool')
        self.ms(self.mst[:], 0.0, [self.b_mst], 'pool')
        self.ms(self.fsum[:], 0.0, [self.b_mst], 'pool')
        self.ms(self.nst[:], 0.0, [self.b_nst], 'pool')
        k.barrier()
        for l in range(DEPTH):
            for r in range(2):
                self.load_fm(self.tail[:, l, 1, r, :], d['sconv'][l, r], 88)
        for l in range(NA):
            k.dma('sp', self.mst[:, l, 1, :], d['sm'][l:l + 1, :].broadcast_to([128, H]), (), [self.b_mst])
            nt = self.fv(20100, 8)
            self.load_fm(nt, d['sn'][l], 8)
            for e in range(2):
                self.cp(self.nst[:, l, 1, :, :, e], nt.rearrange("p (h c) -> p c h", h=H), [self.b_gn], [self.b_nst], 'dve')
        self.cp(self.nsb[:], self.nst[:], [self.b_nst], [self.b_nst], 'dve')
        k.dma('sp', self.cm[:], d['cmask'][:, :], (), [self.b_gn])
        zc = self.fv(0, XW)
        self.ms(zc, 0.0, W, 'pool')
        for l in range(NA):
            for c in range(2):
                k.dma('sp', self.xsC[l][1][c][:, :], zc[:, 0:2048], W, [self.b_xs2[l]])
            k.dma('sp', self.xsS[l][128:256, :], zc[:, 0:16], W, [self.b_xs2[l]])
        for l in range(DEPTH):
            k.dma('sp', self.ts2[l][128:256, :], zc[:, 0:176], W, [self.b_ts2[l]])
        k.barrier()

    def linear(self, w, blocks, kgroups, rhs_fn, T, evac, gcol=None, ps_ids=(0, 1), extra_R=(), sets=None):
        k = self.k
        wv = w.rearrange("(c p) n -> p c n", p=128)
        NS = 4
        if not hasattr(self, 'wslot'):
            self.wslot = 0
            self.b_wst = [Buf() for _ in range(NS)]
            self.b_wr = [Buf() for _ in range(NS)]
        if sets is None:
            sets = [[0, 1, 4, 5], [6, 7, 2, 3]]
        kcs = [kc for (k0, nk) in kgroups for kc in range(k0, k0 + nk)]
        groups = []
        cur = []
        for bi, (c0, m) in enumerate(blocks):
            if cur and (cur[-1][1] + cur[-1][2] == c0) and (sum(x[2] for x in cur) + m <= 512) and (len(cur) < min(len(x) for x in sets)):
                cur.append((bi, c0, m))
            else:
                if cur:
                    groups.append(cur)
                cur = [(bi, c0, m)]
        if cur:
            groups.append(cur)
        si = 0
        pending = None
        for grp_ in groups:
            banks = sets[si % len(sets)]
            si += 1
            cols = sum(x[2] for x in grp_)
            cbase = grp_[0][1]
            nk_t = max(1, min(2048 // cols, len(kcs)))
            ntile = (len(kcs) + nk_t - 1) // nk_t
            for ti in range(ntile):
                kk = kcs[ti * nk_t:(ti + 1) * nk_t]
                assert kk == list(range(kk[0], kk[0] + len(kk)))
                nk = len(kk)
                s = self.wslot
                self.wslot = (self.wslot + 1) % NS
                stg = self.fv(self.O_STG + s * 2048, nk * cols).rearrange("p (c n) -> p c n", n=cols)
                wr = self.bv(self.O_WR + s * 1024, nk * cols).rearrange("p (c n) -> p c n", n=cols)
                k.dma('sp', stg, wv[:, kk[0]:kk[0] + nk, cbase:cbase + cols], (), [self.b_wst[s]], dsem='w%d' % s)
                self.cp(wr, stg, [self.b_wst[s]], [self.b_wr[s]], ('dve', 'act', 'dve', 'act')[s])
                for gi, (bi, c0, m) in enumerate(grp_):
                    ps, pb = self.ps[banks[gi]], self.bps[banks[gi]]
                    for j in range(nk):
                        rhs, rb = rhs_fn(kk[j])
                        first = (ti == 0 and j == 0)
                        last = (ti == ntile - 1 and j == nk - 1)
                        self.mm(ps[0:m, 0:T], wr[:, j, c0 - cbase:c0 - cbase + m], rhs, first, last,
                                [self.b_wr[s], rb] + list(extra_R), [pb])
            if pending is not None:
                for (bi_, ps_, pb_) in pending:
                    evac(bi_, ps_, pb_)
            pending = [(bi, self.ps[banks[gi]][0:m, 0:T], self.bps[banks[gi]]) for gi, (bi, c0, m) in enumerate(grp_)]
        if pending is not None:
            for (bi_, ps_, pb_) in pending:
                evac(bi_, ps_, pb_)

    def sumsq_bc(self, src_fn, nch, T, out, outb, n, rows=128, ps_id=2):
        if not hasattr(self, 'b_sq'):
            self.b_sq = [Buf(), Buf()]
        ps, pb = self.ps[ps_id], self.bps[ps_id]
        for c in range(nch):
            src, sbuf = src_fn(c)
            s = c % 2
            sq = self.bv(self.O_SQ + s * 256, 512)
            self.act(sq[0:rows, 0:T], src, AF.Square, [sbuf], [self.b_sq[s]])
            self.mm(ps[:, 0:T], self.ones_b[0:rows, :], sq[0:rows, 0:T], c == 0, c == nch - 1, [self.b_sq[s], self.b_const], [pb])
        orow = out.shape[0]
        self.act(out, ps[0:orow, 0:T], AF.Sqrt, [pb, self.b_const], [outb], bias=self.epsc[0:orow, 0:1], scale=1.0 / n)
        self.recip(out, out, [outb], [outb])

    def x_rstd(self):
        T = self.T
        rstd = self.fv(self.O_RSTD, SEG)
        b = Buf()
        self.sumsq_bc(lambda c: (self.xT[:, c, 0:T], self.b_x), KC, T, rstd[:, 0:T], b, D)
        return rstd, b

    def xb_refresh(self, gidx):
        T = self.T
        rstd, b_rstd = self.x_rstd()
        tmpf = self.fv(self.O_STG, 2 * SEG).rearrange("p (s t) -> p s t", s=2)
        bt = [Buf(), Buf()]
        j = 0
        for c in range(KC):
            if c % 2 == 0:
                self.stt(self.xb[:, c, 0:T], self.xT[:, c, 0:T], self.gn[:, gidx, c:c + 1], rstd[:, 0:T], ALU.mult, ALU.mult,
                         [self.b_x, self.b_gn, b_rstd], [self.b_xb])
            else:
                sl = j % 2; j += 1
                self.act(tmpf[:, sl, 0:T], self.xT[:, c, 0:T], AF.Copy, [self.b_x, self.b_gn], [bt[sl]], scale=self.gn[:, gidx, c:c + 1])
                self.tt(self.xb[:, c, 0:T], tmpf[:, sl, 0:T], rstd[:, 0:T], ALU.mult, [bt[sl], b_rstd], [self.b_xb], 'pool')

    def resid_evac(self):
        T = self.T
        def ev(bi, ps, pb):
            self.tt(self.xT[:, bi, 0:T], ps, self.xT[:, bi, 0:T], ALU.add, [pb, self.b_x], [self.b_x])
        return ev

    def segment(self, grp, s, T, L):
        nc, k, d, o = self.nc, self.k, self.d, self.o
        self.grp, self.sidx, self.T, self.L = grp, s, T, L
        pos0 = s * SEG if grp == 0 else NR * SEG
        xsrc = d['xp'][s * SEG:(s + 1) * SEG, :] if grp == 0 else d['xs']
        k.barrier()
        tmp = self.fv(self.O_PH, D)
        tb = Buf()
        for t0 in range(0, T, 128):
            n = min(128, T - t0)
            k.dma('sp', tmp[0:n, :], xsrc[t0:t0 + n, :], (), [tb])
            for c4 in range(0, KC, 4):
                ps, pb = self.ps[(c4 // 4) % 2], self.bps[(c4 // 4) % 2]
                for j in range(4):
                    self.tr(ps[:, j * 128:j * 128 + n], tmp[0:n, (c4 + j) * 128:(c4 + j + 1) * 128], n, [tb], [pb])
                self.cp(self.xT[:, c4:c4 + 4, t0:t0 + n], ps[:, :].rearrange("p (j n) -> p j n", j=4)[:, :, 0:n], [pb], [self.b_x])
        self.dbg('xload')
        for layer in range(DEPTH):
            k.barrier()
            self.xb_refresh(layer)
            k.barrier()
            if layer < NA:
                self.mlstm(layer)
            else:
                self.mla(layer - NA)
            k.barrier()
            self.dbg('mixer%d' % layer)
            self.xb_refresh(4 + layer)
            k.barrier()
            self.ffn(layer)
            self.dbg('ffn%d' % layer)
            if layer == NA - 1:
                k.barrier()
                self.xb_refresh(8)
                k.barrier()
                self.shared_kv(pos0)
        k.barrier()
        ydst = o['yp'][s * SEG:(s + 1) * SEG, :] if grp == 0 else o['ys']
        for t0 in range(0, T, 128):
            n = min(128, T - t0)
            for c4 in range(0, KC, 4):
                ps, pb = self.ps[(c4 // 4) % 2], self.bps[(c4 // 4) % 2]
                for j in range(4):
                    self.tr(ps[0:n, j * 128:(j + 1) * 128], self.xT[:, c4 + j, t0:t0 + n], 128, [self.b_x], [pb])
                self.cp(tmp[0:n, c4 * 128:(c4 + 4) * 128], ps[0:n, :], [pb], [tb])
            k.dma('sp', ydst[t0:t0 + n, :], tmp[0:n, :], [tb], [self.b_out], dsem='out')

    def ffn(self, layer):
        nc, k, d, o = self.nc, self.k, self.d, self.o
        T, grp = self.T, self.grp
        P0 = self.O_PH
        ug = self.fv(P0, SEG + 2); b_ug = Buf()
        uv = self.fv(P0 + 514, SEG + 2); b_uv = Buf()
        cg = self.fv(P0 + 1028, SEG); b_cg = Buf()
        cv = self.fv(P0 + 1540, SEG); b_cv = Buf()
        sg4 = self.fv(P0 + 14300, 4 * SEG).rearrange("p (s t) -> p s t", s=4); b_sg4 = [Buf() for _ in range(4)]
        t1 = self.fv(P0 + 2564, 128); t2 = self.fv(P0 + 2692, 128)
        actT = self.bv(P0 + 3000, FC * SEG).rearrange("p (c t) -> p c t", c=FC); b_act = Buf()
        gcol = self.gn[:, 4 + layer, :]
        tl = self.tail[:, layer, grp]
        blocks = []; bmap = []
        for j0 in range(0, FC, 4):
            for j in range(j0, j0 + 4):
                blocks.append((j * 128, 128)); bmap.append((j, 0))
            for j in range(j0, j0 + 4):
                blocks.append((DFF + j * 128, 128)); bmap.append((j, 1))
        sgs = self.fv(self.O_SQ, 4 * SEG).rearrange("p (s t) -> p s t", s=4) if False else None

        def conv(u, ub, fc, out, outb):
            self.ts(out[:, 0:T], u[:, 0:T], self.cw[:, layer, 0, fc:fc + 1], self.cb[:, layer, fc:fc + 1], ALU.mult, ALU.add,
                    [ub, self.b_cw], [outb])
            for jj in (1, 2):
                self.stt(out[:, 0:T], u[:, jj:jj + T], self.cw[:, layer, jj, fc:fc + 1], out[:, 0:T], ALU.mult, ALU.add,
                         [ub, self.b_cw, outb], [outb])

        def evac(bi, ps, pb):
            j, isv = bmap[bi]
            fc = j + (FC if isv else 0)
            u, ub = (uv, b_uv) if isv else (ug, b_ug)
            cc, cb_ = (cv, b_cv) if isv else (cg, b_cg)
            sg, b_sg = sg4[:, j % 4, :], b_sg4[j % 4]
            if grp == 1:
                self.cp(u[:, 0:2], tl[:, :, fc], [self.b_tail], [ub], 'pool')
                self.cp(u[:, 2:2 + T], ps, [pb], [ub], 'act')
                self.cp(tl[:, :, fc], u[:, T:T + 2], [ub], [self.b_tail], 'pool')
                conv(u, ub, fc, cc, cb_)
                lo = 0
            else:
                self.cp(tl[:, :, fc], ps[:, T - 2:T], [pb], [self.b_tail], 'dve')
                self.cp(ufirst[:, fc, :], ps[:, 0:2], [pb], [b_uf], 'dve')
                self.ts(cc[:, 2:T], ps[:, 0:T - 2], self.cw[:, layer, 0, fc:fc + 1], self.cb[:, layer, fc:fc + 1], ALU.mult, ALU.add,
                        [pb, self.b_cw], [cb_])
                for jj in (1, 2):
                    self.stt(cc[:, 2:T], ps[:, jj:T - 2 + jj], self.cw[:, layer, jj, fc:fc + 1], cc[:, 2:T], ALU.mult, ALU.add,
                             [pb, self.b_cw, cb_], [cb_])
                lo = 2
            if not isv:
                self.act(sg[:, lo:T], cg[:, lo:T], AF.Silu, [b_cg], [b_sg])
            else:
                self.tt(actT[:, j, lo:T], sg[:, lo:T], cv[:, lo:T], ALU.mult, [b_sg, b_cv], [b_act], 'pool')

        ufirst = self.fv(P0 + 2820, 176).rearrange("p (c r) -> p c r", r=2); b_uf = Buf()
        self.linear(d['f_w_up'][layer], blocks, [(0, KC)], lambda c: (self.xb[:, c, 0:T], self.b_xb), T, evac)
        if grp == 0:
            k.barrier()
            tcur = self.tail[:, layer, 0].rearrange("p r c -> p (r c)")
            k.dma('sp', self.ts2[layer][0:128, :], tcur, [self.b_tail, self.b_tg[layer]], [self.b_ts2[layer]])
            k.collective(self.ts2[layer][:, :], self.tg[layer][:, :], GROUPS, [self.b_ts2[layer]], [self.b_tg[layer]])
            tgs = self.fv(P0, 1408).rearrange("p (i t c) -> p i t c", i=4, t=2); bfx = Buf()
            k.dma('sp', tgs, self.tg[layer].rearrange("(i t p) c -> p i t c", t=2, p=128), [self.b_tg[layer]], [bfx])
            halo = self.fv(P0 + 1408, 176)
            Rf = [bfx, self.b_gn, b_uf]
            self.ts(halo, tgs[:, 3, 1, :], self.cm[:, 28:29], None, ALU.mult, None, Rf, [bfx])
            for i in range(4):
                self.stt(halo, tgs[:, i, 0, :], self.cm[:, 24 + i:25 + i], halo, ALU.mult, ALU.add, Rf, [bfx])
            k.dma('sp', self.ts2[layer][128:256, :], tcur, [self.b_tail, self.b_tg[layer]], [self.b_ts2[layer]])
            h0 = halo[:, 0:88]; h1 = halo[:, 88:176]
            u0 = ufirst[:, :, 0]; u1 = ufirst[:, :, 1]
            w0, w1, w2 = (self.cw[:, layer, jj, :] for jj in range(3)); bb = self.cb[:, layer, :]
            c0 = self.fv(P0 + 1584, 88); c1 = self.fv(P0 + 1672, 88); ta = self.fv(P0 + 1760, 88); sgl = self.fv(P0 + 1848, 88)
            for (cc, x0, x1, x2) in ((c0, h0, h1, u0), (c1, h1, u0, u1)):
                self.tt(cc, w0, x0, ALU.mult, Rf, [bfx]); self.tt(cc, cc, bb, ALU.add, Rf, [bfx])
                self.tt(ta, w1, x1, ALU.mult, Rf, [bfx]); self.tt(cc, cc, ta, ALU.add, Rf, [bfx])
                self.tt(ta, w2, x2, ALU.mult, Rf, [bfx]); self.tt(cc, cc, ta, ALU.add, Rf, [bfx])
            for t, cc in ((0, c0), (1, c1)):
                self.act(sgl[:, 0:44], cc[:, 0:44], AF.Silu, Rf, [bfx])
                self.tt(actT[:, :, t], sgl[:, 0:44], cc[:, 44:88], ALU.mult, Rf, [b_act])
            k.barrier()
        self.linear(d['f_w_down'][layer], [(c * 128, 128) for c in range(KC)], [(0, FC)],
                    lambda c: (actT[:, c, 0:T], b_act), T, self.resid_evac(), extra_R=[self.b_x])
        if grp == 1 or self.sidx == NR - 1:
            dst = o['sconvo'] if grp == 1 else o['pconv']
            tb = Buf()
            tf = self.tail[:, layer, grp].rearrange("p r c -> p (r c)")
            ps, pb = self.ps[2], self.bps[2]
            self.tr(ps[0:128, 0:128], tf[:, 0:128], 128, [self.b_tail], [pb])
            self.tr(ps[0:48, 128:256], tf[:, 128:176], 128, [self.b_tail], [pb])
            self.cp(t1, ps[0:128, 0:128], [pb], [tb]); self.cp(t2[0:48, :], ps[0:48, 128:256], [pb], [tb])
            dv = [dst[layer, r].rearrange("(c p) -> c p", p=128) for r in range(2)]
            k.dma('sp', dv[0][0:88, :], t1[0:88, :], [tb], [self.b_out], dsem='out')
            k.dma('sp', dv[1][0:40, :], t1[88:128, :], [tb], [self.b_out], dsem='out')
            k.dma('sp', dv[1][40:88, :], t2[0:48, :], [tb], [self.b_out], dsem='out')

    def mlstm(self, layer):
        nc, k, d, o = self.nc, self.k, self.d, self.o
        T, L, grp = self.T, self.L, self.grp
        P0 = self.O_PH
        qT = self.bv(P0, 8 * SEG).rearrange("p (c t) -> p c t", c=8); b_q = Buf()
        kT = self.bv(P0 + 2048, 8 * SEG).rearrange("p (c t) -> p c t", c=8); b_k = Buf()
        vT = self.fv(P0 + 4096, 16 * SEG).rearrange("p (c t) -> p c t", c=16); b_v = Buf()
        gT = self.fv(P0 + 12288, SEG, 8); b_g = Buf()
        sgt2 = self.fv(self.O_SQ, 2 * SEG).rearrange("p (s t) -> p s t", s=2); b_sg2 = [Buf(), Buf()]
        O_SSTG = 6144
        O_GH = 10240
        S0 = P0 + 13312
        gcol = self.gn[:, layer, :]
        w_in = d['a_w_in'][layer]
        sq_, sv_ = H * DK, H * DV
        blocks = [(c * 128, 128) for c in range(8)] + [(sq_ + c * 128, 128) for c in range(8)] + \
                 [(2 * sq_ + c * 128, 128) for c in range(16)] + [(2 * sq_ + 2 * sv_, 8)]

        def evac(bi, ps, pb):
            e = 'act' if bi % 2 else 'dve'
            if bi < 8:
                self.cp(qT[:, bi, 0:T], ps, [pb], [b_q], e)
            elif bi < 16:
                self.cp(kT[:, bi - 8, 0:T], ps, [pb], [b_k], e)
            elif bi < 32:
                self.cp(vT[:, bi - 16, 0:T], ps, [pb], [b_v], e)
            else:
                self.ts(gT[:, 0:T], ps, self.gmisc[0:8, 54 + layer:55 + layer], None, ALU.add, None, [pb, self.b_gm], [b_g])

        self.linear(w_in, blocks, [(0, KC)], lambda c: (self.xb[:, c, 0:T], self.b_xb), T, evac)
        k.barrier()
        self.dbg('mproj')
        CT = self.fv(0, 4096).rearrange("p (c h v) -> p c h v", c=2, h=H); b_ct = Buf()
        CTb = self.bv(4096, 4096).rearrange("p (c h v) -> p c h v", c=2, h=H)
        nT = self.nst[:, layer, grp]
        nTb = self.nsb[:, layer, grp]
        mbc = self.mst[:, layer, grp, :]
        ghb = self.fv(O_GH, 2048).rearrange("p (h v) -> p h v", h=H); b_gh = Buf()
        k.dma('sp', ghb[0:L], d['a_g_head'][layer:layer + 1, :].broadcast_to([L, H * DV]).rearrange("p (h v) -> p h v", h=H), (), [b_gh])
        env = dict(layer=layer, qT=qT, kT=kT, vT=vT, gT=gT, CT=CT, CTb=CTb, nT=nT, nTb=nTb, mbc=mbc, ghb=ghb, S0=S0,
                   b_q=b_q, b_k=b_k, b_v=b_v, b_g=b_g, b_ct=b_ct, b_gh=b_gh)
        if grp == 1:
            stg = self.fv(O_SSTG, 4096).rearrange("p (h c k) -> p h c k", h=H, c=4); sb_ = Buf()
            k.dma('sp', stg, d['sC'][layer].rearrange("h (c p) k -> p h c k", p=128), (), [sb_])
            for h in range(H):
                for kc in range(2):
                    ps, pb = self.ps[(h * 2 + kc) % 2], self.bps[(h * 2 + kc) % 2]
                    for vc in range(4):
                        self.tr(ps[:, vc * 128:(vc + 1) * 128], stg[:, h, vc, kc * 128:(kc + 1) * 128], 128, [sb_], [pb])
                    self.cp(CT[:, kc, h, :], ps[:, :], [pb], [b_ct])
        else:
            self.ms(self.fv(0, 4096), 0.0, [b_ct], 'pool')
            self.ms(nT, 0.0, [self.b_nst], 'dve')
            self.ms(mbc, NEG, [self.b_mst], 'dve')
            self.ms(self.fsum[:], 0.0, [self.b_mst], 'dve')
            k.barrier()
            self.cp(self.bv(4096, 4096), self.fv(0, 4096), [b_ct], [b_ct])
            self.cp(nTb, nT, [self.b_nst], [self.b_nst])
            self.scan_chunks(env, True)
            k.barrier()
            self.exchange_state(layer, env)
            k.barrier()
        self.cp(self.bv(4096, 4096), self.fv(0, 4096), [b_ct], [b_ct])
        self.cp(nTb, nT, [self.b_nst], [self.b_nst])
        self.scan_chunks(env, False)
        k.barrier()
        self.dbg('scan')
        last = (grp == 1) or (self.sidx == NR - 1)
        if grp == 0:
            stt_ = self.fv(S0 + 200, 16); sbb = Buf()
            self.cp(stt_[:, 0:8].rearrange("p (c h) -> p c h", c=2), nT[:, :, :, 0], [self.b_nst], [sbb])
            self.cp(stt_[:, 8:12], mbc, [self.b_mst], [sbb])
            self.ms(stt_[:, 12:16], 0.0, [sbb], 'dve')
            for c in range(2):
                k.dma('sp', self.xsC[layer][1][c][:, :], self.fv(c * 2048, 2048), [b_ct, self.b_xg[layer]], [self.b_xs2[layer]])
            k.dma('sp', self.xsS[layer][128:256, :], stt_, [sbb, self.b_xg[layer]], [self.b_xs2[layer]])
        if last:
            Cd = (o['sCo'] if grp == 1 else o['pC'])[layer]
            nd = (o['sno'] if grp == 1 else o['pn'])[layer]
            md = (o['smo'] if grp == 1 else o['pm'])[layer]
            stg = self.fv(O_SSTG, 4096).rearrange("p (h c k) -> p h c k", h=H, c=4); sb_ = Buf()
            for h in range(H):
                for vc in range(4):
                    ps, pb = self.ps[vc % 2], self.bps[vc % 2]
                    for kc in range(2):
                        self.tr(ps[:, kc * 128:(kc + 1) * 128], CT[:, kc, h, vc * 128:(vc + 1) * 128], 128, [b_ct], [pb])
                    self.cp(stg[:, h, vc, :], ps[:, 0:256], [pb], [sb_])
            k.dma('sp', Cd.rearrange("h (c p) k -> p h c k", p=128), stg, [sb_], [self.b_out], dsem='out')
            nf = self.fv(S0, 8); nb = Buf()
            self.cp(nf.rearrange("p (h c) -> p c h", h=H), nT[:, :, :, 0], [self.b_nst], [nb])
            ps, pb = self.ps[2], self.bps[2]
            self.tr(ps[0:8, 0:128], nf, 128, [nb], [pb])
            nf2 = self.fv(S0 + 16, 128, 8)
            self.cp(nf2, ps[0:8, 0:128], [pb], [nb])
            k.dma('sp', nd.rearrange("(c p) -> c p", p=128), nf2, [nb], [self.b_out], dsem='out')
            k.dma('sp', md.rearrange("(o h) -> o h", o=1), mbc[0:1, :], [self.b_mst], [self.b_out], dsem='out')
        k.barrier()
        def evac_o(bi, ps, pb):
            sg2 = sgt2[:, bi % 2, 0:T]
            self.act(sg2, ps, AF.Sigmoid, [pb], [b_sg2[bi % 2]])
            self.tt(vT[:, bi, 0:T], vT[:, bi, 0:T], sg2, ALU.mult, [b_sg2[bi % 2], b_v], [b_v], 'pool' if bi % 2 else 'dve')
        self.linear(w_in, [(2 * sq_ + sv_ + c * 128, 128) for c in range(16)], [(0, KC)], lambda c: (self.xb[:, c, 0:T], self.b_xb), T, evac_o)
        hr = self.bv(S0, 16 * SEG).rearrange("p (c t) -> p c t", c=16); b_hr = Buf()
        k.barrier()
        for c in range(16):
            self.cp(hr[:, c, 0:T], vT[:, c, 0:T], [b_v], [b_hr], 'dve' if c % 2 else 'act')
        self.linear(d['a_w_out'][layer], [(c * 128, 128) for c in range(KC)], [(0, KC)], lambda c: (hr[:, c, 0:T], b_hr), T,
                    self.resid_evac(), extra_R=[self.b_x])

    def scan_chunks(self, env, state_only):
        nc, k = self.nc, self.k
        T, L = self.T, self.L
        layer = env['layer']
        qT, kT, vT, gT, CT, CTb, nT, nTb, mbc, ghb, S0 = (env[x] for x in ('qT', 'kT', 'vT', 'gT', 'CT', 'CTb', 'nT', 'nTb', 'mbc', 'ghb', 'S0'))
        b_q, b_k, b_v, b_g, b_ct, b_gh = (env[x] for x in ('b_q', 'b_k', 'b_v', 'b_g', 'b_ct', 'b_gh'))
        k_c = self.bv(S0, 1024); v_c = self.bv(S0 + 512, 2048); wv = self.bv(S0 + 1536, 2048)
        junk = self.fv(S0 + 1536, 512)
        hh = self.fv(S0 + 2560, 2048)
        sm_ = self.fv(S0 + 4608, 64)
        dg = self.fv(S0 + 4672, 512); dl = self.fv(S0 + 5184, 512); dT = self.fv(S0 + 5696, 512)
        sdT = self.bv(S0 + 6208, 512)
        qs = self.bv(S0 + 6464, 1024).rearrange("p (c t) -> p c t", c=8)
        bcs = self.fv(S0 + 6976, 32)
        w2r = self.bv(S0 + 7008, 2)
        bS = Buf('scan')
        P = self.ps; B = self.bps
        for c in range(T // L):
            c0 = c * L
            R = [bS, b_q, b_k, b_v, b_g, b_ct, b_gh, self.b_nst, self.b_mst, self.b_const]
            Wb = [bS]
            for g4 in range(2):
                pbf = P[0][0:L, :].bitcast(BF16)
                for j in range(4):
                    self.tr(pbf[:, j * 128:(j + 1) * 128], kT[:, g4 * 4 + j, c0:c0 + L], 128, R, [B[0]], bf=True)
                self.cp(k_c[0:L, g4 * 512:(g4 + 1) * 512], pbf[:, 0:512], [B[0]], Wb)
            for g4 in range(4):
                pp = 1 + g4 % 2
                for j in range(4):
                    self.tr(P[pp][0:L, j * 128:(j + 1) * 128], vT[:, g4 * 4 + j, c0:c0 + L], 128, R, [B[pp]])
                self.cp(v_c[0:L, g4 * 512:(g4 + 1) * 512], P[pp][0:L, :], [B[pp]], Wb, 'act')
            self.tr(P[3][0:L, 0:8], gT[:, c0:c0 + L], 8, R, [B[3]])
            gi = sm_[0:L, 0:4]; sp_ = sm_[0:L, 4:8]; b_ = sm_[0:L, 8:12]; a_ = sm_[0:L, 12:16]
            il = sm_[0:L, 16:20]; mt = sm_[0:L, 20:24]; mb = sm_[0:L, 20:28]; u_ = sm_[0:L, 28:32]
            iw = sm_[0:L, 32:36]; en = sm_[0:L, 36:40]; w_ = sm_[0:L, 40:44]; den = sm_[0:L, 44:48]
            ss = sm_[0:L, 48:52]; rr = sm_[0:L, 52:56]; mloc = sm_[0:L, 56:60]; tmp4 = sm_[0:L, 60:64]
            self.cp(gi, P[3][0:L, 0:4], [B[3]], Wb)
            self.act(sp_, P[3][0:L, 4:8], AF.Exp, [B[3]], Wb, scale=-1.0)
            self.act(sp_, sp_, AF.Ln, [bS], Wb, bias=1.0)
            self.mm(P[3][0:L, 8:12], self.uneg[L][:, :], sp_, True, True, R, [B[3]])
            self.cp(b_, P[3][0:L, 8:12], [B[3]], Wb)
            self.cp(sm_[0:L, 24:28], b_, R, Wb)
            self.tt(a_, gi, b_, ALU.subtract, R, Wb)
            idl = self.ident[0:L, 0:L]
            dg3 = dg[0:L, 0:4 * L].rearrange("p (h s) -> p h s", h=H)
            dl3 = dl[0:L, 0:4 * L].rearrange("p (h s) -> p h s", h=H)
            dT3 = dT[0:L, 0:4 * L].rearrange("p (h s) -> p h s", h=H)
            sd3 = sdT[0:L, 0:4 * L].rearrange("p (h s) -> p h s", h=H)

            def rowbc(col4, ps_ap, psb, lhs):
                self.tt(dg3, bc(idl.unsqueeze(1), [L, H, L]), bc(col4.unsqueeze(2), [L, H, L]), ALU.mult, R, Wb)
                self.mm(ps_ap, lhs, dg[0:L, 0:4 * L], True, True, R, [psb])
            rowbc(a_, P[4][0:L, 0:4 * L], B[4], self.ones_f[0:L, 0:L])
            p4 = P[4][0:L, 0:4 * L].rearrange("p (h s) -> p h s", h=H)
            self.tt(dl3, p4, bc(b_.unsqueeze(2), [L, H, L]), ALU.add, [B[4]] + R, Wb)
            self.tt(dl3, dl3, bc(self.mask[L][:, :].unsqueeze(1), [L, H, L]), ALU.add, R, Wb)
            self.k.op('dve', lambda: nc.vector.tensor_reduce(out=mloc, in_=dl3, axis=AX.X, op=ALU.max), R, Wb)
            self.tt(il, b_, mbc[0:L, :], ALU.add, R, Wb)
            self.tt(mt, il, mloc, ALU.max, R, Wb)
            if not state_only:
                self.tt(u_, b_, mt, ALU.subtract, R, Wb)
                self.tt(tmp4, il, mt, ALU.subtract, R, Wb)
                self.act(iw, tmp4, AF.Exp, R, Wb)
                self.act(en, mt, AF.Exp, R, Wb, scale=-1.0)
                rowbc(u_, P[4][0:L, 0:4 * L], B[4], self.ones_f[0:L, 0:L])
                self.tt(dT3, p4, bc(a_.unsqueeze(2), [L, H, L]), ALU.add, [B[4]] + R, Wb)
                self.tt(dT3, dT3, bc(self.maskT[L][:, :].unsqueeze(1), [L, H, L]), ALU.add, R, Wb)
                self.act(dT[0:L, 0:4 * L], dT[0:L, 0:4 * L], AF.Exp, R, Wb)
                for h in range(H):
                    for kc in range(2):
                        self.mm(P[5][0:L, h * L:(h + 1) * L], kT[:, h * 2 + kc, c0:c0 + L], qT[:, h * 2 + kc, c0:c0 + L], kc == 0, kc == 1, R, [B[5]])
                self.stt(sdT[0:L, 0:4 * L], P[5][0:L, 0:4 * L], DK ** -0.5, dT[0:L, 0:4 * L], ALU.mult, ALU.mult, [B[5]] + R, Wb)
                rowbc(iw, P[4][:, 0:4 * L], B[4], self.ones_f[0:L, :])
                pw = P[4][:, 0:4 * L].rearrange("p (h t) -> p h t", h=H)
                for kc in range(2):
                    qv = qT[:, :, c0:c0 + L].rearrange("p (h c) t -> p h c t", c=2)[:, :, kc, :]
                    qsv = qs[:, :, 0:L].rearrange("p (h c) t -> p h c t", c=2)[:, :, kc, :]
                    self.tt(qsv, qv, pw, ALU.mult, [B[4]] + R, Wb)
                for h in range(H):
                    pn_, bn_ = P[6 + h % 2], B[6 + h % 2]
                    self.mm(pn_[0:L, :], sd3[:, h, :], v_c[0:L, h * DV:(h + 1) * DV], True, False, R, [bn_])
                    for kc in range(2):
                        self.mm(pn_[0:L, :], qs[:, h * 2 + kc, 0:L], CTb[:, kc, h, :], False, kc == 1, R, [bn_])
                    self.mm(P[3][0:L, 16 + 2 * h:18 + 2 * h], sd3[:, h, :], self.ones_b[0:L, 0:2], True, False, R, [B[3]])
                    for kc in range(2):
                        self.mm(P[3][0:L, 16 + 2 * h:18 + 2 * h], qs[:, h * 2 + kc, 0:L], nTb[:, kc, h, :], False, kc == 1, R, [B[3]])
                    qn = P[3][0:L, 16 + 2 * h:17 + 2 * h]
                    self.act(den[:, h:h + 1], qn, AF.Abs, [B[3]] + R, Wb)
                    self.tt(den[:, h:h + 1], den[:, h:h + 1], en[:, h:h + 1], ALU.max, R, Wb)
                    self.recip(den[:, h:h + 1], den[:, h:h + 1], R, Wb)
                    self.act(junk[0:L, :], pn_[0:L, :], AF.Square, [bn_] + R, Wb, scale=den[:, h:h + 1], accum=ss[:, h:h + 1])
                    self.act(rr[:, h:h + 1], ss[:, h:h + 1], AF.Sqrt, R, Wb, bias=self.epsc[0:L, 0:1], scale=1.0 / DV)
                    self.recip(rr[:, h:h + 1], rr[:, h:h + 1], R, Wb)
                    self.tt(rr[:, h:h + 1], rr[:, h:h + 1], den[:, h:h + 1], ALU.mult, R, Wb)
                    self.stt(hh[0:L, h * DV:(h + 1) * DV], pn_[0:L, :], rr[:, h:h + 1], ghb[0:L, h, :], ALU.mult, ALU.mult, [bn_] + R, Wb)
            self.mm(P[3][:, 32:40], self.sel[L][:, :], mb, True, True, R, [B[3]])
            self.cp(bcs[:, 0:8], P[3][:, 32:40], [B[3]], Wb)
            mnew = bcs[:, 0:4]; blast = bcs[:, 4:8]; dec = bcs[:, 8:12]; t12 = bcs[:, 12:16]
            self.tt(self.fsum[:], self.fsum[:], blast, ALU.add, R, [self.b_mst, bS])
            self.tt(t12, blast, mnew, ALU.subtract, R, Wb)
            self.tt(dec, t12, mbc, ALU.add, R, Wb)
            self.act(dec, dec, AF.Exp, R, Wb)
            self.tt(w_, a_, t12[0:L, :], ALU.add, R, Wb)
            self.act(w_, w_, AF.Exp, R, Wb)
            self.ts(w_, w_, DK ** -0.5, None, ALU.mult, None, R, Wb)
            self.tt(wv[0:L, :].rearrange("p (h v) -> p h v", h=H), v_c[0:L, :].rearrange("p (h v) -> p h v", h=H),
                    bc(w_.unsqueeze(2), [L, H, DV]), ALU.mult, R, Wb)
            for h in range(H):
                self.cp(w2r[0:L, :], bc(w_[:, h:h + 1], [L, 2]), R, Wb)
                for kc in range(2):
                    pc, bcb = P[6 + kc], B[6 + kc]
                    self.mm(pc[:, :], k_c[0:L, h * DK + kc * 128:h * DK + (kc + 1) * 128], wv[0:L, h * DV:(h + 1) * DV], True, True, R, [bcb])
                    self.stt(CT[:, kc, h, :], CT[:, kc, h, :], dec[:, h:h + 1], pc[:, :], ALU.mult, ALU.add, [bcb, b_ct] + R, [b_ct])
                    if not state_only:
                        self.cp(CTb[:, kc, h, :], CT[:, kc, h, :], [b_ct], [b_ct], 'act')
                    self.mm(P[3][:, 48:50], k_c[0:L, h * DK + kc * 128:h * DK + (kc + 1) * 128], w2r[0:L, :], True, True, R, [B[3]])
                    self.stt(nT[:, kc, h, :], nT[:, kc, h, :], dec[:, h:h + 1], P[3][:, 48:50], ALU.mult, ALU.add,
                             [B[3], self.b_nst] + R, [self.b_nst])
                    if not state_only:
                        self.cp(nTb[:, kc, h, :], nT[:, kc, h, :], [self.b_nst], [self.b_nst])
            self.cp(mbc, mnew, R, [self.b_mst])
            if not state_only:
                for g4 in range(4):
                    pp = 4 + g4 % 2
                    for j in range(4):
                        self.tr(P[pp][:, j * L:(j + 1) * L], hh[0:L, (g4 * 4 + j) * 128:(g4 * 4 + j + 1) * 128], L, R, [B[pp]])
                    self.cp(vT[:, g4 * 4:g4 * 4 + 4, c0:c0 + L], P[pp][:, 0:4 * L].rearrange("p (j t) -> p j t", j=4), [B[pp]], [b_v, bS])

    def exchange_state(self, layer, env):
        nc, k = self.nc, self.k
        CT, nT, mbc, S0, b_ct = env['CT'], env['nT'], env['mbc'], env['S0'], env['b_ct']
        grp = self.grp
        sc = self.fv(S0 + 300, 600)
        bsc = Buf()
        stt_ = sc[:, 0:16]
        self.cp(stt_[:, 0:8].rearrange("p (c h) -> p c h", c=2), nT[:, :, :, 0], [self.b_nst], [bsc])
        self.cp(stt_[:, 8:12], mbc, [self.b_mst], [bsc])
        self.cp(stt_[:, 12:16], self.fsum[:], [self.b_mst], [bsc])
        for c in range(2):
            k.dma('sp', self.xsC[layer][0][c][:, :], self.fv(c * 2048, 2048), [b_ct, self.b_xg[layer]], [self.b_xs2[layer]])
        k.dma('sp', self.xsS[layer][0:128, :], stt_, [bsc, self.b_xg[layer]], [self.b_xs2[layer]])
        for t in range(2):
            for c in range(2):
                k.collective(self.xsC[layer][t][c][:, :], self.xgC[layer][t][c][:, :], GROUPS, [self.b_xs2[layer]], [self.b_xg[layer]])
        k.collective(self.xsS[layer][:, :], self.xgS[layer][:, :], GROUPS, [self.b_xs2[layer]], [self.b_xg[layer]])
        sg = sc[:, 16:16 + 128].rearrange("p (i t c) -> p i t c", i=4, t=2)
        k.dma('sp', sg, self.xgS[layer].rearrange("(i t p) c -> p i t c", t=2, p=128), [self.b_xg[layer]], [bsc])
        cm = self.cm
        F3 = sg[:, :, 0, 12:16]
        m3 = sg[:, :, 0, 8:12]
        mcar = sg[:, 3, 1, 8:12]
        Rr = [bsc, self.b_gn]
        T1 = sc[:, 144:208].rearrange("p (i h l) -> p i h l", i=4, h=4)
        Mv = cm[:, 8:24].rearrange("p (i l) -> p i l", i=4)
        self.tt(T1, bc(Mv.unsqueeze(2), [128, 4, 4, 4]), bc(F3.rearrange("p l h -> p h l").unsqueeze(1), [128, 4, 4, 4]), ALU.mult, Rr, [bsc])
        G = sc[:, 208:224].rearrange("p (i h) -> p i h", i=4)
        self.k.op('dve', lambda: nc.vector.tensor_reduce(out=G, in_=T1, axis=AX.X, op=ALU.add), Rr, [bsc])
        E = sc[:, 224:240].rearrange("p (i h) -> p i h", i=4)
        self.tt(E, m3, G, ALU.add, Rr, [bsc])
        self.tt(E, E, bc(cm[:, 0:4].unsqueeze(2), [128, 4, 4]), ALU.add, Rr, [bsc])
        T2 = sc[:, 240:256].rearrange("p (h l) -> p h l", h=4)
        self.tt(T2, bc(cm[:, 4:8].unsqueeze(1), [128, 4, 4]), F3.rearrange("p l h -> p h l"), ALU.mult, Rr, [bsc])
        Ec = sc[:, 256:260]
        self.k.op('dve', lambda: nc.vector.tensor_reduce(out=Ec, in_=T2, axis=AX.X, op=ALU.add), Rr, [bsc])
        self.tt(Ec, Ec, mcar, ALU.add, Rr, [bsc])
        mx = sc[:, 260:264]
        self.k.op('dve', lambda: nc.vector.tensor_reduce(out=mx, in_=E.rearrange("p i h -> p h i"), axis=AX.X, op=ALU.max), Rr, [bsc])
        min_ = sc[:, 264:268]
        self.tt(min_, mx, Ec, ALU.max, Rr, [bsc])
        Wt = sc[:, 268:284].rearrange("p (i h) -> p i h", i=4)
        self.tt(Wt, E, bc(min_.unsqueeze(1), [128, 4, 4]), ALU.subtract, Rr, [bsc])
        self.act(sc[:, 268:284], sc[:, 268:284], AF.Exp, Rr, [bsc])
        Wc = sc[:, 284:288]
        self.tt(Wc, Ec, min_, ALU.subtract, Rr, [bsc])
        self.act(Wc, Wc, AF.Exp, Rr, [bsc])
        nacc = sc[:, 288:296].rearrange("p (c h) -> p c h", c=2)
        ntmp = sc[:, 296:304].rearrange("p (c h) -> p c h", c=2)
        self.tt(nacc, sg[:, 3, 1, 0:8].rearrange("p (c h) -> p c h", c=2), bc(Wc.unsqueeze(1), [128, 2, 4]), ALU.mult, Rr, [bsc])
        for i in range(4):
            self.tt(ntmp, sg[:, i, 0, 0:8].rearrange("p (c h) -> p c h", c=2), bc(Wt[:, i, :].unsqueeze(1), [128, 2, 4]), ALU.mult, Rr, [bsc])
            self.tt(nacc, nacc, ntmp, ALU.add, Rr, [bsc])
        for e in range(2):
            self.cp(nT[:, :, :, e], nacc, Rr, [self.b_nst])
        self.cp(mbc, min_, Rr, [self.b_mst])
        stg = self.fv(6144, 4096).rearrange("p (c h v) -> p c h v", c=2, h=H); bst = Buf()
        srcs = [(1, 3, None)] + [(0, i, i) for i in range(4)]
        for si, (tt_, rk, i) in enumerate(srcs):
            for c in range(2):
                k.dma('sp', self.fv(6144 + c * 2048, 2048),
                      self.xgC[layer][tt_][c][rk * 128:(rk + 1) * 128, :], [self.b_xg[layer]], [bst])
            for kc in range(2):
                for h in range(H):
                    if i is None:
                        self.ts(CT[:, kc, h, :], stg[:, kc, h, :], Wc[:, h:h + 1], None, ALU.mult, None, [bst] + Rr, [b_ct])
                    else:
                        self.stt(CT[:, kc, h, :], stg[:, kc, h, :], Wt[:, i, h:h + 1], CT[:, kc, h, :], ALU.mult, ALU.add, [bst, b_ct] + Rr, [b_ct])

    def rope(self, dst, dstb, src, srcb, pos0, T, o_cs):
        cs = self.fv(o_cs, SEG, 64); sn = self.fv(o_cs + 512, SEG, 64)
        if not hasattr(self, 'b_rope'):
            self.b_rope = Buf()
        tb = self.b_rope
        self.k.dma('sp', cs[:, 0:T], self.d['cos2'][:, pos0:pos0 + T], (), [tb])
        self.k.dma('sp', sn[:, 0:T], self.d['sin2'][:, pos0:pos0 + T], (), [tb])
        ps, pb = self.ps[3], self.bps[3]
        self.mm(ps[0:64, 0:T], self.perm[:, :], src, True, True, [srcb, self.b_const], [pb])
        self.tt(sn[:, 0:T], ps[0:64, 0:T], sn[:, 0:T], ALU.mult, [pb, tb], [tb])
        self.tt(cs[:, 0:T], src, cs[:, 0:T], ALU.mult, [srcb, tb], [tb])
        self.tt(dst, cs[:, 0:T], sn[:, 0:T], ALU.add, [tb], [dstb])

    def shared_kv(self, pos0):
        nc, k, d, o = self.nc, self.k, self.d, self.o
        T, grp, s = self.T, self.grp, self.sidx
        P0 = self.O_PH
        zT = self.fv(P0, 5 * SEG).rearrange("p (c t) -> p c t", c=5); b_z = Buf()
        cTf = self.fv(P0 + 2560, 4 * SEG).rearrange("p (c t) -> p c t", c=4); b_c = Buf()
        cTb = self.bv(P0 + 4608, 4 * SEG).rearrange("p (c t) -> p c t", c=4)
        kpT = self.fv(P0 + 5632, SEG, 64); b_kp = Buf()
        kpn = self.bv(P0 + 6144, SEG, 64)
        r2 = self.fv(P0 + 6400, SEG); b_r2 = Buf()
        tm = self.fv(P0 + 6912, 576); ld = self.fv(P0 + 7488, 576)
        self.o_kv = P0 + 8300
        blocks = [(c * 128, 128) for c in range(4)] + [(KVL, 64)]

        def evac(bi, ps, pb):
            m = 128 if bi < 4 else 64
            self.cp(zT[0:m, bi, 0:T], ps, [pb], [b_z], 'act' if bi % 2 else 'dve')
        self.linear(d['kv_w_down'], blocks, [(0, KC)], lambda c: (self.xb[:, c, 0:T], self.b_xb), T, evac)
        self.sumsq_bc(lambda c: (zT[:, c, 0:T], b_z), 4, T, r2[:, 0:T], b_r2, KVL)
        for c in range(4):
            self.stt(cTf[:, c, 0:T], zT[:, c, 0:T], self.gmisc[:, 16 + c:17 + c], r2[:, 0:T], ALU.mult, ALU.mult, [b_z, b_r2, self.b_gm], [b_c])
            self.cp(cTb[:, c, 0:T], cTf[:, c, 0:T], [b_c], [b_c], 'act')
        self.sumsq_bc(lambda c: (zT[0:64, 4, 0:T], b_z), 1, T, r2[0:64, 0:T], b_r2, ROPE, rows=64)
        self.stt(kpn[:, 0:T], zT[0:64, 4, 0:T], self.gmisc[0:64, 20:21], r2[0:64, 0:T], ALU.mult, ALU.mult, [b_z, b_r2, self.b_gm], [b_kp])
        self.rope(kpT[:, 0:T], b_kp, kpn[:, 0:T], b_kp, pos0, T, P0 + 18000)
        k0 = 0 if grp == 0 else PAST
        kb = self.b_kvloc if grp == 0 else self.b_kvd[grp]
        kpb = self.bv(P0 + 8000, SEG, 64); b_kpb = Buf()
        self.cp(kpb[:, 0:T], kpT[:, 0:T], [b_kp], [b_kpb])
        if grp == 0:
            k.dma('sp', self.KRp[:, 0:T], kpb[:, 0:T], [b_kpb] + [self.b_kvall[r] for r in range(NR)], [kb], dsem='kv')
        else:
            k.dma('sp', self.kr_dram[grp][:, k0:k0 + T], kpb[:, 0:T], [b_kpb], [kb], dsem='kv')
        cdst = (o['p_ckv'][s * SEG:(s + 1) * SEG] if grp == 0 else o['s_ckv'])
        kdst = (o['p_kpe'][s * SEG:(s + 1) * SEG] if grp == 0 else o['s_kpe'])
        tb = Buf()
        for t0 in range(0, T, 128):
            n = min(128, T - t0)
            ps, pb = self.ps[2], self.bps[2]
            for c in range(4):
                self.tr(ps[0:n, c * 128:(c + 1) * 128], cTf[:, c, t0:t0 + n], 128, [b_c], [pb])
            self.cp(tm[0:n, 0:512], ps[0:n, :], [pb], [tb])
            ps, pb = self.ps[3], self.bps[3]
            self.tr(ps[0:n, 0:64], kpT[:, t0:t0 + n], 64, [b_kp], [pb])
            self.cp(tm[0:n, 512:576], ps[0:n, 0:64], [pb], [tb])
            k.dma('sp', cdst[t0:t0 + n, :], tm[0:n, 0:512], [tb], [self.b_out], dsem='out')
            k.dma('sp', kdst[t0:t0 + n, :], tm[0:n, 512:576], [tb], [self.b_out], dsem='out')
        self.kv_up(cTb, b_c, T, k0)
        if grp == 0:
            k.barrier()
            for j in range(2):
                k.collective(self.KP[j][:, :], self.KG[s][j][:, :], GROUPS, [self.b_kvloc], [self.b_kvall[s]])
                k.collective(self.VP[j][:, :], self.VG[s][j][:, :], GROUPS, [self.b_kvloc], [self.b_kvall[s]])
            k.collective(self.KRp[:, :], self.KRG[s][:, :], GROUPS, [self.b_kvloc], [self.b_kvall[s]])
        if grp == 1:
            k.barrier()
            lb = Buf()
            for blk in range(PAST // SEG):
                for t0 in range(0, SEG, 128):
                    r0 = blk * SEG + t0
                    k.dma('sp', ld[:, 0:512], d['ckv'][r0:r0 + 128, :], (), [lb])
                    k.dma('sp', ld[:, 512:576], d['kpe'][r0:r0 + 128, :], (), [lb])
                    ps, pb = self.ps[2], self.bps[2]
                    for c in range(4):
                        self.tr(ps[:, c * 128:(c + 1) * 128], ld[:, c * 128:(c + 1) * 128], 128, [lb], [pb])
                    self.cp(cTb[:, :, t0:t0 + 128], ps[:, :].rearrange("p (c t) -> p c t", c=4), [pb], [b_c])
                    ps, pb = self.ps[3], self.bps[3]
                    self.tr(ps[0:64, 0:128], ld[:, 512:576], 128, [lb], [pb])
                    self.cp(kpT[:, t0:t0 + 128], ps[0:64, 0:128], [pb], [b_kp])
                self.cp(kpb[:, 0:SEG], kpT[:, 0:SEG], [b_kp], [b_kpb])
                k.dma('sp', self.kr_dram[1][:, blk * SEG:(blk + 1) * SEG], kpb[:, 0:SEG], [b_kpb], [kb], dsem='kv')
                self.kv_up(cTb, b_c, SEG, blk * SEG)

    def kv_up(self, cTb, b_c, T, k0):
        nc, k, d = self.nc, self.k, self.d
        grp = self.grp
        kb = self.b_kvloc if grp == 0 else self.b_kvd[grp]
        O = self.o_kv
        k.barrier()
        kn = self.fv(O, 2 * SEG).rearrange("p (s t) -> p s t", s=2); bkn = [Buf(), Buf()]
        r3 = self.fv(O + 1024, SEG); b_r3 = Buf()
        wvs = self.bv(O + 1536, 4 * 2048).rearrange("p (c n) -> p c n", c=4); b_wv = Buf()
        stg = self.fv(O + 5632, 2048); b_st = Buf()
        vt = self.bv(O + 7680, 2048); b_vt = Buf()
        knh = self.bv(O + 8704, 2 * SEG).rearrange("p (s t) -> p s t", s=2)
        blocks = [(h * 256, 128) for h in range(BH)]

        def evac(bi, ps, pb):
            sl_ = bi % 2
            self.cp(kn[:, sl_, 0:T], ps, [pb], [bkn[sl_]], 'act')
            self.sumsq_bc(lambda c: (kn[:, sl_, 0:T], bkn[sl_]), 1, T, r3[:, 0:T], b_r3, NOPE, ps_id=3)
            self.stt(kn[:, sl_, 0:T], kn[:, sl_, 0:T], self.gmisc[:, 21:22], r3[:, 0:T], ALU.mult, ALU.mult, [bkn[sl_], b_r3, self.b_gm], [bkn[sl_]])
            self.cp(knh[:, sl_, 0:T], kn[:, sl_, 0:T], [bkn[sl_]], [bkn[sl_]], 'pool')
            kdst = self.KP[bi // 8][(bi % 8) * 128:(bi % 8 + 1) * 128, 0:T] if grp == 0 else self.kn_dram[grp][bi, :, k0:k0 + T]
            k.dma('sp', kdst, knh[:, sl_, 0:T], [bkn[sl_]], [kb], dsem='kv')
        self.linear(d['kv_w_up'], blocks, [(0, 4)], lambda c: (cTb[:, c, 0:T], b_c), T, evac, sets=[[0], [1], [4], [5]])
        wv_ = d['kv_w_up'].rearrange("(c p) (h two x) -> p c h two x", p=128, two=2, x=128)
        for c in range(4):
            k.dma('sp', stg.rearrange("p (h x) -> p h x", h=BH), wv_[:, c, :, 1, :], (), [b_st])
            self.cp(wvs[:, c, :], stg, [b_st], [b_wv])
        for t0 in range(0, T, 128):
            n = min(128, T - t0)
            for q4 in range(4):
                ps, pb = self.ps[q4 % 2], self.bps[q4 % 2]
                for c in range(4):
                    self.mm(ps[0:n, :], cTb[:, c, t0:t0 + n], wvs[:, c, q4 * 512:(q4 + 1) * 512], c == 0, c == 3, [b_c, b_wv], [pb])
                self.cp(vt[0:n, q4 * 512:(q4 + 1) * 512], ps[0:n, :], [pb], [b_vt], 'act' if q4 % 2 else 'dve')
            if grp == 0:
                for q in range(2):
                    k.dma('sp', self.VP[q][t0:t0 + n, :], vt[0:n, q * 1024:(q + 1) * 1024], [b_vt], [kb], dsem='kv')
            else:
                k.dma('sp', self.v_dram[grp][k0 + t0:k0 + t0 + n, :], vt[0:n, :], [b_vt], [kb], dsem='kv')

    def mla(self, j):
        nc, k, d, o = self.nc, self.k, self.d, self.o
        T, grp, s = self.T, self.grp, self.sidx
        pos0 = s * SEG if grp == 0 else NR * SEG
        P0 = self.O_PH
        cq = self.bv(P0, 6 * SEG).rearrange("p (c t) -> p c t", c=6); b_cq = Buf()
        r2 = self.fv(P0 + 1536, SEG); b_r2 = Buf()
        QN = self.bv(P0 + 2048, 16 * SEG).rearrange("p (c t) -> p c t", c=16); b_qn = Buf()
        QR = self.bv(P0 + 6144, 16 * SEG, 64).rearrange("p (c t) -> p c t", c=16); b_qr = Buf()
        r3 = self.fv(P0 + 10240, SEG); b_r3 = Buf()
        qtmp = self.fv(P0 + 10752, SEG); b_qt = Buf()
        qrn = self.bv(P0 + 11264, SEG, 64); b_qrn = Buf()
        O_CS = P0 + 11520
        rd = self.fv(P0 + 12544, SEG); b_rd = Buf()
        cqg = self.bv(P0 + 13100, 6 * SEG).rearrange("p (c t) -> p c t", c=6)

        def evac(bi, ps, pb):
            self.cp(cq[:, bi, 0:T], ps, [pb], [b_cq], 'act')
            self.ts(cqg[:, bi, 0:T], ps, self.gmisc[:, 6 * j + bi:6 * j + bi + 1], None, ALU.mult, None, [pb, self.b_gm], [b_cq])
        self.linear(d['b_w_dq'][j], [(c * 128, 128) for c in range(6)], [(0, KC)], lambda c: (self.xb[:, c, 0:T], self.b_xb), T, evac)
        self.sumsq_bc(lambda c: (cq[:, c, 0:T], b_cq), 6, T, r2[:, 0:T], b_r2, QL)
        blocks = []
        for h in range(BH):
            blocks.append((h * 192, 128)); blocks.append((h * 192 + 128, 64))

        def evac2(bi, ps, pb):
            h, isr = bi // 2, bi % 2
            m = 64 if isr else 128
            self.tt(qtmp[0:m, 0:T], ps, r2[0:m, 0:T], ALU.mult, [pb, b_r2], [b_qt])
            self.sumsq_bc(lambda c: (qtmp[0:m, 0:T], b_qt), 1, T, r3[0:m, 0:T], b_r3, m, rows=m, ps_id=3)
            if not isr:
                self.stt(QN[:, h, 0:T], qtmp[:, 0:T], self.gmisc[:, 12 + j:13 + j], r3[:, 0:T], ALU.mult, ALU.mult, [b_qt, b_r3, self.b_gm], [b_qn])
            else:
                self.stt(qrn[:, 0:T], qtmp[0:64, 0:T], self.gmisc[0:64, 14 + j:15 + j], r3[0:64, 0:T], ALU.mult, ALU.mult, [b_qt, b_r3, self.b_gm], [b_qrn])
                self.rope(QR[:, h, 0:T], b_qr, qrn[:, 0:T], b_qrn, pos0, T, O_CS)
        self.linear(d['b_w_uq'][j], blocks, [(0, 6)], lambda c: (cqg[:, c, 0:T], b_cq), T, evac2, sets=[[0, 1], [4, 5], [6, 7]])
        k.barrier()
        KB = 1024 if grp == 1 else SEG
        knb = self.bv(0, 2 * 1024).rearrange("p (s t) -> p s t", s=2); b_knb = [Buf(), Buf()]
        vb = self.bv(1024, 2 * 1024).rearrange("p (s t x) -> p s t x", s=2, x=128); b_vb = [Buf(), Buf()]
        krall = self.bv(3072, 9 * SEG, 64); b_kr = Buf()
        dacc = self.fv(5376, SEG); daccb = self.bv(5888, SEG); b_da = Buf()
        blocks = []
        if grp == 1:
            nkeys = PAST + DS
            for kk0 in range(0, nkeys, KB):
                nk = min(KB, nkeys - kk0)
                blocks.append((lambda h, kk0=kk0, nk=nk: self.kn_dram[1][h, :, kk0:kk0 + nk],
                               self.kr_dram[1][:, kk0:kk0 + nk],
                               lambda h, t, n, kk0=kk0: self.v_dram[1][kk0 + t * 128:kk0 + t * 128 + n, h * 128:(h + 1) * 128],
                               nk, False, None, self.b_kvd[1]))
        else:
            def mk(KS, KRS, VS, i, diag, bias, dep):
                return (lambda h: KS[h // 8][i * 1024 + (h % 8) * 128:i * 1024 + (h % 8 + 1) * 128, :],
                        KRS[i * ROPE:(i + 1) * ROPE, :],
                        lambda h, t, n: VS[h // 8][i * SEG + t * 128:i * SEG + t * 128 + n, (h % 8) * 128:(h % 8 + 1) * 128],
                        SEG, diag, bias, dep)
            for rdi in range(s):
                for i in range(4):
                    blocks.append(mk(self.KG[rdi], self.KRG[rdi], self.VG[rdi], i, False, None, self.b_kvall[rdi]))
            for i in range(4):
                blocks.append(mk(self.KG[s], self.KRG[s], self.VG[s], i, False, 32 + i, self.b_kvall[s]))
            blocks.append(mk(self.KP, self.KRp, self.VP, 0, True, None, self.b_kvloc))
        ntot = sum((blk[3] + 127) // 128 for blk in blocks)
        kro = 0
        for (knf, krap, vf, nk, diag, biascol, dep) in blocks:
            k.dma('sp', krall[:, kro:kro + nk], krap, [dep], [b_kr], dsem='ld')
            kro += nk
        DEPTH_ = 3
        sbanks = [0, 1, 6, 7]
        pTs = [self.bv(2048 + 256 * i, SEG) for i in range(4)]
        b_pTs = [Buf() for _ in range(4)]
        slot = 0
        pi = 0
        for h in range(BH):
            po, bo = self.ps[4 + h % 2], self.bps[4 + h % 2]
            pd, bd = self.ps[2 + h % 2], self.bps[2 + h % 2]
            tiles = []
            kro = 0
            for (knf, krap, vf, nk, diag, biascol, dep) in blocks:
                tiles.append(('load', knf, vf, nk, dep))
                for t in range((nk + 127) // 128):
                    n = min(128, nk - t * 128)
                    tiles.append(('tile', t, n, t * 128 if diag else 0, diag, biascol, kro))
                kro += nk
            pend = []
            state = dict(first=True, cnt=0, sl=0)

            def finish(item):
                (ps_, bs_, pp, bp, sl_, t, n, q0, diag, biascol) = item
                if biascol is None:
                    self.act(pp[0:n, q0:T], ps_[0:n, q0:T], AF.Exp, [bs_], [bp], scale=ATT_SCALE)
                else:
                    self.act(pp[0:n, q0:T], ps_[0:n, q0:T], AF.Exp, [bs_, self.b_gn], [bp], scale=ATT_SCALE,
                             bias=self.cm[0:n, biascol:biascol + 1])
                if diag:
                    self.ms(pp[64:128, q0:q0 + 64], 0.0, [bp], 'dve')
                state['cnt'] += 1
                self.mm(po[:, q0:T], vb[0:n, sl_, t, :], pp[0:n, q0:T], state['first'], state['cnt'] == ntot, [b_vb[sl_], bp], [bo])
                if state['first']:
                    assert n == 128 and q0 == 0
                    self.cp(dacc[:, 0:T], pp[:, 0:T], [bp], [b_da])
                else:
                    self.tt(dacc[0:n, q0:T], dacc[0:n, q0:T], pp[0:n, q0:T], ALU.add, [bp, b_da], [b_da])
                state['first'] = False

            for it in tiles:
                if it[0] == 'load':
                    _, knf, vf, nk, dep = it
                    sl_ = slot; slot ^= 1
                    state['sl'] = sl_
                    k.dma('sp', knb[:, sl_, 0:nk], knf(h), [dep], [b_knb[sl_]], dsem='ld')
                    nfull = nk // 128
                    if nfull:
                        k.dma('sp', vb[:, sl_, 0:nfull, :], vf(h, 0, nfull * 128).rearrange("(t p) d -> p t d", p=128), [dep], [b_vb[sl_]], dsem='ld')
                    if nk % 128:
                        n_ = nk % 128
                        k.dma('sp', vb[0:n_, sl_, nfull, :], vf(h, nfull, n_), [dep], [b_vb[sl_]], dsem='ld')
                    continue
                _, t, n, q0, diag, biascol, kro_ = it
                sl_ = state['sl']
                ps_, bs_ = self.ps[sbanks[pi % 4]], self.bps[sbanks[pi % 4]]
                pp, bp = pTs[pi % 4], b_pTs[pi % 4]
                pi += 1
                self.mm(ps_[0:n, q0:T], knb[:, sl_, t * 128:t * 128 + n], QN[:, h, q0:T], True, False, [b_knb[sl_], b_qn], [bs_])
                self.mm(ps_[0:n, q0:T], krall[:, kro_ + t * 128:kro_ + t * 128 + n], QR[:, h, q0:T], False, True, [b_kr, b_qr], [bs_])
                pend.append((ps_, bs_, pp, bp, sl_, t, n, q0, diag, biascol))
                if len(pend) > DEPTH_:
                    finish(pend.pop(0))
            while pend:
                finish(pend.pop(0))
            self.cp(daccb[:, 0:T], dacc[:, 0:T], [b_da], [b_da], 'act')
            self.mm(pd[:, 0:T], self.ones_b[:, :], daccb[:, 0:T], True, True, [b_da, self.b_const], [bd])
            self.recip(rd[:, 0:T], pd[:, 0:T], [bd], [b_rd])
            self.tt(QN[:, h, 0:T], po[:, 0:T], rd[:, 0:T], ALU.mult, [bo, b_rd], [b_qn])
        k.barrier()
        self.linear(d['b_w_o'][j], [(c * 128, 128) for c in range(KC)], [(0, KC)], lambda c: (QN[:, c, 0:T], b_qn), T,
                    self.resid_evac(), extra_R=[self.b_x])


_PROG = None


def _rope_tables():
    half = ROPE // 2
    inv = (10000.0 ** (-np.arange(half, dtype=np.float32) / half)).astype(np.float32)
    pos = np.concatenate([np.arange(SEQ), PAST + np.arange(DS)]).astype(np.float32)
    ang = pos[None, :] * inv[:, None]
    c, s = np.cos(ang).astype(np.float32), np.sin(ang).astype(np.float32)
    return np.concatenate([c, c], 0), np.concatenate([-s, s], 0)


def _core_mask(r):
    m = np.zeros((40,), np.float32)
    for i in range(4):
        m[i] = 0.0 if i < r else NEG
        m[4 + i] = 1.0 if i < r else 0.0
        for l in range(4):
            m[8 + 4 * i + l] = 1.0 if (i < l < r) else 0.0
        m[24 + i] = 1.0 if i == r - 1 else 0.0
        m[32 + i] = 0.0 if i < r else -30000.0
    m[28] = 1.0 if r == 0 else 0.0
    return np.ascontiguousarray(np.broadcast_to(m[None, :], (128, 40)))


def kernel(**inp):
    global _PROG
    if _PROG is None:
        _PROG = Prog()
    prog = _PROG
    f = lambda a: np.ascontiguousarray(np.asarray(a, dtype=np.float32))
    cos2, sin2 = _rope_tables()
    shared = {}
    for n in ['norm_mix', 'norm_ffn', 'a_w_in', 'a_b_gate', 'a_w_out', 'kv_w_down', 'kv_w_up', 'b_w_dq', 'b_g_cq', 'b_w_uq',
              'b_g_qn', 'b_g_qr', 'b_w_o', 'f_w_up', 'f_conv_w', 'f_conv_b', 'f_w_down']:
        shared[n] = f(inp[n])
    shared['a_g_head'] = f(inp['a_g_head']).reshape(NA, H * DV)
    for n in ['kv_norm', 'kv_g_c', 'kv_g_r', 'kv_g_kn']:
        shared[n] = f(inp[n]).reshape(1, -1)
    in_maps = []
    for c in range(8):
        g, r = c // 4, c % 4
        m = dict(shared)
        segs = [rd * 4 + r for rd in range(NR)]
        m['xp'] = f(np.concatenate([inp['x_prompt'][g][sg * SEG:(sg + 1) * SEG] for sg in segs], 0))
        cols = np.concatenate([np.arange(sg * SEG, (sg + 1) * SEG) for sg in segs] + [SEQ + np.arange(DS)])
        m['cos2'] = f(cos2[:, cols]); m['sin2'] = f(sin2[:, cols])
        m['cmask'] = _core_mask(r)
        m['xs'] = f(inp['x_sample'][c])
        m['ckv'] = f(inp['cache_ckv'][c]); m['kpe'] = f(inp['cache_kpe'][c])
        m['sC'] = f(inp['state_C'][:, c]); m['sn'] = f(inp['state_n'][:, c]).reshape(NA, H * DK); m['sm'] = f(inp['state_m'][:, c])
        m['sconv'] = f(inp['state_conv'][:, c])
        in_maps.append(m)
    res = run_bass_kernel_spmd(prog.nc, in_maps, core_ids=list(range(8))).results

    def seqcat(n):
        out = []
        for g in range(2):
            parts = [None] * (4 * NR)
            for r in range(4):
                for rd in range(NR):
                    parts[rd * 4 + r] = res[g * 4 + r][n][rd * SEG:(rd + 1) * SEG]
            out.append(np.concatenate(parts, 0))
        return np.stack(out, 0)
    pc = [3, 7]
    st = lambda n, cores, ax: np.stack([res[c][n] for c in cores], axis=ax)
    allc = list(range(8))
    rs = lambda a: a.reshape(a.shape[0], a.shape[1], H, DK)
    return (seqcat('yp'), st('ys', allc, 0), seqcat('p_ckv'), seqcat('p_kpe'),
            st('pC', pc, 1), rs(st('pn', pc, 1)), st('pm', pc, 1), st('pconv', pc, 1),
            st('s_ckv', allc, 0), st('s_kpe', allc, 0), st('sCo', allc, 1), rs(st('sno', allc, 1)), st('smo', allc, 1),
            st('sconvo', allc, 1))
```

```python
import numpy as np
from contextlib import ExitStack
import concourse.bass as bass
import concourse.mybir as mybir
from concourse.bass_utils import run_bass_kernel_spmd

F32 = mybir.dt.float32
BF16 = mybir.dt.bfloat16
AF = mybir.ActivationFunctionType
ALU = mybir.AluOpType
AX = mybir.AxisListType

D = 2048; KC = 16; SEQ = 4096; DEPTH = 4; NA = 2
DS = 16; PAST = 2048
H = 4; DK = 256; DV = 512; APROJ = 6152
BH = 16; QL = 768; KVL = 512; NOPE = 128; ROPE = 64; VD = 128
DFF = 5632; FC = 44
EPS = 1e-6
SEG = 512
NSEG = SEQ // SEG
NR = 2
XW = 2 * H * DV + 16
KVROWS = BH * 128 + ROPE + 4 * SEG
GROUPS = [[0, 1, 2, 3], [4, 5, 6, 7]]
ATT_SCALE = (NOPE + ROPE) ** -0.5
NEG = -1.0e30


class Buf:
    def __init__(self, name=""):
        self.name = name
        self.w = None
        self.r = []


class K:
    def __init__(self, nc, stack):
        self.nc = nc
        self.stack = stack
        self.eng = {'pe': nc.tensor, 'dve': nc.vector, 'act': nc.scalar, 'pool': nc.gpsimd, 'sp': nc.sync}
        self.sem = {}
        self.cnt = {}
        self.waited = {e: {} for e in self.eng}
        for e in self.eng:
            self.sem[e] = stack.enter_context(nc.semaphore('s_' + e))
            self.cnt[e] = 0
        self.nins = 0

    def new_dma_sem(self, key):
        self.sem[key] = self.stack.enter_context(self.nc.semaphore(key))
        self.cnt[key] = 0
        return key

    def _wait(self, e, deps):
        need = {}
        for d in deps:
            if d is None:
                continue
            k, v = d
            if e == 'pe' and k == 'pe':
                continue
            if v > need.get(k, 0):
                need[k] = v
        for k, v in need.items():
            if self.waited[e].get(k, 0) >= v:
                continue
            self.eng[e].wait_ge(self.sem[k], v)
            self.waited[e][k] = v

    @staticmethod
    def _deps(reads, writes):
        deps = []
        for b in reads:
            deps.append(b.w)
        for b in writes:
            deps.append(b.w)
            deps.extend(b.r)
        return deps

    def op(self, e, fn, reads=(), writes=()):
        self._wait(e, self._deps(reads, writes))
        ins = fn()
        self.cnt[e] += 1
        self.nins += 1
        ins.then_inc(self.sem[e], 1)
        tag = (e, self.cnt[e])
        for b in reads:
            b.r.append(tag)
            if len(b.r) > 24:
                b.r = b.r[-24:] if False else self._compact(b.r)
        for b in writes:
            b.w = tag
            b.r = []
        return ins

    @staticmethod
    def _compact(r):
        m = {}
        for k, v in r:
            if v > m.get(k, 0):
                m[k] = v
        return list(m.items())

    def dma(self, q, out, in_, reads=(), writes=(), dsem='io', **kw):
        if dsem in ('io', 'out', 'kv', 'ld'):
            dsem = self.pool[self.pi % len(self.pool)]
            self.pi += 1
        prev = self.cnt[dsem]
        self._wait(q, self._deps(reads, writes) + ([(dsem, prev)] if prev else []))
        ins = self.eng[q].dma_start(out=out, in_=in_, **kw)
        self.cnt[dsem] += 16
        self.nins += 1
        ins.then_inc(self.sem[dsem], 16)
        tag = (dsem, self.cnt[dsem])
        for b in reads:
            b.r.append(tag)
        for b in writes:
            b.w = tag
            b.r = []
        return ins

    def collective(self, src, dst, groups, reads=(), writes=()):
        self._wait('pool', self._deps(reads, writes))
        ins = self.nc.gpsimd.collective_compute("AllGather", mybir.AluOpType.bypass, replica_groups=groups,
                                                ins=[src.opt()], outs=[dst.opt()])
        self.cnt['cc'] += 1
        self.nins += 1
        ins.then_inc(self.sem['cc'], 1)
        tag = ('cc', self.cnt['cc'])
        for b in reads:
            b.r.append(tag)
        for b in writes:
            b.w = tag
            b.r = []
        return ins

    def barrier(self):
        allv = [(k, v) for k, v in self.cnt.items() if v > 0 and k != 'cc']
        for e in self.eng:
            self._wait(e, allv)


def bc(ap, shape):
    return ap.broadcast_to(list(shape))


STOP = None


class _Stop(Exception):
    pass


class Prog:
    def dbg(self, tag):
        if STOP is not None and tag == STOP:
            raise _Stop()

    def __init__(self):
        self.nc = nc = bass.Bass("TRN2", target_bir_lowering=False)
        self.st = ExitStack()
        self.k = K(nc, self.st)
        for s in ['w0', 'w1', 'w2', 'w3', 'cc']:
            self.k.new_dma_sem(s)
        self.k.pool = [self.k.new_dma_sem('p%d' % i) for i in range(40)]
        self.k.pi = 0
        self.build()

    def din(self, name, shape):
        return self.nc.dram_tensor(name, list(shape), F32, kind="ExternalInput").ap()

    def dout(self, name, shape):
        return self.nc.dram_tensor(name, list(shape), F32, kind="ExternalOutput").ap()

    def dint(self, name, shape, dt=F32):
        return self.nc.dram_tensor(name, list(shape), dt, kind="Internal").ap()

    def sb(self, name, shape, dt=F32):
        return self.st.enter_context(self.nc.sbuf_tensor(name, list(shape), dt))

    def fv(self, a, n, rows=128):
        return self.arena[0:rows, a:a + n]

    def bv(self, a, n, rows=128):
        assert n % 2 == 0
        return self.arena[0:rows, a:a + n // 2].bitcast(BF16)

    def V(self, e='dve'):
        return self.nc.vector if e == 'dve' else self.nc.gpsimd

    def tt(self, out, a, b, op, R, W, e='dve'):
        self.k.op(e, lambda: self.V(e).tensor_tensor(out=out, in0=a, in1=b, op=op), R, W)

    def ts(self, out, a, s1, s2, op0, op1, R, W, e='dve'):
        if op1 is None:
            self.k.op(e, lambda: self.V(e).tensor_scalar(out=out, in0=a, scalar1=s1, scalar2=None, op0=op0), R, W)
        else:
            self.k.op(e, lambda: self.V(e).tensor_scalar(out=out, in0=a, scalar1=s1, scalar2=s2, op0=op0, op1=op1), R, W)

    def stt(self, out, a, s, b, op0, op1, R, W):
        self.k.op('dve', lambda: self.nc.vector.scalar_tensor_tensor(out=out, in0=a, scalar=s, in1=b, op0=op0, op1=op1), R, W)

    def cp(self, out, a, R, W, e='dve'):
        if e == 'act':
            self.k.op('act', lambda: self.nc.scalar.copy(out=out, in_=a), R, W)
        else:
            self.k.op(e, lambda: self.V(e).tensor_copy(out=out, in_=a), R, W)

    def act(self, out, a, func, R, W, bias=None, scale=None, accum=None):
        kw = {}
        if bias is not None:
            kw['bias'] = bias
        if scale is not None:
            kw['scale'] = scale
        if accum is not None:
            kw['accum_out'] = accum
        self.k.op('act', lambda: self.nc.scalar.activation(out=out, in_=a, func=func, **kw), R, W)

    def mm(self, out, lhsT, rhs, start, stop, R, W):
        self.k.op('pe', lambda: self.nc.tensor.matmul(out, lhsT=lhsT, rhs=rhs, start=start, stop=stop), R, W)

    def tr(self, out, a, n, R, W, bf=False):
        idn = self.ident_b if bf else self.ident
        self.k.op('pe', lambda: self.nc.tensor.transpose(out, a, idn[0:n, 0:n]), list(R) + [self.b_const], W)

    def ms(self, ap, c, W, e='dve'):
        self.k.op(e, lambda: self.V(e).memset(ap, c), (), W)

    def recip(self, out, a, R, W):
        self.k.op('dve', lambda: self.nc.vector.reciprocal(out=out, in_=a), R, W)

    def load_fm(self, dst, src1d, n, rows=128):
        tmp = self.fv(self.O_TMP, 128)
        if not hasattr(self, 'b_tmpfm'):
            self.b_tmpfm = Buf()
        tb = self.b_tmpfm
        self.k.dma('sp', tmp[0:n, 0:rows], src1d.rearrange("(c p) -> c p", p=rows), (), [tb])
        ps, pb = self.ps[7], self.bps[7]
        self.tr(ps[0:rows, 0:n], tmp[0:n, 0:rows], n, [tb], [pb])
        self.cp(dst, ps[0:rows, 0:n], [pb], [self.b_gn])

    def build(self):
        nc, k = self.nc, self.k
        d = {}
        d['xp'] = self.din('xp', [NR * SEG, D]); d['xs'] = self.din('xs', [DS, D])
        d['ckv'] = self.din('ckv', [PAST, KVL]); d['kpe'] = self.din('kpe', [PAST, ROPE])
        d['sC'] = self.din('sC', [NA, H, DV, DK]); d['sn'] = self.din('sn', [NA, H * DK]); d['sm'] = self.din('sm', [NA, H])
        d['sconv'] = self.din('sconv', [DEPTH, 2, 2 * DFF])
        d['norm_mix'] = self.din('norm_mix', [DEPTH, D]); d['norm_ffn'] = self.din('norm_ffn', [DEPTH, D])
        d['a_w_in'] = self.din('a_w_in', [NA, D, APROJ]); d['a_b_gate'] = self.din('a_b_gate', [NA, 8])
        d['a_g_head'] = self.din('a_g_head', [NA, H * DV]); d['a_w_out'] = self.din('a_w_out', [NA, D, D])
        d['kv_norm'] = self.din('kv_norm', [1, D]); d['kv_w_down'] = self.din('kv_w_down', [D, KVL + ROPE])
        d['kv_g_c'] = self.din('kv_g_c', [1, KVL]); d['kv_g_r'] = self.din('kv_g_r', [1, ROPE])
        d['kv_w_up'] = self.din('kv_w_up', [KVL, BH * 256]); d['kv_g_kn'] = self.din('kv_g_kn', [1, NOPE])
        d['b_w_dq'] = self.din('b_w_dq', [2, D, QL]); d['b_g_cq'] = self.din('b_g_cq', [2, QL])
        d['b_w_uq'] = self.din('b_w_uq', [2, QL, BH * 192]); d['b_g_qn'] = self.din('b_g_qn', [2, NOPE])
        d['b_g_qr'] = self.din('b_g_qr', [2, ROPE]); d['b_w_o'] = self.din('b_w_o', [2, D, D])
        d['f_w_up'] = self.din('f_w_up', [DEPTH, D, 2 * DFF]); d['f_conv_w'] = self.din('f_conv_w', [DEPTH, 3, 2 * DFF])
        d['f_conv_b'] = self.din('f_conv_b', [DEPTH, 2 * DFF]); d['f_w_down'] = self.din('f_w_down', [DEPTH, DFF, D])
        d['cos2'] = self.din('cos2', [ROPE, NR * SEG + DS]); d['sin2'] = self.din('sin2', [ROPE, NR * SEG + DS])
        d['cmask'] = self.din('cmask', [128, 40])
        o = {}
        o['yp'] = self.dout('yp', [NR * SEG, D]); o['ys'] = self.dout('ys', [DS, D])
        o['p_ckv'] = self.dout('p_ckv', [NR * SEG, KVL]); o['p_kpe'] = self.dout('p_kpe', [NR * SEG, ROPE])
        o['pC'] = self.dout('pC', [NA, H, DV, DK]); o['pn'] = self.dout('pn', [NA, H * DK]); o['pm'] = self.dout('pm', [NA, H])
        o['pconv'] = self.dout('pconv', [DEPTH, 2, 2 * DFF])
        o['s_ckv'] = self.dout('s_ckv', [DS, KVL]); o['s_kpe'] = self.dout('s_kpe', [DS, ROPE])
        o['sCo'] = self.dout('sCo', [NA, H, DV, DK]); o['sno'] = self.dout('sno', [NA, H * DK]); o['smo'] = self.dout('smo', [NA, H])
        o['sconvo'] = self.dout('sconvo', [DEPTH, 2, 2 * DFF])
        self.d, self.o = d, o
        self.kn_dram = [None, self.dint('kns', [BH, 128, PAST + DS], BF16)]
        self.kr_dram = [None, self.dint('krs', [ROPE, PAST + DS], BF16)]
        self.v_dram = [None, self.dint('vs', [PAST + DS, BH * VD], BF16)]
        self.xsC = [[[self.dint('xsC%d%d%d' % (l, t, c), [128, 2048]) for c in range(2)] for t in range(2)] for l in range(NA)]
        self.xgC = [[[self.dint('xgC%d%d%d' % (l, t, c), [512, 2048]) for c in range(2)] for t in range(2)] for l in range(NA)]
        self.xsS = [self.dint('xsS%d' % l, [256, 16]) for l in range(NA)]
        self.xgS = [self.dint('xgS%d' % l, [1024, 16]) for l in range(NA)]
        self.b_xs2 = [Buf(), Buf()]; self.b_xg = [Buf(), Buf()]
        self.ts2 = [self.dint('ts2_%d' % l, [256, 176]) for l in range(DEPTH)]
        self.tg = [self.dint('tg_%d' % l, [4 * 256, 176]) for l in range(DEPTH)]
        self.b_ts2 = [Buf() for _ in range(DEPTH)]; self.b_tg = [Buf() for _ in range(DEPTH)]
        self.KP = [self.dint('KP%d' % j, [1024, SEG], BF16) for j in range(2)]
        self.KRp = self.dint('KRp', [ROPE, SEG], BF16)
        self.VP = [self.dint('VP%d' % q, [SEG, 1024], BF16) for q in range(2)]
        self.b_kvloc = Buf()
        self.b_kvd = [Buf(), Buf()]
        self.KG = [[self.dint('KG%d%d' % (r, j), [4 * 1024, SEG], BF16) for j in range(2)] for r in range(NR)]
        self.KRG = [self.dint('KRG%d' % r, [4 * ROPE, SEG], BF16) for r in range(NR)]
        self.VG = [[self.dint('VG%d%d' % (r, q), [4 * SEG, 1024], BF16) for q in range(2)] for r in range(NR)]
        self.b_kvall = [Buf() for _ in range(NR)]
        self.b_out = Buf('out')

        self.ident = self.sb('ident', [128, 128])
        self.ident_b = self.sb('ident_b', [128, 128], BF16)
        self.ones_b = self.sb('ones_b', [128, 128], BF16)
        self.ones_f = self.sb('ones_f', [128, 128])
        self.uneg = {L: self.sb('uneg%d' % L, [L, L]) for L in (128, 16)}
        self.mask = {L: self.sb('mask%d' % L, [L, L]) for L in (128, 16)}
        self.maskT = {L: self.sb('maskT%d' % L, [L, L]) for L in (128, 16)}
        self.sel = {L: self.sb('sel%d' % L, [L, 128]) for L in (128, 16)}
        self.perm = self.sb('perm', [64, 64], BF16)
        self.epsc = self.sb('epsc', [128, 1])
        self.b_const = Buf('const')
        self.xT = self.sb('xT', [128, KC, SEG]); self.b_x = Buf('x')
        self.xb = self.sb('xb', [128, KC, SEG], BF16); self.b_xb = Buf('xb')
        self.tail = self.sb('tail', [128, DEPTH, 2, 2, 88]); self.b_tail = Buf('tail')
        self.nst = self.sb('nst', [128, NA, 2, 2, H, 2]); self.b_nst = Buf('nst')
        self.nsb = self.sb('nsb', [128, NA, 2, 2, H, 2], BF16)
        self.mst = self.sb('mst', [128, NA, 2, H]); self.b_mst = Buf('mst')
        self.gn = self.sb('gn', [128, 9, KC]); self.b_gn = Buf('gn')
        self.cw = self.sb('cw', [128, DEPTH, 3, 88]); self.cb = self.sb('cb', [128, DEPTH, 88])
        self.gmisc = self.sb('gmisc', [128, 64])
        self.cm = self.sb('cm', [128, 40])
        self.fsum = self.sb('fsum', [128, H])
        self.b_cw = self.b_gn; self.b_gm = self.b_gn
        self.arena = self.sb('arena', [128, 34200])
        self.ps = [self.st.enter_context(nc.psum_tensor('ps%d' % i, [128, 512], F32)) for i in range(8)]
        self.bps = [Buf('ps%d' % i) for i in range(8)]
        self.O_STG = 0
        self.O_WR = 8192
        self.O_SQ = 12288
        self.O_RSTD = 12800
        self.O_PH = 13312
        self.O_TMP = 13312

        try:
            self.init_consts()
            self.dbg('init')
            self.segment(grp=1, s=0, T=DS, L=DS)
            for s in range(NR if NSEG > 0 else 0):
                self.segment(grp=0, s=s, T=SEG, L=128)
        except _Stop:
            pass
        k.barrier()
        k._wait('sp', [(p, k.cnt[p]) for p in k.pool if k.cnt[p] > 0])

    def init_consts(self):
        nc, k, d = self.nc, self.k, self.d
        W = [self.b_const]
        g = nc.gpsimd

        def sel(t, pattern, cmp, fill, base, cm):
            k.op('pool', lambda: g.affine_select(out=t, in_=t, pattern=pattern, compare_op=cmp, fill=fill,
                                                base=base, channel_multiplier=cm), W, W)
        self.ms(self.ident[:], 0.0, W, 'pool')
        sel(self.ident[:], [[-1, 128]], ALU.not_equal, 1.0, 0, 1)
        self.cp(self.ident_b[:], self.ident[:], W, W, 'dve')
        self.ms(self.ones_f[:], 1.0, W, 'pool')
        self.cp(self.ones_b[:], self.ones_f[:], W, W, 'dve')
        self.ms(self.epsc[:], EPS, W, 'pool')
        for L in (128, 16):
            self.ms(self.uneg[L][:], -1.0, W, 'pool')
            sel(self.uneg[L][:], [[1, L]], ALU.is_ge, 0.0, 0, -1)
            self.ms(self.mask[L][:], 0.0, W, 'pool')
            sel(self.mask[L][:], [[-1, L]], ALU.is_ge, NEG, 0, 1)
            self.ms(self.maskT[L][:], 0.0, W, 'pool')
            sel(self.maskT[L][:], [[1, L]], ALU.is_ge, NEG, 0, -1)
            self.ms(self.sel[L][:], 1.0, W, 'pool')
            sel(self.sel[L][:], [[0, 128]], ALU.is_equal, 0.0, -(L - 1), 1)
        pf = self.fv(20000, 64, 64)
        self.ms(pf, 0.0, W, 'pool')
        sel(pf, [[-1, 64]], ALU.not_equal, 1.0, -32, 1)
        sel(pf, [[-1, 64]], ALU.not_equal, 1.0, 32, 1)
        self.cp(self.perm[:], pf, W, W, 'dve')
        k.barrier()
        for i in range(4):
            self.load_fm(self.gn[:, i, :], d['norm_mix'][i], KC)
            self.load_fm(self.gn[:, 4 + i, :], d['norm_ffn'][i], KC)
        self.load_fm(self.gn[:, 8, :], d['kv_norm'][0], KC)
        for l in range(DEPTH):
            for j in range(3):
                self.load_fm(self.cw[:, l, j, :], d['f_conv_w'][l, j], 88)
            self.load_fm(self.cb[:, l, :], d['f_conv_b'][l], 88)
        gm = self.gmisc
        self.ms(gm[:], 0.0, [self.b_gn], 'pool')
        for j in range(2):
            self.load_fm(gm[:, 6 * j:6 * j + 6], d['b_g_cq'][j], 6)
            self.load_fm(gm[:, 12 + j:13 + j], d['b_g_qn'][j], 1)
            self.load_fm(gm[0:64, 14 + j:15 + j], d['b_g_qr'][j], 1, rows=64)
            self.load_fm(gm[:, 22 + 16 * j:38 + 16 * j], d['a_g_head'][j], 16)
            self.load_fm(gm[0:8, 54 + j:55 + j], d['a_b_gate'][j], 1, rows=8)
        self.load_fm(gm[:, 16:20], d['kv_g_c'][0], 4)
        self.load_fm(gm[0:64, 20:21], d['kv_g_r'][0], 1, rows=64)
        self.load_fm(gm[:, 21:22], d['kv_g_kn'][0], 1)
        self.ms(self.tail[:], 0.0, [self.b_tail], 'pool')
        self.ms(self.mst[:], 0.0, [self.b_mst], 'pool')
        self.ms(self.fsum[:], 0.0, [self.b_mst], 'pool')
        self.ms(self.nst[:], 0.0, [self.b_nst], 'pool')
        k.barrier()
        for l in range(DEPTH):
            for r in range(2):
                self.load_fm(self.tail[:, l, 1, r, :], d['sconv'][l, r], 88)
        for l in range(NA):
            k.dma('sp', self.mst[:, l, 1, :], d['sm'][l:l + 1, :].broadcast_to([128, H]), (), [self.b_mst])
            nt = self.fv(20100, 8)
            self.load_fm(nt, d['sn'][l], 8)
            for e in range(2):
                self.cp(self.nst[:, l, 1, :, :, e], nt.rearrange("p (h c) -> p c h", h=H), [self.b_gn], [self.b_nst], 'dve')
        self.cp(self.nsb[:], self.nst[:], [self.b_nst], [self.b_nst], 'dve')
        k.dma('sp', self.cm[:], d['cmask'][:, :], (), [self.b_gn])
        zc = self.fv(0, XW)
        self.ms(zc, 0.0, W, 'pool')
        for l in range(NA):
            for c in range(2):
                k.dma('sp', self.xsC[l][1][c][:, :], zc[:, 0:2048], W, [self.b_xs2[l]])
            k.dma('sp', self.xsS[l][128:256, :], zc[:, 0:16], W, [self.b_xs2[l]])
        for l in range(DEPTH):
            k.dma('sp', self.ts2[l][128:256, :], zc[:, 0:176], W, [self.b_ts2[l]])
        k.barrier()

    def linear(self, w, blocks, kgroups, rhs_fn, T, evac, gcol=None, ps_ids=(0, 1), extra_R=(), sets=None):
        k = self.k
        wv = w.rearrange("(c p) n -> p c n", p=128)
        NS = 4
        if not hasattr(self, 'wslot'):
            self.wslot = 0
            self.b_wst = [Buf() for _ in range(NS)]
            self.b_wr = [Buf() for _ in range(NS)]
        if sets is None:
            sets = [[0, 1, 4, 5], [6, 7, 2, 3]]
        kcs = [kc for (k0, nk) in kgroups for kc in range(k0, k0 + nk)]
        groups = []
        cur = []
        for bi, (c0, m) in enumerate(blocks):
            if cur and (cur[-1][1] + cur[-1][2] == c0) and (sum(x[2] for x in cur) + m <= 512) and (len(cur) < min(len(x) for x in sets)):
                cur.append((bi, c0, m))
            else:
                if cur:
                    groups.append(cur)
                cur = [(bi, c0, m)]
        if cur:
            groups.append(cur)
        si = 0
        pending = None
        for grp_ in groups:
            banks = sets[si % len(sets)]
            si += 1
            cols = sum(x[2] for x in grp_)
            cbase = grp_[0][1]
            nk_t = max(1, min(2048 // cols, len(kcs)))
            ntile = (len(kcs) + nk_t - 1) // nk_t
            for ti in range(ntile):
                kk = kcs[ti * nk_t:(ti + 1) * nk_t]
                assert kk == list(range(kk[0], kk[0] + len(kk)))
                nk = len(kk)
                s = self.wslot
                self.wslot = (self.wslot + 1) % NS
                stg = self.fv(self.O_STG + s * 2048, nk * cols).rearrange("p (c n) -> p c n", n=cols)
                wr = self.bv(self.O_WR + s * 1024, nk * cols).rearrange("p (c n) -> p c n", n=cols)
                k.dma('sp', stg, wv[:, kk[0]:kk[0] + nk, cbase:cbase + cols], (), [self.b_wst[s]], dsem='w%d' % s)
                self.cp(wr, stg, [self.b_wst[s]], [self.b_wr[s]], ('dve', 'act', 'dve', 'act')[s])
                for gi, (bi, c0, m) in enumerate(grp_):
                    ps, pb = self.ps[banks[gi]], self.bps[banks[gi]]
                    for j in range(nk):
                        rhs, rb = rhs_fn(kk[j])
                        first = (ti == 0 and j == 0)
                        last = (ti == ntile - 1 and j == nk - 1)
                        self.mm(ps[0:m, 0:T], wr[:, j, c0 - cbase:c0 - cbase + m], rhs, first, last,
                                [self.b_wr[s], rb] + list(extra_R), [pb])
            if pending is not None:
                for (bi_, ps_, pb_) in pending:
                    evac(bi_, ps_, pb_)
            pending = [(bi, self.ps[banks[gi]][0:m, 0:T], self.bps[banks[gi]]) for gi, (bi, c0, m) in enumerate(grp_)]
        if pending is not None:
            for (bi_, ps_, pb_) in pending:
                evac(bi_, ps_, pb_)

    def sumsq_bc(self, src_fn, nch, T, out, outb, n, rows=128, ps_id=2):
        if not hasattr(self, 'b_sq'):
            self.b_sq = [Buf(), Buf()]
        ps, pb = self.ps[ps_id], self.bps[ps_id]
        for c in range(nch):
            src, sbuf = src_fn(c)
            s = c % 2
            sq = self.bv(self.O_SQ + s * 256, 512)
            self.act(sq[0:rows, 0:T], src, AF.Square, [sbuf], [self.b_sq[s]])
            self.mm(ps[:, 0:T], self.ones_b[0:rows, :], sq[0:rows, 0:T], c == 0, c == nch - 1, [self.b_sq[s], self.b_const], [pb])
        orow = out.shape[0]
        self.act(out, ps[0:orow, 0:T], AF.Sqrt, [pb, self.b_const], [outb], bias=self.epsc[0:orow, 0:1], scale=1.0 / n)
        self.recip(out, out, [outb], [outb])

    def x_rstd(self):
        T = self.T
        rstd = self.fv(self.O_RSTD, SEG)
        b = Buf()
        self.sumsq_bc(lambda c: (self.xT[:, c, 0:T], self.b_x), KC, T, rstd[:, 0:T], b, D)
        return rstd, b

    def xb_refresh(self, gidx):
        T = self.T
        rstd, b_rstd = self.x_rstd()
        tmpf = self.fv(self.O_STG, 2 * SEG).rearrange("p (s t) -> p s t", s=2)
        bt = [Buf(), Buf()]
        j = 0
        for c in range(KC):
            if c % 2 == 0:
                self.stt(self.xb[:, c, 0:T], self.xT[:, c, 0:T], self.gn[:, gidx, c:c + 1], rstd[:, 0:T], ALU.mult, ALU.mult,
                         [self.b_x, self.b_gn, b_rstd], [self.b_xb])
            else:
                sl = j % 2; j += 1
                self.act(tmpf[:, sl, 0:T], self.xT[:, c, 0:T], AF.Copy, [self.b_x, self.b_gn], [bt[sl]], scale=self.gn[:, gidx, c:c + 1])
                self.tt(self.xb[:, c, 0:T], tmpf[:, sl, 0:T], rstd[:, 0:T], ALU.mult, [bt[sl], b_rstd], [self.b_xb], 'pool')

    def resid_evac(self):
        T = self.T
        def ev(bi, ps, pb):
            self.tt(self.xT[:, bi, 0:T], ps, self.xT[:, bi, 0:T], ALU.add, [pb, self.b_x], [self.b_x])
        return ev

    def segment(self, grp, s, T, L):
        nc, k, d, o = self.nc, self.k, self.d, self.o
        self.grp, self.sidx, self.T, self.L = grp, s, T, L
        pos0 = s * SEG if grp == 0 else NR * SEG
        xsrc = d['xp'][s * SEG:(s + 1) * SEG, :] if grp == 0 else d['xs']
        k.barrier()
        tmp = self.fv(self.O_PH, D)
        tb = Buf()
        for t0 in range(0, T, 128):
            n = min(128, T - t0)
            k.dma('sp', tmp[0:n, :], xsrc[t0:t0 + n, :], (), [tb])
            for c4 in range(0, KC, 4):
                ps, pb = self.ps[(c4 // 4) % 2], self.bps[(c4 // 4) % 2]
                for j in range(4):
                    self.tr(ps[:, j * 128:j * 128 + n], tmp[0:n, (c4 + j) * 128:(c4 + j + 1) * 128], n, [tb], [pb])
                self.cp(self.xT[:, c4:c4 + 4, t0:t0 + n], ps[:, :].rearrange("p (j n) -> p j n", j=4)[:, :, 0:n], [pb], [self.b_x])
        self.dbg('xload')
        for layer in range(DEPTH):
            k.barrier()
            self.xb_refresh(layer)
            k.barrier()
            if layer < NA:
                self.mlstm(layer)
            else:
                self.mla(layer - NA)
            k.barrier()
            self.dbg('mixer%d' % layer)
            self.xb_refresh(4 + layer)
            k.barrier()
            self.ffn(layer)
            self.dbg('ffn%d' % layer)
            if layer == NA - 1:
                k.barrier()
                self.xb_refresh(8)
                k.barrier()
                self.shared_kv(pos0)
        k.barrier()
        ydst = o['yp'][s * SEG:(s + 1) * SEG, :] if grp == 0 else o['ys']
        for t0 in range(0, T, 128):
            n = min(128, T - t0)
            for c4 in range(0, KC, 4):
                ps, pb = self.ps[(c4 // 4) % 2], self.bps[(c4 // 4) % 2]
                for j in range(4):
                    self.tr(ps[0:n, j * 128:(j + 1) * 128], self.xT[:, c4 + j, t0:t0 + n], 128, [self.b_x], [pb])
                self.cp(tmp[0:n, c4 * 128:(c4 + 4) * 128], ps[0:n, :], [pb], [tb])
            k.dma('sp', ydst[t0:t0 + n, :], tmp[0:n, :], [tb], [self.b_out], dsem='out')

    def ffn(self, layer):
        nc, k, d, o = self.nc, self.k, self.d, self.o
        T, grp = self.T, self.grp
        P0 = self.O_PH
        ug = self.fv(P0, SEG + 2); b_ug = Buf()
        uv = self.fv(P0 + 514, SEG + 2); b_uv = Buf()
        cg = self.fv(P0 + 1028, SEG); b_cg = Buf()
        cv = self.fv(P0 + 1540, SEG); b_cv = Buf()
        sg4 = self.fv(P0 + 14300, 4 * SEG).rearrange("p (s t) -> p s t", s=4); b_sg4 = [Buf() for _ in range(4)]
        t1 = self.fv(P0 + 2564, 128); t2 = self.fv(P0 + 2692, 128)
        actT = self.bv(P0 + 3000, FC * SEG).rearrange("p (c t) -> p c t", c=FC); b_act = Buf()
        gcol = self.gn[:, 4 + layer, :]
        tl = self.tail[:, layer, grp]
        blocks = []; bmap = []
        for j0 in range(0, FC, 4):
            for j in range(j0, j0 + 4):
                blocks.append((j * 128, 128)); bmap.append((j, 0))
            for j in range(j0, j0 + 4):
                blocks.append((DFF + j * 128, 128)); bmap.append((j, 1))
        sgs = self.fv(self.O_SQ, 4 * SEG).rearrange("p (s t) -> p s t", s=4) if False else None

        def conv(u, ub, fc, out, outb):
            self.ts(out[:, 0:T], u[:, 0:T], self.cw[:, layer, 0, fc:fc + 1], self.cb[:, layer, fc:fc + 1], ALU.mult, ALU.add,
                    [ub, self.b_cw], [outb])
            for jj in (1, 2):
                self.stt(out[:, 0:T], u[:, jj:jj + T], self.cw[:, layer, jj, fc:fc + 1], out[:, 0:T], ALU.mult, ALU.add,
                         [ub, self.b_cw, outb], [outb])

        def evac(bi, ps, pb):
            j, isv = bmap[bi]
            fc = j + (FC if isv else 0)
            u, ub = (uv, b_uv) if isv else (ug, b_ug)
            cc, cb_ = (cv, b_cv) if isv else (cg, b_cg)
            sg, b_sg = sg4[:, j % 4, :], b_sg4[j % 4]
            if grp == 1:
                self.cp(u[:, 0:2], tl[:, :, fc], [self.b_tail], [ub], 'pool')
                self.cp(u[:, 2:2 + T], ps, [pb], [ub], 'act')
                self.cp(tl[:, :, fc], u[:, T:T + 2], [ub], [self.b_tail], 'pool')
                conv(u, ub, fc, cc, cb_)
                lo = 0
            else:
                self.cp(tl[:, :, fc], ps[:, T - 2:T], [pb], [self.b_tail], 'dve')
                self.cp(ufirst[:, fc, :], ps[:, 0:2], [pb], [b_uf], 'dve')
                self.ts(cc[:, 2:T], ps[:, 0:T - 2], self.cw[:, layer, 0, fc:fc + 1], self.cb[:, layer, fc:fc + 1], ALU.mult, ALU.add,
                        [pb, self.b_cw], [cb_])
                for jj in (1, 2):
                    self.stt(cc[:, 2:T], ps[:, jj:T - 2 + jj], self.cw[:, layer, jj, fc:fc + 1], cc[:, 2:T], ALU.mult, ALU.add,
                             [pb, self.b_cw, cb_], [cb_])
                lo = 2
            if not isv:
                self.act(sg[:, lo:T], cg[:, lo:T], AF.Silu, [b_cg], [b_sg])
            else:
                self.tt(actT[:, j, lo:T], sg[:, lo:T], cv[:, lo:T], ALU.mult, [b_sg, b_cv], [b_act], 'pool')

        ufirst = self.fv(P0 + 2820, 176).rearrange("p (c r) -> p c r", r=2); b_uf = Buf()
        self.linear(d['f_w_up'][layer], blocks, [(0, KC)], lambda c: (self.xb[:, c, 0:T], self.b_xb), T, evac)
        if grp == 0:
            k.barrier()
            tcur = self.tail[:, layer, 0].rearrange("p r c -> p (r c)")
            k.dma('sp', self.ts2[layer][0:128, :], tcur, [self.b_tail, self.b_tg[layer]], [self.b_ts2[layer]])
            k.collective(self.ts2[layer][:, :], self.tg[layer][:, :], GROUPS, [self.b_ts2[layer]], [self.b_tg[layer]])
            tgs = self.fv(P0, 1408).rearrange("p (i t c) -> p i t c", i=4, t=2); bfx = Buf()
            k.dma('sp', tgs, self.tg[layer].rearrange("(i t p) c -> p i t c", t=2, p=128), [self.b_tg[layer]], [bfx])
            halo = self.fv(P0 + 1408, 176)
            Rf = [bfx, self.b_gn, b_uf]
            self.ts(halo, tgs[:, 3, 1, :], self.cm[:, 28:29], None, ALU.mult, None, Rf, [bfx])
            for i in range(4):
                self.stt(halo, tgs[:, i, 0, :], self.cm[:, 24 + i:25 + i], halo, ALU.mult, ALU.add, Rf, [bfx])
            k.dma('sp', self.ts2[layer][128:256, :], tcur, [self.b_tail, self.b_tg[layer]], [self.b_ts2[layer]])
            h0 = halo[:, 0:88]; h1 = halo[:, 88:176]
            u0 = ufirst[:, :, 0]; u1 = ufirst[:, :, 1]
            w0, w1, w2 = (self.cw[:, layer, jj, :] for jj in range(3)); bb = self.cb[:, layer, :]
            c0 = self.fv(P0 + 1584, 88); c1 = self.fv(P0 + 1672, 88); ta = self.fv(P0 + 1760, 88); sgl = self.fv(P0 + 1848, 88)
            for (cc, x0, x1, x2) in ((c0, h0, h1, u0), (c1, h1, u0, u1)):
                self.tt(cc, w0, x0, ALU.mult, Rf, [bfx]); self.tt(cc, cc, bb, ALU.add, Rf, [bfx])
                self.tt(ta, w1, x1, ALU.mult, Rf, [bfx]); self.tt(cc, cc, ta, ALU.add, Rf, [bfx])
                self.tt(ta, w2, x2, ALU.mult, Rf, [bfx]); self.tt(cc, cc, ta, ALU.add, Rf, [bfx])
            for t, cc in ((0, c0), (1, c1)):
                self.act(sgl[:, 0:44], cc[:, 0:44], AF.Silu, Rf, [bfx])
                self.tt(actT[:, :, t], sgl[:, 0:44], cc[:, 44:88], ALU.mult, Rf, [b_act])
            k.barrier()
        self.linear(d['f_w_down'][layer], [(c * 128, 128) for c in range(KC)], [(0, FC)],
                    lambda c: (actT[:, c, 0:T], b_act), T, self.resid_evac(), extra_R=[self.b_x])
        if grp == 1 or self.sidx == NR - 1:
            dst = o['sconvo'] if grp == 1 else o['pconv']
            tb = Buf()
            tf = self.tail[:, layer, grp].rearrange("p r c -> p (r c)")
            ps, pb = self.ps[2], self.bps[2]
            self.tr(ps[0:128, 0:128], tf[:, 0:128], 128, [self.b_tail], [pb])
            self.tr(ps[0:48, 128:256], tf[:, 128:176], 128, [self.b_tail], [pb])
            self.cp(t1, ps[0:128, 0:128], [pb], [tb]); self.cp(t2[0:48, :], ps[0:48, 128:256], [pb], [tb])
            dv = [dst[layer, r].rearrange("(c p) -> c p", p=128) for r in range(2)]
            k.dma('sp', dv[0][0:88, :], t1[0:88, :], [tb], [self.b_out], dsem='out')
            k.dma('sp', dv[1][0:40, :], t1[88:128, :], [tb], [self.b_out], dsem='out')
            k.dma('sp', dv[1][40:88, :], t2[0:48, :], [tb], [self.b_out], dsem='out')

    def mlstm(self, layer):
        nc, k, d, o = self.nc, self.k, self.d, self.o
        T, L, grp = self.T, self.L, self.grp
        P0 = self.O_PH
        qT = self.bv(P0, 8 * SEG).rearrange("p (c t) -> p c t", c=8); b_q = Buf()
        kT = self.bv(P0 + 2048, 8 * SEG).rearrange("p (c t) -> p c t", c=8); b_k = Buf()
        vT = self.fv(P0 + 4096, 16 * SEG).rearrange("p (c t) -> p c t", c=16); b_v = Buf()
        gT = self.fv(P0 + 12288, SEG, 8); b_g = Buf()
        sgt2 = self.fv(self.O_SQ, 2 * SEG).rearrange("p (s t) -> p s t", s=2); b_sg2 = [Buf(), Buf()]
        O_SSTG = 6144
        O_GH = 10240
        S0 = P0 + 13312
        gcol = self.gn[:, layer, :]
        w_in = d['a_w_in'][layer]
        sq_, sv_ = H * DK, H * DV
        blocks = [(c * 128, 128) for c in range(8)] + [(sq_ + c * 128, 128) for c in range(8)] + \
                 [(2 * sq_ + c * 128, 128) for c in range(16)] + [(2 * sq_ + 2 * sv_, 8)]

        def evac(bi, ps, pb):
            e = 'act' if bi % 2 else 'dve'
            if bi < 8:
                self.cp(qT[:, bi, 0:T], ps, [pb], [b_q], e)
            elif bi < 16:
                self.cp(kT[:, bi - 8, 0:T], ps, [pb], [b_k], e)
            elif bi < 32:
                self.cp(vT[:, bi - 16, 0:T], ps, [pb], [b_v], e)
            else:
                self.ts(gT[:, 0:T], ps, self.gmisc[0:8, 54 + layer:55 + layer], None, ALU.add, None, [pb, self.b_gm], [b_g])

        self.linear(w_in, blocks, [(0, KC)], lambda c: (self.xb[:, c, 0:T], self.b_xb), T, evac)
        k.barrier()
        self.dbg('mproj')
        CT = self.fv(0, 4096).rearrange("p (c h v) -> p c h v", c=2, h=H); b_ct = Buf()
        CTb = self.bv(4096, 4096).rearrange("p (c h v) -> p c h v", c=2, h=H)
        nT = self.nst[:, layer, grp]
        nTb = self.nsb[:, layer, grp]
        mbc = self.mst[:, layer, grp, :]
        ghb = self.fv(O_GH, 2048).rearrange("p (h v) -> p h v", h=H); b_gh = Buf()
        k.dma('sp', ghb[0:L], d['a_g_head'][layer:layer + 1, :].broadcast_to([L, H * DV]).rearrange("p (h v) -> p h v", h=H), (), [b_gh])
        env = dict(layer=layer, qT=qT, kT=kT, vT=vT, gT=gT, CT=CT, CTb=CTb, nT=nT, nTb=nTb, mbc=mbc, ghb=ghb, S0=S0,
                   b_q=b_q, b_k=b_k, b_v=b_v, b_g=b_g, b_ct=b_ct, b_gh=b_gh)
        if grp == 1:
            stg = self.fv(O_SSTG, 4096).rearrange("p (h c k) -> p h c k", h=H, c=4); sb_ = Buf()
            k.dma('sp', stg, d['sC'][layer].rearrange("h (c p) k -> p h c k", p=128), (), [sb_])
            for h in range(H):
                for kc in range(2):
                    ps, pb = self.ps[(h * 2 + kc) % 2], self.bps[(h * 2 + kc) % 2]
                    for vc in range(4):
                        self.tr(ps[:, vc * 128:(vc + 1) * 128], stg[:, h, vc, kc * 128:(kc + 1) * 128], 128, [sb_], [pb])
                    self.cp(CT[:, kc, h, :], ps[:, :], [pb], [b_ct])
        else:
            self.ms(self.fv(0, 4096), 0.0, [b_ct], 'pool')
            self.ms(nT, 0.0, [self.b_nst], 'dve')
            self.ms(mbc, NEG, [self.b_mst], 'dve')
            self.ms(self.fsum[:], 0.0, [self.b_mst], 'dve')
            k.barrier()
            self.cp(self.bv(4096, 4096), self.fv(0, 4096), [b_ct], [b_ct])
            self.cp(nTb, nT, [self.b_nst], [self.b_nst])
            self.scan_chunks(env, True)
            k.barrier()
            self.exchange_state(layer, env)
            k.barrier()
        self.cp(self.bv(4096, 4096), self.fv(0, 4096), [b_ct], [b_ct])
        self.cp(nTb, nT, [self.b_nst], [self.b_nst])
        self.scan_chunks(env, False)
        k.barrier()
        self.dbg('scan')
        last = (grp == 1) or (self.sidx == NR - 1)
        if grp == 0:
            stt_ = self.fv(S0 + 200, 16); sbb = Buf()
            self.cp(stt_[:, 0:8].rearrange("p (c h) -> p c h", c=2), nT[:, :, :, 0], [self.b_nst], [sbb])
            self.cp(stt_[:, 8:12], mbc, [self.b_mst], [sbb])
            self.ms(stt_[:, 12:16], 0.0, [sbb], 'dve')
            for c in range(2):
                k.dma('sp', self.xsC[layer][1][c][:, :], self.fv(c * 2048, 2048), [b_ct, self.b_xg[layer]], [self.b_xs2[layer]])
            k.dma('sp', self.xsS[layer][128:256, :], stt_, [sbb, self.b_xg[layer]], [self.b_xs2[layer]])
        if last:
            Cd = (o['sCo'] if grp == 1 else o['pC'])[layer]
            nd = (o['sno'] if grp == 1 else o['pn'])[layer]
            md = (o['smo'] if grp == 1 else o['pm'])[layer]
            stg = self.fv(O_SSTG, 4096).rearrange("p (h c k) -> p h c k", h=H, c=4); sb_ = Buf()
            for h in range(H):
                for vc in range(4):
                    ps, pb = self.ps[vc % 2], self.bps[vc % 2]
                    for kc in range(2):
                        self.tr(ps[:, kc * 128:(kc + 1) * 128], CT[:, kc, h, vc * 128:(vc + 1) * 128], 128, [b_ct], [pb])
                    self.cp(stg[:, h, vc, :], ps[:, 0:256], [pb], [sb_])
            k.dma('sp', Cd.rearrange("h (c p) k -> p h c k", p=128), stg, [sb_], [self.b_out], dsem='out')
            nf = self.fv(S0, 8); nb = Buf()
            self.cp(nf.rearrange("p (h c) -> p c h", h=H), nT[:, :, :, 0], [self.b_nst], [nb])
            ps, pb = self.ps[2], self.bps[2]
            self.tr(ps[0:8, 0:128], nf, 128, [nb], [pb])
            nf2 = self.fv(S0 + 16, 128, 8)
            self.cp(nf2, ps[0:8, 0:128], [pb], [nb])
            k.dma('sp', nd.rearrange("(c p) -> c p", p=128), nf2, [nb], [self.b_out], dsem='out')
            k.dma('sp', md.rearrange("(o h) -> o h", o=1), mbc[0:1, :], [self.b_mst], [self.b_out], dsem='out')
        k.barrier()
        def evac_o(bi, ps, pb):
            sg2 = sgt2[:, bi % 2, 0:T]
            self.act(sg2, ps, AF.Sigmoid, [pb], [b_sg2[bi % 2]])
            self.tt(vT[:, bi, 0:T], vT[:, bi, 0:T], sg2, ALU.mult, [b_sg2[bi % 2], b_v], [b_v], 'pool' if bi % 2 else 'dve')
        self.linear(w_in, [(2 * sq_ + sv_ + c * 128, 128) for c in range(16)], [(0, KC)], lambda c: (self.xb[:, c, 0:T], self.b_xb), T, evac_o)
        hr = self.bv(S0, 16 * SEG).rearrange("p (c t) -> p c t", c=16); b_hr = Buf()
        k.barrier()
        for c in range(16):
            self.cp(hr[:, c, 0:T], vT[:, c, 0:T], [b_v], [b_hr], 'dve' if c % 2 else 'act')
        self.linear(d['a_w_out'][layer], [(c * 128, 128) for c in range(KC)], [(0, KC)], lambda c: (hr[:, c, 0:T], b_hr), T,
                    self.resid_evac(), extra_R=[self.b_x])

    def scan_chunks(self, env, state_only):
        nc, k = self.nc, self.k
        T, L = self.T, self.L
        layer = env['layer']
        qT, kT, vT, gT, CT, CTb, nT, nTb, mbc, ghb, S0 = (env[x] for x in ('qT', 'kT', 'vT', 'gT', 'CT', 'CTb', 'nT', 'nTb', 'mbc', 'ghb', 'S0'))
        b_q, b_k, b_v, b_g, b_ct, b_gh = (env[x] for x in ('b_q', 'b_k', 'b_v', 'b_g', 'b_ct', 'b_gh'))
        k_c = self.bv(S0, 1024); v_c = self.bv(S0 + 512, 2048); wv = self.bv(S0 + 1536, 2048)
        junk = self.fv(S0 + 1536, 512)
        hh = self.fv(S0 + 2560, 2048)
        sm_ = self.fv(S0 + 4608, 64)
        dg = self.fv(S0 + 4672, 512); dl = self.fv(S0 + 5184, 512); dT = self.fv(S0 + 5696, 512)
        sdT = self.bv(S0 + 6208, 512)
        qs = self.bv(S0 + 6464, 1024).rearrange("p (c t) -> p c t", c=8)
        bcs = self.fv(S0 + 6976, 32)
        w2r = self.bv(S0 + 7008, 2)
        bS = Buf('scan')
        P = self.ps; B = self.bps
        for c in range(T // L):
            c0 = c * L
            R = [bS, b_q, b_k, b_v, b_g, b_ct, b_gh, self.b_nst, self.b_mst, self.b_const]
            Wb = [bS]
            for g4 in range(2):
                pbf = P[0][0:L, :].bitcast(BF16)
                for j in range(4):
                    self.tr(pbf[:, j * 128:(j + 1) * 128], kT[:, g4 * 4 + j, c0:c0 + L], 128, R, [B[0]], bf=True)
                self.cp(k_c[0:L, g4 * 512:(g4 + 1) * 512], pbf[:, 0:512], [B[0]], Wb)
            for g4 in range(4):
                pp = 1 + g4 % 2
                for j in range(4):
                    self.tr(P[pp][0:L, j * 128:(j + 1) * 128], vT[:, g4 * 4 + j, c0:c0 + L], 128, R, [B[pp]])
                self.cp(v_c[0:L, g4 * 512:(g4 + 1) * 512], P[pp][0:L, :], [B[pp]], Wb, 'act')
            self.tr(P[3][0:L, 0:8], gT[:, c0:c0 + L], 8, R, [B[3]])
            gi = sm_[0:L, 0:4]; sp_ = sm_[0:L, 4:8]; b_ = sm_[0:L, 8:12]; a_ = sm_[0:L, 12:16]
            il = sm_[0:L, 16:20]; mt = sm_[0:L, 20:24]; mb = sm_[0:L, 20:28]; u_ = sm_[0:L, 28:32]
            iw = sm_[0:L, 32:36]; en = sm_[0:L, 36:40]; w_ = sm_[0:L, 40:44]; den = sm_[0:L, 44:48]
            ss = sm_[0:L, 48:52]; rr = sm_[0:L, 52:56]; mloc = sm_[0:L, 56:60]; tmp4 = sm_[0:L, 60:64]
            self.cp(gi, P[3][0:L, 0:4], [B[3]], Wb)
            self.act(sp_, P[3][0:L, 4:8], AF.Exp, [B[3]], Wb, scale=-1.0)
            self.act(sp_, sp_, AF.Ln, [bS], Wb, bias=1.0)
            self.mm(P[3][0:L, 8:12], self.uneg[L][:, :], sp_, True, True, R, [B[3]])
            self.cp(b_, P[3][0:L, 8:12], [B[3]], Wb)
            self.cp(sm_[0:L, 24:28], b_, R, Wb)
            self.tt(a_, gi, b_, ALU.subtract, R, Wb)
            idl = self.ident[0:L, 0:L]
            dg3 = dg[0:L, 0:4 * L].rearrange("p (h s) -> p h s", h=H)
            dl3 = dl[0:L, 0:4 * L].rearrange("p (h s) -> p h s", h=H)
            dT3 = dT[0:L, 0:4 * L].rearrange("p (h s) -> p h s", h=H)
            sd3 = sdT[0:L, 0:4 * L].rearrange("p (h s) -> p h s", h=H)

            def rowbc(col4, ps_ap, psb, lhs):
                self.tt(dg3, bc(idl.unsqueeze(1), [L, H, L]), bc(col4.unsqueeze(2), [L, H, L]), ALU.mult, R, Wb)
                self.mm(ps_ap, lhs, dg[0:L, 0:4 * L], True, True, R, [psb])
            rowbc(a_, P[4][0:L, 0:4 * L], B[4], self.ones_f[0:L, 0:L])
            p4 = P[4][0:L, 0:4 * L].rearrange("p (h s) -> p h s", h=H)
            self.tt(dl3, p4, bc(b_.unsqueeze(2), [L, H, L]), ALU.add, [B[4]] + R, Wb)
            self.tt(dl3, dl3, bc(self.mask[L][:, :].unsqueeze(1), [L, H, L]), ALU.add, R, Wb)
            self.k.op('dve', lambda: nc.vector.tensor_reduce(out=mloc, in_=dl3, axis=AX.X, op=ALU.max), R, Wb)
            self.tt(il, b_, mbc[0:L, :], ALU.add, R, Wb)
            self.tt(mt, il, mloc, ALU.max, R, Wb)
            if not state_only:
                self.tt(u_, b_, mt, ALU.subtract, R, Wb)
                self.tt(tmp4, il, mt, ALU.subtract, R, Wb)
                self.act(iw, tmp4, AF.Exp, R, Wb)
                self.act(en, mt, AF.Exp, R, Wb, scale=-1.0)
                rowbc(u_, P[4][0:L, 0:4 * L], B[4], self.ones_f[0:L, 0:L])
                self.tt(dT3, p4, bc(a_.unsqueeze(2), [L, H, L]), ALU.add, [B[4]] + R, Wb)
                self.tt(dT3, dT3, bc(self.maskT[L][:, :].unsqueeze(1), [L, H, L]), ALU.add, R, Wb)
                self.act(dT[0:L, 0:4 * L], dT[0:L, 0:4 * L], AF.Exp, R, Wb)
                for h in range(H):
                    for kc in range(2):
                        self.mm(P[5][0:L, h * L:(h + 1) * L], kT[:, h * 2 + kc, c0:c0 + L], qT[:, h * 2 + kc, c0:c0 + L], kc == 0, kc == 1, R, [B[5]])
                self.stt(sdT[0:L, 0:4 * L], P[5][0:L, 0:4 * L], DK ** -0.5, dT[0:L, 0:4 * L], ALU.mult, ALU.mult, [B[5]] + R, Wb)
                rowbc(iw, P[4][:, 0:4 * L], B[4], self.ones_f[0:L, :])
                pw = P[4][:, 0:4 * L].rearrange("p (h t) -> p h t", h=H)
                for kc in range(2):
                    qv = qT[:, :, c0:c0 + L].rearrange("p (h c) t -> p h c t", c=2)[:, :, kc, :]
                    qsv = qs[:, :, 0:L].rearrange("p (h c) t -> p h c t", c=2)[:, :, kc, :]
                    self.tt(qsv, qv, pw, ALU.mult, [B[4]] + R, Wb)
                for h in range(H):
                    pn_, bn_ = P[6 + h % 2], B[6 + h % 2]
                    self.mm(pn_[0:L, :], sd3[:, h, :], v_c[0:L, h * DV:(h + 1) * DV], True, False, R, [bn_])
                    for kc in range(2):
                        self.mm(pn_[0:L, :], qs[:, h * 2 + kc, 0:L], CTb[:, kc, h, :], False, kc == 1, R, [bn_])
                    self.mm(P[3][0:L, 16 + 2 * h:18 + 2 * h], sd3[:, h, :], self.ones_b[0:L, 0:2], True, False, R, [B[3]])
                    for kc in range(2):
                        self.mm(P[3][0:L, 16 + 2 * h:18 + 2 * h], qs[:, h * 2 + kc, 0:L], nTb[:, kc, h, :], False, kc == 1, R, [B[3]])
                    qn = P[3][0:L, 16 + 2 * h:17 + 2 * h]
                    self.act(den[:, h:h + 1], qn, AF.Abs, [B[3]] + R, Wb)
                    self.tt(den[:, h:h + 1], den[:, h:h + 1], en[:, h:h + 1], ALU.max, R, Wb)
                    self.recip(den[:, h:h + 1], den[:, h:h + 1], R, Wb)
                    self.act(junk[0:L, :], pn_[0:L, :], AF.Square, [bn_] + R, Wb, scale=den[:, h:h + 1], accum=ss[:, h:h + 1])
                    self.act(rr[:, h:h + 1], ss[:, h:h + 1], AF.Sqrt, R, Wb, bias=self.epsc[0:L, 0:1], scale=1.0 / DV)
                    self.recip(rr[:, h:h + 1], rr[:, h:h + 1], R, Wb)
                    self.tt(rr[:, h:h + 1], rr[:, h:h + 1], den[:, h:h + 1], ALU.mult, R, Wb)
                    self.stt(hh[0:L, h * DV:(h + 1) * DV], pn_[0:L, :], rr[:, h:h + 1], ghb[0:L, h, :], ALU.mult, ALU.mult, [bn_] + R, Wb)
            self.mm(P[3][:, 32:40], self.sel[L][:, :], mb, True, True, R, [B[3]])
            self.cp(bcs[:, 0:8], P[3][:, 32:40], [B[3]], Wb)
            mnew = bcs[:, 0:4]; blast = bcs[:, 4:8]; dec = bcs[:, 8:12]; t12 = bcs[:, 12:16]
            self.tt(self.fsum[:], self.fsum[:], blast, ALU.add, R, [self.b_mst, bS])
            self.tt(t12, blast, mnew, ALU.subtract, R, Wb)
            self.tt(dec, t12, mbc, ALU.add, R, Wb)
            self.act(dec, dec, AF.Exp, R, Wb)
            self.tt(w_, a_, t12[0:L, :], ALU.add, R, Wb)
            self.act(w_, w_, AF.Exp, R, Wb)
            self.ts(w_, w_, DK ** -0.5, None, ALU.mult, None, R, Wb)
            self.tt(wv[0:L, :].rearrange("p (h v) -> p h v", h=H), v_c[0:L, :].rearrange("p (h v) -> p h v", h=H),
                    bc(w_.unsqueeze(2), [L, H, DV]), ALU.mult, R, Wb)
            for h in range(H):
                self.cp(w2r[0:L, :], bc(w_[:, h:h + 1], [L, 2]), R, Wb)
                for kc in range(2):
                    pc, bcb = P[6 + kc], B[6 + kc]
                    self.mm(pc[:, :], k_c[0:L, h * DK + kc * 128:h * DK + (kc + 1) * 128], wv[0:L, h * DV:(h + 1) * DV], True, True, R, [bcb])
                    self.stt(CT[:, kc, h, :], CT[:, kc, h, :], dec[:, h:h + 1], pc[:, :], ALU.mult, ALU.add, [bcb, b_ct] + R, [b_ct])
                    if not state_only:
                        self.cp(CTb[:, kc, h, :], CT[:, kc, h, :], [b_ct], [b_ct], 'act')
                    self.mm(P[3][:, 48:50], k_c[0:L, h * DK + kc * 128:h * DK + (kc + 1) * 128], w2r[0:L, :], True, True, R, [B[3]])
                    self.stt(nT[:, kc, h, :], nT[:, kc, h, :], dec[:, h:h + 1], P[3][:, 48:50], ALU.mult, ALU.add,
                             [B[3], self.b_nst] + R, [self.b_nst])
                    if not state_only:
                        self.cp(nTb[:, kc, h, :], nT[:, kc, h, :], [self.b_nst], [self.b_nst])
            self.cp(mbc, mnew, R, [self.b_mst])
            if not state_only:
                for g4 in range(4):
                    pp = 4 + g4 % 2
                    for j in range(4):
                        self.tr(P[pp][:, j * L:(j + 1) * L], hh[0:L, (g4 * 4 + j) * 128:(g4 * 4 + j + 1) * 128], L, R, [B[pp]])
                    self.cp(vT[:, g4 * 4:g4 * 4 + 4, c0:c0 + L], P[pp][:, 0:4 * L].rearrange("p (j t) -> p j t", j=4), [B[pp]], [b_v, bS])

    def exchange_state(self, layer, env):
        nc, k = self.nc, self.k
        CT, nT, mbc, S0, b_ct = env['CT'], env['nT'], env['mbc'], env['S0'], env['b_ct']
        grp = self.grp
        sc = self.fv(S0 + 300, 600)
        bsc = Buf()
        stt_ = sc[:, 0:16]
        self.cp(stt_[:, 0:8].rearrange("p (c h) -> p c h", c=2), nT[:, :, :, 0], [self.b_nst], [bsc])
        self.cp(stt_[:, 8:12], mbc, [self.b_mst], [bsc])
        self.cp(stt_[:, 12:16], self.fsum[:], [self.b_mst], [bsc])
        for c in range(2):
            k.dma('sp', self.xsC[layer][0][c][:, :], self.fv(c * 2048, 2048), [b_ct, self.b_xg[layer]], [self.b_xs2[layer]])
        k.dma('sp', self.xsS[layer][0:128, :], stt_, [bsc, self.b_xg[layer]], [self.b_xs2[layer]])
        for t in range(2):
            for c in range(2):
                k.collective(self.xsC[layer][t][c][:, :], self.xgC[layer][t][c][:, :], GROUPS, [self.b_xs2[layer]], [self.b_xg[layer]])
        k.collective(self.xsS[layer][:, :], self.xgS[layer][:, :], GROUPS, [self.b_xs2[layer]], [self.b_xg[layer]])
        sg = sc[:, 16:16 + 128].rearrange("p (i t c) -> p i t c", i=4, t=2)
        k.dma('sp', sg, self.xgS[layer].rearrange("(i t p) c -> p i t c", t=2, p=128), [self.b_xg[layer]], [bsc])
        cm = self.cm
        F3 = sg[:, :, 0, 12:16]
        m3 = sg[:, :, 0, 8:12]
        mcar = sg[:, 3, 1, 8:12]
        Rr = [bsc, self.b_gn]
        T1 = sc[:, 144:208].rearrange("p (i h l) -> p i h l", i=4, h=4)
        Mv = cm[:, 8:24].rearrange("p (i l) -> p i l", i=4)
        self.tt(T1, bc(Mv.unsqueeze(2), [128, 4, 4, 4]), bc(F3.rearrange("p l h -> p h l").unsqueeze(1), [128, 4, 4, 4]), ALU.mult, Rr, [bsc])
        G = sc[:, 208:224].rearrange("p (i h) -> p i h", i=4)
        self.k.op('dve', lambda: nc.vector.tensor_reduce(out=G, in_=T1, axis=AX.X, op=ALU.add), Rr, [bsc])
        E = sc[:, 224:240].rearrange("p (i h) -> p i h", i=4)
        self.tt(E, m3, G, ALU.add, Rr, [bsc])
        self.tt(E, E, bc(cm[:, 0:4].unsqueeze(2), [128, 4, 4]), ALU.add, Rr, [bsc])
        T2 = sc[:, 240:256].rearrange("p (h l) -> p h l", h=4)
        self.tt(T2, bc(cm[:, 4:8].unsqueeze(1), [128, 4, 4]), F3.rearrange("p l h -> p h l"), ALU.mult, Rr, [bsc])
        Ec = sc[:, 256:260]
        self.k.op('dve', lambda: nc.vector.tensor_reduce(out=Ec, in_=T2, axis=AX.X, op=ALU.add), Rr, [bsc])
        self.tt(Ec, Ec, mcar, ALU.add, Rr, [bsc])
        mx = sc[:, 260:264]
        self.k.op('dve', lambda: nc.vector.tensor_reduce(out=mx, in_=E.rearrange("p i h -> p h i"), axis=AX.X, op=ALU.max), Rr, [bsc])
        min_ = sc[:, 264:268]
        self.tt(min_, mx, Ec, ALU.max, Rr, [bsc])
        Wt = sc[:, 268:284].rearrange("p (i h) -> p i h", i=4)
        self.tt(Wt, E, bc(min_.unsqueeze(1), [128, 4, 4]), ALU.subtract, Rr, [bsc])
        self.act(sc[:, 268:284], sc[:, 268:284], AF.Exp, Rr, [bsc])
        Wc = sc[:, 284:288]
        self.tt(Wc, Ec, min_, ALU.subtract, Rr, [bsc])
        self.act(Wc, Wc, AF.Exp, Rr, [bsc])
        nacc = sc[:, 288:296].rearrange("p (c h) -> p c h", c=2)
        ntmp = sc[:, 296:304].rearrange("p (c h) -> p c h", c=2)
        self.tt(nacc, sg[:, 3, 1, 0:8].rearrange("p (c h) -> p c h", c=2), bc(Wc.unsqueeze(1), [128, 2, 4]), ALU.mult, Rr, [bsc])
        for i in range(4):
            self.tt(ntmp, sg[:, i, 0, 0:8].rearrange("p (c h) -> p c h", c=2), bc(Wt[:, i, :].unsqueeze(1), [128, 2, 4]), ALU.mult, Rr, [bsc])
            self.tt(nacc, nacc, ntmp, ALU.add, Rr, [bsc])
        for e in range(2):
            self.cp(nT[:, :, :, e], nacc, Rr, [self.b_nst])
        self.cp(mbc, min_, Rr, [self.b_mst])
        stg = self.fv(6144, 4096).rearrange("p (c h v) -> p c h v", c=2, h=H); bst = Buf()
        srcs = [(1, 3, None)] + [(0, i, i) for i in range(4)]
        for si, (tt_, rk, i) in enumerate(srcs):
            for c in range(2):
                k.dma('sp', self.fv(6144 + c * 2048, 2048),
                      self.xgC[layer][tt_][c][rk * 128:(rk + 1) * 128, :], [self.b_xg[layer]], [bst])
            for kc in range(2):
                for h in range(H):
                    if i is None:
                        self.ts(CT[:, kc, h, :], stg[:, kc, h, :], Wc[:, h:h + 1], None, ALU.mult, None, [bst] + Rr, [b_ct])
                    else:
                        self.stt(CT[:, kc, h, :], stg[:, kc, h, :], Wt[:, i, h:h + 1], CT[:, kc, h, :], ALU.mult, ALU.add, [bst, b_ct] + Rr, [b_ct])

    def rope_tables(self, pos0, T, o_cs):
        cs = self.fv(o_cs, SEG, 64); sn = self.fv(o_cs + 512, SEG, 64)
        self.b_ropet = Buf()
        self.k.dma('sp', cs[:, 0:T], self.d['cos2'][:, pos0:pos0 + T], (), [self.b_ropet])
        self.k.dma('sp', sn[:, 0:T], self.d['sin2'][:, pos0:pos0 + T], (), [self.b_ropet])
        self.rope_cs, self.rope_sn = cs, sn

    def rope(self, dst, dstb, src, srcb, T, o_tmp, slots=2):
        if not hasattr(self, 'b_rope'):
            self.b_rope = [Buf(), Buf()]
            self.rope_i = 0
        i = self.rope_i % slots
        self.rope_i += 1
        t1 = self.fv(o_tmp + i * 1024, SEG, 64); t2 = self.fv(o_tmp + i * 1024 + 512, SEG, 64)
        tb = self.b_rope[i]
        ps, pb = self.ps[3], self.bps[3]
        self.mm(ps[0:64, 0:T], self.perm[:, :], src, True, True, [srcb, self.b_const], [pb])
        self.tt(t1[:, 0:T], ps[0:64, 0:T], self.rope_sn[:, 0:T], ALU.mult, [pb, self.b_ropet], [tb])
        self.tt(t2[:, 0:T], src, self.rope_cs[:, 0:T], ALU.mult, [srcb, self.b_ropet], [tb], 'pool')
        self.tt(dst, t1[:, 0:T], t2[:, 0:T], ALU.add, [tb], [dstb])

    def shared_kv(self, pos0):
        nc, k, d, o = self.nc, self.k, self.d, self.o
        T, grp, s = self.T, self.grp, self.sidx
        P0 = self.O_PH
        zT = self.fv(P0, 5 * SEG).rearrange("p (c t) -> p c t", c=5); b_z = Buf()
        cTf = self.fv(P0 + 2560, 4 * SEG).rearrange("p (c t) -> p c t", c=4); b_c = Buf()
        cTb = self.bv(P0 + 4608, 4 * SEG).rearrange("p (c t) -> p c t", c=4)
        kpT = self.fv(P0 + 5632, SEG, 64); b_kp = Buf()
        kpn = self.bv(P0 + 6144, SEG, 64)
        r2 = self.fv(P0 + 6400, SEG); b_r2 = Buf()
        tm = self.fv(P0 + 6912, 576); ld = self.fv(P0 + 7488, 576)
        self.o_kv = P0 + 8300
        blocks = [(c * 128, 128) for c in range(4)] + [(KVL, 64)]

        def evac(bi, ps, pb):
            m = 128 if bi < 4 else 64
            self.cp(zT[0:m, bi, 0:T], ps, [pb], [b_z], 'act' if bi % 2 else 'dve')
        self.linear(d['kv_w_down'], blocks, [(0, KC)], lambda c: (self.xb[:, c, 0:T], self.b_xb), T, evac)
        self.sumsq_bc(lambda c: (zT[:, c, 0:T], b_z), 4, T, r2[:, 0:T], b_r2, KVL)
        for c in range(4):
            self.stt(cTf[:, c, 0:T], zT[:, c, 0:T], self.gmisc[:, 16 + c:17 + c], r2[:, 0:T], ALU.mult, ALU.mult, [b_z, b_r2, self.b_gm], [b_c])
            self.cp(cTb[:, c, 0:T], cTf[:, c, 0:T], [b_c], [b_c], 'act')
        self.sumsq_bc(lambda c: (zT[0:64, 4, 0:T], b_z), 1, T, r2[0:64, 0:T], b_r2, ROPE, rows=64)
        self.stt(kpn[:, 0:T], zT[0:64, 4, 0:T], self.gmisc[0:64, 20:21], r2[0:64, 0:T], ALU.mult, ALU.mult, [b_z, b_r2, self.b_gm], [b_kp])
        self.rope_tables(pos0, T, P0 + 18100)
        self.rope(kpT[:, 0:T], b_kp, kpn[:, 0:T], b_kp, T, P0 + 19124, slots=1)
        k0 = 0 if grp == 0 else PAST
        kb = self.b_kvloc if grp == 0 else self.b_kvd[grp]
        kpb = self.bv(P0 + 8000, SEG, 64); b_kpb = Buf()
        self.cp(kpb[:, 0:T], kpT[:, 0:T], [b_kp], [b_kpb])
        if grp == 0:
            k.dma('sp', self.KRp[:, 0:T], kpb[:, 0:T], [b_kpb] + [self.b_kvall[r] for r in range(NR)], [kb], dsem='kv')
        else:
            k.dma('sp', self.kr_dram[grp][:, k0:k0 + T], kpb[:, 0:T], [b_kpb], [kb], dsem='kv')
        cdst = (o['p_ckv'][s * SEG:(s + 1) * SEG] if grp == 0 else o['s_ckv'])
        kdst = (o['p_kpe'][s * SEG:(s + 1) * SEG] if grp == 0 else o['s_kpe'])
        tb = Buf()
        for t0 in range(0, T, 128):
            n = min(128, T - t0)
            ps, pb = self.ps[2], self.bps[2]
            for c in range(4):
                self.tr(ps[0:n, c * 128:(c + 1) * 128], cTf[:, c, t0:t0 + n], 128, [b_c], [pb])
            self.cp(tm[0:n, 0:512], ps[0:n, :], [pb], [tb])
            ps, pb = self.ps[3], self.bps[3]
            self.tr(ps[0:n, 0:64], kpT[:, t0:t0 + n], 64, [b_kp], [pb])
            self.cp(tm[0:n, 512:576], ps[0:n, 0:64], [pb], [tb])
            k.dma('sp', cdst[t0:t0 + n, :], tm[0:n, 0:512], [tb], [self.b_out], dsem='out')
            k.dma('sp', kdst[t0:t0 + n, :], tm[0:n, 512:576], [tb], [self.b_out], dsem='out')
        self.kv_up(cTb, b_c, T, k0)
        if grp == 0:
            k.barrier()
            for j in range(2):
                k.collective(self.KP[j][:, :], self.KG[s][j][:, :], GROUPS, [self.b_kvloc], [self.b_kvall[s]])
                k.collective(self.VP[j][:, :], self.VG[s][j][:, :], GROUPS, [self.b_kvloc], [self.b_kvall[s]])
            k.collective(self.KRp[:, :], self.KRG[s][:, :], GROUPS, [self.b_kvloc], [self.b_kvall[s]])
        if grp == 1:
            k.barrier()
            lb = Buf()
            for blk in range(PAST // SEG):
                for t0 in range(0, SEG, 128):
                    r0 = blk * SEG + t0
                    k.dma('sp', ld[:, 0:512], d['ckv'][r0:r0 + 128, :], (), [lb])
                    k.dma('sp', ld[:, 512:576], d['kpe'][r0:r0 + 128, :], (), [lb])
                    ps, pb = self.ps[2], self.bps[2]
                    for c in range(4):
                        self.tr(ps[:, c * 128:(c + 1) * 128], ld[:, c * 128:(c + 1) * 128], 128, [lb], [pb])
                    self.cp(cTb[:, :, t0:t0 + 128], ps[:, :].rearrange("p (c t) -> p c t", c=4), [pb], [b_c])
                    ps, pb = self.ps[3], self.bps[3]
                    self.tr(ps[0:64, 0:128], ld[:, 512:576], 128, [lb], [pb])
                    self.cp(kpT[:, t0:t0 + 128], ps[0:64, 0:128], [pb], [b_kp])
                self.cp(kpb[:, 0:SEG], kpT[:, 0:SEG], [b_kp], [b_kpb])
                k.dma('sp', self.kr_dram[1][:, blk * SEG:(blk + 1) * SEG], kpb[:, 0:SEG], [b_kpb], [kb], dsem='kv')
                self.kv_up(cTb, b_c, SEG, blk * SEG)

    def kv_up(self, cTb, b_c, T, k0):
        nc, k, d = self.nc, self.k, self.d
        grp = self.grp
        kb = self.b_kvloc if grp == 0 else self.b_kvd[grp]
        O = self.o_kv
        k.barrier()
        kn = self.fv(O, 2 * SEG).rearrange("p (s t) -> p s t", s=2); bkn = [Buf(), Buf()]
        r3s = [self.fv(O + 1024, SEG), self.fv(O + 9216, SEG)]; b_r3s = [Buf(), Buf()]
        wvs = self.bv(O + 1536, 4 * 2048).rearrange("p (c n) -> p c n", c=4); b_wv = Buf()
        stg = self.fv(O + 5632, 2048); b_st = Buf()
        vt = self.bv(O + 7680, 2048); b_vt = Buf()
        knh = self.bv(O + 8704, 2 * SEG).rearrange("p (s t) -> p s t", s=2)
        blocks = [(h * 256, 128) for h in range(BH)]

        def evac(bi, ps, pb):
            sl_ = bi % 2
            r3, b_r3 = r3s[sl_], b_r3s[sl_]
            self.cp(kn[:, sl_, 0:T], ps, [pb], [bkn[sl_]], 'act')
            self.sumsq_bc(lambda c: (kn[:, sl_, 0:T], bkn[sl_]), 1, T, r3[:, 0:T], b_r3, NOPE, ps_id=3 - sl_)
            self.stt(kn[:, sl_, 0:T], kn[:, sl_, 0:T], self.gmisc[:, 21:22], r3[:, 0:T], ALU.mult, ALU.mult, [bkn[sl_], b_r3, self.b_gm], [bkn[sl_]])
            self.cp(knh[:, sl_, 0:T], kn[:, sl_, 0:T], [bkn[sl_]], [bkn[sl_]], 'pool')
            kdst = self.KP[bi // 8][(bi % 8) * 128:(bi % 8 + 1) * 128, 0:T] if grp == 0 else self.kn_dram[grp][bi, :, k0:k0 + T]
            k.dma('sp', kdst, knh[:, sl_, 0:T], [bkn[sl_]], [kb], dsem='kv')
        self.linear(d['kv_w_up'], blocks, [(0, 4)], lambda c: (cTb[:, c, 0:T], b_c), T, evac, sets=[[0], [1], [4], [5]])
        wv_ = d['kv_w_up'].rearrange("(c p) (h two x) -> p c h two x", p=128, two=2, x=128)
        for c in range(4):
            k.dma('sp', stg.rearrange("p (h x) -> p h x", h=BH), wv_[:, c, :, 1, :], (), [b_st])
            self.cp(wvs[:, c, :], stg, [b_st], [b_wv])
        for t0 in range(0, T, 128):
            n = min(128, T - t0)
            for q4 in range(4):
                ps, pb = self.ps[q4 % 2], self.bps[q4 % 2]
                for c in range(4):
                    self.mm(ps[0:n, :], cTb[:, c, t0:t0 + n], wvs[:, c, q4 * 512:(q4 + 1) * 512], c == 0, c == 3, [b_c, b_wv], [pb])
                self.cp(vt[0:n, q4 * 512:(q4 + 1) * 512], ps[0:n, :], [pb], [b_vt], 'act' if q4 % 2 else 'dve')
            if grp == 0:
                for q in range(2):
                    k.dma('sp', self.VP[q][t0:t0 + n, :], vt[0:n, q * 1024:(q + 1) * 1024], [b_vt], [kb], dsem='kv')
            else:
                k.dma('sp', self.v_dram[grp][k0 + t0:k0 + t0 + n, :], vt[0:n, :], [b_vt], [kb], dsem='kv')

    def mla(self, j):
        nc, k, d, o = self.nc, self.k, self.d, self.o
        T, grp, s = self.T, self.grp, self.sidx
        pos0 = s * SEG if grp == 0 else NR * SEG
        P0 = self.O_PH
        cq = self.bv(P0, 6 * SEG).rearrange("p (c t) -> p c t", c=6); b_cq = Buf()
        r2 = self.fv(P0 + 1536, SEG); b_r2 = Buf()
        QN = self.bv(P0 + 2048, 16 * SEG).rearrange("p (c t) -> p c t", c=16); b_qn = Buf()
        QR = self.bv(P0 + 6144, 16 * SEG, 64).rearrange("p (c t) -> p c t", c=16); b_qr = Buf()
        r3 = self.fv(P0 + 10240, SEG); b_r3 = Buf()
        qtmp = self.fv(P0 + 10752, SEG); b_qt = Buf()
        qrn = self.bv(P0 + 11264, SEG, 64); b_qrn = Buf()
        O_CS = P0 + 11520
        rd = self.fv(P0 + 12544, SEG); b_rd = Buf()
        cqg = self.bv(P0 + 13100, 6 * SEG).rearrange("p (c t) -> p c t", c=6)

        def evac(bi, ps, pb):
            self.cp(cq[:, bi, 0:T], ps, [pb], [b_cq], 'act')
            self.ts(cqg[:, bi, 0:T], ps, self.gmisc[:, 6 * j + bi:6 * j + bi + 1], None, ALU.mult, None, [pb, self.b_gm], [b_cq])
        self.linear(d['b_w_dq'][j], [(c * 128, 128) for c in range(6)], [(0, KC)], lambda c: (self.xb[:, c, 0:T], self.b_xb), T, evac)
        self.sumsq_bc(lambda c: (cq[:, c, 0:T], b_cq), 6, T, r2[:, 0:T], b_r2, QL)
        blocks = []
        for h in range(BH):
            blocks.append((h * 192, 128)); blocks.append((h * 192 + 128, 64))

        def evac2(bi, ps, pb):
            h, isr = bi // 2, bi % 2
            m = 64 if isr else 128
            self.tt(qtmp[0:m, 0:T], ps, r2[0:m, 0:T], ALU.mult, [pb, b_r2], [b_qt])
            self.sumsq_bc(lambda c: (qtmp[0:m, 0:T], b_qt), 1, T, r3[0:m, 0:T], b_r3, m, rows=m, ps_id=3)
            if not isr:
                self.stt(QN[:, h, 0:T], qtmp[:, 0:T], self.gmisc[:, 12 + j:13 + j], r3[:, 0:T], ALU.mult, ALU.mult, [b_qt, b_r3, self.b_gm], [b_qn])
            else:
                self.stt(qrn[:, 0:T], qtmp[0:64, 0:T], self.gmisc[0:64, 14 + j:15 + j], r3[0:64, 0:T], ALU.mult, ALU.mult, [b_qt, b_r3, self.b_gm], [b_qrn])
                self.rope(QR[:, h, 0:T], b_qr, qrn[:, 0:T], b_qrn, T, P0 + 14700)
        self.rope_tables(pos0, T, O_CS)
        self.linear(d['b_w_uq'][j], blocks, [(0, 6)], lambda c: (cqg[:, c, 0:T], b_cq), T, evac2, sets=[[0, 1], [4, 5], [6, 7]])
        k.barrier()
        KB = 1024 if grp == 1 else SEG
        knb = self.bv(0, 2 * 1024).rearrange("p (s t) -> p s t", s=2); b_knb = [Buf(), Buf()]
        vb = self.bv(1024, 2 * 1024).rearrange("p (s t x) -> p s t x", s=2, x=128); b_vb = [Buf(), Buf()]
        krall = self.bv(3072, 9 * SEG, 64); b_kr = Buf()
        dacc = self.fv(5376, SEG); daccb = self.bv(5888, SEG); b_da = Buf()
        blocks = []
        if grp == 1:
            nkeys = PAST + DS
            for kk0 in range(0, nkeys, KB):
                nk = min(KB, nkeys - kk0)
                blocks.append((lambda h, kk0=kk0, nk=nk: self.kn_dram[1][h, :, kk0:kk0 + nk],
                               self.kr_dram[1][:, kk0:kk0 + nk],
                               lambda h, t, n, kk0=kk0: self.v_dram[1][kk0 + t * 128:kk0 + t * 128 + n, h * 128:(h + 1) * 128],
                               nk, False, None, self.b_kvd[1]))
        else:
            def mk(KS, KRS, VS, i, diag, bias, dep):
                return (lambda h: KS[h // 8][i * 1024 + (h % 8) * 128:i * 1024 + (h % 8 + 1) * 128, :],
                        KRS[i * ROPE:(i + 1) * ROPE, :],
                        lambda h, t, n: VS[h // 8][i * SEG + t * 128:i * SEG + t * 128 + n, (h % 8) * 128:(h % 8 + 1) * 128],
                        SEG, diag, bias, dep)
            for rdi in range(s):
                for i in range(4):
                    blocks.append(mk(self.KG[rdi], self.KRG[rdi], self.VG[rdi], i, False, None, self.b_kvall[rdi]))
            for i in range(4):
                blocks.append(mk(self.KG[s], self.KRG[s], self.VG[s], i, False, 32 + i, self.b_kvall[s]))
            blocks.append(mk(self.KP, self.KRp, self.VP, 0, True, None, self.b_kvloc))
        ntot = sum((blk[3] + 127) // 128 for blk in blocks)
        kro = 0
        for (knf, krap, vf, nk, diag, biascol, dep) in blocks:
            k.dma('sp', krall[:, kro:kro + nk], krap, [dep], [b_kr], dsem='ld')
            kro += nk
        DEPTH_ = 3
        sbanks = [0, 1, 6, 7]
        pTs = [self.bv(2048 + 256 * i, SEG) for i in range(4)]
        b_pTs = [Buf() for _ in range(4)]
        slot = 0
        pi = 0
        for h in range(BH):
            po, bo = self.ps[4 + h % 2], self.bps[4 + h % 2]
            pd, bd = self.ps[2 + h % 2], self.bps[2 + h % 2]
            tiles = []
            kro = 0
            for (knf, krap, vf, nk, diag, biascol, dep) in blocks:
                tiles.append(('load', knf, vf, nk, dep))
                for t in range((nk + 127) // 128):
                    n = min(128, nk - t * 128)
                    tiles.append(('tile', t, n, t * 128 if diag else 0, diag, biascol, kro))
                kro += nk
            pend = []
            state = dict(first=True, cnt=0, sl=0)

            def finish(item):
                (ps_, bs_, pp, bp, sl_, t, n, q0, diag, biascol) = item
                if biascol is None:
                    self.act(pp[0:n, q0:T], ps_[0:n, q0:T], AF.Exp, [bs_], [bp], scale=ATT_SCALE)
                else:
                    self.act(pp[0:n, q0:T], ps_[0:n, q0:T], AF.Exp, [bs_, self.b_gn], [bp], scale=ATT_SCALE,
                             bias=self.cm[0:n, biascol:biascol + 1])
                if diag:
                    self.ms(pp[64:128, q0:q0 + 64], 0.0, [bp], 'dve')
                state['cnt'] += 1
                self.mm(po[:, q0:T], vb[0:n, sl_, t, :], pp[0:n, q0:T], state['first'], state['cnt'] == ntot, [b_vb[sl_], bp], [bo])
                if state['first']:
                    assert n == 128 and q0 == 0
                    self.cp(dacc[:, 0:T], pp[:, 0:T], [bp], [b_da])
                else:
                    self.tt(dacc[0:n, q0:T], dacc[0:n, q0:T], pp[0:n, q0:T], ALU.add, [bp, b_da], [b_da])
                state['first'] = False

            for it in tiles:
                if it[0] == 'load':
                    _, knf, vf, nk, dep = it
                    sl_ = slot; slot ^= 1
                    state['sl'] = sl_
                    k.dma('sp', knb[:, sl_, 0:nk], knf(h), [dep], [b_knb[sl_]], dsem='ld')
                    nfull = nk // 128
                    if nfull:
                        k.dma('sp', vb[:, sl_, 0:nfull, :], vf(h, 0, nfull * 128).rearrange("(t p) d -> p t d", p=128), [dep], [b_vb[sl_]], dsem='ld')
                    if nk % 128:
                        n_ = nk % 128
                        k.dma('sp', vb[0:n_, sl_, nfull, :], vf(h, nfull, n_), [dep], [b_vb[sl_]], dsem='ld')
                    continue
                _, t, n, q0, diag, biascol, kro_ = it
                sl_ = state['sl']
                ps_, bs_ = self.ps[sbanks[pi % 4]], self.bps[sbanks[pi % 4]]
                pp, bp = pTs[pi % 4], b_pTs[pi % 4]
                pi += 1
                self.mm(ps_[0:n, q0:T], knb[:, sl_, t * 128:t * 128 + n], QN[:, h, q0:T], True, False, [b_knb[sl_], b_qn], [bs_])
                self.mm(ps_[0:n, q0:T], krall[:, kro_ + t * 128:kro_ + t * 128 + n], QR[:, h, q0:T], False, True, [b_kr, b_qr], [bs_])
                pend.append((ps_, bs_, pp, bp, sl_, t, n, q0, diag, biascol))
                if len(pend) > DEPTH_:
                    finish(pend.pop(0))
            while pend:
                finish(pend.pop(0))
            self.cp(daccb[:, 0:T], dacc[:, 0:T], [b_da], [b_da], 'act')
            self.mm(pd[:, 0:T], self.ones_b[:, :], daccb[:, 0:T], True, True, [b_da, self.b_const], [bd])
            self.recip(rd[:, 0:T], pd[:, 0:T], [bd], [b_rd])
            self.tt(QN[:, h, 0:T], po[:, 0:T], rd[:, 0:T], ALU.mult, [bo, b_rd], [b_qn])
        k.barrier()
        self.linear(d['b_w_o'][j], [(c * 128, 128) for c in range(KC)], [(0, KC)], lambda c: (QN[:, c, 0:T], b_qn), T,
                    self.resid_evac(), extra_R=[self.b_x])


_PROG = None


def _rope_tables():
    half = ROPE // 2
    inv = (10000.0 ** (-np.arange(half, dtype=np.float32) / half)).astype(np.float32)
    pos = np.concatenate([np.arange(SEQ), PAST + np.arange(DS)]).astype(np.float32)
    ang = pos[None, :] * inv[:, None]
    c, s = np.cos(ang).astype(np.float32), np.sin(ang).astype(np.float32)
    return np.concatenate([c, c], 0), np.concatenate([-s, s], 0)


def _core_mask(r):
    m = np.zeros((40,), np.float32)
    for i in range(4):
        m[i] = 0.0 if i < r else NEG
        m[4 + i] = 1.0 if i < r else 0.0
        for l in range(4):
            m[8 + 4 * i + l] = 1.0 if (i < l < r) else 0.0
        m[24 + i] = 1.0 if i == r - 1 else 0.0
        m[32 + i] = 0.0 if i < r else -30000.0
    m[28] = 1.0 if r == 0 else 0.0
    return np.ascontiguousarray(np.broadcast_to(m[None, :], (128, 40)))


def kernel(**inp):
    global _PROG
    if _PROG is None:
        _PROG = Prog()
    prog = _PROG
    f = lambda a: np.ascontiguousarray(np.asarray(a, dtype=np.float32))
    cos2, sin2 = _rope_tables()
    shared = {}
    for n in ['norm_mix', 'norm_ffn', 'a_w_in', 'a_b_gate', 'a_w_out', 'kv_w_down', 'kv_w_up', 'b_w_dq', 'b_g_cq', 'b_w_uq',
              'b_g_qn', 'b_g_qr', 'b_w_o', 'f_w_up', 'f_conv_w', 'f_conv_b', 'f_w_down']:
        shared[n] = f(inp[n])
    shared['a_g_head'] = f(inp['a_g_head']).reshape(NA, H * DV)
    for n in ['kv_norm', 'kv_g_c', 'kv_g_r', 'kv_g_kn']:
        shared[n] = f(inp[n]).reshape(1, -1)
    in_maps = []
    for c in range(8):
        g, r = c // 4, c % 4
        m = dict(shared)
        segs = [rd * 4 + r for rd in range(NR)]
        m['xp'] = f(np.concatenate([inp['x_prompt'][g][sg * SEG:(sg + 1) * SEG] for sg in segs], 0))
        cols = np.concatenate([np.arange(sg * SEG, (sg + 1) * SEG) for sg in segs] + [SEQ + np.arange(DS)])
        m['cos2'] = f(cos2[:, cols]); m['sin2'] = f(sin2[:, cols])
        m['cmask'] = _core_mask(r)
        m['xs'] = f(inp['x_sample'][c])
        m['ckv'] = f(inp['cache_ckv'][c]); m['kpe'] = f(inp['cache_kpe'][c])
        m['sC'] = f(inp['state_C'][:, c]); m['sn'] = f(inp['state_n'][:, c]).reshape(NA, H * DK); m['sm'] = f(inp['state_m'][:, c])
        m['sconv'] = f(inp['state_conv'][:, c])
        in_maps.append(m)
    res = run_bass_kernel_spmd(prog.nc, in_maps, core_ids=list(range(8))).results

    def seqcat(n):
        out = []
        for g in range(2):
            parts = [None] * (4 * NR)
            for r in range(4):
                for rd in range(NR):
                    parts[rd * 4 + r] = res[g * 4 + r][n][rd * SEG:(rd + 1) * SEG]
            out.append(np.concatenate(parts, 0))
        return np.stack(out, 0)
    pc = [3, 7]
    st = lambda n, cores, ax: np.stack([res[c][n] for c in cores], axis=ax)
    allc = list(range(8))
    rs = lambda a: a.reshape(a.shape[0], a.shape[1], H, DK)
    return (seqcat('yp'), st('ys', allc, 0), seqcat('p_ckv'), seqcat('p_kpe'),
            st('pC', pc, 1), rs(st('pn', pc, 1)), st('pm', pc, 1), st('pconv', pc, 1),
            st('s_ckv', allc, 0), st('s_kpe', allc, 0), st('sCo', allc, 1), rs(st('sno', allc, 1)), st('smo', allc, 1),
            st('sconvo', allc, 1))
```

```python
import numpy as np
from contextlib import ExitStack
import concourse.bass as bass
import concourse.mybir as mybir
from concourse.bass_utils import run_bass_kernel_spmd

F32 = mybir.dt.float32
BF16 = mybir.dt.bfloat16
AF = mybir.ActivationFunctionType
ALU = mybir.AluOpType
AX = mybir.AxisListType

D = 2048; KC = 16; SEQ = 4096; DEPTH = 4; NA = 2
DS = 16; PAST = 2048
H = 4; DK = 256; DV = 512; APROJ = 6152
BH = 16; QL = 768; KVL = 512; NOPE = 128; ROPE = 64; VD = 128
DFF = 5632; FC = 44
EPS = 1e-6
SEG = 512
NSEG = SEQ // SEG
NR = 2
XW = 2 * H * DV + 16
KVROWS = BH * 128 + ROPE + 4 * SEG
GROUPS = [[0, 1, 2, 3], [4, 5, 6, 7]]
ATT_SCALE = (NOPE + ROPE) ** -0.5
NEG = -1.0e30


class Buf:
    def __init__(self, name=""):
        self.name = name
        self.w = None
        self.r = []


class K:
    def __init__(self, nc, stack):
        self.nc = nc
        self.stack = stack
        self.eng = {'pe': nc.tensor, 'dve': nc.vector, 'act': nc.scalar, 'pool': nc.gpsimd, 'sp': nc.sync}
        self.sem = {}
        self.cnt = {}
        self.waited = {e: {} for e in self.eng}
        for e in self.eng:
            self.sem[e] = stack.enter_context(nc.semaphore('s_' + e))
            self.cnt[e] = 0
        self.nins = 0

    def new_dma_sem(self, key):
        self.sem[key] = self.stack.enter_context(self.nc.semaphore(key))
        self.cnt[key] = 0
        return key

    def _wait(self, e, deps):
        need = {}
        for d in deps:
            if d is None:
                continue
            k, v = d
            if e == 'pe' and k == 'pe':
                continue
            if v > need.get(k, 0):
                need[k] = v
        for k, v in need.items():
            if self.waited[e].get(k, 0) >= v:
                continue
            self.eng[e].wait_ge(self.sem[k], v)
            self.waited[e][k] = v

    @staticmethod
    def _deps(reads, writes):
        deps = []
        for b in reads:
            deps.append(b.w)
        for b in writes:
            deps.append(b.w)
            deps.extend(b.r)
        return deps

    def op(self, e, fn, reads=(), writes=()):
        self._wait(e, self._deps(reads, writes))
        ins = fn()
        self.cnt[e] += 1
        self.nins += 1
        ins.then_inc(self.sem[e], 1)
        tag = (e, self.cnt[e])
        for b in reads:
            b.r.append(tag)
            if len(b.r) > 24:
                b.r = b.r[-24:] if False else self._compact(b.r)
        for b in writes:
            b.w = tag
            b.r = []
        return ins

    @staticmethod
    def _compact(r):
        m = {}
        for k, v in r:
            if v > m.get(k, 0):
                m[k] = v
        return list(m.items())

    def dma(self, q, out, in_, reads=(), writes=(), dsem='io', **kw):
        if dsem in ('io', 'out', 'kv', 'ld'):
            dsem = self.pool[self.pi % len(self.pool)]
            self.pi += 1
        prev = self.cnt[dsem]
        self._wait(q, self._deps(reads, writes) + ([(dsem, prev)] if prev else []))
        ins = self.eng[q].dma_start(out=out, in_=in_, **kw)
        self.cnt[dsem] += 16
        self.nins += 1
        ins.then_inc(self.sem[dsem], 16)
        tag = (dsem, self.cnt[dsem])
        for b in reads:
            b.r.append(tag)
        for b in writes:
            b.w = tag
            b.r = []
        return ins

    def collective(self, src, dst, groups, reads=(), writes=()):
        self._wait('pool', self._deps(reads, writes))
        ins = self.nc.gpsimd.collective_compute("AllGather", mybir.AluOpType.bypass, replica_groups=groups,
                                                ins=[src.opt()], outs=[dst.opt()])
        self.cnt['cc'] += 1
        self.nins += 1
        ins.then_inc(self.sem['cc'], 1)
        tag = ('cc', self.cnt['cc'])
        for b in reads:
            b.r.append(tag)
        for b in writes:
            b.w = tag
            b.r = []
        return ins

    def barrier(self):
        allv = [(k, v) for k, v in self.cnt.items() if v > 0 and k != 'cc']
        for e in self.eng:
            self._wait(e, allv)


def bc(ap, shape):
    return ap.broadcast_to(list(shape))


STOP = None


class _Stop(Exception):
    pass


class Prog:
    def dbg(self, tag):
        if STOP is not None and tag == STOP:
            raise _Stop()

    def __init__(self):
        self.nc = nc = bass.Bass("TRN2", target_bir_lowering=False)
        self.st = ExitStack()
        self.k = K(nc, self.st)
        for s in ['w0', 'w1', 'w2', 'w3', 'cc']:
            self.k.new_dma_sem(s)
        self.k.pool = [self.k.new_dma_sem('p%d' % i) for i in range(40)]
        self.k.pi = 0
        self.build()

    def din(self, name, shape):
        return self.nc.dram_tensor(name, list(shape), F32, kind="ExternalInput").ap()

    def dout(self, name, shape):
        return self.nc.dram_tensor(name, list(shape), F32, kind="ExternalOutput").ap()

    def dint(self, name, shape, dt=F32):
        return self.nc.dram_tensor(name, list(shape), dt, kind="Internal").ap()

    def sb(self, name, shape, dt=F32):
        return self.st.enter_context(self.nc.sbuf_tensor(name, list(shape), dt))

    def fv(self, a, n, rows=128):
        return self.arena[0:rows, a:a + n]

    def bv(self, a, n, rows=128):
        assert n % 2 == 0
        return self.arena[0:rows, a:a + n // 2].bitcast(BF16)

    def V(self, e='dve'):
        return self.nc.vector if e == 'dve' else self.nc.gpsimd

    def tt(self, out, a, b, op, R, W, e='dve'):
        self.k.op(e, lambda: self.V(e).tensor_tensor(out=out, in0=a, in1=b, op=op), R, W)

    def ts(self, out, a, s1, s2, op0, op1, R, W, e='dve'):
        if op1 is None:
            self.k.op(e, lambda: self.V(e).tensor_scalar(out=out, in0=a, scalar1=s1, scalar2=None, op0=op0), R, W)
        else:
            self.k.op(e, lambda: self.V(e).tensor_scalar(out=out, in0=a, scalar1=s1, scalar2=s2, op0=op0, op1=op1), R, W)

    def stt(self, out, a, s, b, op0, op1, R, W):
        self.k.op('dve', lambda: self.nc.vector.scalar_tensor_tensor(out=out, in0=a, scalar=s, in1=b, op0=op0, op1=op1), R, W)

    def cp(self, out, a, R, W, e='dve'):
        if e == 'act':
            self.k.op('act', lambda: self.nc.scalar.copy(out=out, in_=a), R, W)
        else:
            self.k.op(e, lambda: self.V(e).tensor_copy(out=out, in_=a), R, W)

    def act(self, out, a, func, R, W, bias=None, scale=None, accum=None):
        kw = {}
        if bias is not None:
            kw['bias'] = bias
        if scale is not None:
            kw['scale'] = scale
        if accum is not None:
            kw['accum_out'] = accum
        self.k.op('act', lambda: self.nc.scalar.activation(out=out, in_=a, func=func, **kw), R, W)

    def mm(self, out, lhsT, rhs, start, stop, R, W):
        self.k.op('pe', lambda: self.nc.tensor.matmul(out, lhsT=lhsT, rhs=rhs, start=start, stop=stop), R, W)

    def tr(self, out, a, n, R, W, bf=False):
        idn = self.ident_b if bf else self.ident
        self.k.op('pe', lambda: self.nc.tensor.transpose(out, a, idn[0:n, 0:n]), list(R) + [self.b_const], W)

    def ms(self, ap, c, W, e='dve'):
        self.k.op(e, lambda: self.V(e).memset(ap, c), (), W)

    def recip(self, out, a, R, W):
        self.k.op('dve', lambda: self.nc.vector.reciprocal(out=out, in_=a), R, W)

    def load_fm(self, dst, src1d, n, rows=128):
        tmp = self.fv(self.O_TMP, 128)
        if not hasattr(self, 'b_tmpfm'):
            self.b_tmpfm = Buf()
        tb = self.b_tmpfm
        self.k.dma('sp', tmp[0:n, 0:rows], src1d.rearrange("(c p) -> c p", p=rows), (), [tb])
        ps, pb = self.ps[7], self.bps[7]
        self.tr(ps[0:rows, 0:n], tmp[0:n, 0:rows], n, [tb], [pb])
        self.cp(dst, ps[0:rows, 0:n], [pb], [self.b_gn])

    def build(self):
        nc, k = self.nc, self.k
        d = {}
        d['xp'] = self.din('xp', [NR * SEG, D]); d['xs'] = self.din('xs', [DS, D])
        d['ckv'] = self.din('ckv', [PAST, KVL]); d['kpe'] = self.din('kpe', [PAST, ROPE])
        d['sC'] = self.din('sC', [NA, H, DV, DK]); d['sn'] = self.din('sn', [NA, H * DK]); d['sm'] = self.din('sm', [NA, H])
        d['sconv'] = self.din('sconv', [DEPTH, 2, 2 * DFF])
        d['norm_mix'] = self.din('norm_mix', [DEPTH, D]); d['norm_ffn'] = self.din('norm_ffn', [DEPTH, D])
        d['a_w_in'] = self.din('a_w_in', [NA, D, APROJ]); d['a_b_gate'] = self.din('a_b_gate', [NA, 8])
        d['a_g_head'] = self.din('a_g_head', [NA, H * DV]); d['a_w_out'] = self.din('a_w_out', [NA, D, D])
        d['kv_norm'] = self.din('kv_norm', [1, D]); d['kv_w_down'] = self.din('kv_w_down', [D, KVL + ROPE])
        d['kv_g_c'] = self.din('kv_g_c', [1, KVL]); d['kv_g_r'] = self.din('kv_g_r', [1, ROPE])
        d['kv_w_up'] = self.din('kv_w_up', [KVL, BH * 256]); d['kv_g_kn'] = self.din('kv_g_kn', [1, NOPE])
        d['b_w_dq'] = self.din('b_w_dq', [2, D, QL]); d['b_g_cq'] = self.din('b_g_cq', [2, QL])
        d['b_w_uq'] = self.din('b_w_uq', [2, QL, BH * 192]); d['b_g_qn'] = self.din('b_g_qn', [2, NOPE])
        d['b_g_qr'] = self.din('b_g_qr', [2, ROPE]); d['b_w_o'] = self.din('b_w_o', [2, D, D])
        d['f_w_up'] = self.din('f_w_up', [DEPTH, D, 2 * DFF]); d['f_conv_w'] = self.din('f_conv_w', [DEPTH, 3, 2 * DFF])
        d['f_conv_b'] = self.din('f_conv_b', [DEPTH, 2 * DFF]); d['f_w_down'] = self.din('f_w_down', [DEPTH, DFF, D])
        d['cos2'] = self.din('cos2', [ROPE, NR * SEG + DS]); d['sin2'] = self.din('sin2', [ROPE, NR * SEG + DS])
        d['cmask'] = self.din('cmask', [128, 40])
        o = {}
        o['yp'] = self.dout('yp', [NR * SEG, D]); o['ys'] = self.dout('ys', [DS, D])
        o['p_ckv'] = self.dout('p_ckv', [NR * SEG, KVL]); o['p_kpe'] = self.dout('p_kpe', [NR * SEG, ROPE])
        o['pC'] = self.dout('pC', [NA, H, DV, DK]); o['pn'] = self.dout('pn', [NA, H * DK]); o['pm'] = self.dout('pm', [NA, H])
        o['pconv'] = self.dout('pconv', [DEPTH, 2, 2 * DFF])
        o['s_ckv'] = self.dout('s_ckv', [DS, KVL]); o['s_kpe'] = self.dout('s_kpe', [DS, ROPE])
        o['sCo'] = self.dout('sCo', [NA, H, DV, DK]); o['sno'] = self.dout('sno', [NA, H * DK]); o['smo'] = self.dout('smo', [NA, H])
        o['sconvo'] = self.dout('sconvo', [DEPTH, 2, 2 * DFF])
        self.d, self.o = d, o
        self.kn_dram = [None, self.dint('kns', [BH, 128, PAST + DS], BF16)]
        self.kr_dram = [None, self.dint('krs', [ROPE, PAST + DS], BF16)]
        self.v_dram = [None, self.dint('vs', [PAST + DS, BH * VD], BF16)]
        self.xsC = [[[self.dint('xsC%d%d%d' % (l, t, c), [128, 2048]) for c in range(2)] for t in range(2)] for l in range(NA)]
        self.xgC = [[[self.dint('xgC%d%d%d' % (l, t, c), [512, 2048]) for c in range(2)] for t in range(2)] for l in range(NA)]
        self.xsS = [self.dint('xsS%d' % l, [256, 16]) for l in range(NA)]
        self.xgS = [self.dint('xgS%d' % l, [1024, 16]) for l in range(NA)]
        self.b_xs2 = [Buf(), Buf()]; self.b_xg = [Buf(), Buf()]
        self.ts2 = [self.dint('ts2_%d' % l, [256, 176]) for l in range(DEPTH)]
        self.tg = [self.dint('tg_%d' % l, [4 * 256, 176]) for l in range(DEPTH)]
        self.b_ts2 = [Buf() for _ in range(DEPTH)]; self.b_tg = [Buf() for _ in range(DEPTH)]
        self.KP = [self.dint('KP%d' % j, [1024, SEG], BF16) for j in range(2)]
        self.KRp = self.dint('KRp', [ROPE, SEG], BF16)
        self.VP = [self.dint('VP%d' % q, [SEG, 1024], BF16) for q in range(2)]
        self.b_kvloc = Buf()
        self.b_kvd = [Buf(), Buf()]
        self.KG = [[self.dint('KG%d%d' % (r, j), [4 * 1024, SEG], BF16) for j in range(2)] for r in range(NR)]
        self.KRG = [self.dint('KRG%d' % r, [4 * ROPE, SEG], BF16) for r in range(NR)]
        self.VG = [[self.dint('VG%d%d' % (r, q), [4 * SEG, 1024], BF16) for q in range(2)] for r in range(NR)]
        self.b_kvall = [Buf() for _ in range(NR)]
        self.b_out = Buf('out')

        self.ident = self.sb('ident', [128, 128])
        self.ident_b = self.sb('ident_b', [128, 128], BF16)
        self.ones_b = self.sb('ones_b', [128, 128], BF16)
        self.ones_f = self.sb('ones_f', [128, 128])
        self.uneg = {L: self.sb('uneg%d' % L, [L, L]) for L in (128, 16)}
        self.mask = {L: self.sb('mask%d' % L, [L, L]) for L in (128, 16)}
        self.maskT = {L: self.sb('maskT%d' % L, [L, L]) for L in (128, 16)}
        self.sel = {L: self.sb('sel%d' % L, [L, 128]) for L in (128, 16)}
        self.perm = self.sb('perm', [64, 64], BF16)
        self.epsc = self.sb('epsc', [128, 1])
        self.b_const = Buf('const')
        self.xT = self.sb('xT', [128, KC, SEG]); self.b_x = Buf('x')
        self.xb = self.sb('xb', [128, KC, SEG], BF16); self.b_xb = Buf('xb')
        self.tail = self.sb('tail', [128, DEPTH, 2, 2, 88]); self.b_tail = Buf('tail')
        self.nst = self.sb('nst', [128, NA, 2, 2, H, 2]); self.b_nst = Buf('nst')
        self.nsb = self.sb('nsb', [128, NA, 2, 2, H, 2], BF16)
        self.mst = self.sb('mst', [128, NA, 2, H]); self.b_mst = Buf('mst')
        self.gn = self.sb('gn', [128, 9, KC]); self.b_gn = Buf('gn')
        self.cw = self.sb('cw', [128, DEPTH, 3, 88]); self.cb = self.sb('cb', [128, DEPTH, 88])
        self.gmisc = self.sb('gmisc', [128, 64])
        self.cm = self.sb('cm', [128, 40])
        self.fsum = self.sb('fsum', [128, H])
        self.b_cw = self.b_gn; self.b_gm = self.b_gn
        self.arena = self.sb('arena', [128, 34200])
        self.ps = [self.st.enter_context(nc.psum_tensor('ps%d' % i, [128, 512], F32)) for i in range(8)]
        self.bps = [Buf('ps%d' % i) for i in range(8)]
        self.O_STG = 0
        self.O_WR = 8192
        self.O_SQ = 12288
        self.O_RSTD = 12800
        self.O_PH = 13312
        self.O_TMP = 13312

        try:
            self.init_consts()
            self.dbg('init')
            self.segment(grp=1, s=0, T=DS, L=DS)
            for s in range(NR if NSEG > 0 else 0):
                self.segment(grp=0, s=s, T=SEG, L=128)
        except _Stop:
            pass
        k.barrier()
        k._wait('sp', [(p, k.cnt[p]) for p in k.pool if k.cnt[p] > 0])

    def init_consts(self):
        nc, k, d = self.nc, self.k, self.d
        W = [self.b_const]
        g = nc.gpsimd

        def sel(t, pattern, cmp, fill, base, cm):
            k.op('pool', lambda: g.affine_select(out=t, in_=t, pattern=pattern, compare_op=cmp, fill=fill,
                                                base=base, channel_multiplier=cm), W, W)
        self.ms(self.ident[:], 0.0, W, 'pool')
        sel(self.ident[:], [[-1, 128]], ALU.not_equal, 1.0, 0, 1)
        self.cp(self.ident_b[:], self.ident[:], W, W, 'dve')
        self.ms(self.ones_f[:], 1.0, W, 'pool')
        self.cp(self.ones_b[:], self.ones_f[:], W, W, 'dve')
        self.ms(self.epsc[:], EPS, W, 'pool')
        for L in (128, 16):
            self.ms(self.uneg[L][:], -1.0, W, 'pool')
            sel(self.uneg[L][:], [[1, L]], ALU.is_ge, 0.0, 0, -1)
            self.ms(self.mask[L][:], 0.0, W, 'pool')
            sel(self.mask[L][:], [[-1, L]], ALU.is_ge, NEG, 0, 1)
            self.ms(self.maskT[L][:], 0.0, W, 'pool')
            sel(self.maskT[L][:], [[1, L]], ALU.is_ge, NEG, 0, -1)
            self.ms(self.sel[L][:], 1.0, W, 'pool')
            sel(self.sel[L][:], [[0, 128]], ALU.is_equal, 0.0, -(L - 1), 1)
        pf = self.fv(20000, 64, 64)
        self.ms(pf, 0.0, W, 'pool')
        sel(pf, [[-1, 64]], ALU.not_equal, 1.0, -32, 1)
        sel(pf, [[-1, 64]], ALU.not_equal, 1.0, 32, 1)
        self.cp(self.perm[:], pf, W, W, 'dve')
        k.barrier()
        for i in range(4):
            self.load_fm(self.gn[:, i, :], d['norm_mix'][i], KC)
            self.load_fm(self.gn[:, 4 + i, :], d['norm_ffn'][i], KC)
        self.load_fm(self.gn[:, 8, :], d['kv_norm'][0], KC)
        for l in range(DEPTH):
            for j in range(3):
                self.load_fm(self.cw[:, l, j, :], d['f_conv_w'][l, j], 88)
            self.load_fm(self.cb[:, l, :], d['f_conv_b'][l], 88)
        gm = self.gmisc
        self.ms(gm[:], 0.0, [self.b_gn], 'pool')
        for j in range(2):
            self.load_fm(gm[:, 6 * j:6 * j + 6], d['b_g_cq'][j], 6)
            self.load_fm(gm[:, 12 + j:13 + j], d['b_g_qn'][j], 1)
            self.load_fm(gm[0:64, 14 + j:15 + j], d['b_g_qr'][j], 1, rows=64)
            self.load_fm(gm[:, 22 + 16 * j:38 + 16 * j], d['a_g_head'][j], 16)
            self.load_fm(gm[0:8, 54 + j:55 + j], d['a_b_gate'][j], 1, rows=8)
        self.load_fm(gm[:, 16:20], d['kv_g_c'][0], 4)
        self.load_fm(gm[0:64, 20:21], d['kv_g_r'][0], 1, rows=64)
        self.load_fm(gm[:, 21:22], d['kv_g_kn'][0], 1)
        self.ms(self.tail[:], 0.0, [self.b_tail], 'pool')
        self.ms(self.mst[:], 0.0, [self.b_mst], 'pool')
        self.ms(self.fsum[:], 0.0, [self.b_mst], 'pool')
        self.ms(self.nst[:], 0.0, [self.b_nst], 'pool')
        k.barrier()
        for l in range(DEPTH):
            for r in range(2):
                self.load_fm(self.tail[:, l, 1, r, :], d['sconv'][l, r], 88)
        for l in range(NA):
            k.dma('sp', self.mst[:, l, 1, :], d['sm'][l:l + 1, :].broadcast_to([128, H]), (), [self.b_mst])
            nt = self.fv(20100, 8)
            self.load_fm(nt, d['sn'][l], 8)
            for e in range(2):
                self.cp(self.nst[:, l, 1, :, :, e], nt.rearrange("p (h c) -> p c h", h=H), [self.b_gn], [self.b_nst], 'dve')
        self.cp(self.nsb[:], self.nst[:], [self.b_nst], [self.b_nst], 'dve')
        k.dma('sp', self.cm[:], d['cmask'][:, :], (), [self.b_gn])
        zc = self.fv(0, XW)
        self.ms(zc, 0.0, W, 'pool')
        for l in range(NA):
            for c in range(2):
                k.dma('sp', self.xsC[l][1][c][:, :], zc[:, 0:2048], W, [self.b_xs2[l]])
            k.dma('sp', self.xsS[l][128:256, :], zc[:, 0:16], W, [self.b_xs2[l]])
        for l in range(DEPTH):
            k.dma('sp', self.ts2[l][128:256, :], zc[:, 0:176], W, [self.b_ts2[l]])
        k.barrier()

    def linear(self, w, blocks, kgroups, rhs_fn, T, evac, gcol=None, ps_ids=(0, 1), extra_R=(), sets=None):
        k = self.k
        wv = w.rearrange("(c p) n -> p c n", p=128)
        NS = 4
        if not hasattr(self, 'wslot'):
            self.wslot = 0
            self.b_wst = [Buf() for _ in range(NS)]
            self.b_wr = [Buf() for _ in range(NS)]
        if sets is None:
            sets = [[0, 1, 4, 5], [6, 7, 2, 3]]
        kcs = [kc for (k0, nk) in kgroups for kc in range(k0, k0 + nk)]
        groups = []
        cur = []
        for bi, (c0, m) in enumerate(blocks):
            if cur and (cur[-1][1] + cur[-1][2] == c0) and (sum(x[2] for x in cur) + m <= 512) and (len(cur) < min(len(x) for x in sets)):
                cur.append((bi, c0, m))
            else:
                if cur:
                    groups.append(cur)
                cur = [(bi, c0, m)]
        if cur:
            groups.append(cur)
        si = 0
        pending = None
        for grp_ in groups:
            banks = sets[si % len(sets)]
            si += 1
            cols = sum(x[2] for x in grp_)
            cbase = grp_[0][1]
            nk_t = max(1, min(2048 // cols, len(kcs)))
            ntile = (len(kcs) + nk_t - 1) // nk_t
            for ti in range(ntile):
                kk = kcs[ti * nk_t:(ti + 1) * nk_t]
                assert kk == list(range(kk[0], kk[0] + len(kk)))
                nk = len(kk)
                s = self.wslot
                self.wslot = (self.wslot + 1) % NS
                stg = self.fv(self.O_STG + s * 2048, nk * cols).rearrange("p (c n) -> p c n", n=cols)
                wr = self.bv(self.O_WR + s * 1024, nk * cols).rearrange("p (c n) -> p c n", n=cols)
                k.dma('sp', stg, wv[:, kk[0]:kk[0] + nk, cbase:cbase + cols], (), [self.b_wst[s]], dsem='w%d' % s)
                self.cp(wr, stg, [self.b_wst[s]], [self.b_wr[s]], ('act', 'dve', 'act', 'act')[s])
                for gi, (bi, c0, m) in enumerate(grp_):
                    ps, pb = self.ps[banks[gi]], self.bps[banks[gi]]
                    for j in range(nk):
                        rhs, rb = rhs_fn(kk[j])
                        first = (ti == 0 and j == 0)
                        last = (ti == ntile - 1 and j == nk - 1)
                        self.mm(ps[0:m, 0:T], wr[:, j, c0 - cbase:c0 - cbase + m], rhs, first, last,
                                [self.b_wr[s], rb] + list(extra_R), [pb])
            if pending is not None:
                for (bi_, ps_, pb_) in pending:
                    evac(bi_, ps_, pb_)
            pending = [(bi, self.ps[banks[gi]][0:m, 0:T], self.bps[banks[gi]]) for gi, (bi, c0, m) in enumerate(grp_)]
        if pending is not None:
            for (bi_, ps_, pb_) in pending:
                evac(bi_, ps_, pb_)

    def sumsq_bc(self, src_fn, nch, T, out, outb, n, rows=128, ps_id=2):
        if not hasattr(self, 'b_sq'):
            self.b_sq = [Buf(), Buf()]
        ps, pb = self.ps[ps_id], self.bps[ps_id]
        for c in range(nch):
            src, sbuf = src_fn(c)
            s = c % 2
            sq = self.bv(self.O_SQ + s * 256, 512)
            self.act(sq[0:rows, 0:T], src, AF.Square, [sbuf], [self.b_sq[s]])
            self.mm(ps[:, 0:T], self.ones_b[0:rows, :], sq[0:rows, 0:T], c == 0, c == nch - 1, [self.b_sq[s], self.b_const], [pb])
        orow = out.shape[0]
        self.act(out, ps[0:orow, 0:T], AF.Sqrt, [pb, self.b_const], [outb], bias=self.epsc[0:orow, 0:1], scale=1.0 / n)
        self.recip(out, out, [outb], [outb])

    def x_rstd(self):
        T = self.T
        rstd = self.fv(self.O_RSTD, SEG)
        b = Buf()
        self.sumsq_bc(lambda c: (self.xT[:, c, 0:T], self.b_x), KC, T, rstd[:, 0:T], b, D)
        return rstd, b

    def xb_refresh(self, gidx):
        T = self.T
        rstd, b_rstd = self.x_rstd()
        tmpf = self.fv(self.O_STG, 2 * SEG).rearrange("p (s t) -> p s t", s=2)
        if not hasattr(self, 'wslot'):
            self.wslot = 0
            self.b_wst = [Buf() for _ in range(4)]
            self.b_wr = [Buf() for _ in range(4)]
        bt = [self.b_wst[0], self.b_wst[0]]
        j = 0
        for c in range(KC):
            if c % 2 == 0:
                self.stt(self.xb[:, c, 0:T], self.xT[:, c, 0:T], self.gn[:, gidx, c:c + 1], rstd[:, 0:T], ALU.mult, ALU.mult,
                         [self.b_x, self.b_gn, b_rstd], [self.b_xb])
            else:
                sl = j % 2; j += 1
                self.act(tmpf[:, sl, 0:T], self.xT[:, c, 0:T], AF.Copy, [self.b_x, self.b_gn], [bt[sl]], scale=self.gn[:, gidx, c:c + 1])
                self.tt(self.xb[:, c, 0:T], tmpf[:, sl, 0:T], rstd[:, 0:T], ALU.mult, [bt[sl], b_rstd], [self.b_xb], 'pool')

    def resid_evac(self):
        T = self.T
        def ev(bi, ps, pb):
            self.tt(self.xT[:, bi, 0:T], ps, self.xT[:, bi, 0:T], ALU.add, [pb, self.b_x], [self.b_x])
        return ev

    def segment(self, grp, s, T, L):
        nc, k, d, o = self.nc, self.k, self.d, self.o
        self.grp, self.sidx, self.T, self.L = grp, s, T, L
        pos0 = s * SEG if grp == 0 else NR * SEG
        xsrc = d['xp'][s * SEG:(s + 1) * SEG, :] if grp == 0 else d['xs']
        k.barrier()
        tmp = self.fv(self.O_PH, D)
        tb = Buf()
        for t0 in range(0, T, 128):
            n = min(128, T - t0)
            k.dma('sp', tmp[0:n, :], xsrc[t0:t0 + n, :], (), [tb])
            for c4 in range(0, KC, 4):
                ps, pb = self.ps[(c4 // 4) % 2], self.bps[(c4 // 4) % 2]
                for j in range(4):
                    self.tr(ps[:, j * 128:j * 128 + n], tmp[0:n, (c4 + j) * 128:(c4 + j + 1) * 128], n, [tb], [pb])
                self.cp(self.xT[:, c4:c4 + 4, t0:t0 + n], ps[:, :].rearrange("p (j n) -> p j n", j=4)[:, :, 0:n], [pb], [self.b_x])
        self.dbg('xload')
        for layer in range(DEPTH):
            k.barrier()
            self.xb_refresh(layer)
            if layer < NA:
                self.mlstm(layer)
            else:
                self.mla(layer - NA)
            k.barrier()
            self.dbg('mixer%d' % layer)
            self.xb_refresh(4 + layer)
            self.ffn(layer)
            self.dbg('ffn%d' % layer)
            if layer == NA - 1:
                k.barrier()
                self.xb_refresh(8)
                k.barrier()
                self.shared_kv(pos0)
        k.barrier()
        ydst = o['yp'][s * SEG:(s + 1) * SEG, :] if grp == 0 else o['ys']
        for t0 in range(0, T, 128):
            n = min(128, T - t0)
            for c4 in range(0, KC, 4):
                ps, pb = self.ps[(c4 // 4) % 2], self.bps[(c4 // 4) % 2]
                for j in range(4):
                    self.tr(ps[0:n, j * 128:(j + 1) * 128], self.xT[:, c4 + j, t0:t0 + n], 128, [self.b_x], [pb])
                self.cp(tmp[0:n, c4 * 128:(c4 + 4) * 128], ps[0:n, :], [pb], [tb])
            k.dma('sp', ydst[t0:t0 + n, :], tmp[0:n, :], [tb], [self.b_out], dsem='out')

    def ffn(self, layer):
        nc, k, d, o = self.nc, self.k, self.d, self.o
        T, grp = self.T, self.grp
        P0 = self.O_PH
        ug = self.fv(P0, SEG + 2); b_ug = Buf()
        uv = self.fv(P0 + 514, SEG + 2); b_uv = Buf()
        cg = self.fv(P0 + 1028, SEG); b_cg = Buf()
        cv = self.fv(P0 + 1540, SEG); b_cv = Buf()
        sg4 = self.fv(P0 + 14300, 4 * SEG).rearrange("p (s t) -> p s t", s=4); b_sg4 = [Buf() for _ in range(4)]
        t1 = self.fv(P0 + 2564, 128); t2 = self.fv(P0 + 2692, 128)
        actT = self.bv(P0 + 3000, FC * SEG).rearrange("p (c t) -> p c t", c=FC); b_act = Buf()
        gcol = self.gn[:, 4 + layer, :]
        tl = self.tail[:, layer, grp]
        blocks = []; bmap = []
        for j0 in range(0, FC, 4):
            for j in range(j0, j0 + 4):
                blocks.append((j * 128, 128)); bmap.append((j, 0))
            for j in range(j0, j0 + 4):
                blocks.append((DFF + j * 128, 128)); bmap.append((j, 1))
        sgs = self.fv(self.O_SQ, 4 * SEG).rearrange("p (s t) -> p s t", s=4) if False else None

        def conv(u, ub, fc, out, outb):
            self.ts(out[:, 0:T], u[:, 0:T], self.cw[:, layer, 0, fc:fc + 1], self.cb[:, layer, fc:fc + 1], ALU.mult, ALU.add,
                    [ub, self.b_cw], [outb])
            for jj in (1, 2):
                self.stt(out[:, 0:T], u[:, jj:jj + T], self.cw[:, layer, jj, fc:fc + 1], out[:, 0:T], ALU.mult, ALU.add,
                         [ub, self.b_cw, outb], [outb])

        def evac(bi, ps, pb):
            j, isv = bmap[bi]
            fc = j + (FC if isv else 0)
            u, ub = (uv, b_uv) if isv else (ug, b_ug)
            cc, cb_ = (cv, b_cv) if isv else (cg, b_cg)
            sg, b_sg = sg4[:, j % 4, :], b_sg4[j % 4]
            if grp == 1:
                self.cp(u[:, 0:2], tl[:, :, fc], [self.b_tail], [ub], 'pool')
                self.cp(u[:, 2:2 + T], ps, [pb], [ub], 'act')
                self.cp(tl[:, :, fc], u[:, T:T + 2], [ub], [self.b_tail], 'pool')
                conv(u, ub, fc, cc, cb_)
                lo = 0
            else:
                self.cp(tl[:, :, fc], ps[:, T - 2:T], [pb], [self.b_tail], 'dve')
                self.cp(ufirst[:, fc, :], ps[:, 0:2], [pb], [b_uf], 'dve')
                self.ts(cc[:, 2:T], ps[:, 0:T - 2], self.cw[:, layer, 0, fc:fc + 1], self.cb[:, layer, fc:fc + 1], ALU.mult, ALU.add,
                        [pb, self.b_cw], [cb_])
                for jj in (1, 2):
                    self.stt(cc[:, 2:T], ps[:, jj:T - 2 + jj], self.cw[:, layer, jj, fc:fc + 1], cc[:, 2:T], ALU.mult, ALU.add,
                             [pb, self.b_cw, cb_], [cb_])
                lo = 2
            if not isv:
                self.act(sg[:, lo:T], cg[:, lo:T], AF.Silu, [b_cg], [b_sg])
            else:
                self.tt(actT[:, j, lo:T], sg[:, lo:T], cv[:, lo:T], ALU.mult, [b_sg, b_cv], [b_act], 'pool')

        ufirst = self.fv(P0 + 2820, 176).rearrange("p (c r) -> p c r", r=2); b_uf = Buf()
        self.linear(d['f_w_up'][layer], blocks, [(0, KC)], lambda c: (self.xb[:, c, 0:T], self.b_xb), T, evac)
        if grp == 0:
            k.barrier()
            tcur = self.tail[:, layer, 0].rearrange("p r c -> p (r c)")
            k.dma('sp', self.ts2[layer][0:128, :], tcur, [self.b_tail, self.b_tg[layer]], [self.b_ts2[layer]])
            k.collective(self.ts2[layer][:, :], self.tg[layer][:, :], GROUPS, [self.b_ts2[layer]], [self.b_tg[layer]])
            tgs = self.fv(P0, 1408).rearrange("p (i t c) -> p i t c", i=4, t=2); bfx = Buf()
            k.dma('sp', tgs, self.tg[layer].rearrange("(i t p) c -> p i t c", t=2, p=128), [self.b_tg[layer]], [bfx])
            halo = self.fv(P0 + 1408, 176)
            Rf = [bfx, self.b_gn, b_uf]
            self.ts(halo, tgs[:, 3, 1, :], self.cm[:, 28:29], None, ALU.mult, None, Rf, [bfx])
            for i in range(4):
                self.stt(halo, tgs[:, i, 0, :], self.cm[:, 24 + i:25 + i], halo, ALU.mult, ALU.add, Rf, [bfx])
            k.dma('sp', self.ts2[layer][128:256, :], tcur, [self.b_tail, self.b_tg[layer]], [self.b_ts2[layer]])
            h0 = halo[:, 0:88]; h1 = halo[:, 88:176]
            u0 = ufirst[:, :, 0]; u1 = ufirst[:, :, 1]
            w0, w1, w2 = (self.cw[:, layer, jj, :] for jj in range(3)); bb = self.cb[:, layer, :]
            c0 = self.fv(P0 + 1584, 88); c1 = self.fv(P0 + 1672, 88); ta = self.fv(P0 + 1760, 88); sgl = self.fv(P0 + 1848, 88)
            for (cc, x0, x1, x2) in ((c0, h0, h1, u0), (c1, h1, u0, u1)):
                self.tt(cc, w0, x0, ALU.mult, Rf, [bfx]); self.tt(cc, cc, bb, ALU.add, Rf, [bfx])
                self.tt(ta, w1, x1, ALU.mult, Rf, [bfx]); self.tt(cc, cc, ta, ALU.add, Rf, [bfx])
                self.tt(ta, w2, x2, ALU.mult, Rf, [bfx]); self.tt(cc, cc, ta, ALU.add, Rf, [bfx])
            for t, cc in ((0, c0), (1, c1)):
                self.act(sgl[:, 0:44], cc[:, 0:44], AF.Silu, Rf, [bfx])
                self.tt(actT[:, :, t], sgl[:, 0:44], cc[:, 44:88], ALU.mult, Rf, [b_act])
            k.barrier()
        self.linear(d['f_w_down'][layer], [(c * 128, 128) for c in range(KC)], [(0, FC)],
                    lambda c: (actT[:, c, 0:T], b_act), T, self.resid_evac(), extra_R=[self.b_x])
        if grp == 1 or self.sidx == NR - 1:
            dst = o['sconvo'] if grp == 1 else o['pconv']
            tb = Buf()
            tf = self.tail[:, layer, grp].rearrange("p r c -> p (r c)")
            ps, pb = self.ps[2], self.bps[2]
            self.tr(ps[0:128, 0:128], tf[:, 0:128], 128, [self.b_tail], [pb])
            self.tr(ps[0:48, 128:256], tf[:, 128:176], 128, [self.b_tail], [pb])
            self.cp(t1, ps[0:128, 0:128], [pb], [tb]); self.cp(t2[0:48, :], ps[0:48, 128:256], [pb], [tb])
            dv = [dst[layer, r].rearrange("(c p) -> c p", p=128) for r in range(2)]
            k.dma('sp', dv[0][0:88, :], t1[0:88, :], [tb], [self.b_out], dsem='out')
            k.dma('sp', dv[1][0:40, :], t1[88:128, :], [tb], [self.b_out], dsem='out')
            k.dma('sp', dv[1][40:88, :], t2[0:48, :], [tb], [self.b_out], dsem='out')

    def mlstm(self, layer):
        nc, k, d, o = self.nc, self.k, self.d, self.o
        T, L, grp = self.T, self.L, self.grp
        P0 = self.O_PH
        qT = self.bv(P0, 8 * SEG).rearrange("p (c t) -> p c t", c=8); b_q = Buf()
        kT = self.bv(P0 + 2048, 8 * SEG).rearrange("p (c t) -> p c t", c=8); b_k = Buf()
        vT = self.fv(P0 + 4096, 16 * SEG).rearrange("p (c t) -> p c t", c=16); b_v = Buf()
        gT = self.fv(P0 + 12288, SEG, 8); b_g = Buf()
        sgt2 = self.fv(self.O_SQ, 2 * SEG).rearrange("p (s t) -> p s t", s=2); b_sg2 = [Buf(), Buf()]
        O_SSTG = 6144
        O_GH = 10240
        S0 = P0 + 13312
        gcol = self.gn[:, layer, :]
        w_in = d['a_w_in'][layer]
        sq_, sv_ = H * DK, H * DV
        blocks = [(c * 128, 128) for c in range(8)] + [(sq_ + c * 128, 128) for c in range(8)] + \
                 [(2 * sq_ + c * 128, 128) for c in range(16)] + [(2 * sq_ + 2 * sv_, 8)]

        def evac(bi, ps, pb):
            e = 'act' if bi % 2 else 'dve'
            if bi < 8:
                self.cp(qT[:, bi, 0:T], ps, [pb], [b_q], e)
            elif bi < 16:
                self.cp(kT[:, bi - 8, 0:T], ps, [pb], [b_k], e)
            elif bi < 32:
                self.cp(vT[:, bi - 16, 0:T], ps, [pb], [b_v], e)
            else:
                self.ts(gT[:, 0:T], ps, self.gmisc[0:8, 54 + layer:55 + layer], None, ALU.add, None, [pb, self.b_gm], [b_g])

        self.linear(w_in, blocks, [(0, KC)], lambda c: (self.xb[:, c, 0:T], self.b_xb), T, evac)
        k.barrier()
        self.dbg('mproj')
        CT = self.fv(0, 4096).rearrange("p (c h v) -> p c h v", c=2, h=H); b_ct = Buf()
        CTb = self.bv(4096, 4096).rearrange("p (c h v) -> p c h v", c=2, h=H)
        nT = self.nst[:, layer, grp]
        nTb = self.nsb[:, layer, grp]
        mbc = self.mst[:, layer, grp, :]
        ghb = self.fv(O_GH, 2048).rearrange("p (h v) -> p h v", h=H); b_gh = Buf()
        k.dma('sp', ghb[0:L], d['a_g_head'][layer:layer + 1, :].broadcast_to([L, H * DV]).rearrange("p (h v) -> p h v", h=H), (), [b_gh])
        env = dict(layer=layer, qT=qT, kT=kT, vT=vT, gT=gT, CT=CT, CTb=CTb, nT=nT, nTb=nTb, mbc=mbc, ghb=ghb, S0=S0,
                   b_q=b_q, b_k=b_k, b_v=b_v, b_g=b_g, b_ct=b_ct, b_gh=b_gh)
        if grp == 1:
            stg = self.fv(O_SSTG, 4096).rearrange("p (h c k) -> p h c k", h=H, c=4); sb_ = Buf()
            k.dma('sp', stg, d['sC'][layer].rearrange("h (c p) k -> p h c k", p=128), (), [sb_])
            for h in range(H):
                for kc in range(2):
                    ps, pb = self.ps[(h * 2 + kc) % 2], self.bps[(h * 2 + kc) % 2]
                    for vc in range(4):
                        self.tr(ps[:, vc * 128:(vc + 1) * 128], stg[:, h, vc, kc * 128:(kc + 1) * 128], 128, [sb_], [pb])
                    self.cp(CT[:, kc, h, :], ps[:, :], [pb], [b_ct])
        else:
            self.ms(self.fv(0, 4096), 0.0, [b_ct], 'pool')
            self.ms(nT, 0.0, [self.b_nst], 'dve')
            self.ms(mbc, NEG, [self.b_mst], 'dve')
            self.ms(self.fsum[:], 0.0, [self.b_mst], 'dve')
            k.barrier()
            self.cp(self.bv(4096, 4096), self.fv(0, 4096), [b_ct], [b_ct])
            self.cp(nTb, nT, [self.b_nst], [self.b_nst])
            self.scan_chunks(env, True)
            k.barrier()
            self.exchange_state(layer, env)
            k.barrier()
        self.cp(self.bv(4096, 4096), self.fv(0, 4096), [b_ct], [b_ct])
        self.cp(nTb, nT, [self.b_nst], [self.b_nst])
        self.scan_chunks(env, False)
        k.barrier()
        self.dbg('scan')
        last = (grp == 1) or (self.sidx == NR - 1)
        if grp == 0:
            stt_ = self.fv(S0 + 200, 16); sbb = Buf()
            self.cp(stt_[:, 0:8].rearrange("p (c h) -> p c h", c=2), nT[:, :, :, 0], [self.b_nst], [sbb])
            self.cp(stt_[:, 8:12], mbc, [self.b_mst], [sbb])
            self.ms(stt_[:, 12:16], 0.0, [sbb], 'dve')
            for c in range(2):
                k.dma('sp', self.xsC[layer][1][c][:, :], self.fv(c * 2048, 2048), [b_ct, self.b_xg[layer]], [self.b_xs2[layer]])
            k.dma('sp', self.xsS[layer][128:256, :], stt_, [sbb, self.b_xg[layer]], [self.b_xs2[layer]])
        if last:
            Cd = (o['sCo'] if grp == 1 else o['pC'])[layer]
            nd = (o['sno'] if grp == 1 else o['pn'])[layer]
            md = (o['smo'] if grp == 1 else o['pm'])[layer]
            stg = self.fv(O_SSTG, 4096).rearrange("p (h c k) -> p h c k", h=H, c=4); sb_ = Buf()
            for h in range(H):
                for vc in range(4):
                    ps, pb = self.ps[vc % 2], self.bps[vc % 2]
                    for kc in range(2):
                        self.tr(ps[:, kc * 128:(kc + 1) * 128], CT[:, kc, h, vc * 128:(vc + 1) * 128], 128, [b_ct], [pb])
                    self.cp(stg[:, h, vc, :], ps[:, 0:256], [pb], [sb_])
            k.dma('sp', Cd.rearrange("h (c p) k -> p h c k", p=128), stg, [sb_], [self.b_out], dsem='out')
            nf = self.fv(S0, 8); nb = Buf()
            self.cp(nf.rearrange("p (h c) -> p c h", h=H), nT[:, :, :, 0], [self.b_nst], [nb])
            ps, pb = self.ps[2], self.bps[2]
            self.tr(ps[0:8, 0:128], nf, 128, [nb], [pb])
            nf2 = self.fv(S0 + 16, 128, 8)
            self.cp(nf2, ps[0:8, 0:128], [pb], [nb])
            k.dma('sp', nd.rearrange("(c p) -> c p", p=128), nf2, [nb], [self.b_out], dsem='out')
            k.dma('sp', md.rearrange("(o h) -> o h", o=1), mbc[0:1, :], [self.b_mst], [self.b_out], dsem='out')
        k.barrier()
        def evac_o(bi, ps, pb):
            sg2 = sgt2[:, bi % 2, 0:T]
            self.act(sg2, ps, AF.Sigmoid, [pb], [b_sg2[bi % 2]])
            self.tt(vT[:, bi, 0:T], vT[:, bi, 0:T], sg2, ALU.mult, [b_sg2[bi % 2], b_v], [b_v], 'pool' if bi % 2 else 'dve')
        self.linear(w_in, [(2 * sq_ + sv_ + c * 128, 128) for c in range(16)], [(0, KC)], lambda c: (self.xb[:, c, 0:T], self.b_xb), T, evac_o)
        hr = self.bv(S0, 16 * SEG).rearrange("p (c t) -> p c t", c=16); b_hr = Buf()
        k.barrier()
        for c in range(16):
            self.cp(hr[:, c, 0:T], vT[:, c, 0:T], [b_v], [b_hr], 'dve' if c % 2 else 'act')
        self.linear(d['a_w_out'][layer], [(c * 128, 128) for c in range(KC)], [(0, KC)], lambda c: (hr[:, c, 0:T], b_hr), T,
                    self.resid_evac(), extra_R=[self.b_x])

    def scan_chunks(self, env, state_only):
        nc, k = self.nc, self.k
        T, L = self.T, self.L
        layer = env['layer']
        qT, kT, vT, gT, CT, CTb, nT, nTb, mbc, ghb, S0 = (env[x] for x in ('qT', 'kT', 'vT', 'gT', 'CT', 'CTb', 'nT', 'nTb', 'mbc', 'ghb', 'S0'))
        b_q, b_k, b_v, b_g, b_ct, b_gh = (env[x] for x in ('b_q', 'b_k', 'b_v', 'b_g', 'b_ct', 'b_gh'))
        k_c = self.bv(S0, 1024); v_c = self.bv(S0 + 512, 2048); wv = self.bv(S0 + 1536, 2048)
        junk = self.fv(S0 + 1536, 512)
        hh = self.fv(S0 + 2560, 2048)
        sm_ = self.fv(S0 + 4608, 64)
        dg = self.fv(S0 + 4672, 512); dl = self.fv(S0 + 5184, 512); dT = self.fv(S0 + 5696, 512)
        sdT = self.bv(S0 + 6208, 512)
        qs = self.bv(S0 + 6464, 1024).rearrange("p (c t) -> p c t", c=8)
        bcs = self.fv(S0 + 6976, 32)
        w2r = self.bv(S0 + 7008, 2)
        bS = Buf('scan')
        P = self.ps; B = self.bps
        for c in range(T // L):
            c0 = c * L
            R = [bS, b_q, b_k, b_v, b_g, b_ct, b_gh, self.b_nst, self.b_mst, self.b_const]
            Wb = [bS]
            for g4 in range(2):
                pbf = P[0][0:L, :].bitcast(BF16)
                for j in range(4):
                    self.tr(pbf[:, j * 128:(j + 1) * 128], kT[:, g4 * 4 + j, c0:c0 + L], 128, R, [B[0]], bf=True)
                self.cp(k_c[0:L, g4 * 512:(g4 + 1) * 512], pbf[:, 0:512], [B[0]], Wb)
            for g4 in range(4):
                pp = 1 + g4 % 2
                for j in range(4):
                    self.tr(P[pp][0:L, j * 128:(j + 1) * 128], vT[:, g4 * 4 + j, c0:c0 + L], 128, R, [B[pp]])
                self.cp(v_c[0:L, g4 * 512:(g4 + 1) * 512], P[pp][0:L, :], [B[pp]], Wb, 'act')
            self.tr(P[3][0:L, 0:8], gT[:, c0:c0 + L], 8, R, [B[3]])
            gi = sm_[0:L, 0:4]; sp_ = sm_[0:L, 4:8]; b_ = sm_[0:L, 8:12]; a_ = sm_[0:L, 12:16]
            il = sm_[0:L, 16:20]; mt = sm_[0:L, 20:24]; mb = sm_[0:L, 20:28]; u_ = sm_[0:L, 28:32]
            iw = sm_[0:L, 32:36]; en = sm_[0:L, 36:40]; w_ = sm_[0:L, 40:44]; den = sm_[0:L, 44:48]
            ss = sm_[0:L, 48:52]; rr = sm_[0:L, 52:56]; mloc = sm_[0:L, 56:60]; tmp4 = sm_[0:L, 60:64]
            self.cp(gi, P[3][0:L, 0:4], [B[3]], Wb)
            self.act(sp_, P[3][0:L, 4:8], AF.Exp, [B[3]], Wb, scale=-1.0)
            self.act(sp_, sp_, AF.Ln, [bS], Wb, bias=1.0)
            self.mm(P[3][0:L, 8:12], self.uneg[L][:, :], sp_, True, True, R, [B[3]])
            self.cp(b_, P[3][0:L, 8:12], [B[3]], Wb)
            self.cp(sm_[0:L, 24:28], b_, R, Wb)
            self.tt(a_, gi, b_, ALU.subtract, R, Wb)
            idl = self.ident[0:L, 0:L]
            dg3 = dg[0:L, 0:4 * L].rearrange("p (h s) -> p h s", h=H)
            dl3 = dl[0:L, 0:4 * L].rearrange("p (h s) -> p h s", h=H)
            dT3 = dT[0:L, 0:4 * L].rearrange("p (h s) -> p h s", h=H)
            sd3 = sdT[0:L, 0:4 * L].rearrange("p (h s) -> p h s", h=H)

            def rowbc(col4, ps_ap, psb, lhs):
                self.tt(dg3, bc(idl.unsqueeze(1), [L, H, L]), bc(col4.unsqueeze(2), [L, H, L]), ALU.mult, R, Wb)
                self.mm(ps_ap, lhs, dg[0:L, 0:4 * L], True, True, R, [psb])
            rowbc(a_, P[4][0:L, 0:4 * L], B[4], self.ones_f[0:L, 0:L])
            p4 = P[4][0:L, 0:4 * L].rearrange("p (h s) -> p h s", h=H)
            self.tt(dl3, p4, bc(b_.unsqueeze(2), [L, H, L]), ALU.add, [B[4]] + R, Wb)
            self.tt(dl3, dl3, bc(self.mask[L][:, :].unsqueeze(1), [L, H, L]), ALU.add, R, Wb)
            self.k.op('dve', lambda: nc.vector.tensor_reduce(out=mloc, in_=dl3, axis=AX.X, op=ALU.max), R, Wb)
            self.tt(il, b_, mbc[0:L, :], ALU.add, R, Wb)
            self.tt(mt, il, mloc, ALU.max, R, Wb)
            if not state_only:
                self.tt(u_, b_, mt, ALU.subtract, R, Wb)
                self.tt(tmp4, il, mt, ALU.subtract, R, Wb)
                self.act(iw, tmp4, AF.Exp, R, Wb)
                self.act(en, mt, AF.Exp, R, Wb, scale=-1.0)
                rowbc(u_, P[4][0:L, 0:4 * L], B[4], self.ones_f[0:L, 0:L])
                self.tt(dT3, p4, bc(a_.unsqueeze(2), [L, H, L]), ALU.add, [B[4]] + R, Wb)
                self.tt(dT3, dT3, bc(self.maskT[L][:, :].unsqueeze(1), [L, H, L]), ALU.add, R, Wb)
                self.act(dT[0:L, 0:4 * L], dT[0:L, 0:4 * L], AF.Exp, R, Wb)
                for h in range(H):
                    for kc in range(2):
                        self.mm(P[5][0:L, h * L:(h + 1) * L], kT[:, h * 2 + kc, c0:c0 + L], qT[:, h * 2 + kc, c0:c0 + L], kc == 0, kc == 1, R, [B[5]])
                self.stt(sdT[0:L, 0:4 * L], P[5][0:L, 0:4 * L], DK ** -0.5, dT[0:L, 0:4 * L], ALU.mult, ALU.mult, [B[5]] + R, Wb)
                rowbc(iw, P[4][:, 0:4 * L], B[4], self.ones_f[0:L, :])
                pw = P[4][:, 0:4 * L].rearrange("p (h t) -> p h t", h=H)
                for kc in range(2):
                    qv = qT[:, :, c0:c0 + L].rearrange("p (h c) t -> p h c t", c=2)[:, :, kc, :]
                    qsv = qs[:, :, 0:L].rearrange("p (h c) t -> p h c t", c=2)[:, :, kc, :]
                    self.tt(qsv, qv, pw, ALU.mult, [B[4]] + R, Wb)
                for h in range(H):
                    pn_, bn_ = P[6 + h % 2], B[6 + h % 2]
                    self.mm(pn_[0:L, :], sd3[:, h, :], v_c[0:L, h * DV:(h + 1) * DV], True, False, R, [bn_])
                    for kc in range(2):
                        self.mm(pn_[0:L, :], qs[:, h * 2 + kc, 0:L], CTb[:, kc, h, :], False, kc == 1, R, [bn_])
                    self.mm(P[3][0:L, 16 + 2 * h:18 + 2 * h], sd3[:, h, :], self.ones_b[0:L, 0:2], True, False, R, [B[3]])
                    for kc in range(2):
                        self.mm(P[3][0:L, 16 + 2 * h:18 + 2 * h], qs[:, h * 2 + kc, 0:L], nTb[:, kc, h, :], False, kc == 1, R, [B[3]])
                    qn = P[3][0:L, 16 + 2 * h:17 + 2 * h]
                    self.act(den[:, h:h + 1], qn, AF.Abs, [B[3]] + R, Wb)
                    self.tt(den[:, h:h + 1], den[:, h:h + 1], en[:, h:h + 1], ALU.max, R, Wb)
                    self.recip(den[:, h:h + 1], den[:, h:h + 1], R, Wb)
                    self.act(junk[0:L, :], pn_[0:L, :], AF.Square, [bn_] + R, Wb, scale=den[:, h:h + 1], accum=ss[:, h:h + 1])
                    self.act(rr[:, h:h + 1], ss[:, h:h + 1], AF.Sqrt, R, Wb, bias=self.epsc[0:L, 0:1], scale=1.0 / DV)
                    self.recip(rr[:, h:h + 1], rr[:, h:h + 1], R, Wb)
                    self.tt(rr[:, h:h + 1], rr[:, h:h + 1], den[:, h:h + 1], ALU.mult, R, Wb)
                    self.stt(hh[0:L, h * DV:(h + 1) * DV], pn_[0:L, :], rr[:, h:h + 1], ghb[0:L, h, :], ALU.mult, ALU.mult, [bn_] + R, Wb)
            self.mm(P[3][:, 32:40], self.sel[L][:, :], mb, True, True, R, [B[3]])
            self.cp(bcs[:, 0:8], P[3][:, 32:40], [B[3]], Wb)
            mnew = bcs[:, 0:4]; blast = bcs[:, 4:8]; dec = bcs[:, 8:12]; t12 = bcs[:, 12:16]
            self.tt(self.fsum[:], self.fsum[:], blast, ALU.add, R, [self.b_mst, bS])
            self.tt(t12, blast, mnew, ALU.subtract, R, Wb)
            self.tt(dec, t12, mbc, ALU.add, R, Wb)
            self.act(dec, dec, AF.Exp, R, Wb)
            self.tt(w_, a_, t12[0:L, :], ALU.add, R, Wb)
            self.act(w_, w_, AF.Exp, R, Wb)
            self.ts(w_, w_, DK ** -0.5, None, ALU.mult, None, R, Wb)
            self.tt(wv[0:L, :].rearrange("p (h v) -> p h v", h=H), v_c[0:L, :].rearrange("p (h v) -> p h v", h=H),
                    bc(w_.unsqueeze(2), [L, H, DV]), ALU.mult, R, Wb)
            for h in range(H):
                self.cp(w2r[0:L, :], bc(w_[:, h:h + 1], [L, 2]), R, Wb)
                for kc in range(2):
                    pc, bcb = P[6 + kc], B[6 + kc]
                    self.mm(pc[:, :], k_c[0:L, h * DK + kc * 128:h * DK + (kc + 1) * 128], wv[0:L, h * DV:(h + 1) * DV], True, True, R, [bcb])
                    self.stt(CT[:, kc, h, :], CT[:, kc, h, :], dec[:, h:h + 1], pc[:, :], ALU.mult, ALU.add, [bcb, b_ct] + R, [b_ct])
                    if not state_only:
                        self.cp(CTb[:, kc, h, :], CT[:, kc, h, :], [b_ct], [b_ct], 'act')
                    self.mm(P[3][:, 48:50], k_c[0:L, h * DK + kc * 128:h * DK + (kc + 1) * 128], w2r[0:L, :], True, True, R, [B[3]])
                    self.stt(nT[:, kc, h, :], nT[:, kc, h, :], dec[:, h:h + 1], P[3][:, 48:50], ALU.mult, ALU.add,
                             [B[3], self.b_nst] + R, [self.b_nst])
                    if not state_only:
                        self.cp(nTb[:, kc, h, :], nT[:, kc, h, :], [self.b_nst], [self.b_nst])
            self.cp(mbc, mnew, R, [self.b_mst])
            if not state_only:
                for g4 in range(4):
                    pp = 4 + g4 % 2
                    for j in range(4):
                        self.tr(P[pp][:, j * L:(j + 1) * L], hh[0:L, (g4 * 4 + j) * 128:(g4 * 4 + j + 1) * 128], L, R, [B[pp]])
                    self.cp(vT[:, g4 * 4:g4 * 4 + 4, c0:c0 + L], P[pp][:, 0:4 * L].rearrange("p (j t) -> p j t", j=4), [B[pp]], [b_v, bS])

    def exchange_state(self, layer, env):
        nc, k = self.nc, self.k
        CT, nT, mbc, S0, b_ct = env['CT'], env['nT'], env['mbc'], env['S0'], env['b_ct']
        grp = self.grp
        sc = self.fv(S0 + 300, 600)
        bsc = Buf()
        stt_ = sc[:, 0:16]
        self.cp(stt_[:, 0:8].rearrange("p (c h) -> p c h", c=2), nT[:, :, :, 0], [self.b_nst], [bsc])
        self.cp(stt_[:, 8:12], mbc, [self.b_mst], [bsc])
        self.cp(stt_[:, 12:16], self.fsum[:], [self.b_mst], [bsc])
        for c in range(2):
            k.dma('sp', self.xsC[layer][0][c][:, :], self.fv(c * 2048, 2048), [b_ct, self.b_xg[layer]], [self.b_xs2[layer]])
        k.dma('sp', self.xsS[layer][0:128, :], stt_, [bsc, self.b_xg[layer]], [self.b_xs2[layer]])
        for t in range(2):
            for c in range(2):
                k.collective(self.xsC[layer][t][c][:, :], self.xgC[layer][t][c][:, :], GROUPS, [self.b_xs2[layer]], [self.b_xg[layer]])
        k.collective(self.xsS[layer][:, :], self.xgS[layer][:, :], GROUPS, [self.b_xs2[layer]], [self.b_xg[layer]])
        sg = sc[:, 16:16 + 128].rearrange("p (i t c) -> p i t c", i=4, t=2)
        k.dma('sp', sg, self.xgS[layer].rearrange("(i t p) c -> p i t c", t=2, p=128), [self.b_xg[layer]], [bsc])
        cm = self.cm
        F3 = sg[:, :, 0, 12:16]
        m3 = sg[:, :, 0, 8:12]
        mcar = sg[:, 3, 1, 8:12]
        Rr = [bsc, self.b_gn]
        T1 = sc[:, 144:208].rearrange("p (i h l) -> p i h l", i=4, h=4)
        Mv = cm[:, 8:24].rearrange("p (i l) -> p i l", i=4)
        self.tt(T1, bc(Mv.unsqueeze(2), [128, 4, 4, 4]), bc(F3.rearrange("p l h -> p h l").unsqueeze(1), [128, 4, 4, 4]), ALU.mult, Rr, [bsc])
        G = sc[:, 208:224].rearrange("p (i h) -> p i h", i=4)
        self.k.op('dve', lambda: nc.vector.tensor_reduce(out=G, in_=T1, axis=AX.X, op=ALU.add), Rr, [bsc])
        E = sc[:, 224:240].rearrange("p (i h) -> p i h", i=4)
        self.tt(E, m3, G, ALU.add, Rr, [bsc])
        self.tt(E, E, bc(cm[:, 0:4].unsqueeze(2), [128, 4, 4]), ALU.add, Rr, [bsc])
        T2 = sc[:, 240:256].rearrange("p (h l) -> p h l", h=4)
        self.tt(T2, bc(cm[:, 4:8].unsqueeze(1), [128, 4, 4]), F3.rearrange("p l h -> p h l"), ALU.mult, Rr, [bsc])
        Ec = sc[:, 256:260]
        self.k.op('dve', lambda: nc.vector.tensor_reduce(out=Ec, in_=T2, axis=AX.X, op=ALU.add), Rr, [bsc])
        self.tt(Ec, Ec, mcar, ALU.add, Rr, [bsc])
        mx = sc[:, 260:264]
        self.k.op('dve', lambda: nc.vector.tensor_reduce(out=mx, in_=E.rearrange("p i h -> p h i"), axis=AX.X, op=ALU.max), Rr, [bsc])
        min_ = sc[:, 264:268]
        self.tt(min_, mx, Ec, ALU.max, Rr, [bsc])
        Wt = sc[:, 268:284].rearrange("p (i h) -> p i h", i=4)
        self.tt(Wt, E, bc(min_.unsqueeze(1), [128, 4, 4]), ALU.subtract, Rr, [bsc])
        self.act(sc[:, 268:284], sc[:, 268:284], AF.Exp, Rr, [bsc])
        Wc = sc[:, 284:288]
        self.tt(Wc, Ec, min_, ALU.subtract, Rr, [bsc])
        self.act(Wc, Wc, AF.Exp, Rr, [bsc])
        nacc = sc[:, 288:296].rearrange("p (c h) -> p c h", c=2)
        ntmp = sc[:, 296:304].rearrange("p (c h) -> p c h", c=2)
        self.tt(nacc, sg[:, 3, 1, 0:8].rearrange("p (c h) -> p c h", c=2), bc(Wc.unsqueeze(1), [128, 2, 4]), ALU.mult, Rr, [bsc])
        for i in range(4):
            self.tt(ntmp, sg[:, i, 0, 0:8].rearrange("p (c h) -> p c h", c=2), bc(Wt[:, i, :].unsqueeze(1), [128, 2, 4]), ALU.mult, Rr, [bsc])
            self.tt(nacc, nacc, ntmp, ALU.add, Rr, [bsc])
        for e in range(2):
            self.cp(nT[:, :, :, e], nacc, Rr, [self.b_nst])
        self.cp(mbc, min_, Rr, [self.b_mst])
        stg = self.fv(6144, 4096).rearrange("p (c h v) -> p c h v", c=2, h=H); bst = Buf()
        srcs = [(1, 3, None)] + [(0, i, i) for i in range(4)]
        for si, (tt_, rk, i) in enumerate(srcs):
            for c in range(2):
                k.dma('sp', self.fv(6144 + c * 2048, 2048),
                      self.xgC[layer][tt_][c][rk * 128:(rk + 1) * 128, :], [self.b_xg[layer]], [bst])
            for kc in range(2):
                for h in range(H):
                    if i is None:
                        self.ts(CT[:, kc, h, :], stg[:, kc, h, :], Wc[:, h:h + 1], None, ALU.mult, None, [bst] + Rr, [b_ct])
                    else:
                        self.stt(CT[:, kc, h, :], stg[:, kc, h, :], Wt[:, i, h:h + 1], CT[:, kc, h, :], ALU.mult, ALU.add, [bst, b_ct] + Rr, [b_ct])

    def rope_tables(self, pos0, T, o_cs):
        cs = self.fv(o_cs, SEG, 64); sn = self.fv(o_cs + 512, SEG, 64)
        self.b_ropet = Buf()
        self.k.dma('sp', cs[:, 0:T], self.d['cos2'][:, pos0:pos0 + T], (), [self.b_ropet])
        self.k.dma('sp', sn[:, 0:T], self.d['sin2'][:, pos0:pos0 + T], (), [self.b_ropet])
        self.rope_cs, self.rope_sn = cs, sn

    def rope(self, dst, dstb, src, srcb, T, o_tmp, slots=2):
        if not hasattr(self, 'b_rope'):
            self.b_rope = [Buf(), Buf()]
            self.rope_i = 0
        i = self.rope_i % slots
        self.rope_i += 1
        t1 = self.fv(o_tmp + i * 1024, SEG, 64); t2 = self.fv(o_tmp + i * 1024 + 512, SEG, 64)
        tb = self.b_rope[i]
        ps, pb = self.ps[3], self.bps[3]
        self.mm(ps[0:64, 0:T], self.perm[:, :], src, True, True, [srcb, self.b_const], [pb])
        self.tt(t1[:, 0:T], ps[0:64, 0:T], self.rope_sn[:, 0:T], ALU.mult, [pb, self.b_ropet], [tb])
        self.tt(t2[:, 0:T], src, self.rope_cs[:, 0:T], ALU.mult, [srcb, self.b_ropet], [tb], 'pool')
        self.tt(dst, t1[:, 0:T], t2[:, 0:T], ALU.add, [tb], [dstb])

    def shared_kv(self, pos0):
        nc, k, d, o = self.nc, self.k, self.d, self.o
        T, grp, s = self.T, self.grp, self.sidx
        P0 = self.O_PH
        zT = self.fv(P0, 5 * SEG).rearrange("p (c t) -> p c t", c=5); b_z = Buf()
        cTf = self.fv(P0 + 2560, 4 * SEG).rearrange("p (c t) -> p c t", c=4); b_c = Buf()
        cTb = self.bv(P0 + 4608, 4 * SEG).rearrange("p (c t) -> p c t", c=4)
        kpT = self.fv(P0 + 5632, SEG, 64); b_kp = Buf()
        kpn = self.bv(P0 + 6144, SEG, 64)
        r2 = self.fv(P0 + 6400, SEG); b_r2 = Buf()
        tm = self.fv(P0 + 6912, 576); ld = self.fv(P0 + 7488, 576)
        self.o_kv = P0 + 8300
        blocks = [(c * 128, 128) for c in range(4)] + [(KVL, 64)]

        def evac(bi, ps, pb):
            m = 128 if bi < 4 else 64
            self.cp(zT[0:m, bi, 0:T], ps, [pb], [b_z], 'act' if bi % 2 else 'dve')
        self.linear(d['kv_w_down'], blocks, [(0, KC)], lambda c: (self.xb[:, c, 0:T], self.b_xb), T, evac)
        self.sumsq_bc(lambda c: (zT[:, c, 0:T], b_z), 4, T, r2[:, 0:T], b_r2, KVL)
        for c in range(4):
            self.stt(cTf[:, c, 0:T], zT[:, c, 0:T], self.gmisc[:, 16 + c:17 + c], r2[:, 0:T], ALU.mult, ALU.mult, [b_z, b_r2, self.b_gm], [b_c])
            self.cp(cTb[:, c, 0:T], cTf[:, c, 0:T], [b_c], [b_c], 'act')
        self.sumsq_bc(lambda c: (zT[0:64, 4, 0:T], b_z), 1, T, r2[0:64, 0:T], b_r2, ROPE, rows=64)
        self.stt(kpn[:, 0:T], zT[0:64, 4, 0:T], self.gmisc[0:64, 20:21], r2[0:64, 0:T], ALU.mult, ALU.mult, [b_z, b_r2, self.b_gm], [b_kp])
        self.rope_tables(pos0, T, P0 + 18100)
        self.rope(kpT[:, 0:T], b_kp, kpn[:, 0:T], b_kp, T, P0 + 19124, slots=1)
        k0 = 0 if grp == 0 else PAST
        kb = self.b_kvloc if grp == 0 else self.b_kvd[grp]
        kpb = self.bv(P0 + 8000, SEG, 64); b_kpb = Buf()
        self.cp(kpb[:, 0:T], kpT[:, 0:T], [b_kp], [b_kpb])
        if grp == 0:
            k.dma('sp', self.KRp[:, 0:T], kpb[:, 0:T], [b_kpb] + [self.b_kvall[r] for r in range(NR)], [kb], dsem='kv')
        else:
            k.dma('sp', self.kr_dram[grp][:, k0:k0 + T], kpb[:, 0:T], [b_kpb], [kb], dsem='kv')
        cdst = (o['p_ckv'][s * SEG:(s + 1) * SEG] if grp == 0 else o['s_ckv'])
        kdst = (o['p_kpe'][s * SEG:(s + 1) * SEG] if grp == 0 else o['s_kpe'])
        tb = Buf()
        for t0 in range(0, T, 128):
            n = min(128, T - t0)
            ps, pb = self.ps[2], self.bps[2]
            for c in range(4):
                self.tr(ps[0:n, c * 128:(c + 1) * 128], cTf[:, c, t0:t0 + n], 128, [b_c], [pb])
            self.cp(tm[0:n, 0:512], ps[0:n, :], [pb], [tb])
            ps, pb = self.ps[3], self.bps[3]
            self.tr(ps[0:n, 0:64], kpT[:, t0:t0 + n], 64, [b_kp], [pb])
            self.cp(tm[0:n, 512:576], ps[0:n, 0:64], [pb], [tb])
            k.dma('sp', cdst[t0:t0 + n, :], tm[0:n, 0:512], [tb], [self.b_out], dsem='out')
            k.dma('sp', kdst[t0:t0 + n, :], tm[0:n, 512:576], [tb], [self.b_out], dsem='out')
        self.kv_up(cTb, b_c, T, k0)
        if grp == 0:
            k.barrier()
            for j in range(2):
                k.collective(self.KP[j][:, :], self.KG[s][j][:, :], GROUPS, [self.b_kvloc], [self.b_kvall[s]])
                k.collective(self.VP[j][:, :], self.VG[s][j][:, :], GROUPS, [self.b_kvloc], [self.b_kvall[s]])
            k.collective(self.KRp[:, :], self.KRG[s][:, :], GROUPS, [self.b_kvloc], [self.b_kvall[s]])
        if grp == 1:
            k.barrier()
            lb = Buf()
            for blk in range(PAST // SEG):
                for t0 in range(0, SEG, 128):
                    r0 = blk * SEG + t0
                    k.dma('sp', ld[:, 0:512], d['ckv'][r0:r0 + 128, :], (), [lb])
                    k.dma('sp', ld[:, 512:576], d['kpe'][r0:r0 + 128, :], (), [lb])
                    ps, pb = self.ps[2], self.bps[2]
                    for c in range(4):
                        self.tr(ps[:, c * 128:(c + 1) * 128], ld[:, c * 128:(c + 1) * 128], 128, [lb], [pb])
                    self.cp(cTb[:, :, t0:t0 + 128], ps[:, :].rearrange("p (c t) -> p c t", c=4), [pb], [b_c])
                    ps, pb = self.ps[3], self.bps[3]
                    self.tr(ps[0:64, 0:128], ld[:, 512:576], 128, [lb], [pb])
                    self.cp(kpT[:, t0:t0 + 128], ps[0:64, 0:128], [pb], [b_kp])
                self.cp(kpb[:, 0:SEG], kpT[:, 0:SEG], [b_kp], [b_kpb])
                k.dma('sp', self.kr_dram[1][:, blk * SEG:(blk + 1) * SEG], kpb[:, 0:SEG], [b_kpb], [kb], dsem='kv')
                self.kv_up(cTb, b_c, SEG, blk * SEG)

    def kv_up(self, cTb, b_c, T, k0):
        nc, k, d = self.nc, self.k, self.d
        grp = self.grp
        kb = self.b_kvloc if grp == 0 else self.b_kvd[grp]
        O = self.o_kv
        k.barrier()
        kn = self.fv(O, 2 * SEG).rearrange("p (s t) -> p s t", s=2); bkn = [Buf(), Buf()]
        r3s = [self.fv(O + 1024, SEG), self.fv(O + 9216, SEG)]; b_r3s = [Buf(), Buf()]
        wvs = self.bv(O + 1536, 4 * 2048).rearrange("p (c n) -> p c n", c=4); b_wv = Buf()
        stg = self.fv(O + 5632, 2048); b_st = Buf()
        vt = self.bv(O + 7680, 2048); b_vt = Buf()
        knh = self.bv(O + 8704, 2 * SEG).rearrange("p (s t) -> p s t", s=2)
        blocks = [(h * 256, 128) for h in range(BH)]

        def evac(bi, ps, pb):
            sl_ = bi % 2
            r3, b_r3 = r3s[sl_], b_r3s[sl_]
            self.cp(kn[:, sl_, 0:T], ps, [pb], [bkn[sl_]], 'act')
            self.sumsq_bc(lambda c: (kn[:, sl_, 0:T], bkn[sl_]), 1, T, r3[:, 0:T], b_r3, NOPE, ps_id=3 - sl_)
            self.stt(kn[:, sl_, 0:T], kn[:, sl_, 0:T], self.gmisc[:, 21:22], r3[:, 0:T], ALU.mult, ALU.mult, [bkn[sl_], b_r3, self.b_gm], [bkn[sl_]])
            self.cp(knh[:, sl_, 0:T], kn[:, sl_, 0:T], [bkn[sl_]], [bkn[sl_]], 'pool')
            kdst = self.KP[bi // 8][(bi % 8) * 128:(bi % 8 + 1) * 128, 0:T] if grp == 0 else self.kn_dram[grp][bi, :, k0:k0 + T]
            k.dma('sp', kdst, knh[:, sl_, 0:T], [bkn[sl_]], [kb], dsem='kv')
        self.linear(d['kv_w_up'], blocks, [(0, 4)], lambda c: (cTb[:, c, 0:T], b_c), T, evac, sets=[[0], [1], [4], [5]])
        wv_ = d['kv_w_up'].rearrange("(c p) (h two x) -> p c h two x", p=128, two=2, x=128)
        for c in range(4):
            k.dma('sp', stg.rearrange("p (h x) -> p h x", h=BH), wv_[:, c, :, 1, :], (), [b_st])
            self.cp(wvs[:, c, :], stg, [b_st], [b_wv])
        for t0 in range(0, T, 128):
            n = min(128, T - t0)
            for q4 in range(4):
                ps, pb = self.ps[q4 % 2], self.bps[q4 % 2]
                for c in range(4):
                    self.mm(ps[0:n, :], cTb[:, c, t0:t0 + n], wvs[:, c, q4 * 512:(q4 + 1) * 512], c == 0, c == 3, [b_c, b_wv], [pb])
                self.cp(vt[0:n, q4 * 512:(q4 + 1) * 512], ps[0:n, :], [pb], [b_vt], 'act' if q4 % 2 else 'dve')
            if grp == 0:
                for q in range(2):
                    k.dma('sp', self.VP[q][t0:t0 + n, :], vt[0:n, q * 1024:(q + 1) * 1024], [b_vt], [kb], dsem='kv')
            else:
                k.dma('sp', self.v_dram[grp][k0 + t0:k0 + t0 + n, :], vt[0:n, :], [b_vt], [kb], dsem='kv')

    def mla(self, j):
        nc, k, d, o = self.nc, self.k, self.d, self.o
        T, grp, s = self.T, self.grp, self.sidx
        pos0 = s * SEG if grp == 0 else NR * SEG
        P0 = self.O_PH
        cq = self.bv(P0, 6 * SEG).rearrange("p (c t) -> p c t", c=6); b_cq = Buf()
        r2 = self.fv(P0 + 1536, SEG); b_r2 = Buf()
        QN = self.bv(P0 + 2048, 16 * SEG).rearrange("p (c t) -> p c t", c=16); b_qn = Buf()
        QR = self.bv(P0 + 6144, 16 * SEG, 64).rearrange("p (c t) -> p c t", c=16); b_qr = Buf()
        r3 = self.fv(P0 + 10240, SEG); b_r3 = Buf()
        qtmp = self.fv(P0 + 10752, SEG); b_qt = Buf()
        qrn = self.bv(P0 + 11264, SEG, 64); b_qrn = Buf()
        O_CS = P0 + 11520
        rd = self.fv(P0 + 12544, SEG); b_rd = Buf()
        cqg = self.bv(P0 + 13100, 6 * SEG).rearrange("p (c t) -> p c t", c=6)

        def evac(bi, ps, pb):
            self.cp(cq[:, bi, 0:T], ps, [pb], [b_cq], 'act')
            self.ts(cqg[:, bi, 0:T], ps, self.gmisc[:, 6 * j + bi:6 * j + bi + 1], None, ALU.mult, None, [pb, self.b_gm], [b_cq])
        self.linear(d['b_w_dq'][j], [(c * 128, 128) for c in range(6)], [(0, KC)], lambda c: (self.xb[:, c, 0:T], self.b_xb), T, evac)
        self.sumsq_bc(lambda c: (cq[:, c, 0:T], b_cq), 6, T, r2[:, 0:T], b_r2, QL)
        blocks = []
        for h in range(BH):
            blocks.append((h * 192, 128)); blocks.append((h * 192 + 128, 64))

        def evac2(bi, ps, pb):
            h, isr = bi // 2, bi % 2
            m = 64 if isr else 128
            self.tt(qtmp[0:m, 0:T], ps, r2[0:m, 0:T], ALU.mult, [pb, b_r2], [b_qt])
            self.sumsq_bc(lambda c: (qtmp[0:m, 0:T], b_qt), 1, T, r3[0:m, 0:T], b_r3, m, rows=m, ps_id=3)
            if not isr:
                self.stt(QN[:, h, 0:T], qtmp[:, 0:T], self.gmisc[:, 12 + j:13 + j], r3[:, 0:T], ALU.mult, ALU.mult, [b_qt, b_r3, self.b_gm], [b_qn])
            else:
                self.stt(qrn[:, 0:T], qtmp[0:64, 0:T], self.gmisc[0:64, 14 + j:15 + j], r3[0:64, 0:T], ALU.mult, ALU.mult, [b_qt, b_r3, self.b_gm], [b_qrn])
                self.rope(QR[:, h, 0:T], b_qr, qrn[:, 0:T], b_qrn, T, P0 + 14700)
        self.rope_tables(pos0, T, O_CS)
        self.linear(d['b_w_uq'][j], blocks, [(0, 6)], lambda c: (cqg[:, c, 0:T], b_cq), T, evac2, sets=[[0, 1], [4, 5], [6, 7]])
        k.barrier()
        KB = 1024 if grp == 1 else SEG
        knb = self.bv(0, 2 * 1024).rearrange("p (s t) -> p s t", s=2); b_knb = [Buf(), Buf()]
        vb = self.bv(1024, 2 * 1024).rearrange("p (s t x) -> p s t x", s=2, x=128); b_vb = [Buf(), Buf()]
        krall = self.bv(3072, 9 * SEG, 64); b_kr = Buf()
        dacc = self.fv(5376, SEG); daccb = self.bv(5888, SEG); b_da = Buf()
        blocks = []
        if grp == 1:
            nkeys = PAST + DS
            for kk0 in range(0, nkeys, KB):
                nk = min(KB, nkeys - kk0)
                blocks.append((lambda h, kk0=kk0, nk=nk: self.kn_dram[1][h, :, kk0:kk0 + nk],
                               self.kr_dram[1][:, kk0:kk0 + nk],
                               lambda h, t, n, kk0=kk0: self.v_dram[1][kk0 + t * 128:kk0 + t * 128 + n, h * 128:(h + 1) * 128],
                               nk, False, None, self.b_kvd[1]))
        else:
            def mk(KS, KRS, VS, i, diag, bias, dep):
                return (lambda h: KS[h // 8][i * 1024 + (h % 8) * 128:i * 1024 + (h % 8 + 1) * 128, :],
                        KRS[i * ROPE:(i + 1) * ROPE, :],
                        lambda h, t, n: VS[h // 8][i * SEG + t * 128:i * SEG + t * 128 + n, (h % 8) * 128:(h % 8 + 1) * 128],
                        SEG, diag, bias, dep)
            for rdi in range(s):
                for i in range(4):
                    blocks.append(mk(self.KG[rdi], self.KRG[rdi], self.VG[rdi], i, False, None, self.b_kvall[rdi]))
            for i in range(4):
                blocks.append(mk(self.KG[s], self.KRG[s], self.VG[s], i, False, 32 + i, self.b_kvall[s]))
            blocks.append(mk(self.KP, self.KRp, self.VP, 0, True, None, self.b_kvloc))
        ntot = sum((blk[3] + 127) // 128 for blk in blocks)
        kro = 0
        for (knf, krap, vf, nk, diag, biascol, dep) in blocks:
            k.dma('sp', krall[:, kro:kro + nk], krap, [dep], [b_kr], dsem='ld')
            kro += nk
        DEPTH_ = 3
        sbanks = [0, 1, 6, 7]
        pTs = [self.bv(2048 + 256 * i, SEG) for i in range(4)]
        b_pTs = [Buf() for _ in range(4)]
        slot = 0
        pi = 0
        for h in range(BH):
            po, bo = self.ps[4 + h % 2], self.bps[4 + h % 2]
            pd, bd = self.ps[2 + h % 2], self.bps[2 + h % 2]
            tiles = []
            kro = 0
            for (knf, krap, vf, nk, diag, biascol, dep) in blocks:
                tiles.append(('load', knf, vf, nk, dep))
                for t in range((nk + 127) // 128):
                    n = min(128, nk - t * 128)
                    tiles.append(('tile', t, n, t * 128 if diag else 0, diag, biascol, kro))
                kro += nk
            pend = []
            state = dict(first=True, cnt=0, sl=0)

            def finish(item):
                (ps_, bs_, pp, bp, sl_, t, n, q0, diag, biascol) = item
                if biascol is None:
                    self.act(pp[0:n, q0:T], ps_[0:n, q0:T], AF.Exp, [bs_], [bp], scale=ATT_SCALE)
                else:
                    self.act(pp[0:n, q0:T], ps_[0:n, q0:T], AF.Exp, [bs_, self.b_gn], [bp], scale=ATT_SCALE,
                             bias=self.cm[0:n, biascol:biascol + 1])
                if diag:
                    self.ms(pp[64:128, q0:q0 + 64], 0.0, [bp], 'dve')
                state['cnt'] += 1
                self.mm(po[:, q0:T], vb[0:n, sl_, t, :], pp[0:n, q0:T], state['first'], state['cnt'] == ntot, [b_vb[sl_], bp], [bo])
                if state['first']:
                    assert n == 128 and q0 == 0
                    self.cp(dacc[:, 0:T], pp[:, 0:T], [bp], [b_da])
                else:
                    self.tt(dacc[0:n, q0:T], dacc[0:n, q0:T], pp[0:n, q0:T], ALU.add, [bp, b_da], [b_da])
                state['first'] = False

            for it in tiles:
                if it[0] == 'load':
                    _, knf, vf, nk, dep = it
                    sl_ = slot; slot ^= 1
                    state['sl'] = sl_
                    k.dma('sp', knb[:, sl_, 0:nk], knf(h), [dep], [b_knb[sl_]], dsem='ld')
                    nfull = nk // 128
                    if nfull:
                        k.dma('sp', vb[:, sl_, 0:nfull, :], vf(h, 0, nfull * 128).rearrange("(t p) d -> p t d", p=128), [dep], [b_vb[sl_]], dsem='ld')
                    if nk % 128:
                        n_ = nk % 128
                        k.dma('sp', vb[0:n_, sl_, nfull, :], vf(h, nfull, n_), [dep], [b_vb[sl_]], dsem='ld')
                    continue
                _, t, n, q0, diag, biascol, kro_ = it
                sl_ = state['sl']
                ps_, bs_ = self.ps[sbanks[pi % 4]], self.bps[sbanks[pi % 4]]
                pp, bp = pTs[pi % 4], b_pTs[pi % 4]
                pi += 1
                self.mm(ps_[0:n, q0:T], knb[:, sl_, t * 128:t * 128 + n], QN[:, h, q0:T], True, False, [b_knb[sl_], b_qn], [bs_])
                self.mm(ps_[0:n, q0:T], krall[:, kro_ + t * 128:kro_ + t * 128 + n], QR[:, h, q0:T], False, True, [b_kr, b_qr], [bs_])
                pend.append((ps_, bs_, pp, bp, sl_, t, n, q0, diag, biascol))
                if len(pend) > DEPTH_:
                    finish(pend.pop(0))
            while pend:
                finish(pend.pop(0))
            self.cp(daccb[:, 0:T], dacc[:, 0:T], [b_da], [b_da], 'act')
            self.mm(pd[:, 0:T], self.ones_b[:, :], daccb[:, 0:T], True, True, [b_da, self.b_const], [bd])
            self.recip(rd[:, 0:T], pd[:, 0:T], [bd], [b_rd])
            self.tt(QN[:, h, 0:T], po[:, 0:T], rd[:, 0:T], ALU.mult, [bo, b_rd], [b_qn])
        k.barrier()
        self.linear(d['b_w_o'][j], [(c * 128, 128) for c in range(KC)], [(0, KC)], lambda c: (QN[:, c, 0:T], b_qn), T,
                    self.resid_evac(), extra_R=[self.b_x])


_PROG = None


def _rope_tables():
    half = ROPE // 2
    inv = (10000.0 ** (-np.arange(half, dtype=np.float32) / half)).astype(np.float32)
    pos = np.concatenate([np.arange(SEQ), PAST + np.arange(DS)]).astype(np.float32)
    ang = pos[None, :] * inv[:, None]
    c, s = np.cos(ang).astype(np.float32), np.sin(ang).astype(np.float32)
    return np.concatenate([c, c], 0), np.concatenate([-s, s], 0)


def _core_mask(r):
    m = np.zeros((40,), np.float32)
    for i in range(4):
        m[i] = 0.0 if i < r else NEG
        m[4 + i] = 1.0 if i < r else 0.0
        for l in range(4):
            m[8 + 4 * i + l] = 1.0 if (i < l < r) else 0.0
        m[24 + i] = 1.0 if i == r - 1 else 0.0
        m[32 + i] = 0.0 if i < r else -30000.0
    m[28] = 1.0 if r == 0 else 0.0
    return np.ascontiguousarray(np.broadcast_to(m[None, :], (128, 40)))


def kernel(**inp):
    global _PROG
    if _PROG is None:
        _PROG = Prog()
    prog = _PROG
    f = lambda a: np.ascontiguousarray(np.asarray(a, dtype=np.float32))
    cos2, sin2 = _rope_tables()
    shared = {}
    for n in ['norm_mix', 'norm_ffn', 'a_w_in', 'a_b_gate', 'a_w_out', 'kv_w_down', 'kv_w_up', 'b_w_dq', 'b_g_cq', 'b_w_uq',
              'b_g_qn', 'b_g_qr', 'b_w_o', 'f_w_up', 'f_conv_w', 'f_conv_b', 'f_w_down']:
        shared[n] = f(inp[n])
    shared['a_g_head'] = f(inp['a_g_head']).reshape(NA, H * DV)
    for n in ['kv_norm', 'kv_g_c', 'kv_g_r', 'kv_g_kn']:
        shared[n] = f(inp[n]).reshape(1, -1)
    in_maps = []
    for c in range(8):
        g, r = c // 4, c % 4
        m = dict(shared)
        segs = [rd * 4 + r for rd in range(NR)]
        m['xp'] = f(np.concatenate([inp['x_prompt'][g][sg * SEG:(sg + 1) * SEG] for sg in segs], 0))
        cols = np.concatenate([np.arange(sg * SEG, (sg + 1) * SEG) for sg in segs] + [SEQ + np.arange(DS)])
        m['cos2'] = f(cos2[:, cols]); m['sin2'] = f(sin2[:, cols])
        m['cmask'] = _core_mask(r)
        m['xs'] = f(inp['x_sample'][c])
        m['ckv'] = f(inp['cache_ckv'][c]); m['kpe'] = f(inp['cache_kpe'][c])
        m['sC'] = f(inp['state_C'][:, c]); m['sn'] = f(inp['state_n'][:, c]).reshape(NA, H * DK); m['sm'] = f(inp['state_m'][:, c])
        m['sconv'] = f(inp['state_conv'][:, c])
        in_maps.append(m)
    res = run_bass_kernel_spmd(prog.nc, in_maps, core_ids=list(range(8))).results

    def seqcat(n):
        out = []
        for g in range(2):
            parts = [None] * (4 * NR)
            for r in range(4):
                for rd in range(NR):
                    parts[rd * 4 + r] = res[g * 4 + r][n][rd * SEG:(rd + 1) * SEG]
            out.append(np.concatenate(parts, 0))
        return np.stack(out, 0)
    pc = [3, 7]
    st = lambda n, cores, ax: np.stack([res[c][n] for c in cores], axis=ax)
    allc = list(range(8))
    rs = lambda a: a.reshape(a.shape[0], a.shape[1], H, DK)
    return (seqcat('yp'), st('ys', allc, 0), seqcat('p_ckv'), seqcat('p_kpe'),
            st('pC', pc, 1), rs(st('pn', pc, 1)), st('pm', pc, 1), st('pconv', pc, 1),
            st('s_ckv', allc, 0), st('s_kpe', allc, 0), st('sCo', allc, 1), rs(st('sno', allc, 1)), st('smo', allc, 1),
            st('sconvo', allc, 1))
```

```python
import numpy as np
from contextlib import ExitStack
import concourse.bass as bass
import concourse.mybir as mybir
from concourse.bass_utils import run_bass_kernel_spmd

F32 = mybir.dt.float32
BF16 = mybir.dt.bfloat16
AF = mybir.ActivationFunctionType
ALU = mybir.AluOpType
AX = mybir.AxisListType

D = 2048; KC = 16; SEQ = 4096; DEPTH = 4; NA = 2
DS = 16; PAST = 2048
H = 4; DK = 256; DV = 512; APROJ = 6152
BH = 16; QL = 768; KVL = 512; NOPE = 128; ROPE = 64; VD = 128
DFF = 5632; FC = 44
EPS = 1e-6
SEG = 512
NSEG = SEQ // SEG
NR = 2
XW = 2 * H * DV + 16
KVROWS = BH * 128 + ROPE + 4 * SEG
GROUPS = [[0, 1, 2, 3], [4, 5, 6, 7]]
ATT_SCALE = (NOPE + ROPE) ** -0.5
NEG = -1.0e30


class Buf:
    def __init__(self, name=""):
        self.name = name
        self.w = None
        self.r = []


class K:
    def __init__(self, nc, stack):
        self.nc = nc
        self.stack = stack
        self.eng = {'pe': nc.tensor, 'dve': nc.vector, 'act': nc.scalar, 'pool': nc.gpsimd, 'sp': nc.sync}
        self.sem = {}
        self.cnt = {}
        self.waited = {e: {} for e in self.eng}
        for e in self.eng:
            self.sem[e] = stack.enter_context(nc.semaphore('s_' + e))
            self.cnt[e] = 0
        self.nins = 0

    def new_dma_sem(self, key):
        self.sem[key] = self.stack.enter_context(self.nc.semaphore(key))
        self.cnt[key] = 0
        return key

    def _wait(self, e, deps):
        need = {}
        for d in deps:
            if d is None:
                continue
            k, v = d
            if e == 'pe' and k == 'pe':
                continue
            if v > need.get(k, 0):
                need[k] = v
        for k, v in need.items():
            if self.waited[e].get(k, 0) >= v:
                continue
            self.eng[e].wait_ge(self.sem[k], v)
            self.waited[e][k] = v

    @staticmethod
    def _deps(reads, writes):
        deps = []
        for b in reads:
            deps.append(b.w)
        for b in writes:
            deps.append(b.w)
            deps.extend(b.r)
        return deps

    def op(self, e, fn, reads=(), writes=()):
        self._wait(e, self._deps(reads, writes))
        ins = fn()
        self.cnt[e] += 1
        self.nins += 1
        ins.then_inc(self.sem[e], 1)
        tag = (e, self.cnt[e])
        for b in reads:
            b.r.append(tag)
            if len(b.r) > 24:
                b.r = b.r[-24:] if False else self._compact(b.r)
        for b in writes:
            b.w = tag
            b.r = []
        return ins

    @staticmethod
    def _compact(r):
        m = {}
        for k, v in r:
            if v > m.get(k, 0):
                m[k] = v
        return list(m.items())

    def dma(self, q, out, in_, reads=(), writes=(), dsem='io', **kw):
        if dsem in ('io', 'out', 'kv', 'ld'):
            dsem = self.pool[self.pi % len(self.pool)]
            self.pi += 1
        prev = self.cnt[dsem]
        self._wait(q, self._deps(reads, writes) + ([(dsem, prev)] if prev else []))
        ins = self.eng[q].dma_start(out=out, in_=in_, **kw)
        self.cnt[dsem] += 16
        self.nins += 1
        ins.then_inc(self.sem[dsem], 16)
        tag = (dsem, self.cnt[dsem])
        for b in reads:
            b.r.append(tag)
        for b in writes:
            b.w = tag
            b.r = []
        return ins

    def collective(self, src, dst, groups, reads=(), writes=()):
        self._wait('pool', self._deps(reads, writes))
        ins = self.nc.gpsimd.collective_compute("AllGather", mybir.AluOpType.bypass, replica_groups=groups,
                                                ins=[src.opt()], outs=[dst.opt()])
        self.cnt['cc'] += 1
        self.nins += 1
        ins.then_inc(self.sem['cc'], 1)
        tag = ('cc', self.cnt['cc'])
        for b in reads:
            b.r.append(tag)
        for b in writes:
            b.w = tag
            b.r = []
        return ins

    def barrier(self):
        allv = [(k, v) for k, v in self.cnt.items() if v > 0 and k != 'cc']
        for e in self.eng:
            self._wait(e, allv)


def bc(ap, shape):
    return ap.broadcast_to(list(shape))


STOP = None


class _Stop(Exception):
    pass


class Prog:
    def dbg(self, tag):
        if STOP is not None and tag == STOP:
            raise _Stop()

    def __init__(self):
        self.nc = nc = bass.Bass("TRN2", target_bir_lowering=False)
        self.st = ExitStack()
        self.k = K(nc, self.st)
        for s in ['w0', 'w1', 'w2', 'w3', 'cc']:
            self.k.new_dma_sem(s)
        self.k.pool = [self.k.new_dma_sem('p%d' % i) for i in range(40)]
        self.k.pi = 0
        self.build()

    def din(self, name, shape):
        return self.nc.dram_tensor(name, list(shape), F32, kind="ExternalInput").ap()

    def dout(self, name, shape):
        return self.nc.dram_tensor(name, list(shape), F32, kind="ExternalOutput").ap()

    def dint(self, name, shape, dt=F32):
        return self.nc.dram_tensor(name, list(shape), dt, kind="Internal").ap()

    def sb(self, name, shape, dt=F32):
        return self.st.enter_context(self.nc.sbuf_tensor(name, list(shape), dt))

    def fv(self, a, n, rows=128):
        return self.arena[0:rows, a:a + n]

    def bv(self, a, n, rows=128):
        assert n % 2 == 0
        return self.arena[0:rows, a:a + n // 2].bitcast(BF16)

    def V(self, e='dve'):
        return self.nc.vector if e == 'dve' else self.nc.gpsimd

    def tt(self, out, a, b, op, R, W, e='dve'):
        self.k.op(e, lambda: self.V(e).tensor_tensor(out=out, in0=a, in1=b, op=op), R, W)

    def ts(self, out, a, s1, s2, op0, op1, R, W, e='dve'):
        if op1 is None:
            self.k.op(e, lambda: self.V(e).tensor_scalar(out=out, in0=a, scalar1=s1, scalar2=None, op0=op0), R, W)
        else:
            self.k.op(e, lambda: self.V(e).tensor_scalar(out=out, in0=a, scalar1=s1, scalar2=s2, op0=op0, op1=op1), R, W)

    def stt(self, out, a, s, b, op0, op1, R, W):
        self.k.op('dve', lambda: self.nc.vector.scalar_tensor_tensor(out=out, in0=a, scalar=s, in1=b, op0=op0, op1=op1), R, W)

    def cp(self, out, a, R, W, e='dve'):
        if e == 'act':
            self.k.op('act', lambda: self.nc.scalar.copy(out=out, in_=a), R, W)
        else:
            self.k.op(e, lambda: self.V(e).tensor_copy(out=out, in_=a), R, W)

    def act(self, out, a, func, R, W, bias=None, scale=None, accum=None):
        kw = {}
        if bias is not None:
            kw['bias'] = bias
        if scale is not None:
            kw['scale'] = scale
        if accum is not None:
            kw['accum_out'] = accum
        self.k.op('act', lambda: self.nc.scalar.activation(out=out, in_=a, func=func, **kw), R, W)

    def mm(self, out, lhsT, rhs, start, stop, R, W):
        self.k.op('pe', lambda: self.nc.tensor.matmul(out, lhsT=lhsT, rhs=rhs, start=start, stop=stop), R, W)

    def tr(self, out, a, n, R, W, bf=False):
        idn = self.ident_b if bf else self.ident
        self.k.op('pe', lambda: self.nc.tensor.transpose(out, a, idn[0:n, 0:n]), list(R) + [self.b_const], W)

    def ms(self, ap, c, W, e='dve'):
        self.k.op(e, lambda: self.V(e).memset(ap, c), (), W)

    def recip(self, out, a, R, W):
        self.k.op('dve', lambda: self.nc.vector.reciprocal(out=out, in_=a), R, W)

    def load_fm(self, dst, src1d, n, rows=128):
        tmp = self.fv(self.O_TMP, 128)
        if not hasattr(self, 'b_tmpfm'):
            self.b_tmpfm = Buf()
        tb = self.b_tmpfm
        self.k.dma('sp', tmp[0:n, 0:rows], src1d.rearrange("(c p) -> c p", p=rows), (), [tb])
        ps, pb = self.ps[7], self.bps[7]
        self.tr(ps[0:rows, 0:n], tmp[0:n, 0:rows], n, [tb], [pb])
        self.cp(dst, ps[0:rows, 0:n], [pb], [self.b_gn])

    def build(self):
        nc, k = self.nc, self.k
        d = {}
        d['xp'] = self.din('xp', [NR * SEG, D]); d['xs'] = self.din('xs', [DS, D])
        d['ckv'] = self.din('ckv', [PAST, KVL]); d['kpe'] = self.din('kpe', [PAST, ROPE])
        d['sC'] = self.din('sC', [NA, H, DV, DK]); d['sn'] = self.din('sn', [NA, H * DK]); d['sm'] = self.din('sm', [NA, H])
        d['sconv'] = self.din('sconv', [DEPTH, 2, 2 * DFF])
        d['norm_mix'] = self.din('norm_mix', [DEPTH, D]); d['norm_ffn'] = self.din('norm_ffn', [DEPTH, D])
        d['a_w_in'] = self.din('a_w_in', [NA, D, APROJ]); d['a_b_gate'] = self.din('a_b_gate', [NA, 8])
        d['a_g_head'] = self.din('a_g_head', [NA, H * DV]); d['a_w_out'] = self.din('a_w_out', [NA, D, D])
        d['kv_norm'] = self.din('kv_norm', [1, D]); d['kv_w_down'] = self.din('kv_w_down', [D, KVL + ROPE])
        d['kv_g_c'] = self.din('kv_g_c', [1, KVL]); d['kv_g_r'] = self.din('kv_g_r', [1, ROPE])
        d['kv_w_up'] = self.din('kv_w_up', [KVL, BH * 256]); d['kv_g_kn'] = self.din('kv_g_kn', [1, NOPE])
        d['b_w_dq'] = self.din('b_w_dq', [2, D, QL]); d['b_g_cq'] = self.din('b_g_cq', [2, QL])
        d['b_w_uq'] = self.din('b_w_uq', [2, QL, BH * 192]); d['b_g_qn'] = self.din('b_g_qn', [2, NOPE])
        d['b_g_qr'] = self.din('b_g_qr', [2, ROPE]); d['b_w_o'] = self.din('b_w_o', [2, D, D])
        d['f_w_up'] = self.din('f_w_up', [DEPTH, D, 2 * DFF]); d['f_conv_w'] = self.din('f_conv_w', [DEPTH, 3, 2 * DFF])
        d['f_conv_b'] = self.din('f_conv_b', [DEPTH, 2 * DFF]); d['f_w_down'] = self.din('f_w_down', [DEPTH, DFF, D])
        d['cos2'] = self.din('cos2', [ROPE, NR * SEG + DS]); d['sin2'] = self.din('sin2', [ROPE, NR * SEG + DS])
        d['cmask'] = self.din('cmask', [128, 40])
        o = {}
        o['yp'] = self.dout('yp', [NR * SEG, D]); o['ys'] = self.dout('ys', [DS, D])
        o['p_ckv'] = self.dout('p_ckv', [NR * SEG, KVL]); o['p_kpe'] = self.dout('p_kpe', [NR * SEG, ROPE])
        o['pC'] = self.dout('pC', [NA, H, DV, DK]); o['pn'] = self.dout('pn', [NA, H * DK]); o['pm'] = self.dout('pm', [NA, H])
        o['pconv'] = self.dout('pconv', [DEPTH, 2, 2 * DFF])
        o['s_ckv'] = self.dout('s_ckv', [DS, KVL]); o['s_kpe'] = self.dout('s_kpe', [DS, ROPE])
        o['sCo'] = self.dout('sCo', [NA, H, DV, DK]); o['sno'] = self.dout('sno', [NA, H * DK]); o['smo'] = self.dout('smo', [NA, H])
        o['sconvo'] = self.dout('sconvo', [DEPTH, 2, 2 * DFF])
        self.d, self.o = d, o
        self.kn_dram = [None, self.dint('kns', [BH, 128, PAST + DS], BF16)]
        self.kr_dram = [None, self.dint('krs', [ROPE, PAST + DS], BF16)]
        self.v_dram = [None, self.dint('vs', [PAST + DS, BH * VD], BF16)]
        self.xsC = [[[self.dint('xsC%d%d%d' % (l, t, c), [128, 2048]) for c in range(2)] for t in range(2)] for l in range(NA)]
        self.xgC = [[[self.dint('xgC%d%d%d' % (l, t, c), [512, 2048]) for c in range(2)] for t in range(2)] for l in range(NA)]
        self.xsS = [self.dint('xsS%d' % l, [256, 16]) for l in range(NA)]
        self.xgS = [self.dint('xgS%d' % l, [1024, 16]) for l in range(NA)]
        self.b_xs2 = [Buf(), Buf()]; self.b_xg = [Buf(), Buf()]
        self.ts2 = [self.dint('ts2_%d' % l, [256, 176]) for l in range(DEPTH)]
        self.tg = [self.dint('tg_%d' % l, [4 * 256, 176]) for l in range(DEPTH)]
        self.b_ts2 = [Buf() for _ in range(DEPTH)]; self.b_tg = [Buf() for _ in range(DEPTH)]
        self.KP = [self.dint('KP%d' % j, [1024, SEG], BF16) for j in range(2)]
        self.KRp = self.dint('KRp', [ROPE, SEG], BF16)
        self.VP = [self.dint('VP%d' % q, [SEG, 1024], BF16) for q in range(2)]
        self.b_kvloc = Buf()
        self.b_kvd = [Buf(), Buf()]
        self.KG = [[self.dint('KG%d%d' % (r, j), [4 * 1024, SEG], BF16) for j in range(2)] for r in range(NR)]
        self.KRG = [self.dint('KRG%d' % r, [4 * ROPE, SEG], BF16) for r in range(NR)]
        self.VG = [[self.dint('VG%d%d' % (r, q), [4 * SEG, 1024], BF16) for q in range(2)] for r in range(NR)]
        self.b_kvall = [Buf() for _ in range(NR)]
        self.b_out = Buf('out')

        self.ident = self.sb('ident', [128, 128])
        self.ident_b = self.sb('ident_b', [128, 128], BF16)
        self.ones_b = self.sb('ones_b', [128, 128], BF16)
        self.ones_f = self.sb('ones_f', [128, 128])
        self.uneg = {L: self.sb('uneg%d' % L, [L, L]) for L in (128, 16)}
        self.mask = {L: self.sb('mask%d' % L, [L, L]) for L in (128, 16)}
        self.maskT = {L: self.sb('maskT%d' % L, [L, L]) for L in (128, 16)}
        self.sel = {L: self.sb('sel%d' % L, [L, 128]) for L in (128, 16)}
        self.perm = self.sb('perm', [64, 64], BF16)
        self.epsc = self.sb('epsc', [128, 1])
        self.b_const = Buf('const')
        self.xT = self.sb('xT', [128, KC, SEG]); self.b_x = Buf('x')
        self.xb = self.sb('xb', [128, KC, SEG], BF16); self.b_xb = Buf('xb')
        self.tail = self.sb('tail', [128, DEPTH, 2, 2, 88]); self.b_tail = Buf('tail')
        self.nst = self.sb('nst', [128, NA, 2, 2, H, 2]); self.b_nst = Buf('nst')
        self.nsb = self.sb('nsb', [128, NA, 2, 2, H, 2], BF16)
        self.mst = self.sb('mst', [128, NA, 2, H]); self.b_mst = Buf('mst')
        self.gn = self.sb('gn', [128, 9, KC]); self.b_gn = Buf('gn')
        self.cw = self.sb('cw', [128, DEPTH, 3, 88]); self.cb = self.sb('cb', [128, DEPTH, 88])
        self.gmisc = self.sb('gmisc', [128, 64])
        self.cm = self.sb('cm', [128, 40])
        self.fsum = self.sb('fsum', [128, H])
        self.b_cw = self.b_gn; self.b_gm = self.b_gn
        self.arena = self.sb('arena', [128, 34200])
        self.ps = [self.st.enter_context(nc.psum_tensor('ps%d' % i, [128, 512], F32)) for i in range(8)]
        self.bps = [Buf('ps%d' % i) for i in range(8)]
        self.O_STG = 0
        self.O_WR = 8192
        self.O_SQ = 12288
        self.O_RSTD = 12800
        self.O_PH = 13312
        self.O_TMP = 13312

        try:
            self.init_consts()
            self.dbg('init')
            self.segment(grp=1, s=0, T=DS, L=DS)
            for s in range(NR if NSEG > 0 else 0):
                self.segment(grp=0, s=s, T=SEG, L=128)
        except _Stop:
            pass
        k.barrier()
        k._wait('sp', [(p, k.cnt[p]) for p in k.pool if k.cnt[p] > 0])

    def init_consts(self):
        nc, k, d = self.nc, self.k, self.d
        W = [self.b_const]
        g = nc.gpsimd

        def sel(t, pattern, cmp, fill, base, cm):
            k.op('pool', lambda: g.affine_select(out=t, in_=t, pattern=pattern, compare_op=cmp, fill=fill,
                                                base=base, channel_multiplier=cm), W, W)
        self.ms(self.ident[:], 0.0, W, 'pool')
        sel(self.ident[:], [[-1, 128]], ALU.not_equal, 1.0, 0, 1)
        self.cp(self.ident_b[:], self.ident[:], W, W, 'dve')
        self.ms(self.ones_f[:], 1.0, W, 'pool')
        self.cp(self.ones_b[:], self.ones_f[:], W, W, 'dve')
        self.ms(self.epsc[:], EPS, W, 'pool')
        for L in (128, 16):
            self.ms(self.uneg[L][:], -1.0, W, 'pool')
            sel(self.uneg[L][:], [[1, L]], ALU.is_ge, 0.0, 0, -1)
            self.ms(self.mask[L][:], 0.0, W, 'pool')
            sel(self.mask[L][:], [[-1, L]], ALU.is_ge, NEG, 0, 1)
            self.ms(self.maskT[L][:], 0.0, W, 'pool')
            sel(self.maskT[L][:], [[1, L]], ALU.is_ge, NEG, 0, -1)
            self.ms(self.sel[L][:], 1.0, W, 'pool')
            sel(self.sel[L][:], [[0, 128]], ALU.is_equal, 0.0, -(L - 1), 1)
        pf = self.fv(20000, 64, 64)
        self.ms(pf, 0.0, W, 'pool')
        sel(pf, [[-1, 64]], ALU.not_equal, 1.0, -32, 1)
        sel(pf, [[-1, 64]], ALU.not_equal, 1.0, 32, 1)
        self.cp(self.perm[:], pf, W, W, 'dve')
        k.barrier()
        for i in range(4):
            self.load_fm(self.gn[:, i, :], d['norm_mix'][i], KC)
            self.load_fm(self.gn[:, 4 + i, :], d['norm_ffn'][i], KC)
        self.load_fm(self.gn[:, 8, :], d['kv_norm'][0], KC)
        for l in range(DEPTH):
            for j in range(3):
                self.load_fm(self.cw[:, l, j, :], d['f_conv_w'][l, j], 88)
            self.load_fm(self.cb[:, l, :], d['f_conv_b'][l], 88)
        gm = self.gmisc
        self.ms(gm[:], 0.0, [self.b_gn], 'pool')
        for j in range(2):
            self.load_fm(gm[:, 6 * j:6 * j + 6], d['b_g_cq'][j], 6)
            self.load_fm(gm[:, 12 + j:13 + j], d['b_g_qn'][j], 1)
            self.load_fm(gm[0:64, 14 + j:15 + j], d['b_g_qr'][j], 1, rows=64)
            self.load_fm(gm[:, 22 + 16 * j:38 + 16 * j], d['a_g_head'][j], 16)
            self.load_fm(gm[0:8, 54 + j:55 + j], d['a_b_gate'][j], 1, rows=8)
        self.load_fm(gm[:, 16:20], d['kv_g_c'][0], 4)
        self.load_fm(gm[0:64, 20:21], d['kv_g_r'][0], 1, rows=64)
        self.load_fm(gm[:, 21:22], d['kv_g_kn'][0], 1)
        self.ms(self.tail[:], 0.0, [self.b_tail], 'pool')
        self.ms(self.mst[:], 0.0, [self.b_mst], 'pool')
        self.ms(self.fsum[:], 0.0, [self.b_mst], 'pool')
        self.ms(self.nst[:], 0.0, [self.b_nst], 'pool')
        k.barrier()
        for l in range(DEPTH):
            for r in range(2):
                self.load_fm(self.tail[:, l, 1, r, :], d['sconv'][l, r], 88)
        for l in range(NA):
            k.dma('sp', self.mst[:, l, 1, :], d['sm'][l:l + 1, :].broadcast_to([128, H]), (), [self.b_mst])
            nt = self.fv(20100, 8)
            self.load_fm(nt, d['sn'][l], 8)
            for e in range(2):
                self.cp(self.nst[:, l, 1, :, :, e], nt.rearrange("p (h c) -> p c h", h=H), [self.b_gn], [self.b_nst], 'dve')
        self.cp(self.nsb[:], self.nst[:], [self.b_nst], [self.b_nst], 'dve')
        k.dma('sp', self.cm[:], d['cmask'][:, :], (), [self.b_gn])
        zc = self.fv(0, XW)
        self.ms(zc, 0.0, W, 'pool')
        for l in range(NA):
            for c in range(2):
                k.dma('sp', self.xsC[l][1][c][:, :], zc[:, 0:2048], W, [self.b_xs2[l]])
            k.dma('sp', self.xsS[l][128:256, :], zc[:, 0:16], W, [self.b_xs2[l]])
        for l in range(DEPTH):
            k.dma('sp', self.ts2[l][128:256, :], zc[:, 0:176], W, [self.b_ts2[l]])
        k.barrier()

    def linear(self, w, blocks, kgroups, rhs_fn, T, evac, gcol=None, ps_ids=(0, 1), extra_R=(), sets=None):
        k = self.k
        wv = w.rearrange("(c p) n -> p c n", p=128)
        NS = 4
        if not hasattr(self, 'wslot'):
            self.wslot = 0
            self.b_wst = [Buf() for _ in range(NS)]
            self.b_wr = [Buf() for _ in range(NS)]
        if sets is None:
            sets = [[0, 1, 4, 5], [6, 7, 2, 3]]
        kcs = [kc for (k0, nk) in kgroups for kc in range(k0, k0 + nk)]
        groups = []
        cur = []
        for bi, (c0, m) in enumerate(blocks):
            if cur and (cur[-1][1] + cur[-1][2] == c0) and (sum(x[2] for x in cur) + m <= 512) and (len(cur) < min(len(x) for x in sets)):
                cur.append((bi, c0, m))
            else:
                if cur:
                    groups.append(cur)
                cur = [(bi, c0, m)]
        if cur:
            groups.append(cur)
        si = 0
        pending = None
        for grp_ in groups:
            banks = sets[si % len(sets)]
            si += 1
            cols = sum(x[2] for x in grp_)
            cbase = grp_[0][1]
            nk_t = max(1, min(2048 // cols, len(kcs)))
            ntile = (len(kcs) + nk_t - 1) // nk_t
            for ti in range(ntile):
                kk = kcs[ti * nk_t:(ti + 1) * nk_t]
                assert kk == list(range(kk[0], kk[0] + len(kk)))
                nk = len(kk)
                s = self.wslot
                self.wslot = (self.wslot + 1) % NS
                stg = self.fv(self.O_STG + s * 2048, nk * cols).rearrange("p (c n) -> p c n", n=cols)
                wr = self.bv(self.O_WR + s * 1024, nk * cols).rearrange("p (c n) -> p c n", n=cols)
                k.dma('sp', stg, wv[:, kk[0]:kk[0] + nk, cbase:cbase + cols], (), [self.b_wst[s]], dsem='w%d' % s)
                self.cp(wr, stg, [self.b_wst[s]], [self.b_wr[s]], ('act', 'dve', 'act', 'act')[s])
                for gi, (bi, c0, m) in enumerate(grp_):
                    ps, pb = self.ps[banks[gi]], self.bps[banks[gi]]
                    for j in range(nk):
                        rhs, rb = rhs_fn(kk[j])
                        first = (ti == 0 and j == 0)
                        last = (ti == ntile - 1 and j == nk - 1)
                        self.mm(ps[0:m, 0:T], wr[:, j, c0 - cbase:c0 - cbase + m], rhs, first, last,
                                [self.b_wr[s], rb] + list(extra_R), [pb])
            if pending is not None:
                for (bi_, ps_, pb_) in pending:
                    evac(bi_, ps_, pb_)
            pending = [(bi, self.ps[banks[gi]][0:m, 0:T], self.bps[banks[gi]]) for gi, (bi, c0, m) in enumerate(grp_)]
        if pending is not None:
            for (bi_, ps_, pb_) in pending:
                evac(bi_, ps_, pb_)

    def sumsq_bc(self, src_fn, nch, T, out, outb, n, rows=128, ps_id=2):
        if not hasattr(self, 'b_sq'):
            self.b_sq = [Buf(), Buf()]
        ps, pb = self.ps[ps_id], self.bps[ps_id]
        for c in range(nch):
            src, sbuf = src_fn(c)
            s = c % 2
            sq = self.bv(self.O_SQ + s * 256, 512)
            self.act(sq[0:rows, 0:T], src, AF.Square, [sbuf], [self.b_sq[s]])
            self.mm(ps[:, 0:T], self.ones_b[0:rows, :], sq[0:rows, 0:T], c == 0, c == nch - 1, [self.b_sq[s], self.b_const], [pb])
        orow = out.shape[0]
        self.act(out, ps[0:orow, 0:T], AF.Sqrt, [pb, self.b_const], [outb], bias=self.epsc[0:orow, 0:1], scale=1.0 / n)
        self.recip(out, out, [outb], [outb])

    def x_rstd(self):
        T = self.T
        rstd = self.fv(self.O_RSTD, SEG)
        b = Buf()
        self.sumsq_bc(lambda c: (self.xT[:, c, 0:T], self.b_x), KC, T, rstd[:, 0:T], b, D)
        return rstd, b

    def xb_refresh(self, gidx):
        T = self.T
        rstd, b_rstd = self.x_rstd()
        tmpf = self.fv(self.O_STG, 2 * SEG).rearrange("p (s t) -> p s t", s=2)
        if not hasattr(self, 'wslot'):
            self.wslot = 0
            self.b_wst = [Buf() for _ in range(4)]
            self.b_wr = [Buf() for _ in range(4)]
        bt = [self.b_wst[0], self.b_wst[0]]
        j = 0
        for c in range(KC):
            if c % 2 == 0:
                self.stt(self.xb[:, c, 0:T], self.xT[:, c, 0:T], self.gn[:, gidx, c:c + 1], rstd[:, 0:T], ALU.mult, ALU.mult,
                         [self.b_x, self.b_gn, b_rstd], [self.b_xb])
            else:
                sl = j % 2; j += 1
                self.act(tmpf[:, sl, 0:T], self.xT[:, c, 0:T], AF.Copy, [self.b_x, self.b_gn], [bt[sl]], scale=self.gn[:, gidx, c:c + 1])
                self.tt(self.xb[:, c, 0:T], tmpf[:, sl, 0:T], rstd[:, 0:T], ALU.mult, [bt[sl], b_rstd], [self.b_xb], 'pool')

    def resid_evac(self):
        T = self.T
        def ev(bi, ps, pb):
            self.tt(self.xT[:, bi, 0:T], ps, self.xT[:, bi, 0:T], ALU.add, [pb, self.b_x], [self.b_x])
        return ev

    def segment(self, grp, s, T, L):
        nc, k, d, o = self.nc, self.k, self.d, self.o
        self.grp, self.sidx, self.T, self.L = grp, s, T, L
        pos0 = s * SEG if grp == 0 else NR * SEG
        xsrc = d['xp'][s * SEG:(s + 1) * SEG, :] if grp == 0 else d['xs']
        k.barrier()
        tmp = self.fv(self.O_PH, D)
        tb = Buf()
        for t0 in range(0, T, 128):
            n = min(128, T - t0)
            k.dma('sp', tmp[0:n, :], xsrc[t0:t0 + n, :], (), [tb])
            for c4 in range(0, KC, 4):
                ps, pb = self.ps[(c4 // 4) % 2], self.bps[(c4 // 4) % 2]
                for j in range(4):
                    self.tr(ps[:, j * 128:j * 128 + n], tmp[0:n, (c4 + j) * 128:(c4 + j + 1) * 128], n, [tb], [pb])
                self.cp(self.xT[:, c4:c4 + 4, t0:t0 + n], ps[:, :].rearrange("p (j n) -> p j n", j=4)[:, :, 0:n], [pb], [self.b_x])
        self.dbg('xload')
        for layer in range(DEPTH):
            self.xb_refresh(layer)
            if layer < NA:
                self.mlstm(layer)
            else:
                self.mla(layer - NA)
            self.dbg('mixer%d' % layer)
            self.xb_refresh(4 + layer)
            self.ffn(layer)
            self.dbg('ffn%d' % layer)
            if layer == NA - 1:
                k.barrier()
                self.xb_refresh(8)
                k.barrier()
                self.shared_kv(pos0)
        k.barrier()
        ydst = o['yp'][s * SEG:(s + 1) * SEG, :] if grp == 0 else o['ys']
        for t0 in range(0, T, 128):
            n = min(128, T - t0)
            for c4 in range(0, KC, 4):
                ps, pb = self.ps[(c4 // 4) % 2], self.bps[(c4 // 4) % 2]
                for j in range(4):
                    self.tr(ps[0:n, j * 128:(j + 1) * 128], self.xT[:, c4 + j, t0:t0 + n], 128, [self.b_x], [pb])
                self.cp(tmp[0:n, c4 * 128:(c4 + 4) * 128], ps[0:n, :], [pb], [tb])
            k.dma('sp', ydst[t0:t0 + n, :], tmp[0:n, :], [tb], [self.b_out], dsem='out')

    def ffn(self, layer):
        nc, k, d, o = self.nc, self.k, self.d, self.o
        T, grp = self.T, self.grp
        P0 = self.O_PH
        ug = self.fv(P0, SEG + 2); b_ug = Buf()
        uv = self.fv(P0 + 514, SEG + 2); b_uv = Buf()
        cg = self.fv(P0 + 1028, SEG); b_cg = Buf()
        cv = self.fv(P0 + 1540, SEG); b_cv = Buf()
        sg4 = self.fv(P0 + 14300, 4 * SEG).rearrange("p (s t) -> p s t", s=4); b_sg4 = [Buf() for _ in range(4)]
        t1 = self.fv(P0 + 2564, 128); t2 = self.fv(P0 + 2692, 128)
        actT = self.bv(P0 + 3000, FC * SEG).rearrange("p (c t) -> p c t", c=FC); b_act = Buf()
        gcol = self.gn[:, 4 + layer, :]
        tl = self.tail[:, layer, grp]
        blocks = []; bmap = []
        for j0 in range(0, FC, 4):
            for j in range(j0, j0 + 4):
                blocks.append((j * 128, 128)); bmap.append((j, 0))
            for j in range(j0, j0 + 4):
                blocks.append((DFF + j * 128, 128)); bmap.append((j, 1))
        sgs = self.fv(self.O_SQ, 4 * SEG).rearrange("p (s t) -> p s t", s=4) if False else None

        def conv(u, ub, fc, out, outb):
            self.ts(out[:, 0:T], u[:, 0:T], self.cw[:, layer, 0, fc:fc + 1], self.cb[:, layer, fc:fc + 1], ALU.mult, ALU.add,
                    [ub, self.b_cw], [outb])
            for jj in (1, 2):
                self.stt(out[:, 0:T], u[:, jj:jj + T], self.cw[:, layer, jj, fc:fc + 1], out[:, 0:T], ALU.mult, ALU.add,
                         [ub, self.b_cw, outb], [outb])

        def evac(bi, ps, pb):
            j, isv = bmap[bi]
            fc = j + (FC if isv else 0)
            u, ub = (uv, b_uv) if isv else (ug, b_ug)
            cc, cb_ = (cv, b_cv) if isv else (cg, b_cg)
            sg, b_sg = sg4[:, j % 4, :], b_sg4[j % 4]
            if grp == 1:
                self.cp(u[:, 0:2], tl[:, :, fc], [self.b_tail], [ub], 'pool')
                self.cp(u[:, 2:2 + T], ps, [pb], [ub], 'act')
                self.cp(tl[:, :, fc], u[:, T:T + 2], [ub], [self.b_tail], 'pool')
                conv(u, ub, fc, cc, cb_)
                lo = 0
            else:
                self.cp(tl[:, :, fc], ps[:, T - 2:T], [pb], [self.b_tail], 'dve')
                self.cp(ufirst[:, fc, :], ps[:, 0:2], [pb], [b_uf], 'dve')
                self.ts(cc[:, 2:T], ps[:, 0:T - 2], self.cw[:, layer, 0, fc:fc + 1], self.cb[:, layer, fc:fc + 1], ALU.mult, ALU.add,
                        [pb, self.b_cw], [cb_])
                for jj in (1, 2):
                    self.stt(cc[:, 2:T], ps[:, jj:T - 2 + jj], self.cw[:, layer, jj, fc:fc + 1], cc[:, 2:T], ALU.mult, ALU.add,
                             [pb, self.b_cw, cb_], [cb_])
                lo = 2
            if not isv:
                self.act(sg[:, lo:T], cg[:, lo:T], AF.Silu, [b_cg], [b_sg])
            else:
                self.tt(actT[:, j, lo:T], sg[:, lo:T], cv[:, lo:T], ALU.mult, [b_sg, b_cv], [b_act], 'pool')

        ufirst = self.fv(P0 + 2820, 176).rearrange("p (c r) -> p c r", r=2); b_uf = Buf()
        self.linear(d['f_w_up'][layer], blocks, [(0, KC)], lambda c: (self.xb[:, c, 0:T], self.b_xb), T, evac)
        if grp == 0:
            k.barrier()
            tcur = self.tail[:, layer, 0].rearrange("p r c -> p (r c)")
            k.dma('sp', self.ts2[layer][0:128, :], tcur, [self.b_tail, self.b_tg[layer]], [self.b_ts2[layer]])
            k.collective(self.ts2[layer][:, :], self.tg[layer][:, :], GROUPS, [self.b_ts2[layer]], [self.b_tg[layer]])
            tgs = self.fv(P0, 1408).rearrange("p (i t c) -> p i t c", i=4, t=2); bfx = Buf()
            k.dma('sp', tgs, self.tg[layer].rearrange("(i t p) c -> p i t c", t=2, p=128), [self.b_tg[layer]], [bfx])
            halo = self.fv(P0 + 1408, 176)
            Rf = [bfx, self.b_gn, b_uf]
            self.ts(halo, tgs[:, 3, 1, :], self.cm[:, 28:29], None, ALU.mult, None, Rf, [bfx])
            for i in range(4):
                self.stt(halo, tgs[:, i, 0, :], self.cm[:, 24 + i:25 + i], halo, ALU.mult, ALU.add, Rf, [bfx])
            k.dma('sp', self.ts2[layer][128:256, :], tcur, [self.b_tail, self.b_tg[layer]], [self.b_ts2[layer]])
            h0 = halo[:, 0:88]; h1 = halo[:, 88:176]
            u0 = ufirst[:, :, 0]; u1 = ufirst[:, :, 1]
            w0, w1, w2 = (self.cw[:, layer, jj, :] for jj in range(3)); bb = self.cb[:, layer, :]
            c0 = self.fv(P0 + 1584, 88); c1 = self.fv(P0 + 1672, 88); ta = self.fv(P0 + 1760, 88); sgl = self.fv(P0 + 1848, 88)
            for (cc, x0, x1, x2) in ((c0, h0, h1, u0), (c1, h1, u0, u1)):
                self.tt(cc, w0, x0, ALU.mult, Rf, [bfx]); self.tt(cc, cc, bb, ALU.add, Rf, [bfx])
                self.tt(ta, w1, x1, ALU.mult, Rf, [bfx]); self.tt(cc, cc, ta, ALU.add, Rf, [bfx])
                self.tt(ta, w2, x2, ALU.mult, Rf, [bfx]); self.tt(cc, cc, ta, ALU.add, Rf, [bfx])
            for t, cc in ((0, c0), (1, c1)):
                self.act(sgl[:, 0:44], cc[:, 0:44], AF.Silu, Rf, [bfx])
                self.tt(actT[:, :, t], sgl[:, 0:44], cc[:, 44:88], ALU.mult, Rf, [b_act])
            k.barrier()
        self.linear(d['f_w_down'][layer], [(c * 128, 128) for c in range(KC)], [(0, FC)],
                    lambda c: (actT[:, c, 0:T], b_act), T, self.resid_evac(), extra_R=[self.b_x])
        if grp == 1 or self.sidx == NR - 1:
            dst = o['sconvo'] if grp == 1 else o['pconv']
            tb = Buf()
            tf = self.tail[:, layer, grp].rearrange("p r c -> p (r c)")
            ps, pb = self.ps[2], self.bps[2]
            self.tr(ps[0:128, 0:128], tf[:, 0:128], 128, [self.b_tail], [pb])
            self.tr(ps[0:48, 128:256], tf[:, 128:176], 128, [self.b_tail], [pb])
            self.cp(t1, ps[0:128, 0:128], [pb], [tb]); self.cp(t2[0:48, :], ps[0:48, 128:256], [pb], [tb])
            dv = [dst[layer, r].rearrange("(c p) -> c p", p=128) for r in range(2)]
            k.dma('sp', dv[0][0:88, :], t1[0:88, :], [tb], [self.b_out], dsem='out')
            k.dma('sp', dv[1][0:40, :], t1[88:128, :], [tb], [self.b_out], dsem='out')
            k.dma('sp', dv[1][40:88, :], t2[0:48, :], [tb], [self.b_out], dsem='out')

    def mlstm(self, layer):
        nc, k, d, o = self.nc, self.k, self.d, self.o
        T, L, grp = self.T, self.L, self.grp
        P0 = self.O_PH
        qT = self.bv(P0, 8 * SEG).rearrange("p (c t) -> p c t", c=8); b_q = Buf()
        kT = self.bv(P0 + 2048, 8 * SEG).rearrange("p (c t) -> p c t", c=8); b_k = Buf()
        vT = self.fv(P0 + 4096, 16 * SEG).rearrange("p (c t) -> p c t", c=16); b_v = Buf()
        gT = self.fv(P0 + 12288, SEG, 8); b_g = Buf()
        sgt2 = self.fv(self.O_SQ, 2 * SEG).rearrange("p (s t) -> p s t", s=2); b_sg2 = [Buf(), Buf()]
        O_SSTG = 6144
        O_GH = 10240
        S0 = P0 + 13312
        gcol = self.gn[:, layer, :]
        w_in = d['a_w_in'][layer]
        sq_, sv_ = H * DK, H * DV
        blocks = [(c * 128, 128) for c in range(8)] + [(sq_ + c * 128, 128) for c in range(8)] + \
                 [(2 * sq_ + c * 128, 128) for c in range(16)] + [(2 * sq_ + 2 * sv_, 8)]

        def evac(bi, ps, pb):
            e = 'act' if bi % 2 else 'dve'
            if bi < 8:
                self.cp(qT[:, bi, 0:T], ps, [pb], [b_q], e)
            elif bi < 16:
                self.cp(kT[:, bi - 8, 0:T], ps, [pb], [b_k], e)
            elif bi < 32:
                self.cp(vT[:, bi - 16, 0:T], ps, [pb], [b_v], e)
            else:
                self.ts(gT[:, 0:T], ps, self.gmisc[0:8, 54 + layer:55 + layer], None, ALU.add, None, [pb, self.b_gm], [b_g])

        self.linear(w_in, blocks, [(0, KC)], lambda c: (self.xb[:, c, 0:T], self.b_xb), T, evac)
        k.barrier()
        self.dbg('mproj')
        CT = self.fv(0, 4096).rearrange("p (c h v) -> p c h v", c=2, h=H); b_ct = Buf()
        CTb = self.bv(4096, 4096).rearrange("p (c h v) -> p c h v", c=2, h=H)
        nT = self.nst[:, layer, grp]
        nTb = self.nsb[:, layer, grp]
        mbc = self.mst[:, layer, grp, :]
        ghb = self.fv(O_GH, 2048).rearrange("p (h v) -> p h v", h=H); b_gh = Buf()
        k.dma('sp', ghb[0:L], d['a_g_head'][layer:layer + 1, :].broadcast_to([L, H * DV]).rearrange("p (h v) -> p h v", h=H), (), [b_gh])
        env = dict(layer=layer, qT=qT, kT=kT, vT=vT, gT=gT, CT=CT, CTb=CTb, nT=nT, nTb=nTb, mbc=mbc, ghb=ghb, S0=S0,
                   b_q=b_q, b_k=b_k, b_v=b_v, b_g=b_g, b_ct=b_ct, b_gh=b_gh)
        if grp == 1:
            stg = self.fv(O_SSTG, 4096).rearrange("p (h c k) -> p h c k", h=H, c=4); sb_ = Buf()
            k.dma('sp', stg, d['sC'][layer].rearrange("h (c p) k -> p h c k", p=128), (), [sb_])
            for h in range(H):
                for kc in range(2):
                    ps, pb = self.ps[(h * 2 + kc) % 2], self.bps[(h * 2 + kc) % 2]
                    for vc in range(4):
                        self.tr(ps[:, vc * 128:(vc + 1) * 128], stg[:, h, vc, kc * 128:(kc + 1) * 128], 128, [sb_], [pb])
                    self.cp(CT[:, kc, h, :], ps[:, :], [pb], [b_ct])
        else:
            self.ms(self.fv(0, 4096), 0.0, [b_ct], 'pool')
            self.ms(nT, 0.0, [self.b_nst], 'dve')
            self.ms(mbc, NEG, [self.b_mst], 'dve')
            self.ms(self.fsum[:], 0.0, [self.b_mst], 'dve')
            k.barrier()
            self.cp(self.bv(4096, 4096), self.fv(0, 4096), [b_ct], [b_ct])
            self.cp(nTb, nT, [self.b_nst], [self.b_nst])
            self.scan_chunks(env, True)
            k.barrier()
            self.exchange_state(layer, env)
            k.barrier()
        self.cp(self.bv(4096, 4096), self.fv(0, 4096), [b_ct], [b_ct])
        self.cp(nTb, nT, [self.b_nst], [self.b_nst])
        self.scan_chunks(env, False)
        k.barrier()
        self.dbg('scan')
        last = (grp == 1) or (self.sidx == NR - 1)
        if grp == 0:
            stt_ = self.fv(S0 + 200, 16); sbb = Buf()
            self.cp(stt_[:, 0:8].rearrange("p (c h) -> p c h", c=2), nT[:, :, :, 0], [self.b_nst], [sbb])
            self.cp(stt_[:, 8:12], mbc, [self.b_mst], [sbb])
            self.ms(stt_[:, 12:16], 0.0, [sbb], 'dve')
            for c in range(2):
                k.dma('sp', self.xsC[layer][1][c][:, :], self.fv(c * 2048, 2048), [b_ct, self.b_xg[layer]], [self.b_xs2[layer]])
            k.dma('sp', self.xsS[layer][128:256, :], stt_, [sbb, self.b_xg[layer]], [self.b_xs2[layer]])
        if last:
            Cd = (o['sCo'] if grp == 1 else o['pC'])[layer]
            nd = (o['sno'] if grp == 1 else o['pn'])[layer]
            md = (o['smo'] if grp == 1 else o['pm'])[layer]
            stg = self.fv(O_SSTG, 4096).rearrange("p (h c k) -> p h c k", h=H, c=4); sb_ = Buf()
            for h in range(H):
                for vc in range(4):
                    ps, pb = self.ps[vc % 2], self.bps[vc % 2]
                    for kc in range(2):
                        self.tr(ps[:, kc * 128:(kc + 1) * 128], CT[:, kc, h, vc * 128:(vc + 1) * 128], 128, [b_ct], [pb])
                    self.cp(stg[:, h, vc, :], ps[:, 0:256], [pb], [sb_])
            k.dma('sp', Cd.rearrange("h (c p) k -> p h c k", p=128), stg, [sb_], [self.b_out], dsem='out')
            nf = self.fv(S0, 8); nb = Buf()
            self.cp(nf.rearrange("p (h c) -> p c h", h=H), nT[:, :, :, 0], [self.b_nst], [nb])
            ps, pb = self.ps[2], self.bps[2]
            self.tr(ps[0:8, 0:128], nf, 128, [nb], [pb])
            nf2 = self.fv(S0 + 16, 128, 8)
            self.cp(nf2, ps[0:8, 0:128], [pb], [nb])
            k.dma('sp', nd.rearrange("(c p) -> c p", p=128), nf2, [nb], [self.b_out], dsem='out')
            k.dma('sp', md.rearrange("(o h) -> o h", o=1), mbc[0:1, :], [self.b_mst], [self.b_out], dsem='out')
        k.barrier()
        def evac_o(bi, ps, pb):
            sg2 = sgt2[:, bi % 2, 0:T]
            self.act(sg2, ps, AF.Sigmoid, [pb], [b_sg2[bi % 2]])
            self.tt(vT[:, bi, 0:T], vT[:, bi, 0:T], sg2, ALU.mult, [b_sg2[bi % 2], b_v], [b_v], 'pool' if bi % 2 else 'dve')
        self.linear(w_in, [(2 * sq_ + sv_ + c * 128, 128) for c in range(16)], [(0, KC)], lambda c: (self.xb[:, c, 0:T], self.b_xb), T, evac_o)
        hr = self.bv(S0, 16 * SEG).rearrange("p (c t) -> p c t", c=16); b_hr = Buf()
        k.barrier()
        for c in range(16):
            self.cp(hr[:, c, 0:T], vT[:, c, 0:T], [b_v], [b_hr], 'dve' if c % 2 else 'act')
        self.linear(d['a_w_out'][layer], [(c * 128, 128) for c in range(KC)], [(0, KC)], lambda c: (hr[:, c, 0:T], b_hr), T,
                    self.resid_evac(), extra_R=[self.b_x])

    def scan_chunks(self, env, state_only):
        nc, k = self.nc, self.k
        T, L = self.T, self.L
        layer = env['layer']
        qT, kT, vT, gT, CT, CTb, nT, nTb, mbc, ghb, S0 = (env[x] for x in ('qT', 'kT', 'vT', 'gT', 'CT', 'CTb', 'nT', 'nTb', 'mbc', 'ghb', 'S0'))
        b_q, b_k, b_v, b_g, b_ct, b_gh = (env[x] for x in ('b_q', 'b_k', 'b_v', 'b_g', 'b_ct', 'b_gh'))
        k_c = self.bv(S0, 1024); v_c = self.bv(S0 + 512, 2048); wv = self.bv(S0 + 1536, 2048)
        junk = self.fv(S0 + 1536, 512)
        hh = self.fv(S0 + 2560, 2048)
        sm_ = self.fv(S0 + 4608, 64)
        dg = self.fv(S0 + 4672, 512); dl = self.fv(S0 + 5184, 512); dT = self.fv(S0 + 5696, 512)
        sdT = self.bv(S0 + 6208, 512)
        qs = self.bv(S0 + 6464, 1024).rearrange("p (c t) -> p c t", c=8)
        bcs = self.fv(S0 + 6976, 32)
        w2r = self.bv(S0 + 7008, 2)
        bS = Buf('scan')
        P = self.ps; B = self.bps
        for c in range(T // L):
            c0 = c * L
            R = [bS, b_q, b_k, b_v, b_g, b_ct, b_gh, self.b_nst, self.b_mst, self.b_const]
            Wb = [bS]
            for g4 in range(2):
                pbf = P[0][0:L, :].bitcast(BF16)
                for j in range(4):
                    self.tr(pbf[:, j * 128:(j + 1) * 128], kT[:, g4 * 4 + j, c0:c0 + L], 128, R, [B[0]], bf=True)
                self.cp(k_c[0:L, g4 * 512:(g4 + 1) * 512], pbf[:, 0:512], [B[0]], Wb)
            for g4 in range(4):
                pp = 1 + g4 % 2
                for j in range(4):
                    self.tr(P[pp][0:L, j * 128:(j + 1) * 128], vT[:, g4 * 4 + j, c0:c0 + L], 128, R, [B[pp]])
                self.cp(v_c[0:L, g4 * 512:(g4 + 1) * 512], P[pp][0:L, :], [B[pp]], Wb, 'act')
            self.tr(P[3][0:L, 0:8], gT[:, c0:c0 + L], 8, R, [B[3]])
            gi = sm_[0:L, 0:4]; sp_ = sm_[0:L, 4:8]; b_ = sm_[0:L, 8:12]; a_ = sm_[0:L, 12:16]
            il = sm_[0:L, 16:20]; mt = sm_[0:L, 20:24]; mb = sm_[0:L, 20:28]; u_ = sm_[0:L, 28:32]
            iw = sm_[0:L, 32:36]; en = sm_[0:L, 36:40]; w_ = sm_[0:L, 40:44]; den = sm_[0:L, 44:48]
            ss = sm_[0:L, 48:52]; rr = sm_[0:L, 52:56]; mloc = sm_[0:L, 56:60]; tmp4 = sm_[0:L, 60:64]
            self.cp(gi, P[3][0:L, 0:4], [B[3]], Wb)
            self.act(sp_, P[3][0:L, 4:8], AF.Exp, [B[3]], Wb, scale=-1.0)
            self.act(sp_, sp_, AF.Ln, [bS], Wb, bias=1.0)
            self.mm(P[3][0:L, 8:12], self.uneg[L][:, :], sp_, True, True, R, [B[3]])
            self.cp(b_, P[3][0:L, 8:12], [B[3]], Wb)
            self.cp(sm_[0:L, 24:28], b_, R, Wb)
            self.tt(a_, gi, b_, ALU.subtract, R, Wb)
            idl = self.ident[0:L, 0:L]
            dg3 = dg[0:L, 0:4 * L].rearrange("p (h s) -> p h s", h=H)
            dl3 = dl[0:L, 0:4 * L].rearrange("p (h s) -> p h s", h=H)
            dT3 = dT[0:L, 0:4 * L].rearrange("p (h s) -> p h s", h=H)
            sd3 = sdT[0:L, 0:4 * L].rearrange("p (h s) -> p h s", h=H)

            def rowbc(col4, ps_ap, psb, lhs):
                self.tt(dg3, bc(idl.unsqueeze(1), [L, H, L]), bc(col4.unsqueeze(2), [L, H, L]), ALU.mult, R, Wb)
                self.mm(ps_ap, lhs, dg[0:L, 0:4 * L], True, True, R, [psb])
            rowbc(a_, P[4][0:L, 0:4 * L], B[4], self.ones_f[0:L, 0:L])
            p4 = P[4][0:L, 0:4 * L].rearrange("p (h s) -> p h s", h=H)
            self.tt(dl3, p4, bc(b_.unsqueeze(2), [L, H, L]), ALU.add, [B[4]] + R, Wb)
            self.tt(dl3, dl3, bc(self.mask[L][:, :].unsqueeze(1), [L, H, L]), ALU.add, R, Wb)
            self.k.op('dve', lambda: nc.vector.tensor_reduce(out=mloc, in_=dl3, axis=AX.X, op=ALU.max), R, Wb)
            self.tt(il, b_, mbc[0:L, :], ALU.add, R, Wb)
            self.tt(mt, il, mloc, ALU.max, R, Wb)
            if not state_only:
                self.tt(u_, b_, mt, ALU.subtract, R, Wb)
                self.tt(tmp4, il, mt, ALU.subtract, R, Wb)
                self.act(iw, tmp4, AF.Exp, R, Wb)
                self.act(en, mt, AF.Exp, R, Wb, scale=-1.0)
                rowbc(u_, P[4][0:L, 0:4 * L], B[4], self.ones_f[0:L, 0:L])
                self.tt(dT3, p4, bc(a_.unsqueeze(2), [L, H, L]), ALU.add, [B[4]] + R, Wb)
                self.tt(dT3, dT3, bc(self.maskT[L][:, :].unsqueeze(1), [L, H, L]), ALU.add, R, Wb)
                self.act(dT[0:L, 0:4 * L], dT[0:L, 0:4 * L], AF.Exp, R, Wb)
                for h in range(H):
                    for kc in range(2):
                        self.mm(P[5][0:L, h * L:(h + 1) * L], kT[:, h * 2 + kc, c0:c0 + L], qT[:, h * 2 + kc, c0:c0 + L], kc == 0, kc == 1, R, [B[5]])
                self.stt(sdT[0:L, 0:4 * L], P[5][0:L, 0:4 * L], DK ** -0.5, dT[0:L, 0:4 * L], ALU.mult, ALU.mult, [B[5]] + R, Wb)
                rowbc(iw, P[4][:, 0:4 * L], B[4], self.ones_f[0:L, :])
                pw = P[4][:, 0:4 * L].rearrange("p (h t) -> p h t", h=H)
                for kc in range(2):
                    qv = qT[:, :, c0:c0 + L].rearrange("p (h c) t -> p h c t", c=2)[:, :, kc, :]
                    qsv = qs[:, :, 0:L].rearrange("p (h c) t -> p h c t", c=2)[:, :, kc, :]
                    self.tt(qsv, qv, pw, ALU.mult, [B[4]] + R, Wb)
                for h in range(H):
                    pn_, bn_ = P[6 + h % 2], B[6 + h % 2]
                    self.mm(pn_[0:L, :], sd3[:, h, :], v_c[0:L, h * DV:(h + 1) * DV], True, False, R, [bn_])
                    for kc in range(2):
                        self.mm(pn_[0:L, :], qs[:, h * 2 + kc, 0:L], CTb[:, kc, h, :], False, kc == 1, R, [bn_])
                    self.mm(P[3][0:L, 16 + 2 * h:18 + 2 * h], sd3[:, h, :], self.ones_b[0:L, 0:2], True, False, R, [B[3]])
                    for kc in range(2):
                        self.mm(P[3][0:L, 16 + 2 * h:18 + 2 * h], qs[:, h * 2 + kc, 0:L], nTb[:, kc, h, :], False, kc == 1, R, [B[3]])
                    qn = P[3][0:L, 16 + 2 * h:17 + 2 * h]
                    self.act(den[:, h:h + 1], qn, AF.Abs, [B[3]] + R, Wb)
                    self.tt(den[:, h:h + 1], den[:, h:h + 1], en[:, h:h + 1], ALU.max, R, Wb)
                    self.recip(den[:, h:h + 1], den[:, h:h + 1], R, Wb)
                    self.act(junk[0:L, :], pn_[0:L, :], AF.Square, [bn_] + R, Wb, scale=den[:, h:h + 1], accum=ss[:, h:h + 1])
                    self.act(rr[:, h:h + 1], ss[:, h:h + 1], AF.Sqrt, R, Wb, bias=self.epsc[0:L, 0:1], scale=1.0 / DV)
                    self.recip(rr[:, h:h + 1], rr[:, h:h + 1], R, Wb)
                    self.tt(rr[:, h:h + 1], rr[:, h:h + 1], den[:, h:h + 1], ALU.mult, R, Wb)
                    self.stt(hh[0:L, h * DV:(h + 1) * DV], pn_[0:L, :], rr[:, h:h + 1], ghb[0:L, h, :], ALU.mult, ALU.mult, [bn_] + R, Wb)
            self.mm(P[3][:, 32:40], self.sel[L][:, :], mb, True, True, R, [B[3]])
            self.cp(bcs[:, 0:8], P[3][:, 32:40], [B[3]], Wb)
            mnew = bcs[:, 0:4]; blast = bcs[:, 4:8]; dec = bcs[:, 8:12]; t12 = bcs[:, 12:16]
            self.tt(self.fsum[:], self.fsum[:], blast, ALU.add, R, [self.b_mst, bS])
            self.tt(t12, blast, mnew, ALU.subtract, R, Wb)
            self.tt(dec, t12, mbc, ALU.add, R, Wb)
            self.act(dec, dec, AF.Exp, R, Wb)
            self.tt(w_, a_, t12[0:L, :], ALU.add, R, Wb)
            self.act(w_, w_, AF.Exp, R, Wb)
            self.ts(w_, w_, DK ** -0.5, None, ALU.mult, None, R, Wb)
            self.tt(wv[0:L, :].rearrange("p (h v) -> p h v", h=H), v_c[0:L, :].rearrange("p (h v) -> p h v", h=H),
                    bc(w_.unsqueeze(2), [L, H, DV]), ALU.mult, R, Wb)
            for h in range(H):
                self.cp(w2r[0:L, :], bc(w_[:, h:h + 1], [L, 2]), R, Wb)
                for kc in range(2):
                    pc, bcb = P[6 + kc], B[6 + kc]
                    self.mm(pc[:, :], k_c[0:L, h * DK + kc * 128:h * DK + (kc + 1) * 128], wv[0:L, h * DV:(h + 1) * DV], True, True, R, [bcb])
                    self.stt(CT[:, kc, h, :], CT[:, kc, h, :], dec[:, h:h + 1], pc[:, :], ALU.mult, ALU.add, [bcb, b_ct] + R, [b_ct])
                    if not state_only:
                        self.cp(CTb[:, kc, h, :], CT[:, kc, h, :], [b_ct], [b_ct], 'act')
                    self.mm(P[3][:, 48:50], k_c[0:L, h * DK + kc * 128:h * DK + (kc + 1) * 128], w2r[0:L, :], True, True, R, [B[3]])
                    self.stt(nT[:, kc, h, :], nT[:, kc, h, :], dec[:, h:h + 1], P[3][:, 48:50], ALU.mult, ALU.add,
                             [B[3], self.b_nst] + R, [self.b_nst])
                    if not state_only:
                        self.cp(nTb[:, kc, h, :], nT[:, kc, h, :], [self.b_nst], [self.b_nst])
            self.cp(mbc, mnew, R, [self.b_mst])
            if not state_only:
                for g4 in range(4):
                    pp = 4 + g4 % 2
                    for j in range(4):
                        self.tr(P[pp][:, j * L:(j + 1) * L], hh[0:L, (g4 * 4 + j) * 128:(g4 * 4 + j + 1) * 128], L, R, [B[pp]])
                    self.cp(vT[:, g4 * 4:g4 * 4 + 4, c0:c0 + L], P[pp][:, 0:4 * L].rearrange("p (j t) -> p j t", j=4), [B[pp]], [b_v, bS])

    def exchange_state(self, layer, env):
        nc, k = self.nc, self.k
        CT, nT, mbc, S0, b_ct = env['CT'], env['nT'], env['mbc'], env['S0'], env['b_ct']
        grp = self.grp
        sc = self.fv(S0 + 300, 600)
        bsc = Buf()
        stt_ = sc[:, 0:16]
        self.cp(stt_[:, 0:8].rearrange("p (c h) -> p c h", c=2), nT[:, :, :, 0], [self.b_nst], [bsc])
        self.cp(stt_[:, 8:12], mbc, [self.b_mst], [bsc])
        self.cp(stt_[:, 12:16], self.fsum[:], [self.b_mst], [bsc])
        for c in range(2):
            k.dma('sp', self.xsC[layer][0][c][:, :], self.fv(c * 2048, 2048), [b_ct, self.b_xg[layer]], [self.b_xs2[layer]])
        k.dma('sp', self.xsS[layer][0:128, :], stt_, [bsc, self.b_xg[layer]], [self.b_xs2[layer]])
        for t in range(2):
            for c in range(2):
                k.collective(self.xsC[layer][t][c][:, :], self.xgC[layer][t][c][:, :], GROUPS, [self.b_xs2[layer]], [self.b_xg[layer]])
        k.collective(self.xsS[layer][:, :], self.xgS[layer][:, :], GROUPS, [self.b_xs2[layer]], [self.b_xg[layer]])
        sg = sc[:, 16:16 + 128].rearrange("p (i t c) -> p i t c", i=4, t=2)
        k.dma('sp', sg, self.xgS[layer].rearrange("(i t p) c -> p i t c", t=2, p=128), [self.b_xg[layer]], [bsc])
        cm = self.cm
        F3 = sg[:, :, 0, 12:16]
        m3 = sg[:, :, 0, 8:12]
        mcar = sg[:, 3, 1, 8:12]
        Rr = [bsc, self.b_gn]
        T1 = sc[:, 144:208].rearrange("p (i h l) -> p i h l", i=4, h=4)
        Mv = cm[:, 8:24].rearrange("p (i l) -> p i l", i=4)
        self.tt(T1, bc(Mv.unsqueeze(2), [128, 4, 4, 4]), bc(F3.rearrange("p l h -> p h l").unsqueeze(1), [128, 4, 4, 4]), ALU.mult, Rr, [bsc])
        G = sc[:, 208:224].rearrange("p (i h) -> p i h", i=4)
        self.k.op('dve', lambda: nc.vector.tensor_reduce(out=G, in_=T1, axis=AX.X, op=ALU.add), Rr, [bsc])
        E = sc[:, 224:240].rearrange("p (i h) -> p i h", i=4)
        self.tt(E, m3, G, ALU.add, Rr, [bsc])
        self.tt(E, E, bc(cm[:, 0:4].unsqueeze(2), [128, 4, 4]), ALU.add, Rr, [bsc])
        T2 = sc[:, 240:256].rearrange("p (h l) -> p h l", h=4)
        self.tt(T2, bc(cm[:, 4:8].unsqueeze(1), [128, 4, 4]), F3.rearrange("p l h -> p h l"), ALU.mult, Rr, [bsc])
        Ec = sc[:, 256:260]
        self.k.op('dve', lambda: nc.vector.tensor_reduce(out=Ec, in_=T2, axis=AX.X, op=ALU.add), Rr, [bsc])
        self.tt(Ec, Ec, mcar, ALU.add, Rr, [bsc])
        mx = sc[:, 260:264]
        self.k.op('dve', lambda: nc.vector.tensor_reduce(out=mx, in_=E.rearrange("p i h -> p h i"), axis=AX.X, op=ALU.max), Rr, [bsc])
        min_ = sc[:, 264:268]
        self.tt(min_, mx, Ec, ALU.max, Rr, [bsc])
        Wt = sc[:, 268:284].rearrange("p (i h) -> p i h", i=4)
        self.tt(Wt, E, bc(min_.unsqueeze(1), [128, 4, 4]), ALU.subtract, Rr, [bsc])
        self.act(sc[:, 268:284], sc[:, 268:284], AF.Exp, Rr, [bsc])
        Wc = sc[:, 284:288]
        self.tt(Wc, Ec, min_, ALU.subtract, Rr, [bsc])
        self.act(Wc, Wc, AF.Exp, Rr, [bsc])
        nacc = sc[:, 288:296].rearrange("p (c h) -> p c h", c=2)
        ntmp = sc[:, 296:304].rearrange("p (c h) -> p c h", c=2)
        self.tt(nacc, sg[:, 3, 1, 0:8].rearrange("p (c h) -> p c h", c=2), bc(Wc.unsqueeze(1), [128, 2, 4]), ALU.mult, Rr, [bsc])
        for i in range(4):
            self.tt(ntmp, sg[:, i, 0, 0:8].rearrange("p (c h) -> p c h", c=2), bc(Wt[:, i, :].unsqueeze(1), [128, 2, 4]), ALU.mult, Rr, [bsc])
            self.tt(nacc, nacc, ntmp, ALU.add, Rr, [bsc])
        for e in range(2):
            self.cp(nT[:, :, :, e], nacc, Rr, [self.b_nst])
        self.cp(mbc, min_, Rr, [self.b_mst])
        stg = self.fv(6144, 4096).rearrange("p (c h v) -> p c h v", c=2, h=H); bst = Buf()
        srcs = [(1, 3, None)] + [(0, i, i) for i in range(4)]
        for si, (tt_, rk, i) in enumerate(srcs):
            for c in range(2):
                k.dma('sp', self.fv(6144 + c * 2048, 2048),
                      self.xgC[layer][tt_][c][rk * 128:(rk + 1) * 128, :], [self.b_xg[layer]], [bst])
            for kc in range(2):
                for h in range(H):
                    if i is None:
                        self.ts(CT[:, kc, h, :], stg[:, kc, h, :], Wc[:, h:h + 1], None, ALU.mult, None, [bst] + Rr, [b_ct])
                    else:
                        self.stt(CT[:, kc, h, :], stg[:, kc, h, :], Wt[:, i, h:h + 1], CT[:, kc, h, :], ALU.mult, ALU.add, [bst, b_ct] + Rr, [b_ct])

    def rope_tables(self, pos0, T, o_cs):
        cs = self.fv(o_cs, SEG, 64); sn = self.fv(o_cs + 512, SEG, 64)
        self.b_ropet = Buf()
        self.k.dma('sp', cs[:, 0:T], self.d['cos2'][:, pos0:pos0 + T], (), [self.b_ropet])
        self.k.dma('sp', sn[:, 0:T], self.d['sin2'][:, pos0:pos0 + T], (), [self.b_ropet])
        self.rope_cs, self.rope_sn = cs, sn

    def rope(self, dst, dstb, src, srcb, T, o_tmp, slots=2):
        if not hasattr(self, 'b_rope'):
            self.b_rope = [Buf(), Buf()]
            self.rope_i = 0
        i = self.rope_i % slots
        self.rope_i += 1
        t1 = self.fv(o_tmp + i * 1024, SEG, 64); t2 = self.fv(o_tmp + i * 1024 + 512, SEG, 64)
        tb = self.b_rope[i]
        ps, pb = self.ps[3], self.bps[3]
        self.mm(ps[0:64, 0:T], self.perm[:, :], src, True, True, [srcb, self.b_const], [pb])
        self.tt(t1[:, 0:T], ps[0:64, 0:T], self.rope_sn[:, 0:T], ALU.mult, [pb, self.b_ropet], [tb])
        self.tt(t2[:, 0:T], src, self.rope_cs[:, 0:T], ALU.mult, [srcb, self.b_ropet], [tb], 'pool')
        self.tt(dst, t1[:, 0:T], t2[:, 0:T], ALU.add, [tb], [dstb])

    def shared_kv(self, pos0):
        nc, k, d, o = self.nc, self.k, self.d, self.o
        T, grp, s = self.T, self.grp, self.sidx
        P0 = self.O_PH
        zT = self.fv(P0, 5 * SEG).rearrange("p (c t) -> p c t", c=5); b_z = Buf()
        cTf = self.fv(P0 + 2560, 4 * SEG).rearrange("p (c t) -> p c t", c=4); b_c = Buf()
        cTb = self.bv(P0 + 4608, 4 * SEG).rearrange("p (c t) -> p c t", c=4)
        kpT = self.fv(P0 + 5632, SEG, 64); b_kp = Buf()
        kpn = self.bv(P0 + 6144, SEG, 64)
        r2 = self.fv(P0 + 6400, SEG); b_r2 = Buf()
        tm = self.fv(P0 + 6912, 576); ld = self.fv(P0 + 7488, 576)
        self.o_kv = P0 + 8300
        blocks = [(c * 128, 128) for c in range(4)] + [(KVL, 64)]

        def evac(bi, ps, pb):
            m = 128 if bi < 4 else 64
            self.cp(zT[0:m, bi, 0:T], ps, [pb], [b_z], 'act' if bi % 2 else 'dve')
        self.linear(d['kv_w_down'], blocks, [(0, KC)], lambda c: (self.xb[:, c, 0:T], self.b_xb), T, evac)
        self.sumsq_bc(lambda c: (zT[:, c, 0:T], b_z), 4, T, r2[:, 0:T], b_r2, KVL)
        for c in range(4):
            self.stt(cTf[:, c, 0:T], zT[:, c, 0:T], self.gmisc[:, 16 + c:17 + c], r2[:, 0:T], ALU.mult, ALU.mult, [b_z, b_r2, self.b_gm], [b_c])
            self.cp(cTb[:, c, 0:T], cTf[:, c, 0:T], [b_c], [b_c], 'act')
        self.sumsq_bc(lambda c: (zT[0:64, 4, 0:T], b_z), 1, T, r2[0:64, 0:T], b_r2, ROPE, rows=64)
        self.stt(kpn[:, 0:T], zT[0:64, 4, 0:T], self.gmisc[0:64, 20:21], r2[0:64, 0:T], ALU.mult, ALU.mult, [b_z, b_r2, self.b_gm], [b_kp])
        self.rope_tables(pos0, T, P0 + 18100)
        self.rope(kpT[:, 0:T], b_kp, kpn[:, 0:T], b_kp, T, P0 + 19124, slots=1)
        k0 = 0 if grp == 0 else PAST
        kb = self.b_kvloc if grp == 0 else self.b_kvd[grp]
        kpb = self.bv(P0 + 8000, SEG, 64); b_kpb = Buf()
        self.cp(kpb[:, 0:T], kpT[:, 0:T], [b_kp], [b_kpb])
        if grp == 0:
            k.dma('sp', self.KRp[:, 0:T], kpb[:, 0:T], [b_kpb] + [self.b_kvall[r] for r in range(NR)], [kb], dsem='kv')
        else:
            k.dma('sp', self.kr_dram[grp][:, k0:k0 + T], kpb[:, 0:T], [b_kpb], [kb], dsem='kv')
        cdst = (o['p_ckv'][s * SEG:(s + 1) * SEG] if grp == 0 else o['s_ckv'])
        kdst = (o['p_kpe'][s * SEG:(s + 1) * SEG] if grp == 0 else o['s_kpe'])
        tb = Buf()
        for t0 in range(0, T, 128):
            n = min(128, T - t0)
            ps, pb = self.ps[2], self.bps[2]
            for c in range(4):
                self.tr(ps[0:n, c * 128:(c + 1) * 128], cTf[:, c, t0:t0 + n], 128, [b_c], [pb])
            self.cp(tm[0:n, 0:512], ps[0:n, :], [pb], [tb])
            ps, pb = self.ps[3], self.bps[3]
            self.tr(ps[0:n, 0:64], kpT[:, t0:t0 + n], 64, [b_kp], [pb])
            self.cp(tm[0:n, 512:576], ps[0:n, 0:64], [pb], [tb])
            k.dma('sp', cdst[t0:t0 + n, :], tm[0:n, 0:512], [tb], [self.b_out], dsem='out')
            k.dma('sp', kdst[t0:t0 + n, :], tm[0:n, 512:576], [tb], [self.b_out], dsem='out')
        self.kv_up(cTb, b_c, T, k0)
        if grp == 0:
            k.barrier()
            for j in range(2):
                k.collective(self.KP[j][:, :], self.KG[s][j][:, :], GROUPS, [self.b_kvloc], [self.b_kvall[s]])
                k.collective(self.VP[j][:, :], self.VG[s][j][:, :], GROUPS, [self.b_kvloc], [self.b_kvall[s]])
            k.collective(self.KRp[:, :], self.KRG[s][:, :], GROUPS, [self.b_kvloc], [self.b_kvall[s]])
        if grp == 1:
            k.barrier()
            lb = Buf()
            for blk in range(PAST // SEG):
                for t0 in range(0, SEG, 128):
                    r0 = blk * SEG + t0
                    k.dma('sp', ld[:, 0:512], d['ckv'][r0:r0 + 128, :], (), [lb])
                    k.dma('sp', ld[:, 512:576], d['kpe'][r0:r0 + 128, :], (), [lb])
                    ps, pb = self.ps[2], self.bps[2]
                    for c in range(4):
                        self.tr(ps[:, c * 128:(c + 1) * 128], ld[:, c * 128:(c + 1) * 128], 128, [lb], [pb])
                    self.cp(cTb[:, :, t0:t0 + 128], ps[:, :].rearrange("p (c t) -> p c t", c=4), [pb], [b_c])
                    ps, pb = self.ps[3], self.bps[3]
                    self.tr(ps[0:64, 0:128], ld[:, 512:576], 128, [lb], [pb])
                    self.cp(kpT[:, t0:t0 + 128], ps[0:64, 0:128], [pb], [b_kp])
                self.cp(kpb[:, 0:SEG], kpT[:, 0:SEG], [b_kp], [b_kpb])
                k.dma('sp', self.kr_dram[1][:, blk * SEG:(blk + 1) * SEG], kpb[:, 0:SEG], [b_kpb], [kb], dsem='kv')
                self.kv_up(cTb, b_c, SEG, blk * SEG)

    def kv_up(self, cTb, b_c, T, k0):
        nc, k, d = self.nc, self.k, self.d
        grp = self.grp
        kb = self.b_kvloc if grp == 0 else self.b_kvd[grp]
        O = self.o_kv
        k.barrier()
        kn = self.fv(O, 2 * SEG).rearrange("p (s t) -> p s t", s=2); bkn = [Buf(), Buf()]
        r3s = [self.fv(O + 1024, SEG), self.fv(O + 9216, SEG)]; b_r3s = [Buf(), Buf()]
        wvs = self.bv(O + 1536, 4 * 2048).rearrange("p (c n) -> p c n", c=4); b_wv = Buf()
        stg = self.fv(O + 5632, 2048); b_st = Buf()
        vt = self.bv(O + 7680, 2048); b_vt = Buf()
        knh = self.bv(O + 8704, 2 * SEG).rearrange("p (s t) -> p s t", s=2)
        blocks = [(h * 256, 128) for h in range(BH)]

        def evac(bi, ps, pb):
            sl_ = bi % 2
            r3, b_r3 = r3s[sl_], b_r3s[sl_]
            self.cp(kn[:, sl_, 0:T], ps, [pb], [bkn[sl_]], 'act')
            self.sumsq_bc(lambda c: (kn[:, sl_, 0:T], bkn[sl_]), 1, T, r3[:, 0:T], b_r3, NOPE, ps_id=3 - sl_)
            self.stt(kn[:, sl_, 0:T], kn[:, sl_, 0:T], self.gmisc[:, 21:22], r3[:, 0:T], ALU.mult, ALU.mult, [bkn[sl_], b_r3, self.b_gm], [bkn[sl_]])
            self.cp(knh[:, sl_, 0:T], kn[:, sl_, 0:T], [bkn[sl_]], [bkn[sl_]], 'pool')
            kdst = self.KP[bi // 8][(bi % 8) * 128:(bi % 8 + 1) * 128, 0:T] if grp == 0 else self.kn_dram[grp][bi, :, k0:k0 + T]
            k.dma('sp', kdst, knh[:, sl_, 0:T], [bkn[sl_]], [kb], dsem='kv')
        self.linear(d['kv_w_up'], blocks, [(0, 4)], lambda c: (cTb[:, c, 0:T], b_c), T, evac, sets=[[0], [1], [4], [5]])
        wv_ = d['kv_w_up'].rearrange("(c p) (h two x) -> p c h two x", p=128, two=2, x=128)
        for c in range(4):
            k.dma('sp', stg.rearrange("p (h x) -> p h x", h=BH), wv_[:, c, :, 1, :], (), [b_st])
            self.cp(wvs[:, c, :], stg, [b_st], [b_wv])
        for t0 in range(0, T, 128):
            n = min(128, T - t0)
            for q4 in range(4):
                ps, pb = self.ps[q4 % 2], self.bps[q4 % 2]
                for c in range(4):
                    self.mm(ps[0:n, :], cTb[:, c, t0:t0 + n], wvs[:, c, q4 * 512:(q4 + 1) * 512], c == 0, c == 3, [b_c, b_wv], [pb])
                self.cp(vt[0:n, q4 * 512:(q4 + 1) * 512], ps[0:n, :], [pb], [b_vt], 'act' if q4 % 2 else 'dve')
            if grp == 0:
                for q in range(2):
                    k.dma('sp', self.VP[q][t0:t0 + n, :], vt[0:n, q * 1024:(q + 1) * 1024], [b_vt], [kb], dsem='kv')
            else:
                k.dma('sp', self.v_dram[grp][k0 + t0:k0 + t0 + n, :], vt[0:n, :], [b_vt], [kb], dsem='kv')

    def mla(self, j):
        nc, k, d, o = self.nc, self.k, self.d, self.o
        T, grp, s = self.T, self.grp, self.sidx
        pos0 = s * SEG if grp == 0 else NR * SEG
        P0 = self.O_PH
        cq = self.bv(P0, 6 * SEG).rearrange("p (c t) -> p c t", c=6); b_cq = Buf()
        r2 = self.fv(P0 + 1536, SEG); b_r2 = Buf()
        QN = self.bv(P0 + 2048, 16 * SEG).rearrange("p (c t) -> p c t", c=16); b_qn = Buf()
        QR = self.bv(P0 + 6144, 16 * SEG, 64).rearrange("p (c t) -> p c t", c=16); b_qr = Buf()
        r3 = self.fv(P0 + 10240, SEG); b_r3 = Buf()
        qtmp = self.fv(P0 + 10752, SEG); b_qt = Buf()
        qrn = self.bv(P0 + 11264, SEG, 64); b_qrn = Buf()
        O_CS = P0 + 11520
        rd = self.fv(P0 + 12544, SEG); b_rd = Buf()
        cqg = self.bv(P0 + 13100, 6 * SEG).rearrange("p (c t) -> p c t", c=6)

        def evac(bi, ps, pb):
            self.cp(cq[:, bi, 0:T], ps, [pb], [b_cq], 'act')
            self.ts(cqg[:, bi, 0:T], ps, self.gmisc[:, 6 * j + bi:6 * j + bi + 1], None, ALU.mult, None, [pb, self.b_gm], [b_cq])
        self.linear(d['b_w_dq'][j], [(c * 128, 128) for c in range(6)], [(0, KC)], lambda c: (self.xb[:, c, 0:T], self.b_xb), T, evac)
        self.sumsq_bc(lambda c: (cq[:, c, 0:T], b_cq), 6, T, r2[:, 0:T], b_r2, QL)
        blocks = []
        for h in range(BH):
            blocks.append((h * 192, 128)); blocks.append((h * 192 + 128, 64))

        def evac2(bi, ps, pb):
            h, isr = bi // 2, bi % 2
            m = 64 if isr else 128
            self.tt(qtmp[0:m, 0:T], ps, r2[0:m, 0:T], ALU.mult, [pb, b_r2], [b_qt])
            self.sumsq_bc(lambda c: (qtmp[0:m, 0:T], b_qt), 1, T, r3[0:m, 0:T], b_r3, m, rows=m, ps_id=3)
            if not isr:
                self.stt(QN[:, h, 0:T], qtmp[:, 0:T], self.gmisc[:, 12 + j:13 + j], r3[:, 0:T], ALU.mult, ALU.mult, [b_qt, b_r3, self.b_gm], [b_qn])
            else:
                self.stt(qrn[:, 0:T], qtmp[0:64, 0:T], self.gmisc[0:64, 14 + j:15 + j], r3[0:64, 0:T], ALU.mult, ALU.mult, [b_qt, b_r3, self.b_gm], [b_qrn])
                self.rope(QR[:, h, 0:T], b_qr, qrn[:, 0:T], b_qrn, T, P0 + 14700)
        self.rope_tables(pos0, T, O_CS)
        self.linear(d['b_w_uq'][j], blocks, [(0, 6)], lambda c: (cqg[:, c, 0:T], b_cq), T, evac2, sets=[[0, 1], [4, 5], [6, 7]])
        k.barrier()
        KB = 1024 if grp == 1 else SEG
        knb = self.bv(0, 2 * 1024).rearrange("p (s t) -> p s t", s=2); b_knb = [Buf(), Buf()]
        vb = self.bv(1024, 2 * 1024).rearrange("p (s t x) -> p s t x", s=2, x=128); b_vb = [Buf(), Buf()]
        krall = self.bv(3072, 9 * SEG, 64); b_kr = Buf()
        dacc = self.fv(5376, SEG); daccb = self.bv(5888, SEG); b_da = Buf()
        blocks = []
        if grp == 1:
            nkeys = PAST + DS
            for kk0 in range(0, nkeys, KB):
                nk = min(KB, nkeys - kk0)
                blocks.append((lambda h, kk0=kk0, nk=nk: self.kn_dram[1][h, :, kk0:kk0 + nk],
                               self.kr_dram[1][:, kk0:kk0 + nk],
                               lambda h, t, n, kk0=kk0: self.v_dram[1][kk0 + t * 128:kk0 + t * 128 + n, h * 128:(h + 1) * 128],
                               nk, False, None, self.b_kvd[1]))
        else:
            def mk(KS, KRS, VS, i, diag, bias, dep):
                return (lambda h: KS[h // 8][i * 1024 + (h % 8) * 128:i * 1024 + (h % 8 + 1) * 128, :],
                        KRS[i * ROPE:(i + 1) * ROPE, :],
                        lambda h, t, n: VS[h // 8][i * SEG + t * 128:i * SEG + t * 128 + n, (h % 8) * 128:(h % 8 + 1) * 128],
                        SEG, diag, bias, dep)
            for rdi in range(s):
                for i in range(4):
                    blocks.append(mk(self.KG[rdi], self.KRG[rdi], self.VG[rdi], i, False, None, self.b_kvall[rdi]))
            for i in range(4):
                blocks.append(mk(self.KG[s], self.KRG[s], self.VG[s], i, False, 32 + i, self.b_kvall[s]))
            blocks.append(mk(self.KP, self.KRp, self.VP, 0, True, None, self.b_kvloc))
        ntot = sum((blk[3] + 127) // 128 for blk in blocks)
        kro = 0
        for (knf, krap, vf, nk, diag, biascol, dep) in blocks:
            k.dma('sp', krall[:, kro:kro + nk], krap, [dep], [b_kr], dsem='ld')
            kro += nk
        DEPTH_ = 3
        sbanks = [0, 1, 6, 7]
        pTs = [self.bv(2048 + 256 * i, SEG) for i in range(4)]
        b_pTs = [Buf() for _ in range(4)]
        slot = 0
        pi = 0
        for h in range(BH):
            po, bo = self.ps[4 + h % 2], self.bps[4 + h % 2]
            pd, bd = self.ps[2 + h % 2], self.bps[2 + h % 2]
            tiles = []
            kro = 0
            for (knf, krap, vf, nk, diag, biascol, dep) in blocks:
                tiles.append(('load', knf, vf, nk, dep))
                for t in range((nk + 127) // 128):
                    n = min(128, nk - t * 128)
                    tiles.append(('tile', t, n, t * 128 if diag else 0, diag, biascol, kro))
                kro += nk
            pend = []
            state = dict(first=True, cnt=0, sl=0)

            def finish(item):
                (ps_, bs_, pp, bp, sl_, t, n, q0, diag, biascol) = item
                if biascol is None:
                    self.act(pp[0:n, q0:T], ps_[0:n, q0:T], AF.Exp, [bs_], [bp], scale=ATT_SCALE)
                else:
                    self.act(pp[0:n, q0:T], ps_[0:n, q0:T], AF.Exp, [bs_, self.b_gn], [bp], scale=ATT_SCALE,
                             bias=self.cm[0:n, biascol:biascol + 1])
                if diag:
                    self.ms(pp[64:128, q0:q0 + 64], 0.0, [bp], 'dve')
                state['cnt'] += 1
                self.mm(po[:, q0:T], vb[0:n, sl_, t, :], pp[0:n, q0:T], state['first'], state['cnt'] == ntot, [b_vb[sl_], bp], [bo])
                if state['first']:
                    assert n == 128 and q0 == 0
                    self.cp(dacc[:, 0:T], pp[:, 0:T], [bp], [b_da])
                else:
                    self.tt(dacc[0:n, q0:T], dacc[0:n, q0:T], pp[0:n, q0:T], ALU.add, [bp, b_da], [b_da])
                state['first'] = False

            for it in tiles:
                if it[0] == 'load':
                    _, knf, vf, nk, dep = it
                    sl_ = slot; slot ^= 1
                    state['sl'] = sl_
                    k.dma('sp', knb[:, sl_, 0:nk], knf(h), [dep], [b_knb[sl_]], dsem='ld')
                    nfull = nk // 128
                    if nfull:
                        k.dma('sp', vb[:, sl_, 0:nfull, :], vf(h, 0, nfull * 128).rearrange("(t p) d -> p t d", p=128), [dep], [b_vb[sl_]], dsem='ld')
                    if nk % 128:
                        n_ = nk % 128
                        k.dma('sp', vb[0:n_, sl_, nfull, :], vf(h, nfull, n_), [dep], [b_vb[sl_]], dsem='ld')
                    continue
                _, t, n, q0, diag, biascol, kro_ = it
                sl_ = state['sl']
                ps_, bs_ = self.ps[sbanks[pi % 4]], self.bps[sbanks[pi % 4]]
                pp, bp = pTs[pi % 4], b_pTs[pi % 4]
                pi += 1
                self.mm(ps_[0:n, q0:T], knb[:, sl_, t * 128:t * 128 + n], QN[:, h, q0:T], True, False, [b_knb[sl_], b_qn], [bs_])
                self.mm(ps_[0:n, q0:T], krall[:, kro_ + t * 128:kro_ + t * 128 + n], QR[:, h, q0:T], False, True, [b_kr, b_qr], [bs_])
                pend.append((ps_, bs_, pp, bp, sl_, t, n, q0, diag, biascol))
                if len(pend) > DEPTH_:
                    finish(pend.pop(0))
            while pend:
                finish(pend.pop(0))
            self.cp(daccb[:, 0:T], dacc[:, 0:T], [b_da], [b_da], 'act')
            self.mm(pd[:, 0:T], self.ones_b[:, :], daccb[:, 0:T], True, True, [b_da, self.b_const], [bd])
            self.recip(rd[:, 0:T], pd[:, 0:T], [bd], [b_rd])
            self.tt(QN[:, h, 0:T], po[:, 0:T], rd[:, 0:T], ALU.mult, [bo, b_rd], [b_qn])
        k.barrier()
        self.linear(d['b_w_o'][j], [(c * 128, 128) for c in range(KC)], [(0, KC)], lambda c: (QN[:, c, 0:T], b_qn), T,
                    self.resid_evac(), extra_R=[self.b_x])


_PROG = None


def _rope_tables():
    half = ROPE // 2
    inv = (10000.0 ** (-np.arange(half, dtype=np.float32) / half)).astype(np.float32)
    pos = np.concatenate([np.arange(SEQ), PAST + np.arange(DS)]).astype(np.float32)
    ang = pos[None, :] * inv[:, None]
    c, s = np.cos(ang).astype(np.float32), np.sin(ang).astype(np.float32)
    return np.concatenate([c, c], 0), np.concatenate([-s, s], 0)


def _core_mask(r):
    m = np.zeros((40,), np.float32)
    for i in range(4):
        m[i] = 0.0 if i < r else NEG
        m[4 + i] = 1.0 if i < r else 0.0
        for l in range(4):
            m[8 + 4 * i + l] = 1.0 if (i < l < r) else 0.0
        m[24 + i] = 1.0 if i == r - 1 else 0.0
        m[32 + i] = 0.0 if i < r else -30000.0
    m[28] = 1.0 if r == 0 else 0.0
    return np.ascontiguousarray(np.broadcast_to(m[None, :], (128, 40)))


def kernel(**inp):
    global _PROG
    if _PROG is None:
        _PROG = Prog()
    prog = _PROG
    f = lambda a: np.ascontiguousarray(np.asarray(a, dtype=np.float32))
    cos2, sin2 = _rope_tables()
    shared = {}
    for n in ['norm_mix', 'norm_ffn', 'a_w_in', 'a_b_gate', 'a_w_out', 'kv_w_down', 'kv_w_up', 'b_w_dq', 'b_g_cq', 'b_w_uq',
              'b_g_qn', 'b_g_qr', 'b_w_o', 'f_w_up', 'f_conv_w', 'f_conv_b', 'f_w_down']:
        shared[n] = f(inp[n])
    shared['a_g_head'] = f(inp['a_g_head']).reshape(NA, H * DV)
    for n in ['kv_norm', 'kv_g_c', 'kv_g_r', 'kv_g_kn']:
        shared[n] = f(inp[n]).reshape(1, -1)
    in_maps = []
    for c in range(8):
        g, r = c // 4, c % 4
        m = dict(shared)
        segs = [rd * 4 + r for rd in range(NR)]
        m['xp'] = f(np.concatenate([inp['x_prompt'][g][sg * SEG:(sg + 1) * SEG] for sg in segs], 0))
        cols = np.concatenate([np.arange(sg * SEG, (sg + 1) * SEG) for sg in segs] + [SEQ + np.arange(DS)])
        m['cos2'] = f(cos2[:, cols]); m['sin2'] = f(sin2[:, cols])
        m['cmask'] = _core_mask(r)
        m['xs'] = f(inp['x_sample'][c])
        m['ckv'] = f(inp['cache_ckv'][c]); m['kpe'] = f(inp['cache_kpe'][c])
        m['sC'] = f(inp['state_C'][:, c]); m['sn'] = f(inp['state_n'][:, c]).reshape(NA, H * DK); m['sm'] = f(inp['state_m'][:, c])
        m['sconv'] = f(inp['state_conv'][:, c])
        in_maps.append(m)
    res = run_bass_kernel_spmd(prog.nc, in_maps, core_ids=list(range(8))).results

    def seqcat(n):
        out = []
        for g in range(2):
            parts = [None] * (4 * NR)
            for r in range(4):
                for rd in range(NR):
                    parts[rd * 4 + r] = res[g * 4 + r][n][rd * SEG:(rd + 1) * SEG]
            out.append(np.concatenate(parts, 0))
        return np.stack(out, 0)
    pc = [3, 7]
    st = lambda n, cores, ax: np.stack([res[c][n] for c in cores], axis=ax)
    allc = list(range(8))
    rs = lambda a: a.reshape(a.shape[0], a.shape[1], H, DK)
    return (seqcat('yp'), st('ys', allc, 0), seqcat('p_ckv'), seqcat('p_kpe'),
            st('pC', pc, 1), rs(st('pn', pc, 1)), st('pm', pc, 1), st('pconv', pc, 1),
            st('s_ckv', allc, 0), st('s_kpe', allc, 0), st('sCo', allc, 1), rs(st('sno', allc, 1)), st('smo', allc, 1),
            st('sconvo', allc, 1))
```
